# Optimizing a Trainium2 kernel written in Bass

```python
import jax, jax.numpy as jnp
from jax import lax
import numpy as np

D_MODEL = 1024
BATCH = 16
SEQ = 256
DEPTH = 1
DEC_BATCH = 4
DEC_SEQ = 4096
PAST_LEN = 512

GRID_W = 64
H_A = 4
DK_A = 128
DV_A = 128
W_A = H_A * DV_A
H_B = 4
DK_B = 128
DV_B = 128
W_B = H_B * DV_B
QKV_B = 2 * H_B * DK_B + W_B
CONV_W = 3
CHUNK = 64
D_FF = -(-(8 * D_MODEL) // (3 * 256)) * 256
EPS = 1e-6
PROJ_SIZES = (H_A * DK_A, H_A * DK_A, H_A * DK_A, W_A, W_A, H_B * DK_B, H_B * DK_B, W_B, W_B, 2 * H_B, 2 * H_B)
D_PROJ = sum(PROJ_SIZES)

kernel_name = 'hybrid_hgrn2_gdn_diffusion_step'


def rms_norm(x, w):
    xf = x.astype(jnp.float32)
    y = xf * lax.rsqrt(jnp.mean(xf * xf, axis=-1, keepdims=True) + EPS)
    return (y * w.astype(jnp.float32)).astype(x.dtype)


def head_rms_norm(o, w):
    o = o * lax.rsqrt(jnp.mean(o * o, axis=-1, keepdims=True) + EPS)
    return o.reshape(o.shape[0], o.shape[1], -1) * w.astype(jnp.float32)


def l2norm(x):
    x = x.astype(jnp.float32)
    return x * lax.rsqrt(jnp.sum(x * x, axis=-1, keepdims=True) + EPS)


def _flip(a):
    return jnp.flip(a, axis=1)


def _chunks(a):
    B, T, H, d = a.shape
    return a.reshape(B, T // CHUNK, CHUNK, H, d).transpose(0, 3, 1, 2, 4).astype(jnp.float32)


def _unchunks(o):
    B, H, N, C, d = o.shape
    return o.transpose(0, 2, 3, 1, 4).reshape(B, N * C, H, d)


def to_columns(a, rows):
    B, T, C = a.shape
    return a.reshape(B, rows, GRID_W, C).transpose(0, 2, 1, 3)


def from_columns(a, rows):
    B, T, C = a.shape
    return a.reshape(B, GRID_W, rows, C).transpose(0, 2, 1, 3).reshape(B, T, C)


def centred_conv(x, w):
    pad = CONV_W // 2
    L = x.shape[-2]
    xp = jnp.pad(x, [(0, 0)] * (x.ndim - 2) + [(pad, pad), (0, 0)])
    return sum(xp[..., j:j + L, :] * w[j] for j in range(CONV_W))


def hgrn2_scan(q, k, v, g, s0):
    q, k, v, g = (_chunks(a) for a in (q, k, v, g))
    G = jnp.cumsum(g, axis=-2)
    G_last = G[..., -1:, :]
    q_dec = q * jnp.exp(G)
    k_dec = k * jnp.exp(-G)
    k_tail = k * jnp.exp(G_last - G)
    causal = jnp.tril(jnp.ones((CHUNK, CHUNK), dtype=bool))
    attn = jnp.where(causal, jnp.einsum('bhntk,bhnsk->bhnts', q_dec, k_dec), 0.0)
    o_intra = jnp.einsum('bhnts,bhnsv->bhntv', attn, v)
    u = jnp.einsum('bhnsk,bhnsv->bhnkv', k_tail, v)
    decay = jnp.exp(G_last[..., 0, :])

    def step(s, inp):
        d_c, u_c = inp
        return d_c[..., None] * s + u_c, s

    s_fin, s_in = lax.scan(step, s0.astype(jnp.float32), (jnp.moveaxis(decay, 2, 0), jnp.moveaxis(u, 2, 0)))
    s_in = jnp.moveaxis(s_in, 0, 2)
    o = o_intra + jnp.einsum('bhntk,bhnkv->bhntv', q_dec, s_in)
    return _unchunks(o), s_fin


def gated_delta_scan(q, k, v, beta, g, s0):
    dv = v.shape[-1]
    q, k, v = (_chunks(a) for a in (q, k, v))
    beta, g = (_chunks(a[..., None])[..., 0] for a in (beta, g))
    G = jnp.cumsum(g, axis=-1)
    causal = jnp.tril(jnp.ones((CHUNK, CHUNK), dtype=bool))
    strict = jnp.tril(jnp.ones((CHUNK, CHUNK), dtype=bool), -1)
    diff = G[..., :, None] - G[..., None, :]
    decay_mask = jnp.where(causal, jnp.exp(jnp.where(causal, diff, 0.0)), 0.0)
    k_beta = k * beta[..., None]
    lower = jnp.where(strict, jnp.einsum('bhntk,bhnsk->bhnts', k_beta, k) * decay_mask, 0.0)
    rhs = jnp.concatenate([v * beta[..., None], k_beta * jnp.exp(G)[..., None]], axis=-1)
    sol = lax.linalg.triangular_solve(lower + jnp.eye(CHUNK, dtype=jnp.float32), rhs,
                                      left_side=True, lower=True, unit_diagonal=True)
    u, w = sol[..., :dv], sol[..., dv:]
    attn = jnp.einsum('bhntk,bhnsk->bhnts', q, k) * decay_mask
    q_dec = q * jnp.exp(G)[..., None]
    G_last = G[..., -1:]
    k_tail = k * jnp.exp(G_last - G)[..., None]
    decay = jnp.exp(G_last[..., 0])

    def step(s, inp):
        u_c, w_c, attn_c, q_c, kt_c, d_c = inp
        v_new = u_c - jnp.einsum('bhck,bhkv->bhcv', w_c, s)
        o_c = jnp.einsum('bhtk,bhkv->bhtv', q_c, s) + jnp.einsum('bhts,bhsv->bhtv', attn_c, v_new)
        s = s * d_c[..., None, None] + jnp.einsum('bhsk,bhsv->bhkv', kt_c, v_new)
        return s, o_c

    xs = tuple(jnp.moveaxis(a, 2, 0) for a in (u, w, attn, q_dec, k_tail, decay))
    s_fin, o = lax.scan(step, s0.astype(jnp.float32), xs)
    return _unchunks(jnp.moveaxis(o, 0, 2)), s_fin


def _mixer(h, st_a, st_b, p, grid_rows):
    B, T, _ = h.shape
    split_at = np.cumsum(PROJ_SIZES)[:-1].tolist()
    a_q, a_ff, a_fb, a_i, a_g, b_q, b_k, b_v, b_z, b_beta, b_a = jnp.split(h @ p['w_in'], split_at, axis=-1)
    heads = lambda a, n: a.reshape(B, T, n, -1)

    q_a = heads(jax.nn.silu(a_q), H_A)
    v_a = heads(a_i, H_A)
    lb = p['lb']
    f_fwd = lb[0] + (1.0 - lb[0]) * jax.nn.sigmoid(a_ff.astype(jnp.float32))
    f_bwd = lb[1] + (1.0 - lb[1]) * jax.nn.sigmoid(a_fb.astype(jnp.float32))
    oa_f, sa_f = hgrn2_scan(q_a, heads(1.0 - f_fwd, H_A), v_a, heads(jnp.log(f_fwd), H_A), st_a[:, 0])
    oa_b, sa_b = hgrn2_scan(_flip(q_a), _flip(heads(1.0 - f_bwd, H_A)), _flip(v_a),
                            _flip(heads(jnp.log(f_bwd), H_A)), st_a[:, 1])
    o_a = head_rms_norm(oa_f + _flip(oa_b), p['hgrn_out_norm']) * jax.nn.silu(a_g.astype(jnp.float32))

    qkv = jnp.concatenate([b_q, b_k, b_v], axis=-1)
    gates = jnp.concatenate([b_beta, b_a], axis=-1)
    if grid_rows is None:
        qkv = qkv[:, None]
    else:
        qkv = to_columns(qkv, grid_rows)
        gates = to_columns(gates, grid_rows).reshape(B, T, -1)
    qkv = jax.nn.silu(centred_conv(qkv, p['conv_w'])).reshape(B, T, -1)
    q_b, k_b, v_b = jnp.split(qkv, [H_B * DK_B, 2 * H_B * DK_B], axis=-1)
    q_b = l2norm(heads(q_b, H_B)) * (DK_B ** -0.5)
    k_b = l2norm(heads(k_b, H_B))
    v_b = heads(v_b, H_B)
    gates = gates.astype(jnp.float32)
    beta = jax.nn.sigmoid(gates[..., :2 * H_B]).reshape(B, T, 2, H_B)
    g = -jnp.exp(p['A_log'].astype(jnp.float32)) * jax.nn.softplus(
        gates[..., 2 * H_B:].reshape(B, T, 2, H_B) + p['dt_bias'].astype(jnp.float32))
    ob_f, sb_f = gated_delta_scan(q_b, k_b, v_b, beta[:, :, 0], g[:, :, 0], st_b[:, 0])
    ob_b, sb_b = gated_delta_scan(_flip(q_b), _flip(k_b), _flip(v_b), _flip(beta[:, :, 1]),
                                  _flip(g[:, :, 1]), st_b[:, 1])
    o_b = ob_f + _flip(ob_b)
    if grid_rows is not None:
        o_b = from_columns(o_b.reshape(B, T, -1), grid_rows).reshape(B, T, H_B, DV_B)
    o_b = head_rms_norm(o_b, p['gdn_out_norm']) * jax.nn.silu(b_z.astype(jnp.float32))

    y = jnp.concatenate([o_a, o_b], axis=-1).astype(h.dtype) @ p['w_out']
    return y, jnp.stack([sa_f, sa_b], axis=1), jnp.stack([sb_f, sb_b], axis=1)


def _layer(x, mod, st_a, st_b, p, grid_rows):
    sh1, sc1, g1, sh2, sc2, g2 = jnp.split(mod[:, None, :], 6, axis=-1)
    h = rms_norm(x, p['norm1']) * (1 + sc1) + sh1
    y, st_a, st_b = _mixer(h, st_a, st_b, p, grid_rows)
    x = x + g1 * y
    h = rms_norm(x, p['norm2']) * (1 + sc2) + sh2
    ff = (jax.nn.silu(h @ p['w_gate']) * (h @ p['w_up'])) @ p['w_down']
    return x + g2 * ff, st_a, st_b


def setup_inputs(seed: int = 0) -> dict:
    key = jax.random.key(seed)
    ks = jax.random.split(key, 22)
    nrm = lambda k, s, sc: jax.random.normal(k, s, jnp.float32) * sc
    dt = jnp.exp(jax.random.uniform(ks[14], (DEPTH, 2, H_B), jnp.float32, np.log(1e-3), np.log(1e-1)))
    return {
        'x_prompt': nrm(ks[0], (BATCH, SEQ, D_MODEL), 1.0),
        'x_sample': nrm(ks[1], (DEC_BATCH, DEC_SEQ, D_MODEL), 1.0),
        'c': nrm(ks[2], (DEC_BATCH, D_MODEL), 1.0),
        'state_hgrn': nrm(ks[3], (DEC_BATCH, DEPTH, 2, H_A, DK_A, DV_A), 0.5),
        'state_gdn': nrm(ks[4], (DEC_BATCH, DEPTH, 2, H_B, DK_B, DV_B), 0.5),
        'c_ctx': nrm(ks[5], (D_MODEL,), 1.0),
        'w_ada': nrm(ks[6], (DEPTH, D_MODEL, 6 * D_MODEL), 0.5 * D_MODEL ** -0.5),
        'b_ada': nrm(ks[7], (DEPTH, 6 * D_MODEL), 0.02),
        'norm1': 1.0 + nrm(ks[8], (DEPTH, D_MODEL), 0.02),
        'norm2': 1.0 + nrm(ks[9], (DEPTH, D_MODEL), 0.02),
        'w_in': nrm(ks[10], (DEPTH, D_MODEL, D_PROJ), D_MODEL ** -0.5),
        'conv_w': nrm(ks[11], (DEPTH, CONV_W, QKV_B), CONV_W ** -0.5),
        'hgrn_lb': nrm(ks[12], (DEPTH + 1, 2, H_A * DK_A), 0.1),
        'gdn_A_log': jnp.log(jax.random.uniform(ks[13], (DEPTH, 2, H_B), jnp.float32, 1.0, 16.0)),
        'gdn_dt_bias': dt + jnp.log(-jnp.expm1(-dt)),
        'hgrn_out_norm': 1.0 + nrm(ks[15], (DEPTH, W_A), 0.02),
        'gdn_out_norm': 1.0 + nrm(ks[16], (DEPTH, W_B), 0.02),
        'w_out': nrm(ks[17], (DEPTH, W_A + W_B, D_MODEL), (W_A + W_B) ** -0.5),
        'w_gate': nrm(ks[18], (DEPTH, D_MODEL, D_FF), D_MODEL ** -0.5),
        'w_up': nrm(ks[19], (DEPTH, D_MODEL, D_FF), D_MODEL ** -0.5),
        'w_down': nrm(ks[20], (DEPTH, D_FF, D_MODEL), D_FF ** -0.5),
        'norm_f': 1.0 + nrm(ks[21], (D_MODEL,), 0.02),
    }


def reference(x_prompt, x_sample, c, state_hgrn, state_gdn, c_ctx, w_ada, b_ada, norm1, norm2, w_in,
              conv_w, hgrn_lb, gdn_A_log, gdn_dt_bias, hgrn_out_norm, gdn_out_norm, w_out, w_gate, w_up,
              w_down, norm_f):
    rows = x_sample.shape[1] // GRID_W
    lb_all = jnp.cumsum(jax.nn.softmax(hgrn_lb.astype(jnp.float32), axis=0), axis=0)
    s_ctx = jax.nn.silu(c_ctx)[None]
    s_lat = jax.nn.silu(c)
    xp, xs = x_prompt, x_sample
    zeros_a = jnp.zeros((x_prompt.shape[0], 2, H_A, DK_A, DV_A), jnp.float32)
    zeros_b = jnp.zeros((x_prompt.shape[0], 2, H_B, DK_B, DV_B), jnp.float32)
    new_a, new_b = [], []
    for l in range(DEPTH):
        p = {'norm1': norm1[l], 'norm2': norm2[l], 'w_in': w_in[l], 'conv_w': conv_w[l], 'lb': lb_all[l],
             'A_log': gdn_A_log[l], 'dt_bias': gdn_dt_bias[l], 'hgrn_out_norm': hgrn_out_norm[l],
             'gdn_out_norm': gdn_out_norm[l], 'w_out': w_out[l], 'w_gate': w_gate[l], 'w_up': w_up[l],
             'w_down': w_down[l]}
        m_ctx = s_ctx @ w_ada[l] + b_ada[l]
        m_lat = s_lat @ w_ada[l] + b_ada[l]
        xp, sa, sb = _layer(xp, m_ctx, zeros_a, zeros_b, p, None)
        xs, _, _ = _layer(xs, m_lat, state_hgrn[:, l], state_gdn[:, l], p, rows)
        new_a.append(sa)
        new_b.append(sb)
    new_state_hgrn = jnp.stack(new_a, axis=1)
    new_state_gdn = jnp.stack(new_b, axis=1)
    y_prompt = rms_norm(xp, norm_f)
    y_sample = rms_norm(xs, norm_f)
    return (y_prompt, y_sample, new_state_hgrn, new_state_gdn)
```

```python
import numpy as np
import ml_dtypes
import concourse.bass as bass
import concourse.mybir as mybir
from concourse.bass_utils import run_bass_kernel_spmd
from contextlib import ExitStack

F32 = mybir.dt.float32
BF16 = mybir.dt.bfloat16
AF = mybir.ActivationFunctionType
ALU = mybir.AluOpType

D = 1024
NT = 4608
NTI = 36
NBLK = 9
NOWN = 2560
DFF = 2816
NF = 22
EPS = 1e-6
SAME_ENG_SYNC = True


class Buf:
    __slots__ = ("lw", "rd", "excl")

    def __init__(self, excl=False):
        self.lw = None
        self.rd = {}
        self.excl = excl


def bufs(n):
    return [Buf() for _ in range(n)]


NDSEM = {"sp": 40, "pool": 16, "bg": 40}
DQ_ENG = {"sp": "sp", "pool": "pool", "bg": "pool"}


class Ctx:
    def __init__(self, nc, es):
        self.nc = nc
        self.eng = {"pe": nc.tensor, "dve": nc.vector, "act": nc.scalar, "pool": nc.gpsimd, "sp": nc.sync}
        self.sem = {}
        self.cnt = {}
        for k in list(self.eng):
            self.sem[k] = es.enter_context(nc.semaphore("s_" + k))
            self.cnt[k] = 0
        self.dnames = {}
        self.drr = {}
        for q, n in NDSEM.items():
            self.dnames[q] = []
            self.drr[q] = 0
            for i in range(n):
                nm = "d%s%d" % (q, i)
                self.sem[nm] = es.enter_context(nc.semaphore("s_" + nm))
                self.cnt[nm] = 0
                self.dnames[q].append(nm)
        self.waited = {k: {} for k in self.eng}

    def _deps(self, en, reads, writes, extra=None):
        deps = {}
        if extra is not None:
            deps[extra[0]] = extra[1]
        for b in reads:
            if b.lw is not None:
                s, v = b.lw
                if deps.get(s, 0) < v:
                    deps[s] = v
            if b.excl:
                for s, v in b.rd.items():
                    if s != en and deps.get(s, 0) < v:
                        deps[s] = v
        for b in writes:
            if b.lw is not None:
                s, v = b.lw
                if deps.get(s, 0) < v:
                    deps[s] = v
            for s, v in b.rd.items():
                if deps.get(s, 0) < v:
                    deps[s] = v
        e = self.eng[en]
        w = self.waited[en]
        for s, v in deps.items():
            if v <= 0:
                continue
            if s == en and (en == "pe" or not SAME_ENG_SYNC):
                continue
            if w.get(s, 0) < v:
                e.wait_ge(self.sem[s], v)
                w[s] = v

    def barrier(self, skip=()):
        for en in self.eng:
            e = self.eng[en]
            w = self.waited[en]
            for s, v in self.cnt.items():
                if any(s.startswith(p) for p in skip):
                    continue
                if v > 0 and s != en and w.get(s, 0) < v:
                    e.wait_ge(self.sem[s], v)
                    w[s] = v

    def op(self, en, fn, reads=(), writes=(), serial=False):
        self._deps(en, reads, writes)
        if serial and self.cnt[en] > self.waited[en].get(en, 0):
            self.eng[en].wait_ge(self.sem[en], self.cnt[en])
            self.waited[en][en] = self.cnt[en]
        ins = fn(self.eng[en])
        ins.then_inc(self.sem[en], 1)
        self.cnt[en] += 1
        c = self.cnt[en]
        for b in reads:
            b.rd[en] = c
        for b in writes:
            b.lw = (en, c)
            b.rd = {}
        return ins

    def dma(self, q, out, in_, reads=(), writes=()):
        i = self.drr[q] % len(self.dnames[q])
        self.drr[q] += 1
        ds = self.dnames[q][i]
        en = DQ_ENG[q]
        self._deps(en, reads, writes, extra=(ds, self.cnt[ds]))
        ins = self.eng[en].dma_start(out=out, in_=in_)
        ins.then_inc(self.sem[ds], 16)
        self.cnt[ds] += 16
        c = self.cnt[ds]
        for b in reads:
            b.rd[ds] = c
        for b in writes:
            b.lw = (ds, c)
            b.rd = {}
        return ins


class _Stop(Exception):
    pass


STOP = [None]


def _chk(tag):
    if STOP[0] == tag:
        raise _Stop()


def build_program():
    nc = bass.Bass("TRN2", target_bir_lowering=False)
    try:
        _build_body(nc)
    except AssertionError:
        if STOP[0] is None:
            raise
    return nc


DEBUG = [False]
DUMPS = {}


def _build_body(nc):
    DUMPS.clear()
    din = lambda n, s, d=F32: nc.dram_tensor(n, list(s), d, kind="ExternalInput").ap()
    dout = lambda n, s, d=F32: nc.dram_tensor(n, list(s), d, kind="ExternalOutput").ap()
    xall = din("xall", [NT, D])
    svec = din("svec", [128, 16])
    w_ada = din("w_ada", [D, 6 * D])
    bada_fm = din("bada_fm", [128, 48])
    bada_row = din("bada_row", [1, 6 * D])
    n12_fm = din("n12_fm", [128, 16])
    normf = din("normf", [1, D])
    win_a = din("win_a", [4, D, 640])
    win_b = din("win_b", [4, D, 512])
    win_g = din("win_g", [D, 16])
    hlb_fm = din("hlb_fm", [128, 16])
    convw_fm = din("convw_fm", [128, 36])
    alog_rep = din("alog_rep", [128, 288])
    dtb_rep = din("dtb_rep", [128, 288])
    onorm_fm = din("onorm_fm", [128, 8])
    st_a = din("st_a", [2, 4, 128, 128])
    st_b = din("st_b", [2, 4, 128, 128])
    w_out = din("w_out", [D, D])
    w_gate = din("w_gate", [D, DFF])
    w_up = din("w_up", [D, DFF])
    w_down = din("w_down", [DFF, D])
    cmask = din("cmask", [128, 7 * 128])
    csel = din("csel", [128, 256])
    cseg = din("cseg", [128, 512])
    y_own = dout("y_own", [NOWN, D])
    ns_a = dout("ns_a", [2, 2, 4, 128, 128])
    ns_b = dout("ns_b", [2, 2, 4, 128, 128])
    hT_scr = nc.dram_tensor("hT_scr", [128, 8, NT], BF16, kind="Internal").ap()
    oT_scr = nc.dram_tensor("oT_scr", [8, 128, NT], BF16, kind="Internal").ap()
    wo_bf = nc.dram_tensor("wo_bf", [D, D], BF16, kind="Internal").ap()
    wg_bf = nc.dram_tensor("wg_bf", [D, DFF], BF16, kind="Internal").ap()
    wu_bf = nc.dram_tensor("wu_bf", [D, DFF], BF16, kind="Internal").ap()
    wd_bf = nc.dram_tensor("wd_bf", [DFF, D], BF16, kind="Internal").ap()

    with ExitStack() as es:
        c = Ctx(nc, es)
        SB = lambda st, n, s, d=F32: st.enter_context(nc.sbuf_tensor(n, list(s), d))

        def dump(name, ap, rb):
            if not DEBUG[0] or name in DUMPS:
                return
            shp = list(ap.shape)
            t = nc.dram_tensor("dbg_" + name, shp, ap.dtype, kind="ExternalOutput").ap()
            DUMPS[name] = shp
            c.dma("sp", t, ap, reads=rb, writes=[Buf()])
        PB = [es.enter_context(nc.psum_tensor("pb%d" % i, [128, 512], F32)) for i in range(7)]
        PBb = [Buf(True) for _ in range(7)]
        PTh = [None, None]
        cm = SB(es, "cm", [128, 7, 128]); bcm = Buf()
        c.dma("sp", cm[:].rearrange("p a b -> p (a b)"), cmask, writes=[bcm])
        IDN, UT, SLO, LT, SUP, BLK, ONES = [cm[:, i, :] for i in range(7)]
        sel = SB(es, "sel", [128, 2, 128]); bsel = Buf()
        c.dma("sp", sel[:].rearrange("p a b -> p (a b)"), csel, writes=[bsel])
        seg = SB(es, "seg", [128, 512]); bseg = Buf()
        c.dma("sp", seg[:], cseg, writes=[bseg])
        idb = SB(es, "idb", [128, 128], BF16); bidb = Buf()
        c.op("dve", lambda e: e.tensor_copy(out=idb[:], in_=IDN), reads=[bcm], writes=[bidb])
        modp = SB(es, "modp", [128, 64]); bmod = Buf()
        lbt = SB(es, "lbt", [128, 16]); blb = Buf()
        cw = SB(es, "cw", [128, 36]); bcw = Buf()
        onw = SB(es, "onw", [128, 8]); bonw = Buf()
        c.dma("sp", cw[:], convw_fm, writes=[bcw])
        c.dma("sp", onw[:], onorm_fm, writes=[bonw])
        epsb = SB(es, "epsb", [128, 1]); beps = Buf()
        c.op("dve", lambda e: e.memset(epsb[:], EPS), writes=[beps])

        def rstd_from(ss_ap, out_ap, scale, rb, wb, n=1):
            c.op("act", lambda e: e.activation(out=out_ap, in_=ss_ap, func=AF.Ln, scale=scale, bias=epsb[:, 0:1]),
                 reads=rb + [beps], writes=wb)
            c.op("act", lambda e: e.activation(out=out_ap, in_=out_ap, func=AF.Exp, scale=-0.5), reads=wb, writes=wb)

        try:
            with ExitStack() as s0:
                sv = SB(s0, "sv", [128, 16]); bsv = Buf()
                c.dma("sp", sv[:], svec, writes=[bsv])
                ssil = SB(s0, "ssil", [128, 8, 2]); bss = Buf()
                c.op("act", lambda e: e.activation(out=ssil[:].rearrange("p k v -> p v k"), in_=sv[:].rearrange("p (v k) -> p v k", v=2), func=AF.Silu),
                     reads=[bsv], writes=[bss])
                bfm = SB(s0, "bfm", [128, 48]); bbfm = Buf()
                c.dma("sp", bfm[:], bada_fm, writes=[bbfm])
                n12 = SB(s0, "n12", [128, 16]); bn12 = Buf()
                c.dma("sp", n12[:], n12_fm, writes=[bn12])
                mfm = SB(s0, "mfm", [128, 48, 2]); bmfm = Buf()
                wad = [SB(s0, "wad%d" % i, [128, 8, 512]) for i in range(2)]
                bwad = bufs(2)
                wv = w_ada.rearrange("(k p) n -> p k n", p=128)
                ci = 0
                for cb in (0, 1, 2, 3, 6, 7, 8, 9):
                    t = wad[ci % 2]; bt = bwad[ci % 2]; ci += 1
                    c.dma("sp", t[:], wv[:, :, cb * 512:(cb + 1) * 512], writes=[bt])
                    pb = PB[ci % 2]; bpb = PBb[ci % 2]
                    for jj in range(4):
                        for k in range(8):
                            c.op("pe", lambda e: e.matmul(pb[:, jj * 2:jj * 2 + 2], lhsT=t[:, k, jj * 128:(jj + 1) * 128], rhs=ssil[:, k, :],
                                                          start=(k == 0), stop=(k == 7)), reads=[bt, bss], writes=[bpb])
                    j0 = cb * 4
                    c.op("dve", lambda e: e.tensor_tensor(out=mfm[:, j0:j0 + 4, :], in0=pb[:, 0:8].rearrange("p (j v) -> p j v", v=2),
                                                          in1=bfm[:, j0:j0 + 4].unsqueeze(2).broadcast_to([128, 4, 2]), op=ALU.add),
                         reads=[bpb, bbfm], writes=[bmfm])
                for which, (jsh, jsc, noff) in enumerate(((0, 8, 0), (24, 32, 8))):
                    for v in range(2):
                        o0 = (which * 2 + v) * 16
                        c.op("dve", lambda e: e.scalar_tensor_tensor(out=modp[:, o0:o0 + 8], in0=mfm[:, jsc:jsc + 8, v], scalar=1.0, in1=n12[:, noff:noff + 8],
                                                                     op0=ALU.add, op1=ALU.mult), reads=[bmfm, bn12], writes=[bmod])
                        c.op("dve", lambda e: e.tensor_copy(out=modp[:, o0 + 8:o0 + 16], in_=mfm[:, jsh:jsh + 8, v]), reads=[bmfm], writes=[bmod])
                hl = SB(s0, "hl", [128, 16]); bhl = Buf()
                c.dma("sp", hl[:], hlb_fm, writes=[bhl])
                c.op("dve", lambda e: e.tensor_tensor(out=hl[:, 0:8], in0=hl[:, 0:8], in1=hl[:, 8:16], op=ALU.subtract), reads=[bhl], writes=[bhl])
                c.op("act", lambda e: e.activation(out=lbt[:, 0:8], in_=hl[:, 0:8], func=AF.Sigmoid), reads=[bhl], writes=[blb])
                c.op("act", lambda e: e.activation(out=lbt[:, 8:16], in_=hl[:, 0:8], func=AF.Sigmoid, scale=-1.0), reads=[bhl], writes=[blb])

            c.barrier()
            dump('modp', modp[:], [bmod])
            dump('lbt', lbt[:], [blb])
            _chk('p0')
            A1 = lambda which, v, k: modp[:, (which * 2 + v) * 16 + k:(which * 2 + v) * 16 + k + 1]
            SH = lambda which, v, k: modp[:, (which * 2 + v) * 16 + 8 + k:(which * 2 + v) * 16 + 9 + k]

            bhT = bufs(NTI)
            boT = bufs(8)

            def norm_mod_transpose(st_pool, xt, bxt, which, v, hdst, bh, tmpn):
                ss, bs_, xn, bxn, junk, bj = tmpn
                PT, PTb = PTh
                c.op("pool", lambda e: e.memset(ss[:, 0:1], 0.0), writes=[bs_])
                c.op("act", lambda e: e.activation(out=junk[:], in_=xt, func=AF.Square, accum_out=ss[:, 0:1]), reads=[bxt], writes=[bj, bs_])
                rstd_from(ss[:, 0:1], ss[:, 1:2], 1.0 / D, [bs_], [bs_])
                c.op("act", lambda e: e.activation(out=xn[:], in_=xt, func=AF.Copy, scale=ss[:, 1:2]), reads=[bxt, bs_], writes=[bxn])
                if st_pool == "norm_only":
                    return
                mod_transpose(which, v, hdst, bh, tmpn)

            def mod_transpose(which, v, hdst, bh, tmpn):
                ss, bs_, xn, bxn, junk, bj = tmpn
                PT, PTb = PTh
                for k in range(8):
                    c.op("pe", lambda e: e.transpose(PT[:, k * 128:(k + 1) * 128], xn[:, k * 128:(k + 1) * 128], idb[:]), reads=[bxn, bidb], writes=[PTb])
                for k in range(8):
                    c.op("dve", lambda e: e.tensor_scalar(out=hdst[:, k, :], in0=PT[:, k * 128:(k + 1) * 128], scalar1=A1(which, v, k), scalar2=SH(which, v, k),
                                                          op0=ALU.mult, op1=ALU.add), reads=[PTb, bmod], writes=[bh])

            with ExitStack() as s2:
                SLOT = [SB(s2, "slot%d" % i, [128, NT]) for i in range(5)]
                SLB = [bufs(NTI) for _ in range(5)]
                ZT = SB(s2, "zt", [128, NT], BF16); bZT = bufs(NTI)
                gsm = {n: SB(s2, "g_" + n, [128, 2, NTI, 4]) for n in ("beta", "gg", "gc", "glt", "gl0", "gl1", "eg", "beg", "ekt", "dec0", "dec1")}
                bgs = Buf()
                with ExitStack() as s1:
                    PTh[0] = s1.enter_context(nc.psum_tensor("pbt1", [128, 1024], BF16)); PTh[1] = Buf(True)
                    GTM = SB(s1, "gtm", [128, NTI, 16]); bGTM = Buf()
                    W1 = 4
                    xts = [SB(s1, "xt%d" % i, [128, D]) for i in range(W1)]; bxts = bufs(W1)
                    hts = [SB(s1, "ht%d" % i, [128, 8, 128], BF16) for i in range(W1)]; bhts = bufs(W1)
                    ss1 = [SB(s1, "ss1_%d" % i, [128, 2]) for i in range(W1)]; bss1 = bufs(W1)
                    xn1 = [SB(s1, "xn1_%d" % i, [128, D], BF16) for i in range(W1)]; bxn1 = bufs(W1)
                    jk1 = [SB(s1, "jk1_%d" % i, [128, D], BF16) for i in range(W1)]; bjk1 = bufs(W1)
                    PT1 = [PTh[0]] * 2
                    bPT1 = [PTh[1]] * 2
                    wgf = SB(s1, "wgf", [128, 8, 16]); bwgf = Buf()
                    c.dma("sp", wgf[:], win_g.rearrange("(k p) n -> p k n", p=128), writes=[bwgf])
                    wgb = SB(s1, "wgb", [128, 8, 16], BF16); bwgb = Buf()
                    c.op("dve", lambda e: e.tensor_copy(out=wgb[:], in_=wgf[:]), reads=[bwgf], writes=[bwgb])
                    gT = SLOT[4]
                    bgT = SLB[4]

                    def p1_unit(i, sl):
                        xt = xts[sl]; bx = bxts[sl]; ht = hts[sl]; bh = bhts[sl]
                        ss = ss1[sl]; bs_ = bss1[sl]; xn = xn1[sl]; bxn = bxn1[sl]; junk = jk1[sl]; bj = bjk1[sl]
                        PT = PT1[sl % 2]; PTb = bPT1[sl % 2]
                        v = 1 if i < 32 else 0
                        c.dma("sp", xt[:], xall[i * 128:(i + 1) * 128, :], writes=[bx])
                        c.op("pool", lambda e: e.memset(ss[:, 0:1], 0.0), writes=[bs_])
                        yield
                        c.op("act", lambda e: e.activation(out=junk[:], in_=xt[:], func=AF.Square, accum_out=ss[:, 0:1]), reads=[bx], writes=[bj, bs_])
                        yield
                        c.op("act", lambda e: e.activation(out=ss[:, 1:2], in_=ss[:, 0:1], func=AF.Ln, scale=1.0 / D, bias=epsb[:, 0:1]), reads=[bs_, beps], writes=[bs_])
                        yield
                        c.op("act", lambda e: e.activation(out=ss[:, 1:2], in_=ss[:, 1:2], func=AF.Exp, scale=-0.5), reads=[bs_], writes=[bs_])
                        yield
                        c.op("act", lambda e: e.activation(out=xn[:], in_=xt[:], func=AF.Copy, scale=ss[:, 1:2]), reads=[bx, bs_], writes=[bxn])
                        yield
                        for k in range(8):
                            c.op("pe", lambda e: e.transpose(PT[:, k * 128:(k + 1) * 128], xn[:, k * 128:(k + 1) * 128], idb[:]), reads=[bxn, bidb], writes=[PTb])
                        for k in range(8):
                            eng_ = "dve" if k % 2 == 0 else "pool"
                            if eng_ == "pool":
                                eng_ = "dve"
                            c.op(eng_, lambda e: e.tensor_scalar(out=ht[:, k, :], in0=PT[:, k * 128:(k + 1) * 128], scalar1=A1(0, v, k), scalar2=SH(0, v, k),
                                                                  op0=ALU.mult, op1=ALU.add), reads=[PTb, bmod], writes=[bh])
                        yield
                        pg = PB[2 + sl]; bpg = PBb[2 + sl]
                        for k in range(8):
                            c.op("pe", lambda e: e.matmul(pg[0:16, 0:128], lhsT=wgb[:, k, :], rhs=ht[:, k, :], start=(k == 0), stop=(k == 7)),
                                 reads=[bwgb, bh], writes=[bpg])
                        c.dma("pool", hT_scr[:, :, i * 128:(i + 1) * 128], ht[:], reads=[bh], writes=[bhT[i]])
                        yield
                        c.op("act", lambda e: e.activation(out=gT[0:16, i * 128:(i + 1) * 128], in_=pg[0:16, 0:128], func=AF.Copy), reads=[bpg], writes=[bgT[i]])

                    pending = list(range(NTI))
                    active = []
                    free = list(range(W1))
                    while pending or active:
                        while pending and free:
                            i = pending.pop(0); sl = free.pop(0)
                            active.append((sl, p1_unit(i, sl)))
                        nxt_active = []
                        for sl, g in active:
                            try:
                                next(g)
                                nxt_active.append((sl, g))
                            except StopIteration:
                                free.append(sl)
                        active = nxt_active
                    gTc = SLOT[3]; bgTc = SLB[3]
                    c.op("act", lambda e: e.activation(out=gTc[0:16, 0:4096].rearrange("g (w r) -> g w r", r=64),
                                                       in_=gT[0:16, 0:4096].rearrange("g (r w) -> g w r", w=64), func=AF.Copy), reads=bgT[0:32], writes=bgTc[0:32])
                    c.op("act", lambda e: e.activation(out=gTc[0:16, 4096:NT], in_=gT[0:16, 4096:NT], func=AF.Copy), reads=bgT[32:], writes=bgTc[32:])
                    for j in range(NTI):
                        pg = PB[2 + j % 2]; bpg = PBb[2 + j % 2]
                        src = gTc[0:16, j * 128:(j + 1) * 128]
                        c.op("pe", lambda e: e.transpose(pg[:, 0:16], src, IDN[0:16, 0:16]), reads=[bgTc[j], bcm], writes=[bpg])
                        c.op("act", lambda e: e.activation(out=GTM[:, j, :], in_=pg[:, 0:16], func=AF.Copy), reads=[bpg], writes=[bGTM])
                    al = SB(s1, "al", [128, 2, NTI, 4]); dtb = SB(s1, "dtb", [128, 2, NTI, 4]); bal = Buf()
                    c.dma("sp", al[:].rearrange("p a b c -> p (a b c)"), alog_rep, writes=[bal])
                    c.dma("sp", dtb[:].rearrange("p a b c -> p (a b c)"), dtb_rep, writes=[bal])
                    gview = lambda lo: GTM[:, :, lo:lo + 8].rearrange("p t (d h) -> p d t h", d=2)
                    c.op("act", lambda e: e.activation(out=gsm["beta"][:], in_=gview(0), func=AF.Sigmoid), reads=[bGTM], writes=[bgs])
                    c.op("dve", lambda e: e.tensor_tensor(out=gsm["gg"][:], in0=gview(8), in1=dtb[:], op=ALU.add), reads=[bGTM, bal], writes=[bgs])
                    c.op("act", lambda e: e.activation(out=gsm["gg"][:], in_=gsm["gg"][:], func=AF.Exp), reads=[bgs], writes=[bgs])
                    c.op("act", lambda e: e.activation(out=gsm["gg"][:], in_=gsm["gg"][:], func=AF.Ln, bias=1.0), reads=[bgs], writes=[bgs])
                    c.op("act", lambda e: e.activation(out=al[:], in_=al[:], func=AF.Exp), reads=[bal], writes=[bal])
                    c.op("dve", lambda e: e.scalar_tensor_tensor(out=gsm["gg"][:], in0=gsm["gg"][:], scalar=-1.0, in1=al[:], op0=ALU.mult, op1=ALU.mult),
                         reads=[bgs, bal], writes=[bgs])
                    fl = lambda t: t[:].rearrange("p d t h -> p (d t h)")
                    pq = PB[4]; bpq = PBb[4]
                    for d in range(2):
                        rhs = gsm["gg"][:, d].rearrange("p t h -> p (t h)")
                        c.op("pe", lambda e: e.matmul(pq[:, d * 144:(d + 1) * 144], lhsT=(UT if d == 0 else LT), rhs=rhs, start=True, stop=True),
                             reads=[bgs, bcm], writes=[bpq])
                    c.op("dve", lambda e: e.tensor_copy(out=fl(gsm["gc"]), in_=pq[:, 0:288]), reads=[bpq], writes=[bgs])
                    for nm, lh, bl in (("glt", BLK, bcm), ("gl0", sel[:, 0, :], bsel), ("gl1", sel[:, 1, :], bsel)):
                        c.op("pe", lambda e: e.matmul(pq[:, 0:288], lhsT=lh, rhs=fl(gsm["gg"]), start=True, stop=True), reads=[bgs, bl], writes=[bpq])
                        c.op("dve", lambda e: e.tensor_copy(out=fl(gsm[nm]), in_=pq[:, 0:288]), reads=[bpq], writes=[bgs])
                    c.op("act", lambda e: e.activation(out=fl(gsm["eg"]), in_=fl(gsm["gc"]), func=AF.Exp), reads=[bgs], writes=[bgs])
                    c.op("dve", lambda e: e.tensor_tensor(out=fl(gsm["beg"]), in0=fl(gsm["beta"]), in1=fl(gsm["eg"]), op=ALU.mult), reads=[bgs], writes=[bgs])
                    c.op("dve", lambda e: e.tensor_tensor(out=fl(gsm["ekt"]), in0=fl(gsm["glt"]), in1=fl(gsm["gc"]), op=ALU.subtract), reads=[bgs], writes=[bgs])
                    c.op("act", lambda e: e.activation(out=fl(gsm["ekt"]), in_=fl(gsm["ekt"]), func=AF.Exp), reads=[bgs], writes=[bgs])
                    c.op("act", lambda e: e.activation(out=fl(gsm["dec0"]), in_=fl(gsm["gl0"]), func=AF.Exp), reads=[bgs], writes=[bgs])
                    c.op("act", lambda e: e.activation(out=fl(gsm["dec1"]), in_=fl(gsm["gl1"]), func=AF.Exp), reads=[bgs], writes=[bgs])

                c.barrier()
                for _n in gsm:
                    dump('g_' + _n, gsm[_n][:], [bgs])
                _chk('p1')
                bWo, bWg, bWu, bWd = bufs(8), bufs(8), bufs(8), bufs(NF)

                def issue_bg_precast():
                    for k in range(8):
                        c.dma("bg", wo_bf[k * 128:(k + 1) * 128, :], w_out[k * 128:(k + 1) * 128, :], writes=[bWo[k]])
                    for k in range(8):
                        c.dma("bg", wg_bf[k * 128:(k + 1) * 128, :], w_gate[k * 128:(k + 1) * 128, :], writes=[bWg[k]])
                        c.dma("bg", wu_bf[k * 128:(k + 1) * 128, :], w_up[k * 128:(k + 1) * 128, :], writes=[bWu[k]])
                    for f in range(NF):
                        c.dma("bg", wd_bf[f * 128:(f + 1) * 128, :], w_down[f * 128:(f + 1) * 128, :], writes=[bWd[f]])

                PBK = 256
                NPB = NT // PBK
                hTb = [SB(s2, "htb%d" % i, [128, 8, PBK], BF16) for i in range(2)]; bhTb = bufs(2)
                wbf = SB(s2, "wbf", [128, 8, 640], BF16); bwbf = Buf()
                PB.append(s2.enter_context(nc.psum_tensor("pb7", [128, 512], F32))); PBb.append(Buf(True))
                tmp512 = [SB(s2, "tmpw%d" % i, [128, 512]) for i in range(2)]; btmp512 = bufs(2)
                Sbuf = [SB(s2, "S%d" % i, [128, 128]) for i in range(18)]; bS = bufs(18)
                small = SB(s2, "small", [128, 16]); bsmall = bufs(4)
                s2a = ExitStack()
                TMPN = 0
                tmp = [SB(s2a, "tmp%d" % i, [128, 256]) for i in range(TMPN)]; btmp = bufs(TMPN)
                OF = SB(s2a, "of", [128, NT], BF16); bOF = bufs(NTI)
                seqs = [(0, 32), (32, 34), (34, 36)]
                hblk_state = [0]

                def load_hblk(blk):
                    i = hblk_state[0] % 2; hblk_state[0] += 1
                    c.dma("sp", hTb[i][:], hT_scr[:, :, blk * PBK:(blk + 1) * PBK], reads=bhT[blk * 2:blk * 2 + 2], writes=[bhTb[i]])
                    return hTb[i], bhTb[i]

                def proj_fm(ht, bh, col0, pbi):
                    pb = PB[pbi]; bpb = PBb[pbi]
                    for k in range(8):
                        c.op("pe", lambda e: e.matmul(pb[:, 0:PBK], lhsT=wbf[:, k, col0:col0 + 128], rhs=ht[:, k, :], start=(k == 0), stop=(k == 7)),
                             reads=[bwbf, bh], writes=[bpb])
                    return pb[:, 0:PBK], bpb

                rr = [0]

                def T(n=1):
                    i = rr[0] % TMPN; rr[0] += 1
                    return tmp[i], btmp[i]

                WUA = 6
                HNAMES = ("teg", "tqd", "tkd", "tkt", "tkt2", "tktm", "tat", "tos", "tsq")
                HW_ = {"tqd": 128, "tkd": 128, "tktm": 128, "tat": 128, "tkt": 128, "tkt2": 128, "tsq": 128}
                HBF = ("tqd", "tkd", "tkt2", "tktm", "tat", "tsq")
                hsets = []
                for w_ in range(WUA):
                    hsets.append({n: (SB(s2a, "hu%d_%s" % (w_, n), [128, HW_.get(n, 256)], BF16 if n in HBF else F32), Buf()) for n in HNAMES})
                Sbf = [SB(s2a, "Sbf%d" % i, [128, 128], BF16) for i in range(18)]; bSbf = bufs(18)
                onesb = SB(s2a, "onesb", [128, 128], BF16); bonesb = Buf()
                c.op("dve", lambda e: e.tensor_copy(out=onesb[:], in_=ONES), reads=[bcm], writes=[bonesb])
                VTMb = SLOT[1][:].bitcast(BF16)[:, 0:NT]
                hsm = [SB(s2a, "hsm%d" % w_, [128, 2]) for w_ in range(WUA)]; bhsm = bufs(WUA)
                GA1 = SB(s2a, "ga1", [128, NT]); bGA1 = bufs(NTI)

                def hgrn_unit(h, d, i, slot, ch, kidx, claimed, done):
                    FD, bFD = (FF, bFF) if d == 0 else (FB, bFB)
                    GA, bGA = GAs[d]
                    ts = hsets[slot]
                    pU, bU = PB[slot], PBb[slot]
                    pUb = pU[:, 0:64].bitcast(BF16)
                    MASK = UT if d == 0 else LT
                    glpos = 63 if d == 0 else 0
                    tsl = slice(i * 128, (i + 1) * 128)
                    G3 = GA[:, tsl].rearrange("p (c j) -> p c j", j=64)
                    teg, bteg = ts["teg"]; tqd, btqd = ts["tqd"]; tkd, btkd = ts["tkd"]; tkt, btkt = ts["tkt"]; tkt2, btkt2 = ts["tkt2"]
                    tktm, btktm = ts["tktm"]; tat, btat = ts["tat"]; tos, btos = ts["tos"]; tsq, btsq = ts["tsq"]
                    tdc = hsm[slot]; btdc = bhsm[slot]
                    c.op("act", lambda e: e.activation(out=teg[:, 0:128], in_=GA[:, tsl], func=AF.Exp), reads=[bGA[i]], writes=[bteg])
                    c.op("act", lambda e: e.activation(out=teg[:, 128:256], in_=GA[:, tsl], func=AF.Exp, scale=-1.0), reads=[bGA[i]], writes=[bteg])
                    c.op("dve", lambda e: e.tensor_tensor(out=tkt[:, 0:128].rearrange("p (c j) -> p c j", j=64), in0=G3[:, :, glpos:glpos + 1].broadcast_to([128, 2, 64]),
                                                          in1=G3, op=ALU.subtract), reads=[bGA[i]], writes=[btkt])
                    c.op("act", lambda e: e.activation(out=tdc[:, 0:2], in_=G3[:, :, glpos], func=AF.Exp), reads=[bGA[i]], writes=[btdc])
                    yield
                    c.op("pool", lambda e: e.tensor_tensor(out=tqd[:, 0:128], in0=QT[:, tsl], in1=teg[:, 0:128], op=ALU.mult), reads=[bQT[i], bteg], writes=[btqd])
                    c.op("pool", lambda e: e.tensor_tensor(out=tkd[:, 0:128], in0=FD[:, tsl], in1=teg[:, 128:256], op=ALU.mult), reads=[bFD[i], bteg], writes=[btkd])
                    c.op("act", lambda e: e.activation(out=tkt[:, 0:128], in_=tkt[:, 0:128], func=AF.Exp), reads=[btkt], writes=[btkt])
                    yield
                    c.op("dve", lambda e: e.tensor_tensor(out=tkt2[:, 0:128], in0=FD[:, tsl], in1=tkt[:, 0:128], op=ALU.mult), reads=[bFD[i], btkt], writes=[btkt2])
                    c.op("pe", lambda e: e.matmul(pU[:, 128:256], lhsT=tkd[:, 0:128], rhs=tqd[:, 0:128], start=True, stop=True), reads=[btkd, btqd], writes=[bU])
                    yield
                    c.op("pe", lambda e: e.transpose(pUb, tkt2[:, 0:128], idb[:]), reads=[btkt2, bidb], writes=[bU])
                    yield
                    c.op("dve", lambda e: e.tensor_tensor(out=tat[:, 0:128], in0=pU[:, 128:256], in1=MASK, op=ALU.mult), reads=[bU, bcm], writes=[btat])
                    c.op("dve", lambda e: e.tensor_copy(out=tktm[:, 0:128], in_=pUb), reads=[bU], writes=[btktm])
                    yield
                    corder = (0, 1) if d == 0 else (1, 0)
                    for ci_, cc in enumerate(corder):
                        pb_ = cc * 64
                        ucol = 384 if ci_ == 0 else 0
                        c.op("pe", lambda e: e.matmul(pU[:, ucol:ucol + 128], lhsT=tktm[pb_:pb_ + 64, 0:128], rhs=VTMb[pb_:pb_ + 64, tsl], start=True, stop=True),
                             reads=[btktm, bVTM[i]], writes=[bU], serial=(ci_ == 1))
                        if ci_ == 0:
                            yield
                    yield
                    while ch["turn"] != kidx:
                        yield
                    c.op("pe", lambda e: e.matmul(pU[:, 256:384], lhsT=VTMb[:, tsl], rhs=tat[:, 0:128], start=True, stop=False), reads=[bVTM[i], btat], writes=[bU])
                    for ci_, cc in enumerate(corder):
                        pb_ = cc * 64
                        S, bSc = ch["S"], ch["bS"]
                        Sb_, bSb_ = ch["Sb"], ch["bSb"]
                        ucol = 384 if ci_ == 0 else 0
                        c.op("pe", lambda e: e.matmul(pU[:, 256 + pb_:256 + pb_ + 64], lhsT=Sb_[:], rhs=tqd[:, pb_:pb_ + 64], start=False, stop=(ci_ == 1)),
                             reads=[bSb_, btqd], writes=[bU])
                        ch["si"] += 1
                        Sn, bSn = ch["bufs"][ch["si"] % 3]
                        Sbn, bSbn = ch["bbufs"][ch["si"] % 3]
                        c.op("dve", lambda e: e.scalar_tensor_tensor(out=Sn[:], in0=S[:], scalar=tdc[:, cc:cc + 1], in1=pU[:, ucol:ucol + 128],
                                                                     op0=ALU.mult, op1=ALU.add), reads=[bSc, btdc, bU], writes=[bSn])
                        c.op("pool", lambda e: e.tensor_copy(out=Sbn[:], in_=Sn[:]), reads=[bSn], writes=[bSbn])
                        ch["S"], ch["bS"] = Sn, bSn
                        ch["Sb"], ch["bSb"] = Sbn, bSbn
                        yield
                    if ch["last"] == i and ch["out"] is not None:
                        c.dma("sp", ch["out"], ch["S"][:], reads=[ch["bS"]], writes=[Buf()])
                    ch["turn"] += 1
                    if not claimed[i]:
                        claimed[i] = True
                        c.op("act", lambda e: e.activation(out=OF[:, tsl], in_=pU[:, 256:384], func=AF.Copy), reads=[bU], writes=[bOF[i]])
                        done[i] = True
                        return
                    while not done[i]:
                        yield
                    c.op("dve", lambda e: e.tensor_tensor(out=tos[:, 0:128], in0=pU[:, 256:384], in1=OF[:, tsl], op=ALU.add), reads=[bU, bOF[i]], writes=[btos])
                    yield
                    c.op("act", lambda e: e.activation(out=tsq[:, 0:128], in_=tos[:, 0:128], func=AF.Square), reads=[btos], writes=[btsq])
                    yield
                    c.op("pe", lambda e: e.matmul(pU[:, 128:256], lhsT=onesb[:], rhs=tsq[:, 0:128], start=True, stop=True), reads=[btsq, bonesb], writes=[bU])
                    yield
                    rstd_from(pU[:, 128:256], tos[:, 128:256], 1.0 / 128, [bU], [btos])
                    yield
                    c.op("dve", lambda e: e.tensor_tensor(out=tos[:, 0:128], in0=tos[:, 0:128], in1=tos[:, 128:256], op=ALU.mult), reads=[btos], writes=[btos])
                    yield
                    c.op("dve", lambda e: e.scalar_tensor_tensor(out=ZT[:, tsl], in0=tos[:, 0:128], scalar=onw[:, h:h + 1], in1=ZT[:, tsl],
                                                                 op0=ALU.mult, op1=ALU.mult), reads=[btos, bonw, bZT[i]], writes=[bZT[i]])

                def hgrn_scan(h):
                    claimed = [False] * NTI
                    done = [False] * NTI
                    per_dir = []
                    ci = 0
                    for d in range(2):
                        pend = []
                        for si_, (t0, t1) in enumerate(seqs):
                            bl = [(Sbuf[ci * 3 + i_], bS[ci * 3 + i_]) for i_ in range(3)]
                            ci += 1
                            bbl = [(Sbf[(ci - 1) * 3 + i_], bSbf[(ci - 1) * 3 + i_]) for i_ in range(3)]
                            ch = dict(S=bl[0][0], bS=bl[0][1], Sb=bbl[0][0], bSb=bbl[0][1], si=0, turn=0, bufs=bl, bbufs=bbl, last=(t1 - 1 if d == 0 else t0),
                                      out=(None if t0 == 0 else ns_a[(t0 - 32) // 2, d, h]))
                            if t0 == 0:
                                c.dma("sp", ch["S"][:], st_a[d, h], writes=[ch["bS"]])
                            else:
                                c.op("pool", lambda e: e.memset(ch["S"][:], 0.0), writes=[ch["bS"]])
                            c.op("act", lambda e: e.activation(out=ch["Sb"][:], in_=ch["S"][:], func=AF.Copy), reads=[ch["bS"]], writes=[ch["bSb"]])
                            tl = list(range(t0, t1)) if d == 0 else list(range(t1 - 1, t0 - 1, -1))
                            for kidx, i in enumerate(tl):
                                pend.append((d, i, ch, kidx))
                        per_dir.append(pend[:4] + pend[32:] + pend[4:32])
                    pending = []
                    for a_, b_ in zip(per_dir[0], per_dir[1]):
                        pending.append(a_); pending.append(b_)
                    active = []
                    free = list(range(WUA))
                    while pending or active:
                        while pending and free:
                            d, i, ch, kidx = pending.pop(0)
                            sl = free.pop(0)
                            active.append((sl, hgrn_unit(h, d, i, sl, ch, kidx, claimed, done)))
                        nxt_active = []
                        for sl, g in active:
                            try:
                                next(g)
                                nxt_active.append((sl, g))
                            except StopIteration:
                                free.append(sl)
                        active = nxt_active

                QT, VTM, FF, FB, GA = SLOT
                bQT, bVTM, bFF, bFB, bGA = SLB
                VTM3 = VTM[:].rearrange("p (t v) -> p t v", v=128)
                for h in range(4):
                    for k in range(8):
                        c.dma("pool", wbf[:, k, 0:640], win_a[h, k * 128:(k + 1) * 128, :], writes=[bwbf])
                    for blk in range(NPB):
                        ht, bh = load_hblk(blk)
                        bsl = slice(blk * PBK, (blk + 1) * PBK)
                        tb = list(range(blk * 2, blk * 2 + 2))
                        pbk = (blk % 2) * 4 if False else 0
                        pb, bpb = proj_fm(ht, bh, 0, 0)
                        c.op("act", lambda e: e.activation(out=QT[:, bsl], in_=pb, func=AF.Silu), reads=[bpb], writes=[bQT[t] for t in tb])
                        pb, bpb = proj_fm(ht, bh, 128, 1)
                        c.op("act", lambda e: e.activation(out=FF[:, bsl], in_=pb, func=AF.Sigmoid), reads=[bpb], writes=[bFF[t] for t in tb])
                        pb, bpb = proj_fm(ht, bh, 256, 2)
                        c.op("act", lambda e: e.activation(out=FB[:, bsl], in_=pb, func=AF.Sigmoid), reads=[bpb], writes=[bFB[t] for t in tb])
                        pb, bpb = proj_fm(ht, bh, 512, 3)
                        c.op("act", lambda e: e.activation(out=ZT[:, bsl], in_=pb, func=AF.Silu), reads=[bpb], writes=[bZT[t] for t in tb])
                        pb = PB[4 + blk % 2]; bpb = PBb[4 + blk % 2]
                        for tt in range(2):
                            for k in range(8):
                                c.op("pe", lambda e: e.matmul(pb[:, tt * 128:(tt + 1) * 128], lhsT=ht[:, k, tt * 128:(tt + 1) * 128], rhs=wbf[:, k, 384:512],
                                                              start=(k == 0), stop=(k == 7)), reads=[bwbf, bh], writes=[bpb])
                        c.op("dve", lambda e: e.tensor_copy(out=VTMb[:, bsl], in_=pb[:, 0:PBK]), reads=[bpb], writes=[bVTM[t] for t in tb])
                    _chk('a1')
                    if h == 0:
                        issue_bg_precast()
                    GAs = ((GA, bGA), (GA1, bGA1))
                    for d in range(2):
                        FD = FF if d == 0 else FB
                        bFD = bFF if d == 0 else bFB
                        GAd, bGAd = GAs[d]
                        lbc = lbt[:, d * 4 + h:d * 4 + h + 1]
                        omc = lbt[:, 8 + d * 4 + h:8 + d * 4 + h + 1]
                        for blk in range(NBLK):
                            bsl = slice(blk * 512, (blk + 1) * 512)
                            tb = list(range(blk * 4, blk * 4 + 4))
                            fbufs = [bFD[t] for t in tb]
                            gbufs = [bGAd[t] for t in tb]
                            c.op("dve", lambda e: e.tensor_scalar(out=FD[:, bsl], in0=FD[:, bsl], scalar1=omc, scalar2=lbc, op0=ALU.mult, op1=ALU.add),
                                 reads=fbufs + [blb], writes=fbufs)
                            tw = tmp512[blk % 2]; btw = btmp512[blk % 2]
                            c.op("act", lambda e: e.activation(out=tw[:], in_=FD[:, bsl], func=AF.Ln), reads=fbufs, writes=[btw])
                            if d == 0:
                                c.op("dve", lambda e: e.tensor_tensor_scan(out=GAd[:, bsl], data0=seg[:], data1=tw[:], initial=0.0, op0=ALU.mult, op1=ALU.add),
                                     reads=[btw, bseg], writes=gbufs)
                            else:
                                c.op("dve", lambda e: e.tensor_tensor_scan(out=GAd[:, bsl][:, ::-1], data0=seg[:], data1=tw[:][:, ::-1], initial=0.0,
                                                                           op0=ALU.mult, op1=ALU.add), reads=[btw, bseg], writes=gbufs)
                            c.op("pool", lambda e: e.tensor_scalar(out=FD[:, bsl], in0=FD[:, bsl], scalar1=-1.0, scalar2=1.0, op0=ALU.mult, op1=ALU.add),
                                 reads=fbufs, writes=fbufs)
                    hgrn_scan(h)
                    dump('OF', OF[:], bOF)
                    dump('ZTa', ZT[:], bZT)
                    c.dma("pool", oT_scr[h], ZT[:], reads=bZT, writes=[boT[h]])
                    _chk('a4')

                s2a.close()
                c.barrier()
                _chk('p2a')
                RAW, QN, KN, VC, VT2 = SLOT
                bRAW, bQN, bKN, bVC, bVT2 = SLB
                KTM = RAW; bKTM = bRAW
                OTM = VC; bOTM = bVC
                KTM3 = KTM[:].rearrange("p (t v) -> p t v", v=128)
                VT23 = VT2[:].rearrange("p (t v) -> p t v", v=128)
                OTM3 = OTM[:].rearrange("p (t v) -> p t v", v=128)

                def scanview(arr, j):
                    return arr[:, j * 128:(j + 1) * 128]

                def scantiles(j):
                    return [j]

                def zview(j):
                    if j < 32:
                        return ZT[:, 0:4096].rearrange("p (r w) -> p w r", w=64)[:, 2 * j:2 * j + 2, :]
                    return ZT[:, j * 128:(j + 1) * 128]

                def ztiles(j):
                    return list(range(32)) if j < 32 else [j]

                WU = 6
                TNAMES = ("tkb", "tr", "te", "tdm", "tdm2", "tat", "tul", "X", "tkk", "tvb", "twu", "tqb", "Xb", "twb", "tvnb")
                TW = {"tdm2": 128, "tat": 128, "tvb": 128, "twu": 128, "tqb": 128, "Xb": 128, "twb": 128, "tvnb": 128}
                TBF = ("tat", "tkk", "tvb", "tqb", "Xb", "twb", "tvnb")
                s2b = ExitStack()
                tsets = []
                for w_ in range(WU):
                    tsets.append({n: (SB(s2b, "u%d_%s" % (w_, n), [128, TW.get(n, 256)], BF16 if n in TBF else F32), Buf()) for n in TNAMES})
                Sbf2 = [SB(s2b, "Sbg%d" % i, [128, 128], BF16) for i in range(12)]; bSbf2 = bufs(12)
                smalls = [SB(s2b, "usm%d" % w_, [128, 4]) for w_ in range(WU)]; bsmalls = bufs(WU)

                def gdn_unit(h, d, j, slot, ch, done, claimed, kidx):
                    hg = 4 + h
                    ts = tsets[slot]
                    gs = lambda nm: gsm[nm][:, d, j, h:h + 1]
                    MA, MB = (UT, SLO) if d == 0 else (LT, SUP)
                    CMK, SMK, SMK2 = (UT, SUP, SLO) if d == 0 else (LT, SLO, SUP)
                    pU, bU = PB[slot], PBb[slot]
                    kT = scanview(KN, j); qT = scanview(QN, j)
                    rdK = [bKN[j]]; rdQ = [bQN[j]]
                    tkb, btkb = ts["tkb"]; tr, btr = ts["tr"]; te, bte = ts["te"]; tdm, btdm = ts["tdm"]; tdm2, btdm2 = ts["tdm2"]
                    tat, btat = ts["tat"]; tul, btul = ts["tul"]; X, bX = ts["X"]; tkk, btkk = ts["tkk"]; tvb, btvb = ts["tvb"]; twu, btwu = ts["twu"]
                    tqb, btqb = ts["tqb"]; Xb, bXb = ts["Xb"]; twb, btwb = ts["twb"]; tvnb, btvnb = ts["tvnb"]
                    c.op("pool", lambda e: e.tensor_copy(out=tqb[:], in_=qT), reads=rdQ, writes=[btqb])
                    c.op("dve", lambda e: e.tensor_scalar(out=tkb[:, 0:128], in0=KTM3[:, j, :], scalar1=gs("beta"), scalar2=None, op0=ALU.mult),
                         reads=[bKTM[j], bgs], writes=[btkb])
                    c.op("act", lambda e: e.activation(out=tr[:, 0:128], in_=MA, func=AF.Copy, scale=gs("gg")), reads=[bcm, bgs], writes=[btr])
                    yield
                    c.op("pe", lambda e: e.transpose(pU[:, 384:512], tkb[:, 0:128], IDN), reads=[btkb, bcm], writes=[bU])
                    c.op("pe", lambda e: e.matmul(pU[:, 0:128], lhsT=MB, rhs=tr[:, 0:128], start=True, stop=True), reads=[btr, bcm], writes=[bU])
                    yield
                    c.op("act", lambda e: e.activation(out=tkb[:, 128:256], in_=pU[:, 384:512], func=AF.Copy), reads=[bU], writes=[btkb])
                    kbT = tkb[:, 128:256]
                    c.op("act", lambda e: e.activation(out=te[:, 0:128], in_=pU[:, 0:128], func=AF.Exp), reads=[bU], writes=[bte])
                    yield
                    c.op("pe", lambda e: e.matmul(pU[:, 0:128], lhsT=kT, rhs=qT, start=True, stop=True), reads=rdK + rdQ, writes=[bU])
                    c.op("pe", lambda e: e.matmul(pU[:, 128:256], lhsT=kT, rhs=kbT, start=True, stop=True), reads=rdK + [btkb], writes=[bU])
                    c.op("pool", lambda e: e.tensor_tensor(out=tdm[:, 0:128], in0=te[:, 0:128], in1=CMK, op=ALU.mult), reads=[bte, bcm], writes=[btdm])
                    c.op("pool", lambda e: e.tensor_tensor(out=tdm[:, 128:256], in0=te[:, 0:128], in1=SMK, op=ALU.mult), reads=[bte, bcm], writes=[btdm])
                    yield
                    c.op("dve", lambda e: e.tensor_tensor(out=tat[:, 0:128], in0=pU[:, 0:128], in1=tdm[:, 0:128], op=ALU.mult), reads=[bU, btdm], writes=[btat])
                    c.op("dve", lambda e: e.tensor_tensor(out=tul[:, 0:128], in0=pU[:, 128:256], in1=tdm[:, 128:256], op=ALU.mult), reads=[bU, btdm], writes=[btul])
                    yield
                    c.op("pe", lambda e: e.transpose(pU[:, 256:384], tul[:, 0:128], IDN), reads=[btul, bcm], writes=[bU])
                    yield
                    c.op("act", lambda e: e.activation(out=tul[:, 128:256], in_=pU[:, 256:384], func=AF.Copy), reads=[bU], writes=[btul])
                    yield
                    c.op("dve", lambda e: e.tensor_tensor(out=X[:, 0:128], in0=IDN, in1=tul[:, 0:128], op=ALU.subtract), reads=[bcm, btul], writes=[bX])
                    cur, bcur = tul, btul
                    xs = 0
                    for lev in range(5):
                        nxt, bnxt = ts["te"] if lev % 2 == 0 else ts["tr"]
                        if lev < 4:
                            c.op("pe", lambda e: e.matmul(pU[:, 0:128], lhsT=cur[:, 128:256], rhs=cur[:, 0:128], start=True, stop=True), reads=[bcur], writes=[bU])
                        c.op("pe", lambda e: e.matmul(pU[:, 128:256], lhsT=cur[:, 0:128], rhs=cur[:, 128:256], start=True, stop=True), reads=[bcur], writes=[bU])
                        yield
                        if lev < 4:
                            c.op("act", lambda e: e.activation(out=nxt[:, :], in_=pU[:, 0:256], func=AF.Copy), reads=[bU], writes=[bnxt])
                        else:
                            c.op("act", lambda e: e.activation(out=nxt[:, 128:256], in_=pU[:, 128:256], func=AF.Copy), reads=[bU], writes=[bnxt])
                        yield
                        c.op("pe", lambda e: e.matmul(pU[:, 256:384], lhsT=nxt[:, 128:256], rhs=X[:, xs:xs + 128], start=True, stop=True), reads=[bnxt, bX], writes=[bU])
                        yield
                        c.op("dve", lambda e: e.tensor_tensor(out=X[:, 128 - xs:256 - xs], in0=X[:, xs:xs + 128], in1=pU[:, 256:384], op=ALU.add), reads=[bU, bX], writes=[bX])
                        xs = 128 - xs
                        cur, bcur = nxt, bnxt
                        yield
                    XT = X[:, xs:xs + 128]
                    c.op("act", lambda e: e.activation(out=Xb[:], in_=XT, func=AF.Copy), reads=[bX], writes=[bXb])
                    c.op("act", lambda e: e.activation(out=tkk[:, 0:128], in_=KTM3[:, j, :], func=AF.Copy, scale=gs("beg")), reads=[bKTM[j], bgs], writes=[btkk])
                    c.op("pool", lambda e: e.tensor_scalar(out=tkk[:, 128:256], in0=KTM3[:, j, :], scalar1=gs("ekt"), scalar2=None, op0=ALU.mult), reads=[bKTM[j], bgs], writes=[btkk])
                    c.op("act", lambda e: e.activation(out=tvb[:, 0:128], in_=VT23[:, j, :], func=AF.Copy, scale=gs("beta")), reads=[bVT2[j], bgs], writes=[btvb])
                    yield
                    c.op("pe", lambda e: e.matmul(pU[:, 0:128], lhsT=tkk[:, 0:128], rhs=Xb[:], start=True, stop=True), reads=[btkk, bXb], writes=[bU])
                    c.op("pe", lambda e: e.matmul(pU[:, 128:256], lhsT=Xb[:], rhs=tvb[:, 0:128], start=True, stop=True), reads=[btvb, bXb], writes=[bU])
                    yield
                    c.op("act", lambda e: e.activation(out=twb[:], in_=pU[:, 0:128], func=AF.Copy), reads=[bU], writes=[btwb])
                    c.op("act", lambda e: e.activation(out=twu[:], in_=pU[:, 128:256], func=AF.Copy), reads=[bU], writes=[btwu])
                    yield
                    while ch["turn"] != kidx:
                        yield
                    if not claimed[j]:
                        claimed[j] = True
                        first = True
                    else:
                        first = False
                        while not done[j]:
                            yield
                    tvns = (ts["tkb"], ts["tdm"])
                    corder = (0, 1) if d == 0 else (1, 0)
                    for ci_, cc in enumerate(corder):
                        pr = slice(cc * 64, cc * 64 + 64)
                        tvn, btvn = tvns[ci_]
                        pR, bR = pU, bU
                        S, bSc = ch["S"], ch["bS"]
                        Sb_, bSb_ = ch["Sb"], ch["bSb"]
                        c.op("pe", lambda e: e.matmul(pR[:, 0:128], lhsT=twb[:], rhs=Sb_[:], start=True, stop=True), reads=[btwb, bSb_], writes=[bR])
                        c.op("pe", lambda e: e.matmul(pR[:, 128:256], lhsT=tqb[:], rhs=Sb_[:], start=True, stop=True), reads=[btqb, bSb_], writes=[bR])
                        yield
                        c.op("dve", lambda e: e.tensor_tensor(out=tvnb[pr, :], in0=twu[pr, :], in1=pR[pr, 0:128], op=ALU.subtract), reads=[btwu, bR], writes=[btvnb])
                        yield
                        c.op("pe", lambda e: e.matmul(pR[:, 256:384], lhsT=tat[pr, 0:128], rhs=tvnb[pr, :], start=True, stop=True), reads=[btat, btvnb], writes=[bR])
                        c.op("pe", lambda e: e.matmul(pR[:, 384:512], lhsT=tkk[pr, 128:256], rhs=tvnb[pr, :], start=True, stop=True), reads=[btkk, btvnb], writes=[bR])
                        yield
                        ch["si"] += 1
                        Sn, bSn = ch["bufs"][ch["si"] % 3]
                        Sbn, bSbn = ch["bbufs"][ch["si"] % 2]
                        c.op("dve", lambda e: e.scalar_tensor_tensor(out=Sn[:], in0=S[:], scalar=gs("dec%d" % cc), in1=pR[:, 384:512], op0=ALU.mult, op1=ALU.add),
                             reads=[bSc, bgs, bR], writes=[bSn])
                        c.op("act", lambda e: e.activation(out=Sbn[:], in_=Sn[:], func=AF.Copy), reads=[bSn], writes=[bSbn])
                        ch["S"], ch["bS"] = Sn, bSn
                        ch["Sb"], ch["bSb"] = Sbn, bSbn
                        c.op("dve", lambda e: e.tensor_scalar(out=tvn[pr, 128:256], in0=pR[pr, 128:256], scalar1=gsm["eg"][pr, d, j, h:h + 1], scalar2=None, op0=ALU.mult),
                             reads=[bR, bgs], writes=[btvn])
                        c.op("dve", lambda e: e.tensor_tensor(out=tvn[pr, 128:256], in0=tvn[pr, 128:256], in1=pR[pr, 256:384], op=ALU.add), reads=[btvn, bR], writes=[btvn])
                        if first:
                            c.op("act", lambda e: e.activation(out=OTM3[pr, j, :], in_=tvn[pr, 128:256], func=AF.Copy), reads=[btvn], writes=[bOTM[j]])
                        else:
                            c.op("pool", lambda e: e.tensor_tensor(out=OTM3[pr, j, :], in0=OTM3[pr, j, :], in1=tvn[pr, 128:256], op=ALU.add), reads=[btvn, bOTM[j]], writes=[bOTM[j]])
                        yield
                    if ch["last"] == j and ch["out"] is not None:
                        c.dma("sp", ch["out"], ch["S"][:], reads=[ch["bS"]], writes=[Buf()])
                    ch["turn"] += 1
                    if first:
                        done[j] = True
                        return
                    tn, btn = ts["te"]
                    sm = smalls[slot]; bsm = bsmalls[slot]
                    c.op("pool", lambda e: e.memset(sm[:, 0:1], 0.0), writes=[bsm])
                    c.op("act", lambda e: e.activation(out=tn[:, 0:128], in_=OTM3[:, j, :], func=AF.Square, accum_out=sm[:, 0:1]), reads=[bOTM[j]], writes=[btn, bsm])
                    yield
                    rstd_from(sm[:, 0:1], sm[:, 1:2], 1.0 / 128, [bsm], [bsm])
                    yield
                    c.op("dve", lambda e: e.tensor_scalar(out=tn[:, 128:256], in0=OTM3[:, j, :], scalar1=sm[:, 1:2], scalar2=None, op0=ALU.mult), reads=[bOTM[j], bsm], writes=[btn])
                    yield
                    c.op("pe", lambda e: e.transpose(pU[:, 256:384], tn[:, 128:256], IDN), reads=[btn, bcm], writes=[bU])
                    yield
                    zv = zview(j)
                    zb = [bZT[t] for t in ztiles(j)]
                    pin = pU[:, 256:384].rearrange("p (c r) -> p c r", c=2) if j < 32 else pU[:, 256:384]
                    c.op("dve", lambda e: e.scalar_tensor_tensor(out=zv, in0=pin, scalar=onw[:, hg:hg + 1], in1=zv, op0=ALU.mult, op1=ALU.mult),
                         reads=[bU, bonw] + zb, writes=zb)

                def gdn_scan(h):
                    done = [False] * NTI
                    claimed = [False] * NTI
                    chains = {}
                    order = []
                    ci = 0
                    for si_, (t0, t1) in enumerate(seqs):
                        for d in range(2):
                            bl = [(Sbuf[ci * 3 + i], bS[ci * 3 + i]) for i in range(3)]
                            bbl = [(Sbf2[ci * 2 + i], bSbf2[ci * 2 + i]) for i in range(2)]
                            ch = dict(S=bl[0][0], bS=bl[0][1], Sb=bbl[0][0], bSb=bbl[0][1], si=0, turn=0, t0=t0, t1=t1, bufs=bl, bbufs=bbl,
                                      last=(t1 - 1 if d == 0 else t0), out=(None if t0 == 0 else ns_b[(t0 - 32) // 2, d, h]))
                            if t0 == 0:
                                c.dma("sp", ch["S"][:], st_b[d, h], writes=[ch["bS"]])
                            else:
                                c.op("pool", lambda e: e.memset(ch["S"][:], 0.0), writes=[ch["bS"]])
                            c.op("act", lambda e: e.activation(out=ch["Sb"][:], in_=ch["S"][:], func=AF.Copy), reads=[ch["bS"]], writes=[ch["bSb"]])
                            chains[(si_, d)] = ch
                            ci += 1
                    samp = []
                    for i in range(32):
                        samp.append((0, 0, i)); samp.append((0, 1, 31 - i))
                    pro = []
                    for si_ in (1, 2):
                        t0, t1 = seqs[si_]
                        for i in range(t1 - t0):
                            pro.append((si_, 0, t0 + i)); pro.append((si_, 1, t1 - 1 - i))
                    order = samp[:8] + pro + samp[8:]
                    pending = list(order)
                    active = []
                    free = list(range(WU))
                    while pending or active:
                        while pending and free:
                            si_, d, j = pending.pop(0)
                            sl = free.pop(0)
                            ch_ = chains[(si_, d)]
                            kidx = (j - ch_["t0"]) if d == 0 else (ch_["t1"] - 1 - j)
                            active.append((sl, gdn_unit(h, d, j, sl, ch_, done, claimed, kidx)))
                        nxt_active = []
                        for sl, g in active:
                            try:
                                next(g)
                                nxt_active.append((sl, g))
                            except StopIteration:
                                free.append(sl)
                        active = nxt_active

                for h in range(4):
                    hg = 4 + h
                    for k in range(8):
                        c.dma("pool", wbf[:, k, 0:512], win_b[h, k * 128:(k + 1) * 128, :], writes=[bwbf])
                    for blk in range(NPB):
                        ht, bh = load_hblk(blk)
                        bsl = slice(blk * PBK, (blk + 1) * PBK)
                        tb = list(range(blk * 2, blk * 2 + 2))
                        for qi, (DST, bDST) in enumerate(((QN, bQN), (KN, bKN), (VC, bVC))):
                            pb, bpb = proj_fm(ht, bh, qi * 128, qi)
                            if qi == 1:
                                c.op("dve", lambda e: e.tensor_copy(out=DST[:, bsl], in_=pb), reads=[bpb], writes=[bDST[t] for t in tb])
                            else:
                                c.op("act", lambda e: e.activation(out=DST[:, bsl], in_=pb, func=AF.Copy), reads=[bpb], writes=[bDST[t] for t in tb])
                        pb, bpb = proj_fm(ht, bh, 384, 3)
                        c.op("act", lambda e: e.activation(out=ZT[:, bsl], in_=pb, func=AF.Silu), reads=[bpb], writes=[bZT[t] for t in tb])
                    for qi, (DST, bDST) in enumerate(((QN, bQN), (KN, bKN), (VC, bVC))):
                        w0 = cw[:, h * 9 + qi * 3 + 0:h * 9 + qi * 3 + 1]
                        w1 = cw[:, h * 9 + qi * 3 + 1:h * 9 + qi * 3 + 2]
                        w2 = cw[:, h * 9 + qi * 3 + 2:h * 9 + qi * 3 + 3]
                        ceng = "dve"
                        RAWq, bRAWq = (RAW, bRAW) if qi != 1 else (VT2, bVT2)
                        c.op("act", lambda e: e.activation(out=RAWq[:, :], in_=DST[:, :], func=AF.Copy, scale=w1), reads=bDST + [bcw], writes=bRAWq)
                        for (lo, hi, sh) in ((0, 4096, 64), (4096, 4352, 1), (4352, 4608, 1)):
                            c.op(ceng, lambda e: e.scalar_tensor_tensor(out=RAWq[:, lo + sh:hi], in0=DST[:, lo:hi - sh], scalar=w0, in1=RAWq[:, lo + sh:hi],
                                                                         op0=ALU.mult, op1=ALU.add), reads=bDST + [bcw], writes=bRAWq)
                            c.op(ceng, lambda e: e.scalar_tensor_tensor(out=RAWq[:, lo:hi - sh], in0=DST[:, lo + sh:hi], scalar=w2, in1=RAWq[:, lo:hi - sh],
                                                                         op0=ALU.mult, op1=ALU.add), reads=bDST + [bcw], writes=bRAWq)
                        c.op("act", lambda e: e.activation(out=DST[:, 0:4096].rearrange("p (w r) -> p w r", r=64),
                                                           in_=RAWq[:, 0:4096].rearrange("p (r w) -> p w r", w=64), func=AF.Silu), reads=bRAWq[0:32], writes=bDST[0:32])
                        c.op("act", lambda e: e.activation(out=DST[:, 4096:NT], in_=RAWq[:, 4096:NT], func=AF.Silu), reads=bRAWq[32:], writes=bDST[32:])
                        if qi == 2:
                            for j in range(NTI):
                                p5 = PB[4 + j % 2]; b5 = PBb[4 + j % 2]
                                c.op("pe", lambda e: e.transpose(p5[:, 128:256], DST[:, j * 128:(j + 1) * 128], IDN), reads=[bDST[j], bcm], writes=[b5])
                                c.op("dve", lambda e: e.tensor_copy(out=VT23[:, j, :], in_=p5[:, 128:256]), reads=[b5], writes=[bVT2[j]])
                            continue
                        for blk in range(NBLK):
                            bsl = slice(blk * 512, (blk + 1) * 512)
                            db = [bDST[t] for t in range(blk * 4, blk * 4 + 4)]
                            tw = tmp512[blk % 2]; btw = btmp512[blk % 2]
                            c.op("act", lambda e: e.activation(out=tw[:], in_=DST[:, bsl], func=AF.Square), reads=db, writes=[btw])
                            pb = PB[4 + blk % 2]; bpb = PBb[4 + blk % 2]
                            c.op("pe", lambda e: e.matmul(pb[:, :], lhsT=ONES, rhs=tw[:], start=True, stop=True), reads=[btw, bcm], writes=[bpb])
                            rstd_from(pb[:, :], tw[:], 1.0, [bpb], [btw])
                            if qi == 0:
                                c.op("dve", lambda e: e.scalar_tensor_tensor(out=DST[:, bsl], in0=DST[:, bsl], scalar=float(128 ** -0.5), in1=tw[:],
                                                                             op0=ALU.mult, op1=ALU.mult), reads=db + [btw], writes=db)
                            else:
                                c.op("dve", lambda e: e.tensor_tensor(out=DST[:, bsl], in0=DST[:, bsl], in1=tw[:], op=ALU.mult), reads=db + [btw], writes=db)
                    for j in range(NTI):
                        p5 = PB[4 + j % 2]; b5 = PBb[4 + j % 2]
                        c.op("pe", lambda e: e.transpose(p5[:, 0:128], scanview(KN, j), IDN), reads=[bKN[j], bcm], writes=[b5])
                        c.op("act", lambda e: e.activation(out=KTM3[:, j, :], in_=p5[:, 0:128], func=AF.Copy), reads=[b5], writes=[bKTM[j]])
                    gdn_scan(h)
                    c.dma("pool", oT_scr[hg], ZT[:], reads=bZT, writes=[boT[hg]])

                s2b.close()
            PB.pop(); PBb.pop()
            c.barrier()
            _chk('p2b')
            with ExitStack() as s4:
                PTh[0] = s4.enter_context(nc.psum_tensor("pbt4", [128, 1024], BF16)); PTh[1] = Buf(True)
                wo = SB(s4, "wo", [128, 8, D], BF16); bwo = Buf()
                wg = SB(s4, "wg", [128, 8, DFF], BF16); bwg = Buf()
                wu = SB(s4, "wu", [128, 8, DFF], BF16); bwu = Buf()
                wd = SB(s4, "wd", [128, NF, D], BF16); bwd = Buf()
                for k in range(8):
                    c.dma("sp", wo[:, k, :], wo_bf[k * 128:(k + 1) * 128, :], reads=[bWo[k]], writes=[bwo])
                for k in range(8):
                    c.dma("sp", wg[:, k, :], wg_bf[k * 128:(k + 1) * 128, :], reads=[bWg[k]], writes=[bwg])
                    c.dma("sp", wu[:, k, :], wu_bf[k * 128:(k + 1) * 128, :], reads=[bWu[k]], writes=[bwu])
                for f in range(NF):
                    c.dma("sp", wd[:, f, :], wd_bf[f * 128:(f + 1) * 128, :], reads=[bWd[f]], writes=[bwd])
                gbc = SB(s4, "gbc", [128, 4, D]); bgbc = Buf()
                nfb = SB(s4, "nfb", [128, D]); bnfb = Buf()
                c.dma("sp", nfb[:], normf.partition_broadcast(128), writes=[bnfb])
                with ExitStack() as s40:
                    sv = SB(s40, "sv4", [128, 16]); bsv = Buf()
                    c.dma("sp", sv[:], svec, writes=[bsv])
                    c.op("act", lambda e: e.activation(out=sv[:], in_=sv[:], func=AF.Silu), reads=[bsv], writes=[bsv])
                    srep = SB(s40, "srep", [128, 16, 128]); bsrep = Buf()
                    c.op("dve", lambda e: e.tensor_copy(out=srep[:], in_=sv[:].unsqueeze(2).broadcast_to([128, 16, 128])), reads=[bsv], writes=[bsrep])
                    brow = SB(s40, "brow", [128, 512]); bbrow = Buf()
                    wad = [SB(s40, "wad40", [128, 8, 512])] * 2; bwad = [Buf()] * 2
                    wv = w_ada.rearrange("(k p) n -> p k n", p=128)
                    ci = 0
                    for gi, cb0 in enumerate((4, 10)):
                        for hh in range(2):
                            cb = cb0 + hh
                            t = wad[ci % 2]; bt = bwad[ci % 2]; ci += 1
                            c.dma("sp", t[:], wv[:, :, cb * 512:(cb + 1) * 512], writes=[bt])
                            c.dma("sp", brow[:], bada_row[:, cb * 512:(cb + 1) * 512].partition_broadcast(128), writes=[bbrow])
                            for v in range(2):
                                pb = PB[v]; bpb = PBb[v]
                                for k in range(8):
                                    c.op("pe", lambda e: e.matmul(pb[:, :], lhsT=srep[:, v * 8 + k, :], rhs=t[:, k, :], start=(k == 0), stop=(k == 7)), reads=[bsrep, bt], writes=[bpb])
                                c.op("dve", lambda e: e.tensor_tensor(out=gbc[:, gi * 2 + v, hh * 512:(hh + 1) * 512], in0=pb[:, :], in1=brow[:], op=ALU.add),
                                     reads=[bpb, bbrow], writes=[bgbc])
                c.barrier(skip=("dpool",))
                BT = 128
                oTb = [SB(s4, "otb0", [128, 8, BT], BF16)] * 2; boTb = [Buf()] * 2
                xts = [SB(s4, "x4%d" % i, [128, D]) for i in range(2)]; bxts = bufs(2)
                x1s = [SB(s4, "x1%d" % i, [128, D]) for i in range(2)]; bx1s = bufs(2)
                h2s = [SB(s4, "h20", [128, 8, BT], BF16)] * 2; bh2s = [Buf()] * 2
                aT = SB(s4, "aT", [128, NF, BT], BF16); baT = bufs(NF)
                ss = SB(s4, "ss4", [128, 4]); xn = SB(s4, "xn4", [128, D], BF16); junk = SB(s4, "junk4", [128, D], BF16)
                tmpn = (ss, Buf(), xn, Buf(), junk, Buf())
                tsg = [SB(s4, "tsg%d" % i, [128, BT]) for i in range(2)]; btsg = bufs(2)
                byout = Buf()
                nown_blk = NOWN // BT
                nsamp_blk = 2048 // BT
                vof = lambda blk: 1 if blk < nsamp_blk else 0

                def st_op(blk):
                    tok0 = blk * BT if blk < nsamp_blk else 4096 + (blk - nsamp_blk) * BT
                    v = vof(blk)
                    ob = oTb[blk % 2]; bob = boTb[blk % 2]
                    xt = xts[blk % 2]; bx = bxts[blk % 2]; x1 = x1s[blk % 2]; bx1 = bx1s[blk % 2]
                    c.dma("sp", ob[:], oT_scr[:, :, tok0:tok0 + BT].rearrange("h p t -> p h t"), reads=boT, writes=[bob])
                    c.dma("sp", xt[:], xall[tok0:tok0 + 128, :], writes=[bx])
                    for hh in range(2):
                        pb = PB[hh]; bpb = PBb[hh]
                        for hd in range(8):
                            c.op("pe", lambda e: e.matmul(pb[:, :], lhsT=ob[:, hd, :], rhs=wo[:, hd, hh * 512:(hh + 1) * 512],
                                                          start=(hd == 0), stop=(hd == 7)), reads=[bob, bwo], writes=[bpb])
                        hs = slice(hh * 512, (hh + 1) * 512)
                        c.op("dve", lambda e: e.tensor_tensor(out=x1[:, hs], in0=pb[:, :], in1=gbc[:, v, hs], op=ALU.mult), reads=[bpb, bgbc], writes=[bx1])
                        c.op("pool", lambda e: e.tensor_tensor(out=x1[:, hs], in0=x1[:, hs], in1=xt[:, hs], op=ALU.add), reads=[bx1, bx], writes=[bx1])
                    norm_mod_transpose("norm_only", x1[:], bx1, 1, v, None, None, tmpn)

                def st_tr(blk):
                    mod_transpose(1, vof(blk), h2s[blk % 2], bh2s[blk % 2], tmpn)

                def st_gu(blk):
                    h2 = h2s[blk % 2]; bh2 = bh2s[blk % 2]
                    for f in range(NF):
                        pg = PB[2 + f % 2]; bpg = PBb[2 + f % 2]
                        pu = PB[4 + f % 2]; bpu = PBb[4 + f % 2]
                        for k in range(8):
                            c.op("pe", lambda e: e.matmul(pg[:, 0:BT], lhsT=wg[:, k, f * 128:(f + 1) * 128], rhs=h2[:, k, :], start=(k == 0), stop=(k == 7)),
                                 reads=[bwg, bh2], writes=[bpg])
                        for k in range(8):
                            c.op("pe", lambda e: e.matmul(pu[:, 0:BT], lhsT=wu[:, k, f * 128:(f + 1) * 128], rhs=h2[:, k, :], start=(k == 0), stop=(k == 7)),
                                 reads=[bwu, bh2], writes=[bpu])
                        tg = tsg[f % 2]; btg = btsg[f % 2]
                        c.op("act", lambda e: e.activation(out=tg[:], in_=pg[:, 0:BT], func=AF.Silu), reads=[bpg], writes=[btg])
                        c.op("dve", lambda e: e.tensor_tensor(out=aT[:, f, :], in0=tg[:], in1=pu[:, 0:BT], op=ALU.mult), reads=[btg, bpu], writes=[baT[f]])

                def st_down(blk):
                    v = vof(blk)
                    x1 = x1s[blk % 2]; bx1 = bx1s[blk % 2]
                    yo = xts[blk % 2]; byo = bxts[blk % 2]
                    for hh in range(2):
                        pb = PB[hh]; bpb = PBb[hh]
                        for f in range(NF):
                            c.op("pe", lambda e: e.matmul(pb[:, :], lhsT=aT[:, f, :], rhs=wd[:, f, hh * 512:(hh + 1) * 512],
                                                          start=(f == 0), stop=(f == NF - 1)), reads=[baT[f], bwd], writes=[bpb])
                        hs = slice(hh * 512, (hh + 1) * 512)
                        c.op("dve", lambda e: e.tensor_tensor(out=yo[:, hs], in0=pb[:, :], in1=gbc[:, 2 + v, hs], op=ALU.mult), reads=[bpb, bgbc], writes=[byo])
                        c.op("pool", lambda e: e.tensor_tensor(out=yo[:, hs], in0=yo[:, hs], in1=x1[:, hs], op=ALU.add), reads=[byo, bx1], writes=[byo])
                    bs_ = tmpn[1]
                    c.op("pool", lambda e: e.memset(ss[:, 2:3], 0.0), writes=[bs_])
                    c.op("act", lambda e: e.activation(out=junk[:], in_=yo[:], func=AF.Square, accum_out=ss[:, 2:3]), reads=[byo], writes=[tmpn[5], bs_])
                    rstd_from(ss[:, 2:3], ss[:, 3:4], 1.0 / D, [bs_], [bs_])
                    c.op("dve", lambda e: e.scalar_tensor_tensor(out=yo[:], in0=yo[:], scalar=ss[:, 3:4], in1=nfb[:], op0=ALU.mult, op1=ALU.mult),
                         reads=[byo, bs_, bnfb], writes=[byo])
                    r0 = blk * BT
                    c.dma("sp", y_own[r0:r0 + 128, :], yo[:], reads=[byo], writes=[byout])

                st_op(0)
                st_tr(0)
                for blk in range(nown_blk):
                    if blk + 1 < nown_blk:
                        st_op(blk + 1)
                    st_gu(blk)
                    if blk + 1 < nown_blk:
                        st_tr(blk + 1)
                    st_down(blk)
        except _Stop:
            pass
        for q in NDSEM:
            for nm in c.dnames[q]:
                if c.cnt[nm]:
                    nc.sync.wait_ge(c.sem[nm], c.cnt[nm])


_CONST = {}


def _consts():
    if _CONST:
        return _CONST
    u = np.arange(128)[:, None]
    t = np.arange(128)[None, :]
    same = (u // 64) == (t // 64)
    ident = (u == t)
    UT = same & (u <= t)
    SLO = same & (u > t)
    LT = same & (u >= t)
    SUP = same & (u < t)
    ones = np.ones((128, 128), bool)
    cm = np.stack([ident, UT, SLO, LT, SUP, same, ones], axis=1).astype(np.float32).reshape(128, 7 * 128)
    sel = np.stack([np.broadcast_to(u < 64, (128, 128)), np.broadcast_to(u >= 64, (128, 128))], axis=1).astype(np.float32).reshape(128, 256)
    seg = np.ones((128, 512), np.float32)
    seg[:, ::64] = 0.0
    _CONST.update(cmask=np.ascontiguousarray(cm), csel=np.ascontiguousarray(sel), cseg=seg)
    return _CONST


def _fm(vec):
    return np.ascontiguousarray(np.asarray(vec, np.float32).reshape(-1, 128).T)


_PROG = {}


def kernel(x_prompt, x_sample, c, state_hgrn, state_gdn, c_ctx, w_ada, b_ada, norm1, norm2, w_in, conv_w, hgrn_lb,
           gdn_A_log, gdn_dt_bias, hgrn_out_norm, gdn_out_norm, w_out, w_gate, w_up, w_down, norm_f):
    f32 = lambda a: np.ascontiguousarray(np.asarray(a, dtype=np.float32))
    x_prompt, x_sample, c, state_hgrn, state_gdn, c_ctx = map(f32, (x_prompt, x_sample, c, state_hgrn, state_gdn, c_ctx))
    w_ada, b_ada, norm1, norm2, w_in, conv_w, hgrn_lb = map(f32, (w_ada, b_ada, norm1, norm2, w_in, conv_w, hgrn_lb))
    gdn_A_log, gdn_dt_bias, hgrn_out_norm, gdn_out_norm = map(f32, (gdn_A_log, gdn_dt_bias, hgrn_out_norm, gdn_out_norm))
    w_out, w_gate, w_up, w_down, norm_f = map(f32, (w_out, w_gate, w_up, w_down, norm_f))
    if "nc" not in _PROG:
        _PROG["nc"] = build_program()
    nc = _PROG["nc"]
    cst = _consts()
    W = w_in[0]
    offs = np.cumsum([0, 512, 512, 512, 512, 512, 512, 512, 512, 512, 8, 8])
    a_q, a_ff, a_fb, a_i, a_g, b_q, b_k, b_v, b_z, b_beta, b_a = [W[:, offs[i]:offs[i + 1]] for i in range(11)]
    a_f = [a_ff, a_fb]
    in_maps = []
    for core in range(8):
        p, e = core // 2, core % 2
        ds = [0, 1] if e == 0 else [1, 0]
        fl = (lambda a: a[::-1]) if e else (lambda a: a)
        pr = [4 * p + 2 * e, 4 * p + 2 * e + 1]
        xall = np.concatenate([fl(x_sample[p]), fl(x_prompt[pr[0]]), fl(x_prompt[pr[1]])], axis=0)
        hs = lambda a, h: a[:, h * 128:(h + 1) * 128]
        win_a = np.stack([np.concatenate([hs(a_q, h), hs(a_f[ds[0]], h), hs(a_f[ds[1]], h), hs(a_i, h), hs(a_g, h)], axis=1) for h in range(4)])
        win_b = np.stack([np.concatenate([hs(b_q, h), hs(b_k, h), hs(b_v, h), hs(b_z, h)], axis=1) for h in range(4)])
        win_g = np.concatenate([b_beta[:, ds[0] * 4:ds[0] * 4 + 4], b_beta[:, ds[1] * 4:ds[1] * 4 + 4],
                                b_a[:, ds[0] * 4:ds[0] * 4 + 4], b_a[:, ds[1] * 4:ds[1] * 4 + 4]], axis=1)
        hlb = np.stack([np.stack([_fm(hgrn_lb[l, ds[d]]) for d in range(2)], axis=1) for l in range(2)], axis=1)
        cwt = conv_w[0][::-1] if e else conv_w[0]
        convw = np.zeros((128, 4, 3, 3), np.float32)
        for h in range(4):
            for qi in range(3):
                convw[:, h, qi, :] = cwt[:, qi * 512 + h * 128:qi * 512 + (h + 1) * 128].T
        alog = np.broadcast_to(gdn_A_log[0][ds][:, None, :], (2, NTI, 4)).reshape(1, -1)
        dtb = np.broadcast_to(gdn_dt_bias[0][ds][:, None, :], (2, NTI, 4)).reshape(1, -1)
        m = dict(
            xall=np.ascontiguousarray(xall),
            svec=np.ascontiguousarray(np.concatenate([_fm(c_ctx), _fm(c[p])], axis=1)),
            w_ada=w_ada[0], bada_fm=_fm(b_ada[0]), bada_row=b_ada[0][None, :],
            n12_fm=np.ascontiguousarray(np.concatenate([_fm(norm1[0]), _fm(norm2[0])], axis=1)),
            normf=norm_f[None, :],
            win_a=np.ascontiguousarray(win_a), win_b=np.ascontiguousarray(win_b), win_g=np.ascontiguousarray(win_g),
            hlb_fm=np.ascontiguousarray(hlb.reshape(128, 16)),
            convw_fm=np.ascontiguousarray(convw.reshape(128, 36)),
            alog_rep=np.ascontiguousarray(np.broadcast_to(alog, (128, 288))),
            dtb_rep=np.ascontiguousarray(np.broadcast_to(dtb, (128, 288))),
            onorm_fm=np.ascontiguousarray(np.concatenate([_fm(hgrn_out_norm[0]), _fm(gdn_out_norm[0])], axis=1)),
            st_a=np.ascontiguousarray(state_hgrn[p, 0][ds]), st_b=np.ascontiguousarray(state_gdn[p, 0][ds]),
            w_out=w_out[0], w_gate=w_gate[0], w_up=w_up[0], w_down=w_down[0],
            cmask=cst["cmask"], csel=cst["csel"], cseg=cst["cseg"],
        )
        in_maps.append(m)
    if _PROG.get('debug_hook'):
        return _PROG['debug_hook'](nc, in_maps)
    res = run_bass_kernel_spmd(nc, in_maps, core_ids=list(range(8)))
    y_prompt = np.zeros((16, 256, D), np.float32)
    y_sample = np.zeros((4, 4096, D), np.float32)
    nsa = np.zeros((16, 1, 2, 4, 128, 128), np.float32)
    nsb = np.zeros((16, 1, 2, 4, 128, 128), np.float32)
    for core in range(8):
        p, e = core // 2, core % 2
        ds = [0, 1] if e == 0 else [1, 0]
        r = res.results[core]
        yo = np.asarray(r["y_own"], np.float32)
        if e == 0:
            y_sample[p, 0:2048] = yo[0:2048]
        else:
            y_sample[p, 2048:4096] = yo[0:2048][::-1]
        for j in range(2):
            seq = 4 * p + 2 * e + j
            blk = yo[2048 + 256 * j:2048 + 256 * (j + 1)]
            y_prompt[seq] = blk[::-1] if e else blk
            for d in range(2):
                nsa[seq, 0, ds[d]] = np.asarray(r["ns_a"], np.float32)[j, d]
                nsb[seq, 0, ds[d]] = np.asarray(r["ns_b"], np.float32)[j, d]
    return (y_prompt, y_sample, nsa, nsb)
```

```python
import numpy as np
import ml_dtypes
import concourse.bass as bass
import concourse.mybir as mybir
from concourse.bass_utils import run_bass_kernel_spmd
from contextlib import ExitStack

F32 = mybir.dt.float32
BF16 = mybir.dt.bfloat16
AF = mybir.ActivationFunctionType
ALU = mybir.AluOpType

D = 1024
NT = 4608
NTI = 36
NBLK = 9
NOWN = 2560
DFF = 2816
NF = 22
EPS = 1e-6
SAME_ENG_SYNC = True
ATTACH_WAIT = True


class Buf:
    __slots__ = ("lw", "rd", "excl")

    def __init__(self, excl=False):
        self.lw = None
        self.rd = {}
        self.excl = excl


def bufs(n):
    return [Buf() for _ in range(n)]


NDSEM = {"sp": 40, "pool": 16, "bg": 40}
DQ_ENG = {"sp": "sp", "pool": "pool", "bg": "pool"}


class Ctx:
    def __init__(self, nc, es):
        self.nc = nc
        self.eng = {"pe": nc.tensor, "dve": nc.vector, "act": nc.scalar, "pool": nc.gpsimd, "sp": nc.sync}
        self.sem = {}
        self.cnt = {}
        for k in list(self.eng):
            self.sem[k] = es.enter_context(nc.semaphore("s_" + k))
            self.cnt[k] = 0
        self.dnames = {}
        self.drr = {}
        for q, n in NDSEM.items():
            self.dnames[q] = []
            self.drr[q] = 0
            for i in range(n):
                nm = "d%s%d" % (q, i)
                self.sem[nm] = es.enter_context(nc.semaphore("s_" + nm))
                self.cnt[nm] = 0
                self.dnames[q].append(nm)
        self.waited = {k: {} for k in self.eng}

    def _deps(self, en, reads, writes, extra=None):
        deps = {}
        if extra is not None:
            deps[extra[0]] = extra[1]
        for b in reads:
            if b.lw is not None:
                s, v = b.lw
                if deps.get(s, 0) < v:
                    deps[s] = v
            if b.excl:
                for s, v in b.rd.items():
                    if s != en and deps.get(s, 0) < v:
                        deps[s] = v
        for b in writes:
            if b.lw is not None:
                s, v = b.lw
                if deps.get(s, 0) < v:
                    deps[s] = v
            for s, v in b.rd.items():
                if deps.get(s, 0) < v:
                    deps[s] = v
        e = self.eng[en]
        w = self.waited[en]
        need = []
        for s, v in deps.items():
            if v <= 0:
                continue
            if s == en and (en == "pe" or not SAME_ENG_SYNC):
                continue
            if w.get(s, 0) < v:
                need.append((s, v))
                w[s] = v
        attach = need.pop() if (need and ATTACH_WAIT) else None
        for s, v in need:
            e.wait_ge(self.sem[s], v)
        return attach

    def barrier(self, skip=()):
        for en in self.eng:
            e = self.eng[en]
            w = self.waited[en]
            for s, v in self.cnt.items():
                if any(s.startswith(p) for p in skip):
                    continue
                if v > 0 and s != en and w.get(s, 0) < v:
                    e.wait_ge(self.sem[s], v)
                    w[s] = v

    def op(self, en, fn, reads=(), writes=(), serial=False):
        attach = self._deps(en, reads, writes)
        if serial and self.cnt[en] > self.waited[en].get(en, 0):
            self.eng[en].wait_ge(self.sem[en], self.cnt[en])
            self.waited[en][en] = self.cnt[en]
        ins = fn(self.eng[en])
        if attach is not None:
            ins._wait_ge(self.sem[attach[0]], attach[1])
        ins.then_inc(self.sem[en], 1)
        self.cnt[en] += 1
        c = self.cnt[en]
        for b in reads:
            b.rd[en] = c
        for b in writes:
            b.lw = (en, c)
            b.rd = {}
        return ins

    def dma(self, q, out, in_, reads=(), writes=()):
        i = self.drr[q] % len(self.dnames[q])
        self.drr[q] += 1
        ds = self.dnames[q][i]
        en = DQ_ENG[q]
        attach = self._deps(en, reads, writes, extra=(ds, self.cnt[ds]))
        ins = self.eng[en].dma_start(out=out, in_=in_)
        if attach is not None:
            ins._wait_ge(self.sem[attach[0]], attach[1])
        ins.then_inc(self.sem[ds], 16)
        self.cnt[ds] += 16
        c = self.cnt[ds]
        for b in reads:
            b.rd[ds] = c
        for b in writes:
            b.lw = (ds, c)
            b.rd = {}
        return ins


class _Stop(Exception):
    pass


STOP = [None]


def _chk(tag):
    if STOP[0] == tag:
        raise _Stop()


def build_program():
    nc = bass.Bass("TRN2", target_bir_lowering=False)
    try:
        _build_body(nc)
    except AssertionError:
        if STOP[0] is None:
            raise
    return nc


DEBUG = [False]
DUMPS = {}


def _build_body(nc):
    DUMPS.clear()
    din = lambda n, s, d=F32: nc.dram_tensor(n, list(s), d, kind="ExternalInput").ap()
    dout = lambda n, s, d=F32: nc.dram_tensor(n, list(s), d, kind="ExternalOutput").ap()
    xall = din("xall", [NT, D])
    svec = din("svec", [128, 16])
    w_ada = din("w_ada", [D, 6 * D])
    bada_fm = din("bada_fm", [128, 48])
    bada_row = din("bada_row", [1, 6 * D])
    n12_fm = din("n12_fm", [128, 16])
    normf = din("normf", [1, D])
    win_a = din("win_a", [4, D, 640])
    win_b = din("win_b", [4, D, 512])
    win_g = din("win_g", [D, 16])
    hlb_fm = din("hlb_fm", [128, 16])
    convw_fm = din("convw_fm", [128, 36])
    alog_rep = din("alog_rep", [128, 288])
    dtb_rep = din("dtb_rep", [128, 288])
    onorm_fm = din("onorm_fm", [128, 8])
    st_a = din("st_a", [2, 4, 128, 128])
    st_b = din("st_b", [2, 4, 128, 128])
    w_out = din("w_out", [D, D])
    w_gate = din("w_gate", [D, DFF])
    w_up = din("w_up", [D, DFF])
    w_down = din("w_down", [DFF, D])
    cmask = din("cmask", [128, 7 * 128])
    csel = din("csel", [128, 256])
    cseg = din("cseg", [128, 512])
    y_own = dout("y_own", [NOWN, D])
    ns_a = dout("ns_a", [2, 2, 4, 128, 128])
    ns_b = dout("ns_b", [2, 2, 4, 128, 128])
    hT_scr = nc.dram_tensor("hT_scr", [128, 8, NT], BF16, kind="Internal").ap()
    oT_scr = nc.dram_tensor("oT_scr", [8, 128, NT], BF16, kind="Internal").ap()
    wo_bf = nc.dram_tensor("wo_bf", [D, D], BF16, kind="Internal").ap()
    wg_bf = nc.dram_tensor("wg_bf", [D, DFF], BF16, kind="Internal").ap()
    wu_bf = nc.dram_tensor("wu_bf", [D, DFF], BF16, kind="Internal").ap()
    wd_bf = nc.dram_tensor("wd_bf", [DFF, D], BF16, kind="Internal").ap()

    with ExitStack() as es:
        c = Ctx(nc, es)
        SB = lambda st, n, s, d=F32: st.enter_context(nc.sbuf_tensor(n, list(s), d))

        def dump(name, ap, rb):
            if not DEBUG[0] or name in DUMPS:
                return
            shp = list(ap.shape)
            t = nc.dram_tensor("dbg_" + name, shp, ap.dtype, kind="ExternalOutput").ap()
            DUMPS[name] = shp
            c.dma("sp", t, ap, reads=rb, writes=[Buf()])
        PB = [es.enter_context(nc.psum_tensor("pb%d" % i, [128, 512], F32)) for i in range(7)]
        PBb = [Buf(True) for _ in range(7)]
        PTh = [None, None]
        cm = SB(es, "cm", [128, 7, 128]); bcm = Buf()
        c.dma("sp", cm[:].rearrange("p a b -> p (a b)"), cmask, writes=[bcm])
        IDN, UT, SLO, LT, SUP, BLK, ONES = [cm[:, i, :] for i in range(7)]
        sel = SB(es, "sel", [128, 2, 128]); bsel = Buf()
        c.dma("sp", sel[:].rearrange("p a b -> p (a b)"), csel, writes=[bsel])
        seg = SB(es, "seg", [128, 512]); bseg = Buf()
        c.dma("sp", seg[:], cseg, writes=[bseg])
        idb = SB(es, "idb", [128, 128], BF16); bidb = Buf()
        c.op("dve", lambda e: e.tensor_copy(out=idb[:], in_=IDN), reads=[bcm], writes=[bidb])
        modp = SB(es, "modp", [128, 64]); bmod = Buf()
        lbt = SB(es, "lbt", [128, 16]); blb = Buf()
        cw = SB(es, "cw", [128, 36]); bcw = Buf()
        onw = SB(es, "onw", [128, 8]); bonw = Buf()
        c.dma("sp", cw[:], convw_fm, writes=[bcw])
        c.dma("sp", onw[:], onorm_fm, writes=[bonw])
        epsb = SB(es, "epsb", [128, 1]); beps = Buf()
        c.op("dve", lambda e: e.memset(epsb[:], EPS), writes=[beps])

        def rstd_from(ss_ap, out_ap, scale, rb, wb, n=1):
            c.op("act", lambda e: e.activation(out=out_ap, in_=ss_ap, func=AF.Ln, scale=scale, bias=epsb[:, 0:1]),
                 reads=rb + [beps], writes=wb)
            c.op("act", lambda e: e.activation(out=out_ap, in_=out_ap, func=AF.Exp, scale=-0.5), reads=wb, writes=wb)

        try:
            with ExitStack() as s0:
                sv = SB(s0, "sv", [128, 16]); bsv = Buf()
                c.dma("sp", sv[:], svec, writes=[bsv])
                ssil = SB(s0, "ssil", [128, 8, 2]); bss = Buf()
                c.op("act", lambda e: e.activation(out=ssil[:].rearrange("p k v -> p v k"), in_=sv[:].rearrange("p (v k) -> p v k", v=2), func=AF.Silu),
                     reads=[bsv], writes=[bss])
                bfm = SB(s0, "bfm", [128, 48]); bbfm = Buf()
                c.dma("sp", bfm[:], bada_fm, writes=[bbfm])
                n12 = SB(s0, "n12", [128, 16]); bn12 = Buf()
                c.dma("sp", n12[:], n12_fm, writes=[bn12])
                mfm = SB(s0, "mfm", [128, 48, 2]); bmfm = Buf()
                wad = [SB(s0, "wad%d" % i, [128, 8, 512]) for i in range(2)]
                bwad = bufs(2)
                wv = w_ada.rearrange("(k p) n -> p k n", p=128)
                ci = 0
                for cb in (0, 1, 2, 3, 6, 7, 8, 9):
                    t = wad[ci % 2]; bt = bwad[ci % 2]; ci += 1
                    c.dma("sp", t[:], wv[:, :, cb * 512:(cb + 1) * 512], writes=[bt])
                    pb = PB[ci % 2]; bpb = PBb[ci % 2]
                    for jj in range(4):
                        for k in range(8):
                            c.op("pe", lambda e: e.matmul(pb[:, jj * 2:jj * 2 + 2], lhsT=t[:, k, jj * 128:(jj + 1) * 128], rhs=ssil[:, k, :],
                                                          start=(k == 0), stop=(k == 7)), reads=[bt, bss], writes=[bpb])
                    j0 = cb * 4
                    c.op("dve", lambda e: e.tensor_tensor(out=mfm[:, j0:j0 + 4, :], in0=pb[:, 0:8].rearrange("p (j v) -> p j v", v=2),
                                                          in1=bfm[:, j0:j0 + 4].unsqueeze(2).broadcast_to([128, 4, 2]), op=ALU.add),
                         reads=[bpb, bbfm], writes=[bmfm])
                for which, (jsh, jsc, noff) in enumerate(((0, 8, 0), (24, 32, 8))):
                    for v in range(2):
                        o0 = (which * 2 + v) * 16
                        c.op("dve", lambda e: e.scalar_tensor_tensor(out=modp[:, o0:o0 + 8], in0=mfm[:, jsc:jsc + 8, v], scalar=1.0, in1=n12[:, noff:noff + 8],
                                                                     op0=ALU.add, op1=ALU.mult), reads=[bmfm, bn12], writes=[bmod])
                        c.op("dve", lambda e: e.tensor_copy(out=modp[:, o0 + 8:o0 + 16], in_=mfm[:, jsh:jsh + 8, v]), reads=[bmfm], writes=[bmod])
                hl = SB(s0, "hl", [128, 16]); bhl = Buf()
                c.dma("sp", hl[:], hlb_fm, writes=[bhl])
                c.op("dve", lambda e: e.tensor_tensor(out=hl[:, 0:8], in0=hl[:, 0:8], in1=hl[:, 8:16], op=ALU.subtract), reads=[bhl], writes=[bhl])
                c.op("act", lambda e: e.activation(out=lbt[:, 0:8], in_=hl[:, 0:8], func=AF.Sigmoid), reads=[bhl], writes=[blb])
                c.op("act", lambda e: e.activation(out=lbt[:, 8:16], in_=hl[:, 0:8], func=AF.Sigmoid, scale=-1.0), reads=[bhl], writes=[blb])

            c.barrier()
            dump('modp', modp[:], [bmod])
            dump('lbt', lbt[:], [blb])
            _chk('p0')
            A1 = lambda which, v, k: modp[:, (which * 2 + v) * 16 + k:(which * 2 + v) * 16 + k + 1]
            SH = lambda which, v, k: modp[:, (which * 2 + v) * 16 + 8 + k:(which * 2 + v) * 16 + 9 + k]

            bhT = bufs(NTI)
            boT = bufs(8)

            def norm_mod_transpose(st_pool, xt, bxt, which, v, hdst, bh, tmpn):
                ss, bs_, xn, bxn, junk, bj = tmpn
                PT, PTb = PTh
                c.op("pool", lambda e: e.memset(ss[:, 0:1], 0.0), writes=[bs_])
                c.op("act", lambda e: e.activation(out=junk[:], in_=xt, func=AF.Square, accum_out=ss[:, 0:1]), reads=[bxt], writes=[bj, bs_])
                rstd_from(ss[:, 0:1], ss[:, 1:2], 1.0 / D, [bs_], [bs_])
                c.op("act", lambda e: e.activation(out=xn[:], in_=xt, func=AF.Copy, scale=ss[:, 1:2]), reads=[bxt, bs_], writes=[bxn])
                if st_pool == "norm_only":
                    return
                mod_transpose(which, v, hdst, bh, tmpn)

            def mod_transpose(which, v, hdst, bh, tmpn):
                ss, bs_, xn, bxn, junk, bj = tmpn
                PT, PTb = PTh
                for k in range(8):
                    c.op("pe", lambda e: e.transpose(PT[:, k * 128:(k + 1) * 128], xn[:, k * 128:(k + 1) * 128], idb[:]), reads=[bxn, bidb], writes=[PTb])
                for k in range(8):
                    c.op("dve", lambda e: e.tensor_scalar(out=hdst[:, k, :], in0=PT[:, k * 128:(k + 1) * 128], scalar1=A1(which, v, k), scalar2=SH(which, v, k),
                                                          op0=ALU.mult, op1=ALU.add), reads=[PTb, bmod], writes=[bh])

            with ExitStack() as s2:
                SLOT = [SB(s2, "slot%d" % i, [128, NT]) for i in range(5)]
                SLB = [bufs(NTI) for _ in range(5)]
                ZT = SB(s2, "zt", [128, NT], BF16); bZT = bufs(NTI)
                gsm = {n: SB(s2, "g_" + n, [128, 2, NTI, 4]) for n in ("beta", "gg", "gc", "glt", "gl0", "gl1", "eg", "beg", "ekt", "dec0", "dec1")}
                bgs = Buf()
                with ExitStack() as s1:
                    PTh[0] = s1.enter_context(nc.psum_tensor("pbt1", [128, 1024], BF16)); PTh[1] = Buf(True)
                    GTM = SB(s1, "gtm", [128, NTI, 16]); bGTM = Buf()
                    W1 = 4
                    xts = [SB(s1, "xt%d" % i, [128, D]) for i in range(W1)]; bxts = bufs(W1)
                    hts = [SB(s1, "ht%d" % i, [128, 8, 128], BF16) for i in range(W1)]; bhts = bufs(W1)
                    ss1 = [SB(s1, "ss1_%d" % i, [128, 2]) for i in range(W1)]; bss1 = bufs(W1)
                    xn1 = [SB(s1, "xn1_%d" % i, [128, D], BF16) for i in range(W1)]; bxn1 = bufs(W1)
                    jk1 = [SB(s1, "jk1_%d" % i, [128, D], BF16) for i in range(W1)]; bjk1 = bufs(W1)
                    PT1 = [PTh[0]] * 2
                    bPT1 = [PTh[1]] * 2
                    wgf = SB(s1, "wgf", [128, 8, 16]); bwgf = Buf()
                    c.dma("sp", wgf[:], win_g.rearrange("(k p) n -> p k n", p=128), writes=[bwgf])
                    wgb = SB(s1, "wgb", [128, 8, 16], BF16); bwgb = Buf()
                    c.op("dve", lambda e: e.tensor_copy(out=wgb[:], in_=wgf[:]), reads=[bwgf], writes=[bwgb])
                    gT = SLOT[4]
                    bgT = SLB[4]

                    def p1_unit(i, sl):
                        xt = xts[sl]; bx = bxts[sl]; ht = hts[sl]; bh = bhts[sl]
                        ss = ss1[sl]; bs_ = bss1[sl]; xn = xn1[sl]; bxn = bxn1[sl]; junk = jk1[sl]; bj = bjk1[sl]
                        PT = PT1[sl % 2]; PTb = bPT1[sl % 2]
                        v = 1 if i < 32 else 0
                        c.dma("sp", xt[:], xall[i * 128:(i + 1) * 128, :], writes=[bx])
                        c.op("pool", lambda e: e.memset(ss[:, 0:1], 0.0), writes=[bs_])
                        yield
                        c.op("act", lambda e: e.activation(out=junk[:], in_=xt[:], func=AF.Square, accum_out=ss[:, 0:1]), reads=[bx], writes=[bj, bs_])
                        yield
                        c.op("act", lambda e: e.activation(out=ss[:, 1:2], in_=ss[:, 0:1], func=AF.Ln, scale=1.0 / D, bias=epsb[:, 0:1]), reads=[bs_, beps], writes=[bs_])
                        yield
                        c.op("act", lambda e: e.activation(out=ss[:, 1:2], in_=ss[:, 1:2], func=AF.Exp, scale=-0.5), reads=[bs_], writes=[bs_])
                        yield
                        c.op("act", lambda e: e.activation(out=xn[:], in_=xt[:], func=AF.Copy, scale=ss[:, 1:2]), reads=[bx, bs_], writes=[bxn])
                        yield
                        for k in range(8):
                            c.op("pe", lambda e: e.transpose(PT[:, k * 128:(k + 1) * 128], xn[:, k * 128:(k + 1) * 128], idb[:]), reads=[bxn, bidb], writes=[PTb])
                        for k in range(8):
                            eng_ = "dve" if k % 2 == 0 else "pool"
                            if eng_ == "pool":
                                eng_ = "dve"
                            c.op(eng_, lambda e: e.tensor_scalar(out=ht[:, k, :], in0=PT[:, k * 128:(k + 1) * 128], scalar1=A1(0, v, k), scalar2=SH(0, v, k),
                                                                  op0=ALU.mult, op1=ALU.add), reads=[PTb, bmod], writes=[bh])
                        yield
                        pg = PB[2 + sl]; bpg = PBb[2 + sl]
                        for k in range(8):
                            c.op("pe", lambda e: e.matmul(pg[0:16, 0:128], lhsT=wgb[:, k, :], rhs=ht[:, k, :], start=(k == 0), stop=(k == 7)),
                                 reads=[bwgb, bh], writes=[bpg])
                        c.dma("pool", hT_scr[:, :, i * 128:(i + 1) * 128], ht[:], reads=[bh], writes=[bhT[i]])
                        yield
                        c.op("act", lambda e: e.activation(out=gT[0:16, i * 128:(i + 1) * 128], in_=pg[0:16, 0:128], func=AF.Copy), reads=[bpg], writes=[bgT[i]])

                    pending = list(range(NTI))
                    active = []
                    free = list(range(W1))
                    while pending or active:
                        while pending and free:
                            i = pending.pop(0); sl = free.pop(0)
                            active.append((sl, p1_unit(i, sl)))
                        nxt_active = []
                        for sl, g in active:
                            try:
                                next(g)
                                nxt_active.append((sl, g))
                            except StopIteration:
                                free.append(sl)
                        active = nxt_active
                    gTc = SLOT[3]; bgTc = SLB[3]
                    c.op("act", lambda e: e.activation(out=gTc[0:16, 0:4096].rearrange("g (w r) -> g w r", r=64),
                                                       in_=gT[0:16, 0:4096].rearrange("g (r w) -> g w r", w=64), func=AF.Copy), reads=bgT[0:32], writes=bgTc[0:32])
                    c.op("act", lambda e: e.activation(out=gTc[0:16, 4096:NT], in_=gT[0:16, 4096:NT], func=AF.Copy), reads=bgT[32:], writes=bgTc[32:])
                    for j in range(NTI):
                        pg = PB[2 + j % 2]; bpg = PBb[2 + j % 2]
                        src = gTc[0:16, j * 128:(j + 1) * 128]
                        c.op("pe", lambda e: e.transpose(pg[:, 0:16], src, IDN[0:16, 0:16]), reads=[bgTc[j], bcm], writes=[bpg])
                        c.op("act", lambda e: e.activation(out=GTM[:, j, :], in_=pg[:, 0:16], func=AF.Copy), reads=[bpg], writes=[bGTM])
                    al = SB(s1, "al", [128, 2, NTI, 4]); dtb = SB(s1, "dtb", [128, 2, NTI, 4]); bal = Buf()
                    c.dma("sp", al[:].rearrange("p a b c -> p (a b c)"), alog_rep, writes=[bal])
                    c.dma("sp", dtb[:].rearrange("p a b c -> p (a b c)"), dtb_rep, writes=[bal])
                    gview = lambda lo: GTM[:, :, lo:lo + 8].rearrange("p t (d h) -> p d t h", d=2)
                    c.op("act", lambda e: e.activation(out=gsm["beta"][:], in_=gview(0), func=AF.Sigmoid), reads=[bGTM], writes=[bgs])
                    c.op("dve", lambda e: e.tensor_tensor(out=gsm["gg"][:], in0=gview(8), in1=dtb[:], op=ALU.add), reads=[bGTM, bal], writes=[bgs])
                    c.op("act", lambda e: e.activation(out=gsm["gg"][:], in_=gsm["gg"][:], func=AF.Exp), reads=[bgs], writes=[bgs])
                    c.op("act", lambda e: e.activation(out=gsm["gg"][:], in_=gsm["gg"][:], func=AF.Ln, bias=1.0), reads=[bgs], writes=[bgs])
                    c.op("act", lambda e: e.activation(out=al[:], in_=al[:], func=AF.Exp), reads=[bal], writes=[bal])
                    c.op("dve", lambda e: e.scalar_tensor_tensor(out=gsm["gg"][:], in0=gsm["gg"][:], scalar=-1.0, in1=al[:], op0=ALU.mult, op1=ALU.mult),
                         reads=[bgs, bal], writes=[bgs])
                    fl = lambda t: t[:].rearrange("p d t h -> p (d t h)")
                    pq = PB[4]; bpq = PBb[4]
                    for d in range(2):
                        rhs = gsm["gg"][:, d].rearrange("p t h -> p (t h)")
                        c.op("pe", lambda e: e.matmul(pq[:, d * 144:(d + 1) * 144], lhsT=(UT if d == 0 else LT), rhs=rhs, start=True, stop=True),
                             reads=[bgs, bcm], writes=[bpq])
                    c.op("dve", lambda e: e.tensor_copy(out=fl(gsm["gc"]), in_=pq[:, 0:288]), reads=[bpq], writes=[bgs])
                    for nm, lh, bl in (("glt", BLK, bcm), ("gl0", sel[:, 0, :], bsel), ("gl1", sel[:, 1, :], bsel)):
                        c.op("pe", lambda e: e.matmul(pq[:, 0:288], lhsT=lh, rhs=fl(gsm["gg"]), start=True, stop=True), reads=[bgs, bl], writes=[bpq])
                        c.op("dve", lambda e: e.tensor_copy(out=fl(gsm[nm]), in_=pq[:, 0:288]), reads=[bpq], writes=[bgs])
                    c.op("act", lambda e: e.activation(out=fl(gsm["eg"]), in_=fl(gsm["gc"]), func=AF.Exp), reads=[bgs], writes=[bgs])
                    c.op("dve", lambda e: e.tensor_tensor(out=fl(gsm["beg"]), in0=fl(gsm["beta"]), in1=fl(gsm["eg"]), op=ALU.mult), reads=[bgs], writes=[bgs])
                    c.op("dve", lambda e: e.tensor_tensor(out=fl(gsm["ekt"]), in0=fl(gsm["glt"]), in1=fl(gsm["gc"]), op=ALU.subtract), reads=[bgs], writes=[bgs])
                    c.op("act", lambda e: e.activation(out=fl(gsm["ekt"]), in_=fl(gsm["ekt"]), func=AF.Exp), reads=[bgs], writes=[bgs])
                    c.op("act", lambda e: e.activation(out=fl(gsm["dec0"]), in_=fl(gsm["gl0"]), func=AF.Exp), reads=[bgs], writes=[bgs])
                    c.op("act", lambda e: e.activation(out=fl(gsm["dec1"]), in_=fl(gsm["gl1"]), func=AF.Exp), reads=[bgs], writes=[bgs])

                c.barrier()
                for _n in gsm:
                    dump('g_' + _n, gsm[_n][:], [bgs])
                _chk('p1')
                bWo, bWg, bWu, bWd = bufs(8), bufs(8), bufs(8), bufs(NF)

                def issue_bg_precast():
                    for k in range(8):
                        c.dma("bg", wo_bf[k * 128:(k + 1) * 128, :], w_out[k * 128:(k + 1) * 128, :], writes=[bWo[k]])
                    for k in range(8):
                        c.dma("bg", wg_bf[k * 128:(k + 1) * 128, :], w_gate[k * 128:(k + 1) * 128, :], writes=[bWg[k]])
                        c.dma("bg", wu_bf[k * 128:(k + 1) * 128, :], w_up[k * 128:(k + 1) * 128, :], writes=[bWu[k]])
                    for f in range(NF):
                        c.dma("bg", wd_bf[f * 128:(f + 1) * 128, :], w_down[f * 128:(f + 1) * 128, :], writes=[bWd[f]])

                PBK = 256
                NPB = NT // PBK
                hTb = [SB(s2, "htb%d" % i, [128, 8, PBK], BF16) for i in range(2)]; bhTb = bufs(2)
                wbf = SB(s2, "wbf", [128, 8, 640], BF16); bwbf = Buf()
                PB.append(s2.enter_context(nc.psum_tensor("pb7", [128, 512], F32))); PBb.append(Buf(True))
                tmp512 = [SB(s2, "tmpw%d" % i, [128, 512]) for i in range(2)]; btmp512 = bufs(2)
                Sbuf = [SB(s2, "S%d" % i, [128, 128]) for i in range(18)]; bS = bufs(18)
                small = SB(s2, "small", [128, 16]); bsmall = bufs(4)
                s2a = ExitStack()
                TMPN = 0
                tmp = [SB(s2a, "tmp%d" % i, [128, 256]) for i in range(TMPN)]; btmp = bufs(TMPN)
                OF = SB(s2a, "of", [128, NT], BF16); bOF = bufs(NTI)
                seqs = [(0, 32), (32, 34), (34, 36)]
                hblk_state = [0]

                def load_hblk(blk):
                    i = hblk_state[0] % 2; hblk_state[0] += 1
                    c.dma("sp", hTb[i][:], hT_scr[:, :, blk * PBK:(blk + 1) * PBK], reads=bhT[blk * 2:blk * 2 + 2], writes=[bhTb[i]])
                    return hTb[i], bhTb[i]

                def proj_fm(ht, bh, col0, pbi):
                    pb = PB[pbi]; bpb = PBb[pbi]
                    for k in range(8):
                        c.op("pe", lambda e: e.matmul(pb[:, 0:PBK], lhsT=wbf[:, k, col0:col0 + 128], rhs=ht[:, k, :], start=(k == 0), stop=(k == 7)),
                             reads=[bwbf, bh], writes=[bpb])
                    return pb[:, 0:PBK], bpb

                rr = [0]

                def T(n=1):
                    i = rr[0] % TMPN; rr[0] += 1
                    return tmp[i], btmp[i]

                WUA = 6
                HNAMES = ("teg", "tqd", "tkd", "tkt", "tkt2", "tktm", "tat", "tos", "tsq")
                HW_ = {"tqd": 128, "tkd": 128, "tktm": 128, "tat": 128, "tkt": 128, "tkt2": 128, "tsq": 128}
                HBF = ("tqd", "tkd", "tkt2", "tktm", "tat", "tsq")
                hsets = []
                for w_ in range(WUA):
                    hsets.append({n: (SB(s2a, "hu%d_%s" % (w_, n), [128, HW_.get(n, 256)], BF16 if n in HBF else F32), Buf()) for n in HNAMES})
                Sbf = [SB(s2a, "Sbf%d" % i, [128, 128], BF16) for i in range(18)]; bSbf = bufs(18)
                onesb = SB(s2a, "onesb", [128, 128], BF16); bonesb = Buf()
                c.op("dve", lambda e: e.tensor_copy(out=onesb[:], in_=ONES), reads=[bcm], writes=[bonesb])
                VTMb = SLOT[1][:].bitcast(BF16)[:, 0:NT]
                hsm = [SB(s2a, "hsm%d" % w_, [128, 2]) for w_ in range(WUA)]; bhsm = bufs(WUA)
                GA1 = SB(s2a, "ga1", [128, NT]); bGA1 = bufs(NTI)

                def hgrn_unit(h, d, i, slot, ch, kidx, claimed, done):
                    FD, bFD = (FF, bFF) if d == 0 else (FB, bFB)
                    GA, bGA = GAs[d]
                    ts = hsets[slot]
                    pU, bU = PB[slot], PBb[slot]
                    pUb = pU[:, 0:64].bitcast(BF16)
                    MASK = UT if d == 0 else LT
                    glpos = 63 if d == 0 else 0
                    tsl = slice(i * 128, (i + 1) * 128)
                    G3 = GA[:, tsl].rearrange("p (c j) -> p c j", j=64)
                    teg, bteg = ts["teg"]; tqd, btqd = ts["tqd"]; tkd, btkd = ts["tkd"]; tkt, btkt = ts["tkt"]; tkt2, btkt2 = ts["tkt2"]
                    tktm, btktm = ts["tktm"]; tat, btat = ts["tat"]; tos, btos = ts["tos"]; tsq, btsq = ts["tsq"]
                    tdc = hsm[slot]; btdc = bhsm[slot]
                    c.op("act", lambda e: e.activation(out=teg[:, 0:128], in_=GA[:, tsl], func=AF.Exp), reads=[bGA[i]], writes=[bteg])
                    c.op("act", lambda e: e.activation(out=teg[:, 128:256], in_=GA[:, tsl], func=AF.Exp, scale=-1.0), reads=[bGA[i]], writes=[bteg])
                    c.op("dve", lambda e: e.tensor_tensor(out=tkt[:, 0:128].rearrange("p (c j) -> p c j", j=64), in0=G3[:, :, glpos:glpos + 1].broadcast_to([128, 2, 64]),
                                                          in1=G3, op=ALU.subtract), reads=[bGA[i]], writes=[btkt])
                    c.op("act", lambda e: e.activation(out=tdc[:, 0:2], in_=G3[:, :, glpos], func=AF.Exp), reads=[bGA[i]], writes=[btdc])
                    yield
                    c.op("pool", lambda e: e.tensor_tensor(out=tqd[:, 0:128], in0=QT[:, tsl], in1=teg[:, 0:128], op=ALU.mult), reads=[bQT[i], bteg], writes=[btqd])
                    c.op("pool", lambda e: e.tensor_tensor(out=tkd[:, 0:128], in0=FD[:, tsl], in1=teg[:, 128:256], op=ALU.mult), reads=[bFD[i], bteg], writes=[btkd])
                    c.op("act", lambda e: e.activation(out=tkt[:, 0:128], in_=tkt[:, 0:128], func=AF.Exp), reads=[btkt], writes=[btkt])
                    yield
                    c.op("dve", lambda e: e.tensor_tensor(out=tkt2[:, 0:128], in0=FD[:, tsl], in1=tkt[:, 0:128], op=ALU.mult), reads=[bFD[i], btkt], writes=[btkt2])
                    c.op("pe", lambda e: e.matmul(pU[:, 128:256], lhsT=tkd[:, 0:128], rhs=tqd[:, 0:128], start=True, stop=True), reads=[btkd, btqd], writes=[bU])
                    yield
                    c.op("pe", lambda e: e.transpose(pUb, tkt2[:, 0:128], idb[:]), reads=[btkt2, bidb], writes=[bU])
                    yield
                    c.op("dve", lambda e: e.tensor_tensor(out=tat[:, 0:128], in0=pU[:, 128:256], in1=MASK, op=ALU.mult), reads=[bU, bcm], writes=[btat])
                    c.op("dve", lambda e: e.tensor_copy(out=tktm[:, 0:128], in_=pUb), reads=[bU], writes=[btktm])
                    yield
                    corder = (0, 1) if d == 0 else (1, 0)
                    for ci_, cc in enumerate(corder):
                        pb_ = cc * 64
                        ucol = 384 if ci_ == 0 else 0
                        c.op("pe", lambda e: e.matmul(pU[:, ucol:ucol + 128], lhsT=tktm[pb_:pb_ + 64, 0:128], rhs=VTMb[pb_:pb_ + 64, tsl], start=True, stop=True),
                             reads=[btktm, bVTM[i]], writes=[bU], serial=(ci_ == 1))
                        if ci_ == 0:
                            yield
                    yield
                    while ch["turn"] != kidx:
                        yield
                    c.op("pe", lambda e: e.matmul(pU[:, 256:384], lhsT=VTMb[:, tsl], rhs=tat[:, 0:128], start=True, stop=False), reads=[bVTM[i], btat], writes=[bU])
                    for ci_, cc in enumerate(corder):
                        pb_ = cc * 64
                        S, bSc = ch["S"], ch["bS"]
                        Sb_, bSb_ = ch["Sb"], ch["bSb"]
                        ucol = 384 if ci_ == 0 else 0
                        c.op("pe", lambda e: e.matmul(pU[:, 256 + pb_:256 + pb_ + 64], lhsT=Sb_[:], rhs=tqd[:, pb_:pb_ + 64], start=False, stop=(ci_ == 1)),
                             reads=[bSb_, btqd], writes=[bU])
                        ch["si"] += 1
                        Sn, bSn = ch["bufs"][ch["si"] % 3]
                        Sbn, bSbn = ch["bbufs"][ch["si"] % 3]
                        c.op("dve", lambda e: e.scalar_tensor_tensor(out=Sn[:], in0=S[:], scalar=tdc[:, cc:cc + 1], in1=pU[:, ucol:ucol + 128],
                                                                     op0=ALU.mult, op1=ALU.add), reads=[bSc, btdc, bU], writes=[bSn])
                        c.op("pool", lambda e: e.tensor_copy(out=Sbn[:], in_=Sn[:]), reads=[bSn], writes=[bSbn])
                        ch["S"], ch["bS"] = Sn, bSn
                        ch["Sb"], ch["bSb"] = Sbn, bSbn
                        yield
                    if ch["last"] == i and ch["out"] is not None:
                        c.dma("sp", ch["out"], ch["S"][:], reads=[ch["bS"]], writes=[Buf()])
                    ch["turn"] += 1
                    if not claimed[i]:
                        claimed[i] = True
                        c.op("act", lambda e: e.activation(out=OF[:, tsl], in_=pU[:, 256:384], func=AF.Copy), reads=[bU], writes=[bOF[i]])
                        done[i] = True
                        return
                    while not done[i]:
                        yield
                    c.op("dve", lambda e: e.tensor_tensor(out=tos[:, 0:128], in0=pU[:, 256:384], in1=OF[:, tsl], op=ALU.add), reads=[bU, bOF[i]], writes=[btos])
                    yield
                    c.op("act", lambda e: e.activation(out=tsq[:, 0:128], in_=tos[:, 0:128], func=AF.Square), reads=[btos], writes=[btsq])
                    yield
                    c.op("pe", lambda e: e.matmul(pU[:, 128:256], lhsT=onesb[:], rhs=tsq[:, 0:128], start=True, stop=True), reads=[btsq, bonesb], writes=[bU])
                    yield
                    rstd_from(pU[:, 128:256], tos[:, 128:256], 1.0 / 128, [bU], [btos])
                    yield
                    c.op("dve", lambda e: e.tensor_tensor(out=tos[:, 0:128], in0=tos[:, 0:128], in1=tos[:, 128:256], op=ALU.mult), reads=[btos], writes=[btos])
                    yield
                    c.op("dve", lambda e: e.scalar_tensor_tensor(out=ZT[:, tsl], in0=tos[:, 0:128], scalar=onw[:, h:h + 1], in1=ZT[:, tsl],
                                                                 op0=ALU.mult, op1=ALU.mult), reads=[btos, bonw, bZT[i]], writes=[bZT[i]])

                def hgrn_scan(h):
                    claimed = [False] * NTI
                    done = [False] * NTI
                    per_dir = []
                    ci = 0
                    for d in range(2):
                        pend = []
                        for si_, (t0, t1) in enumerate(seqs):
                            bl = [(Sbuf[ci * 3 + i_], bS[ci * 3 + i_]) for i_ in range(3)]
                            ci += 1
                            bbl = [(Sbf[(ci - 1) * 3 + i_], bSbf[(ci - 1) * 3 + i_]) for i_ in range(3)]
                            ch = dict(S=bl[0][0], bS=bl[0][1], Sb=bbl[0][0], bSb=bbl[0][1], si=0, turn=0, bufs=bl, bbufs=bbl, last=(t1 - 1 if d == 0 else t0),
                                      out=(None if t0 == 0 else ns_a[(t0 - 32) // 2, d, h]))
                            if t0 == 0:
                                c.dma("sp", ch["S"][:], st_a[d, h], writes=[ch["bS"]])
                            else:
                                c.op("pool", lambda e: e.memset(ch["S"][:], 0.0), writes=[ch["bS"]])
                            c.op("act", lambda e: e.activation(out=ch["Sb"][:], in_=ch["S"][:], func=AF.Copy), reads=[ch["bS"]], writes=[ch["bSb"]])
                            tl = list(range(t0, t1)) if d == 0 else list(range(t1 - 1, t0 - 1, -1))
                            for kidx, i in enumerate(tl):
                                pend.append((d, i, ch, kidx))
                        per_dir.append(pend[:4] + pend[32:] + pend[4:32])
                    pending = []
                    for a_, b_ in zip(per_dir[0], per_dir[1]):
                        pending.append(a_); pending.append(b_)
                    active = []
                    free = list(range(WUA))
                    while pending or active:
                        while pending and free:
                            d, i, ch, kidx = pending.pop(0)
                            sl = free.pop(0)
                            active.append((sl, hgrn_unit(h, d, i, sl, ch, kidx, claimed, done)))
                        nxt_active = []
                        for sl, g in active:
                            try:
                                next(g)
                                nxt_active.append((sl, g))
                            except StopIteration:
                                free.append(sl)
                        active = nxt_active

                QT, VTM, FF, FB, GA = SLOT
                bQT, bVTM, bFF, bFB, bGA = SLB
                VTM3 = VTM[:].rearrange("p (t v) -> p t v", v=128)
                for h in range(4):
                    for k in range(8):
                        c.dma("pool", wbf[:, k, 0:640], win_a[h, k * 128:(k + 1) * 128, :], writes=[bwbf])
                    for blk in range(NPB):
                        ht, bh = load_hblk(blk)
                        bsl = slice(blk * PBK, (blk + 1) * PBK)
                        tb = list(range(blk * 2, blk * 2 + 2))
                        pbk = (blk % 2) * 4 if False else 0
                        pb, bpb = proj_fm(ht, bh, 0, 0)
                        c.op("act", lambda e: e.activation(out=QT[:, bsl], in_=pb, func=AF.Silu), reads=[bpb], writes=[bQT[t] for t in tb])
                        pb, bpb = proj_fm(ht, bh, 128, 1)
                        c.op("act", lambda e: e.activation(out=FF[:, bsl], in_=pb, func=AF.Sigmoid), reads=[bpb], writes=[bFF[t] for t in tb])
                        pb, bpb = proj_fm(ht, bh, 256, 2)
                        c.op("act", lambda e: e.activation(out=FB[:, bsl], in_=pb, func=AF.Sigmoid), reads=[bpb], writes=[bFB[t] for t in tb])
                        pb, bpb = proj_fm(ht, bh, 512, 3)
                        c.op("act", lambda e: e.activation(out=ZT[:, bsl], in_=pb, func=AF.Silu), reads=[bpb], writes=[bZT[t] for t in tb])
                        pb = PB[4 + blk % 2]; bpb = PBb[4 + blk % 2]
                        for tt in range(2):
                            for k in range(8):
                                c.op("pe", lambda e: e.matmul(pb[:, tt * 128:(tt + 1) * 128], lhsT=ht[:, k, tt * 128:(tt + 1) * 128], rhs=wbf[:, k, 384:512],
                                                              start=(k == 0), stop=(k == 7)), reads=[bwbf, bh], writes=[bpb])
                        c.op("dve", lambda e: e.tensor_copy(out=VTMb[:, bsl], in_=pb[:, 0:PBK]), reads=[bpb], writes=[bVTM[t] for t in tb])
                    _chk('a1')
                    if h == 0:
                        issue_bg_precast()
                    GAs = ((GA, bGA), (GA1, bGA1))
                    for d in range(2):
                        FD = FF if d == 0 else FB
                        bFD = bFF if d == 0 else bFB
                        GAd, bGAd = GAs[d]
                        lbc = lbt[:, d * 4 + h:d * 4 + h + 1]
                        omc = lbt[:, 8 + d * 4 + h:8 + d * 4 + h + 1]
                        for blk in range(NBLK):
                            bsl = slice(blk * 512, (blk + 1) * 512)
                            tb = list(range(blk * 4, blk * 4 + 4))
                            fbufs = [bFD[t] for t in tb]
                            gbufs = [bGAd[t] for t in tb]
                            c.op("dve", lambda e: e.tensor_scalar(out=FD[:, bsl], in0=FD[:, bsl], scalar1=omc, scalar2=lbc, op0=ALU.mult, op1=ALU.add),
                                 reads=fbufs + [blb], writes=fbufs)
                            tw = tmp512[blk % 2]; btw = btmp512[blk % 2]
                            c.op("act", lambda e: e.activation(out=tw[:], in_=FD[:, bsl], func=AF.Ln), reads=fbufs, writes=[btw])
                            if d == 0:
                                c.op("dve", lambda e: e.tensor_tensor_scan(out=GAd[:, bsl], data0=seg[:], data1=tw[:], initial=0.0, op0=ALU.mult, op1=ALU.add),
                                     reads=[btw, bseg], writes=gbufs)
                            else:
                                c.op("dve", lambda e: e.tensor_tensor_scan(out=GAd[:, bsl][:, ::-1], data0=seg[:], data1=tw[:][:, ::-1], initial=0.0,
                                                                           op0=ALU.mult, op1=ALU.add), reads=[btw, bseg], writes=gbufs)
                            c.op("pool", lambda e: e.tensor_scalar(out=FD[:, bsl], in0=FD[:, bsl], scalar1=-1.0, scalar2=1.0, op0=ALU.mult, op1=ALU.add),
                                 reads=fbufs, writes=fbufs)
                    hgrn_scan(h)
                    dump('OF', OF[:], bOF)
                    dump('ZTa', ZT[:], bZT)
                    c.dma("pool", oT_scr[h], ZT[:], reads=bZT, writes=[boT[h]])
                    _chk('a4')

                s2a.close()
                c.barrier()
                _chk('p2a')
                RAW, QN, KN, VC, VT2 = SLOT
                bRAW, bQN, bKN, bVC, bVT2 = SLB
                KTM = RAW; bKTM = bRAW
                OTM = VC; bOTM = bVC
                KTM3 = KTM[:].rearrange("p (t v) -> p t v", v=128)
                VT23 = VT2[:].rearrange("p (t v) -> p t v", v=128)
                OTM3 = OTM[:].rearrange("p (t v) -> p t v", v=128)

                def scanview(arr, j):
                    return arr[:, j * 128:(j + 1) * 128]

                def scantiles(j):
                    return [j]

                def zview(j):
                    if j < 32:
                        return ZT[:, 0:4096].rearrange("p (r w) -> p w r", w=64)[:, 2 * j:2 * j + 2, :]
                    return ZT[:, j * 128:(j + 1) * 128]

                def ztiles(j):
                    return list(range(32)) if j < 32 else [j]

                WU = 6
                TNAMES = ("tkb", "tr", "te", "tdm", "tdm2", "tat", "tul", "X", "tkk", "tvb", "twu", "tqb", "Xb", "twb", "tvnb")
                TW = {"tdm2": 128, "tat": 128, "tvb": 128, "twu": 128, "tqb": 128, "Xb": 128, "twb": 128, "tvnb": 128}
                TBF = ("tat", "tkk", "tvb", "tqb", "Xb", "twb", "tvnb")
                s2b = ExitStack()
                tsets = []
                for w_ in range(WU):
                    tsets.append({n: (SB(s2b, "u%d_%s" % (w_, n), [128, TW.get(n, 256)], BF16 if n in TBF else F32), Buf()) for n in TNAMES})
                Sbf2 = [SB(s2b, "Sbg%d" % i, [128, 128], BF16) for i in range(12)]; bSbf2 = bufs(12)
                smalls = [SB(s2b, "usm%d" % w_, [128, 4]) for w_ in range(WU)]; bsmalls = bufs(WU)

                def gdn_unit(h, d, j, slot, ch, done, claimed, kidx):
                    hg = 4 + h
                    ts = tsets[slot]
                    gs = lambda nm: gsm[nm][:, d, j, h:h + 1]
                    MA, MB = (UT, SLO) if d == 0 else (LT, SUP)
                    CMK, SMK, SMK2 = (UT, SUP, SLO) if d == 0 else (LT, SLO, SUP)
                    pU, bU = PB[slot], PBb[slot]
                    kT = scanview(KN, j); qT = scanview(QN, j)
                    rdK = [bKN[j]]; rdQ = [bQN[j]]
                    tkb, btkb = ts["tkb"]; tr, btr = ts["tr"]; te, bte = ts["te"]; tdm, btdm = ts["tdm"]; tdm2, btdm2 = ts["tdm2"]
                    tat, btat = ts["tat"]; tul, btul = ts["tul"]; X, bX = ts["X"]; tkk, btkk = ts["tkk"]; tvb, btvb = ts["tvb"]; twu, btwu = ts["twu"]
                    tqb, btqb = ts["tqb"]; Xb, bXb = ts["Xb"]; twb, btwb = ts["twb"]; tvnb, btvnb = ts["tvnb"]
                    c.op("pool", lambda e: e.tensor_copy(out=tqb[:], in_=qT), reads=rdQ, writes=[btqb])
                    c.op("dve", lambda e: e.tensor_scalar(out=tkb[:, 0:128], in0=KTM3[:, j, :], scalar1=gs("beta"), scalar2=None, op0=ALU.mult),
                         reads=[bKTM[j], bgs], writes=[btkb])
                    c.op("act", lambda e: e.activation(out=tr[:, 0:128], in_=MA, func=AF.Copy, scale=gs("gg")), reads=[bcm, bgs], writes=[btr])
                    yield
                    c.op("pe", lambda e: e.transpose(pU[:, 384:512], tkb[:, 0:128], IDN), reads=[btkb, bcm], writes=[bU])
                    c.op("pe", lambda e: e.matmul(pU[:, 0:128], lhsT=MB, rhs=tr[:, 0:128], start=True, stop=True), reads=[btr, bcm], writes=[bU])
                    yield
                    c.op("act", lambda e: e.activation(out=tkb[:, 128:256], in_=pU[:, 384:512], func=AF.Copy), reads=[bU], writes=[btkb])
                    kbT = tkb[:, 128:256]
                    c.op("act", lambda e: e.activation(out=te[:, 0:128], in_=pU[:, 0:128], func=AF.Exp), reads=[bU], writes=[bte])
                    yield
                    c.op("pe", lambda e: e.matmul(pU[:, 0:128], lhsT=kT, rhs=qT, start=True, stop=True), reads=rdK + rdQ, writes=[bU])
                    c.op("pe", lambda e: e.matmul(pU[:, 128:256], lhsT=kT, rhs=kbT, start=True, stop=True), reads=rdK + [btkb], writes=[bU])
                    c.op("pool", lambda e: e.tensor_tensor(out=tdm[:, 0:128], in0=te[:, 0:128], in1=CMK, op=ALU.mult), reads=[bte, bcm], writes=[btdm])
                    c.op("pool", lambda e: e.tensor_tensor(out=tdm[:, 128:256], in0=te[:, 0:128], in1=SMK, op=ALU.mult), reads=[bte, bcm], writes=[btdm])
                    yield
                    c.op("dve", lambda e: e.tensor_tensor(out=tat[:, 0:128], in0=pU[:, 0:128], in1=tdm[:, 0:128], op=ALU.mult), reads=[bU, btdm], writes=[btat])
                    c.op("dve", lambda e: e.tensor_tensor(out=tul[:, 0:128], in0=pU[:, 128:256], in1=tdm[:, 128:256], op=ALU.mult), reads=[bU, btdm], writes=[btul])
                    yield
                    c.op("pe", lambda e: e.transpose(pU[:, 256:384], tul[:, 0:128], IDN), reads=[btul, bcm], writes=[bU])
                    yield
                    c.op("act", lambda e: e.activation(out=tul[:, 128:256], in_=pU[:, 256:384], func=AF.Copy), reads=[bU], writes=[btul])
                    yield
                    c.op("dve", lambda e: e.tensor_tensor(out=X[:, 0:128], in0=IDN, in1=tul[:, 0:128], op=ALU.subtract), reads=[bcm, btul], writes=[bX])
                    cur, bcur = tul, btul
                    xs = 0
                    for lev in range(5):
                        nxt, bnxt = ts["te"] if lev % 2 == 0 else ts["tr"]
                        if lev < 4:
                            c.op("pe", lambda e: e.matmul(pU[:, 0:128], lhsT=cur[:, 128:256], rhs=cur[:, 0:128], start=True, stop=True), reads=[bcur], writes=[bU])
                        c.op("pe", lambda e: e.matmul(pU[:, 128:256], lhsT=cur[:, 0:128], rhs=cur[:, 128:256], start=True, stop=True), reads=[bcur], writes=[bU])
                        yield
                        if lev < 4:
                            c.op("act", lambda e: e.activation(out=nxt[:, :], in_=pU[:, 0:256], func=AF.Copy), reads=[bU], writes=[bnxt])
                        else:
                            c.op("act", lambda e: e.activation(out=nxt[:, 128:256], in_=pU[:, 128:256], func=AF.Copy), reads=[bU], writes=[bnxt])
                        yield
                        c.op("pe", lambda e: e.matmul(pU[:, 256:384], lhsT=nxt[:, 128:256], rhs=X[:, xs:xs + 128], start=True, stop=True), reads=[bnxt, bX], writes=[bU])
                        yield
                        c.op("dve", lambda e: e.tensor_tensor(out=X[:, 128 - xs:256 - xs], in0=X[:, xs:xs + 128], in1=pU[:, 256:384], op=ALU.add), reads=[bU, bX], writes=[bX])
                        xs = 128 - xs
                        cur, bcur = nxt, bnxt
                        yield
                    XT = X[:, xs:xs + 128]
                    c.op("act", lambda e: e.activation(out=Xb[:], in_=XT, func=AF.Copy), reads=[bX], writes=[bXb])
                    c.op("act", lambda e: e.activation(out=tkk[:, 0:128], in_=KTM3[:, j, :], func=AF.Copy, scale=gs("beg")), reads=[bKTM[j], bgs], writes=[btkk])
                    c.op("pool", lambda e: e.tensor_scalar(out=tkk[:, 128:256], in0=KTM3[:, j, :], scalar1=gs("ekt"), scalar2=None, op0=ALU.mult), reads=[bKTM[j], bgs], writes=[btkk])
                    c.op("act", lambda e: e.activation(out=tvb[:, 0:128], in_=VT23[:, j, :], func=AF.Copy, scale=gs("beta")), reads=[bVT2[j], bgs], writes=[btvb])
                    yield
                    c.op("pe", lambda e: e.matmul(pU[:, 0:128], lhsT=tkk[:, 0:128], rhs=Xb[:], start=True, stop=True), reads=[btkk, bXb], writes=[bU])
                    c.op("pe", lambda e: e.matmul(pU[:, 128:256], lhsT=Xb[:], rhs=tvb[:, 0:128], start=True, stop=True), reads=[btvb, bXb], writes=[bU])
                    yield
                    c.op("act", lambda e: e.activation(out=twb[:], in_=pU[:, 0:128], func=AF.Copy), reads=[bU], writes=[btwb])
                    c.op("act", lambda e: e.activation(out=twu[:], in_=pU[:, 128:256], func=AF.Copy), reads=[bU], writes=[btwu])
                    yield
                    while ch["turn"] != kidx:
                        yield
                    if not claimed[j]:
                        claimed[j] = True
                        first = True
                    else:
                        first = False
                        while not done[j]:
                            yield
                    tvns = (ts["tkb"], ts["tdm"])
                    corder = (0, 1) if d == 0 else (1, 0)
                    for ci_, cc in enumerate(corder):
                        pr = slice(cc * 64, cc * 64 + 64)
                        tvn, btvn = tvns[ci_]
                        pR, bR = pU, bU
                        S, bSc = ch["S"], ch["bS"]
                        Sb_, bSb_ = ch["Sb"], ch["bSb"]
                        c.op("pe", lambda e: e.matmul(pR[:, 0:128], lhsT=twb[:], rhs=Sb_[:], start=True, stop=True), reads=[btwb, bSb_], writes=[bR])
                        c.op("pe", lambda e: e.matmul(pR[:, 128:256], lhsT=tqb[:], rhs=Sb_[:], start=True, stop=True), reads=[btqb, bSb_], writes=[bR])
                        yield
                        c.op("dve", lambda e: e.tensor_tensor(out=tvnb[pr, :], in0=twu[pr, :], in1=pR[pr, 0:128], op=ALU.subtract), reads=[btwu, bR], writes=[btvnb])
                        yield
                        c.op("pe", lambda e: e.matmul(pR[:, 256:384], lhsT=tat[pr, 0:128], rhs=tvnb[pr, :], start=True, stop=True), reads=[btat, btvnb], writes=[bR])
                        c.op("pe", lambda e: e.matmul(pR[:, 384:512], lhsT=tkk[pr, 128:256], rhs=tvnb[pr, :], start=True, stop=True), reads=[btkk, btvnb], writes=[bR])
                        yield
                        ch["si"] += 1
                        Sn, bSn = ch["bufs"][ch["si"] % 3]
                        Sbn, bSbn = ch["bbufs"][ch["si"] % 2]
                        c.op("dve", lambda e: e.scalar_tensor_tensor(out=Sn[:], in0=S[:], scalar=gs("dec%d" % cc), in1=pR[:, 384:512], op0=ALU.mult, op1=ALU.add),
                             reads=[bSc, bgs, bR], writes=[bSn])
                        c.op("act", lambda e: e.activation(out=Sbn[:], in_=Sn[:], func=AF.Copy), reads=[bSn], writes=[bSbn])
                        ch["S"], ch["bS"] = Sn, bSn
                        ch["Sb"], ch["bSb"] = Sbn, bSbn
                        c.op("dve", lambda e: e.tensor_scalar(out=tvn[pr, 128:256], in0=pR[pr, 128:256], scalar1=gsm["eg"][pr, d, j, h:h + 1], scalar2=None, op0=ALU.mult),
                             reads=[bR, bgs], writes=[btvn])
                        c.op("dve", lambda e: e.tensor_tensor(out=tvn[pr, 128:256], in0=tvn[pr, 128:256], in1=pR[pr, 256:384], op=ALU.add), reads=[btvn, bR], writes=[btvn])
                        if first:
                            c.op("act", lambda e: e.activation(out=OTM3[pr, j, :], in_=tvn[pr, 128:256], func=AF.Copy), reads=[btvn], writes=[bOTM[j]])
                        else:
                            c.op("pool", lambda e: e.tensor_tensor(out=OTM3[pr, j, :], in0=OTM3[pr, j, :], in1=tvn[pr, 128:256], op=ALU.add), reads=[btvn, bOTM[j]], writes=[bOTM[j]])
                        yield
                    if ch["last"] == j and ch["out"] is not None:
                        c.dma("sp", ch["out"], ch["S"][:], reads=[ch["bS"]], writes=[Buf()])
                    ch["turn"] += 1
                    if first:
                        done[j] = True
                        return
                    tn, btn = ts["te"]
                    sm = smalls[slot]; bsm = bsmalls[slot]
                    c.op("pool", lambda e: e.memset(sm[:, 0:1], 0.0), writes=[bsm])
                    c.op("act", lambda e: e.activation(out=tn[:, 0:128], in_=OTM3[:, j, :], func=AF.Square, accum_out=sm[:, 0:1]), reads=[bOTM[j]], writes=[btn, bsm])
                    yield
                    rstd_from(sm[:, 0:1], sm[:, 1:2], 1.0 / 128, [bsm], [bsm])
                    yield
                    c.op("dve", lambda e: e.tensor_scalar(out=tn[:, 128:256], in0=OTM3[:, j, :], scalar1=sm[:, 1:2], scalar2=None, op0=ALU.mult), reads=[bOTM[j], bsm], writes=[btn])
                    yield
                    c.op("pe", lambda e: e.transpose(pU[:, 256:384], tn[:, 128:256], IDN), reads=[btn, bcm], writes=[bU])
                    yield
                    zv = zview(j)
                    zb = [bZT[t] for t in ztiles(j)]
                    pin = pU[:, 256:384].rearrange("p (c r) -> p c r", c=2) if j < 32 else pU[:, 256:384]
                    c.op("dve", lambda e: e.scalar_tensor_tensor(out=zv, in0=pin, scalar=onw[:, hg:hg + 1], in1=zv, op0=ALU.mult, op1=ALU.mult),
                         reads=[bU, bonw] + zb, writes=zb)

                def gdn_scan(h):
                    done = [False] * NTI
                    claimed = [False] * NTI
                    chains = {}
                    order = []
                    ci = 0
                    for si_, (t0, t1) in enumerate(seqs):
                        for d in range(2):
                            bl = [(Sbuf[ci * 3 + i], bS[ci * 3 + i]) for i in range(3)]
                            bbl = [(Sbf2[ci * 2 + i], bSbf2[ci * 2 + i]) for i in range(2)]
                            ch = dict(S=bl[0][0], bS=bl[0][1], Sb=bbl[0][0], bSb=bbl[0][1], si=0, turn=0, t0=t0, t1=t1, bufs=bl, bbufs=bbl,
                                      last=(t1 - 1 if d == 0 else t0), out=(None if t0 == 0 else ns_b[(t0 - 32) // 2, d, h]))
                            if t0 == 0:
                                c.dma("sp", ch["S"][:], st_b[d, h], writes=[ch["bS"]])
                            else:
                                c.op("pool", lambda e: e.memset(ch["S"][:], 0.0), writes=[ch["bS"]])
                            c.op("act", lambda e: e.activation(out=ch["Sb"][:], in_=ch["S"][:], func=AF.Copy), reads=[ch["bS"]], writes=[ch["bSb"]])
                            chains[(si_, d)] = ch
                            ci += 1
                    samp = []
                    for i in range(32):
                        samp.append((0, 0, i)); samp.append((0, 1, 31 - i))
                    pro = []
                    for si_ in (1, 2):
                        t0, t1 = seqs[si_]
                        for i in range(t1 - t0):
                            pro.append((si_, 0, t0 + i)); pro.append((si_, 1, t1 - 1 - i))
                    order = samp[:8] + pro + samp[8:]
                    pending = list(order)
                    active = []
                    free = list(range(WU))
                    while pending or active:
                        while pending and free:
                            si_, d, j = pending.pop(0)
                            sl = free.pop(0)
                            ch_ = chains[(si_, d)]
                            kidx = (j - ch_["t0"]) if d == 0 else (ch_["t1"] - 1 - j)
                            active.append((sl, gdn_unit(h, d, j, sl, ch_, done, claimed, kidx)))
                        nxt_active = []
                        for sl, g in active:
                            try:
                                next(g)
                                nxt_active.append((sl, g))
                            except StopIteration:
                                free.append(sl)
                        active = nxt_active

                for h in range(4):
                    hg = 4 + h
                    for k in range(8):
                        c.dma("pool", wbf[:, k, 0:512], win_b[h, k * 128:(k + 1) * 128, :], writes=[bwbf])
                    for blk in range(NPB):
                        ht, bh = load_hblk(blk)
                        bsl = slice(blk * PBK, (blk + 1) * PBK)
                        tb = list(range(blk * 2, blk * 2 + 2))
                        for qi, (DST, bDST) in enumerate(((QN, bQN), (KN, bKN), (VC, bVC))):
                            pb, bpb = proj_fm(ht, bh, qi * 128, qi)
                            if qi == 1:
                                c.op("dve", lambda e: e.tensor_copy(out=DST[:, bsl], in_=pb), reads=[bpb], writes=[bDST[t] for t in tb])
                            else:
                                c.op("act", lambda e: e.activation(out=DST[:, bsl], in_=pb, func=AF.Copy), reads=[bpb], writes=[bDST[t] for t in tb])
                        pb, bpb = proj_fm(ht, bh, 384, 3)
                        c.op("act", lambda e: e.activation(out=ZT[:, bsl], in_=pb, func=AF.Silu), reads=[bpb], writes=[bZT[t] for t in tb])
                    for qi, (DST, bDST) in enumerate(((QN, bQN), (KN, bKN), (VC, bVC))):
                        w0 = cw[:, h * 9 + qi * 3 + 0:h * 9 + qi * 3 + 1]
                        w1 = cw[:, h * 9 + qi * 3 + 1:h * 9 + qi * 3 + 2]
                        w2 = cw[:, h * 9 + qi * 3 + 2:h * 9 + qi * 3 + 3]
                        ceng = "dve"
                        RAWq, bRAWq = (RAW, bRAW) if qi != 1 else (VT2, bVT2)
                        c.op("act", lambda e: e.activation(out=RAWq[:, :], in_=DST[:, :], func=AF.Copy, scale=w1), reads=bDST + [bcw], writes=bRAWq)
                        for (lo, hi, sh) in ((0, 4096, 64), (4096, 4352, 1), (4352, 4608, 1)):
                            c.op(ceng, lambda e: e.scalar_tensor_tensor(out=RAWq[:, lo + sh:hi], in0=DST[:, lo:hi - sh], scalar=w0, in1=RAWq[:, lo + sh:hi],
                                                                         op0=ALU.mult, op1=ALU.add), reads=bDST + [bcw], writes=bRAWq)
                            c.op(ceng, lambda e: e.scalar_tensor_tensor(out=RAWq[:, lo:hi - sh], in0=DST[:, lo + sh:hi], scalar=w2, in1=RAWq[:, lo:hi - sh],
                                                                         op0=ALU.mult, op1=ALU.add), reads=bDST + [bcw], writes=bRAWq)
                        c.op("act", lambda e: e.activation(out=DST[:, 0:4096].rearrange("p (w r) -> p w r", r=64),
                                                           in_=RAWq[:, 0:4096].rearrange("p (r w) -> p w r", w=64), func=AF.Silu), reads=bRAWq[0:32], writes=bDST[0:32])
                        c.op("act", lambda e: e.activation(out=DST[:, 4096:NT], in_=RAWq[:, 4096:NT], func=AF.Silu), reads=bRAWq[32:], writes=bDST[32:])
                        if qi == 2:
                            for j in range(NTI):
                                p5 = PB[4 + j % 2]; b5 = PBb[4 + j % 2]
                                c.op("pe", lambda e: e.transpose(p5[:, 128:256], DST[:, j * 128:(j + 1) * 128], IDN), reads=[bDST[j], bcm], writes=[b5])
                                c.op("dve", lambda e: e.tensor_copy(out=VT23[:, j, :], in_=p5[:, 128:256]), reads=[b5], writes=[bVT2[j]])
                            continue
                        for blk in range(NBLK):
                            bsl = slice(blk * 512, (blk + 1) * 512)
                            db = [bDST[t] for t in range(blk * 4, blk * 4 + 4)]
                            tw = tmp512[blk % 2]; btw = btmp512[blk % 2]
                            c.op("act", lambda e: e.activation(out=tw[:], in_=DST[:, bsl], func=AF.Square), reads=db, writes=[btw])
                            pb = PB[4 + blk % 2]; bpb = PBb[4 + blk % 2]
                            c.op("pe", lambda e: e.matmul(pb[:, :], lhsT=ONES, rhs=tw[:], start=True, stop=True), reads=[btw, bcm], writes=[bpb])
                            rstd_from(pb[:, :], tw[:], 1.0, [bpb], [btw])
                            if qi == 0:
                                c.op("dve", lambda e: e.scalar_tensor_tensor(out=DST[:, bsl], in0=DST[:, bsl], scalar=float(128 ** -0.5), in1=tw[:],
                                                                             op0=ALU.mult, op1=ALU.mult), reads=db + [btw], writes=db)
                            else:
                                c.op("dve", lambda e: e.tensor_tensor(out=DST[:, bsl], in0=DST[:, bsl], in1=tw[:], op=ALU.mult), reads=db + [btw], writes=db)
                    for j in range(NTI):
                        p5 = PB[4 + j % 2]; b5 = PBb[4 + j % 2]
                        c.op("pe", lambda e: e.transpose(p5[:, 0:128], scanview(KN, j), IDN), reads=[bKN[j], bcm], writes=[b5])
                        c.op("act", lambda e: e.activation(out=KTM3[:, j, :], in_=p5[:, 0:128], func=AF.Copy), reads=[b5], writes=[bKTM[j]])
                    gdn_scan(h)
                    c.dma("pool", oT_scr[hg], ZT[:], reads=bZT, writes=[boT[hg]])

                s2b.close()
            PB.pop(); PBb.pop()
            c.barrier()
            _chk('p2b')
            with ExitStack() as s4:
                PTh[0] = s4.enter_context(nc.psum_tensor("pbt4", [128, 1024], BF16)); PTh[1] = Buf(True)
                wo = SB(s4, "wo", [128, 8, D], BF16); bwo = Buf()
                wg = SB(s4, "wg", [128, 8, DFF], BF16); bwg = Buf()
                wu = SB(s4, "wu", [128, 8, DFF], BF16); bwu = Buf()
                wd = SB(s4, "wd", [128, NF, D], BF16); bwd = Buf()
                for k in range(8):
                    c.dma("pool", wo[:, k, :], wo_bf[k * 128:(k + 1) * 128, :], reads=[bWo[k]], writes=[bwo])
                for k in range(8):
                    c.dma("pool", wg[:, k, :], wg_bf[k * 128:(k + 1) * 128, :], reads=[bWg[k]], writes=[bwg])
                    c.dma("pool", wu[:, k, :], wu_bf[k * 128:(k + 1) * 128, :], reads=[bWu[k]], writes=[bwu])
                for f in range(NF):
                    c.dma("pool", wd[:, f, :], wd_bf[f * 128:(f + 1) * 128, :], reads=[bWd[f]], writes=[bwd])
                gbc = SB(s4, "gbc", [128, 4, D]); bgbc = Buf()
                nfb = SB(s4, "nfb", [128, D]); bnfb = Buf()
                c.dma("sp", nfb[:], normf.partition_broadcast(128), writes=[bnfb])
                with ExitStack() as s40:
                    sv = SB(s40, "sv4", [128, 16]); bsv = Buf()
                    c.dma("sp", sv[:], svec, writes=[bsv])
                    c.op("act", lambda e: e.activation(out=sv[:], in_=sv[:], func=AF.Silu), reads=[bsv], writes=[bsv])
                    srep = SB(s40, "srep", [128, 16, 128]); bsrep = Buf()
                    c.op("dve", lambda e: e.tensor_copy(out=srep[:], in_=sv[:].unsqueeze(2).broadcast_to([128, 16, 128])), reads=[bsv], writes=[bsrep])
                    brow = SB(s40, "brow", [128, 512]); bbrow = Buf()
                    wad = [SB(s40, "wad40", [128, 8, 512])] * 2; bwad = [Buf()] * 2
                    wv = w_ada.rearrange("(k p) n -> p k n", p=128)
                    ci = 0
                    for gi, cb0 in enumerate((4, 10)):
                        for hh in range(2):
                            cb = cb0 + hh
                            t = wad[ci % 2]; bt = bwad[ci % 2]; ci += 1
                            c.dma("sp", t[:], wv[:, :, cb * 512:(cb + 1) * 512], writes=[bt])
                            c.dma("sp", brow[:], bada_row[:, cb * 512:(cb + 1) * 512].partition_broadcast(128), writes=[bbrow])
                            for v in range(2):
                                pb = PB[v]; bpb = PBb[v]
                                for k in range(8):
                                    c.op("pe", lambda e: e.matmul(pb[:, :], lhsT=srep[:, v * 8 + k, :], rhs=t[:, k, :], start=(k == 0), stop=(k == 7)), reads=[bsrep, bt], writes=[bpb])
                                c.op("dve", lambda e: e.tensor_tensor(out=gbc[:, gi * 2 + v, hh * 512:(hh + 1) * 512], in0=pb[:, :], in1=brow[:], op=ALU.add),
                                     reads=[bpb, bbrow], writes=[bgbc])
                c.barrier(skip=("dpool",))
                BT = 128
                oTb = [SB(s4, "otb0", [128, 8, BT], BF16)] * 2; boTb = [Buf()] * 2
                xts = [SB(s4, "x4%d" % i, [128, D]) for i in range(2)]; bxts = bufs(2)
                x1s = [SB(s4, "x1%d" % i, [128, D]) for i in range(2)]; bx1s = bufs(2)
                h2s = [SB(s4, "h20", [128, 8, BT], BF16)] * 2; bh2s = [Buf()] * 2
                aT = SB(s4, "aT", [128, NF, BT], BF16); baT = bufs(NF)
                ss = SB(s4, "ss4", [128, 4]); xn = SB(s4, "xn4", [128, D], BF16); junk = SB(s4, "junk4", [128, D], BF16)
                tmpn = (ss, Buf(), xn, Buf(), junk, Buf())
                tsg = [SB(s4, "tsg%d" % i, [128, BT]) for i in range(2)]; btsg = bufs(2)
                byout = Buf()
                nown_blk = NOWN // BT
                nsamp_blk = 2048 // BT
                vof = lambda blk: 1 if blk < nsamp_blk else 0

                def st_op(blk):
                    tok0 = blk * BT if blk < nsamp_blk else 4096 + (blk - nsamp_blk) * BT
                    v = vof(blk)
                    ob = oTb[blk % 2]; bob = boTb[blk % 2]
                    xt = xts[blk % 2]; bx = bxts[blk % 2]; x1 = x1s[blk % 2]; bx1 = bx1s[blk % 2]
                    c.dma("sp", ob[:], oT_scr[:, :, tok0:tok0 + BT].rearrange("h p t -> p h t"), reads=boT, writes=[bob])
                    c.dma("sp", xt[:], xall[tok0:tok0 + 128, :], writes=[bx])
                    for hh in range(2):
                        pb = PB[hh]; bpb = PBb[hh]
                        for hd in range(8):
                            c.op("pe", lambda e: e.matmul(pb[:, :], lhsT=ob[:, hd, :], rhs=wo[:, hd, hh * 512:(hh + 1) * 512],
                                                          start=(hd == 0), stop=(hd == 7)), reads=[bob, bwo], writes=[bpb])
                        hs = slice(hh * 512, (hh + 1) * 512)
                        c.op("dve", lambda e: e.tensor_tensor(out=x1[:, hs], in0=pb[:, :], in1=gbc[:, v, hs], op=ALU.mult), reads=[bpb, bgbc], writes=[bx1])
                        c.op("pool", lambda e: e.tensor_tensor(out=x1[:, hs], in0=x1[:, hs], in1=xt[:, hs], op=ALU.add), reads=[bx1, bx], writes=[bx1])
                    norm_mod_transpose("norm_only", x1[:], bx1, 1, v, None, None, tmpn)

                def st_tr(blk):
                    mod_transpose(1, vof(blk), h2s[blk % 2], bh2s[blk % 2], tmpn)

                def st_gu(blk):
                    h2 = h2s[blk % 2]; bh2 = bh2s[blk % 2]
                    for f in range(NF):
                        pg = PB[2 + f % 2]; bpg = PBb[2 + f % 2]
                        pu = PB[4 + f % 2]; bpu = PBb[4 + f % 2]
                        for k in range(8):
                            c.op("pe", lambda e: e.matmul(pg[:, 0:BT], lhsT=wg[:, k, f * 128:(f + 1) * 128], rhs=h2[:, k, :], start=(k == 0), stop=(k == 7)),
                                 reads=[bwg, bh2], writes=[bpg])
                        for k in range(8):
                            c.op("pe", lambda e: e.matmul(pu[:, 0:BT], lhsT=wu[:, k, f * 128:(f + 1) * 128], rhs=h2[:, k, :], start=(k == 0), stop=(k == 7)),
                                 reads=[bwu, bh2], writes=[bpu])
                        tg = tsg[f % 2]; btg = btsg[f % 2]
                        c.op("act", lambda e: e.activation(out=tg[:], in_=pg[:, 0:BT], func=AF.Silu), reads=[bpg], writes=[btg])
                        c.op("dve", lambda e: e.tensor_tensor(out=aT[:, f, :], in0=tg[:], in1=pu[:, 0:BT], op=ALU.mult), reads=[btg, bpu], writes=[baT[f]])

                def st_down(blk):
                    v = vof(blk)
                    x1 = x1s[blk % 2]; bx1 = bx1s[blk % 2]
                    yo = xts[blk % 2]; byo = bxts[blk % 2]
                    for hh in range(2):
                        pb = PB[hh]; bpb = PBb[hh]
                        for f in range(NF):
                            c.op("pe", lambda e: e.matmul(pb[:, :], lhsT=aT[:, f, :], rhs=wd[:, f, hh * 512:(hh + 1) * 512],
                                                          start=(f == 0), stop=(f == NF - 1)), reads=[baT[f], bwd], writes=[bpb])
                        hs = slice(hh * 512, (hh + 1) * 512)
                        c.op("dve", lambda e: e.tensor_tensor(out=yo[:, hs], in0=pb[:, :], in1=gbc[:, 2 + v, hs], op=ALU.mult), reads=[bpb, bgbc], writes=[byo])
                        c.op("pool", lambda e: e.tensor_tensor(out=yo[:, hs], in0=yo[:, hs], in1=x1[:, hs], op=ALU.add), reads=[byo, bx1], writes=[byo])
                    bs_ = tmpn[1]
                    c.op("pool", lambda e: e.memset(ss[:, 2:3], 0.0), writes=[bs_])
                    c.op("act", lambda e: e.activation(out=junk[:], in_=yo[:], func=AF.Square, accum_out=ss[:, 2:3]), reads=[byo], writes=[tmpn[5], bs_])
                    rstd_from(ss[:, 2:3], ss[:, 3:4], 1.0 / D, [bs_], [bs_])
                    c.op("dve", lambda e: e.scalar_tensor_tensor(out=yo[:], in0=yo[:], scalar=ss[:, 3:4], in1=nfb[:], op0=ALU.mult, op1=ALU.mult),
                         reads=[byo, bs_, bnfb], writes=[byo])
                    r0 = blk * BT
                    c.dma("sp", y_own[r0:r0 + 128, :], yo[:], reads=[byo], writes=[byout])

                st_op(0)
                st_tr(0)
                for blk in range(nown_blk):
                    if blk + 1 < nown_blk:
                        st_op(blk + 1)
                    st_gu(blk)
                    if blk + 1 < nown_blk:
                        st_tr(blk + 1)
                    st_down(blk)
        except _Stop:
            pass
        for q in NDSEM:
            for nm in c.dnames[q]:
                if c.cnt[nm]:
                    nc.sync.wait_ge(c.sem[nm], c.cnt[nm])


_CONST = {}


def _consts():
    if _CONST:
        return _CONST
    u = np.arange(128)[:, None]
    t = np.arange(128)[None, :]
    same = (u // 64) == (t // 64)
    ident = (u == t)
    UT = same & (u <= t)
    SLO = same & (u > t)
    LT = same & (u >= t)
    SUP = same & (u < t)
    ones = np.ones((128, 128), bool)
    cm = np.stack([ident, UT, SLO, LT, SUP, same, ones], axis=1).astype(np.float32).reshape(128, 7 * 128)
    sel = np.stack([np.broadcast_to(u < 64, (128, 128)), np.broadcast_to(u >= 64, (128, 128))], axis=1).astype(np.float32).reshape(128, 256)
    seg = np.ones((128, 512), np.float32)
    seg[:, ::64] = 0.0
    _CONST.update(cmask=np.ascontiguousarray(cm), csel=np.ascontiguousarray(sel), cseg=seg)
    return _CONST


def _fm(vec):
    return np.ascontiguousarray(np.asarray(vec, np.float32).reshape(-1, 128).T)


_PROG = {}


def kernel(x_prompt, x_sample, c, state_hgrn, state_gdn, c_ctx, w_ada, b_ada, norm1, norm2, w_in, conv_w, hgrn_lb,
           gdn_A_log, gdn_dt_bias, hgrn_out_norm, gdn_out_norm, w_out, w_gate, w_up, w_down, norm_f):
    f32 = lambda a: np.ascontiguousarray(np.asarray(a, dtype=np.float32))
    x_prompt, x_sample, c, state_hgrn, state_gdn, c_ctx = map(f32, (x_prompt, x_sample, c, state_hgrn, state_gdn, c_ctx))
    w_ada, b_ada, norm1, norm2, w_in, conv_w, hgrn_lb = map(f32, (w_ada, b_ada, norm1, norm2, w_in, conv_w, hgrn_lb))
    gdn_A_log, gdn_dt_bias, hgrn_out_norm, gdn_out_norm = map(f32, (gdn_A_log, gdn_dt_bias, hgrn_out_norm, gdn_out_norm))
    w_out, w_gate, w_up, w_down, norm_f = map(f32, (w_out, w_gate, w_up, w_down, norm_f))
    if "nc" not in _PROG:
        _PROG["nc"] = build_program()
    nc = _PROG["nc"]
    cst = _consts()
    W = w_in[0]
    offs = np.cumsum([0, 512, 512, 512, 512, 512, 512, 512, 512, 512, 8, 8])
    a_q, a_ff, a_fb, a_i, a_g, b_q, b_k, b_v, b_z, b_beta, b_a = [W[:, offs[i]:offs[i + 1]] for i in range(11)]
    a_f = [a_ff, a_fb]
    in_maps = []
    for core in range(8):
        p, e = core // 2, core % 2
        ds = [0, 1] if e == 0 else [1, 0]
        fl = (lambda a: a[::-1]) if e else (lambda a: a)
        pr = [4 * p + 2 * e, 4 * p + 2 * e + 1]
        xall = np.concatenate([fl(x_sample[p]), fl(x_prompt[pr[0]]), fl(x_prompt[pr[1]])], axis=0)
        hs = lambda a, h: a[:, h * 128:(h + 1) * 128]
        win_a = np.stack([np.concatenate([hs(a_q, h), hs(a_f[ds[0]], h), hs(a_f[ds[1]], h), hs(a_i, h), hs(a_g, h)], axis=1) for h in range(4)])
        win_b = np.stack([np.concatenate([hs(b_q, h), hs(b_k, h), hs(b_v, h), hs(b_z, h)], axis=1) for h in range(4)])
        win_g = np.concatenate([b_beta[:, ds[0] * 4:ds[0] * 4 + 4], b_beta[:, ds[1] * 4:ds[1] * 4 + 4],
                                b_a[:, ds[0] * 4:ds[0] * 4 + 4], b_a[:, ds[1] * 4:ds[1] * 4 + 4]], axis=1)
        hlb = np.stack([np.stack([_fm(hgrn_lb[l, ds[d]]) for d in range(2)], axis=1) for l in range(2)], axis=1)
        cwt = conv_w[0][::-1] if e else conv_w[0]
        convw = np.zeros((128, 4, 3, 3), np.float32)
        for h in range(4):
            for qi in range(3):
                convw[:, h, qi, :] = cwt[:, qi * 512 + h * 128:qi * 512 + (h + 1) * 128].T
        alog = np.broadcast_to(gdn_A_log[0][ds][:, None, :], (2, NTI, 4)).reshape(1, -1)
        dtb = np.broadcast_to(gdn_dt_bias[0][ds][:, None, :], (2, NTI, 4)).reshape(1, -1)
        m = dict(
            xall=np.ascontiguousarray(xall),
            svec=np.ascontiguousarray(np.concatenate([_fm(c_ctx), _fm(c[p])], axis=1)),
            w_ada=w_ada[0], bada_fm=_fm(b_ada[0]), bada_row=b_ada[0][None, :],
            n12_fm=np.ascontiguousarray(np.concatenate([_fm(norm1[0]), _fm(norm2[0])], axis=1)),
            normf=norm_f[None, :],
            win_a=np.ascontiguousarray(win_a), win_b=np.ascontiguousarray(win_b), win_g=np.ascontiguousarray(win_g),
            hlb_fm=np.ascontiguousarray(hlb.reshape(128, 16)),
            convw_fm=np.ascontiguousarray(convw.reshape(128, 36)),
            alog_rep=np.ascontiguousarray(np.broadcast_to(alog, (128, 288))),
            dtb_rep=np.ascontiguousarray(np.broadcast_to(dtb, (128, 288))),
            onorm_fm=np.ascontiguousarray(np.concatenate([_fm(hgrn_out_norm[0]), _fm(gdn_out_norm[0])], axis=1)),
            st_a=np.ascontiguousarray(state_hgrn[p, 0][ds]), st_b=np.ascontiguousarray(state_gdn[p, 0][ds]),
            w_out=w_out[0], w_gate=w_gate[0], w_up=w_up[0], w_down=w_down[0],
            cmask=cst["cmask"], csel=cst["csel"], cseg=cst["cseg"],
        )
        in_maps.append(m)
    if _PROG.get('debug_hook'):
        return _PROG['debug_hook'](nc, in_maps)
    res = run_bass_kernel_spmd(nc, in_maps, core_ids=list(range(8)))
    y_prompt = np.zeros((16, 256, D), np.float32)
    y_sample = np.zeros((4, 4096, D), np.float32)
    nsa = np.zeros((16, 1, 2, 4, 128, 128), np.float32)
    nsb = np.zeros((16, 1, 2, 4, 128, 128), np.float32)
    for core in range(8):
        p, e = core // 2, core % 2
        ds = [0, 1] if e == 0 else [1, 0]
        r = res.results[core]
        yo = np.asarray(r["y_own"], np.float32)
        if e == 0:
            y_sample[p, 0:2048] = yo[0:2048]
        else:
            y_sample[p, 2048:4096] = yo[0:2048][::-1]
        for j in range(2):
            seq = 4 * p + 2 * e + j
            blk = yo[2048 + 256 * j:2048 + 256 * (j + 1)]
            y_prompt[seq] = blk[::-1] if e else blk
            for d in range(2):
                nsa[seq, 0, ds[d]] = np.asarray(r["ns_a"], np.float32)[j, d]
                nsb[seq, 0, ds[d]] = np.asarray(r["ns_b"], np.float32)[j, d]
    return (y_prompt, y_sample, nsa, nsb)
```

```python
import numpy as np
import ml_dtypes
import concourse.bass as bass
import concourse.mybir as mybir
from concourse.bass_utils import run_bass_kernel_spmd
from contextlib import ExitStack

F32 = mybir.dt.float32
BF16 = mybir.dt.bfloat16
AF = mybir.ActivationFunctionType
ALU = mybir.AluOpType

D = 1024
NT = 4608
NTI = 36
NBLK = 9
NOWN = 2560
DFF = 2816
NF = 22
EPS = 1e-6
SAME_ENG_SYNC = True
ATTACH_WAIT = True


class Buf:
    __slots__ = ("lw", "rd", "excl")

    def __init__(self, excl=False):
        self.lw = None
        self.rd = {}
        self.excl = excl


def bufs(n):
    return [Buf() for _ in range(n)]


NDSEM = {"sp": 40, "pool": 16, "bg": 40}
DQ_ENG = {"sp": "sp", "pool": "pool", "bg": "pool"}


class Ctx:
    def __init__(self, nc, es):
        self.nc = nc
        self.eng = {"pe": nc.tensor, "dve": nc.vector, "act": nc.scalar, "pool": nc.gpsimd, "sp": nc.sync}
        self.sem = {}
        self.cnt = {}
        for k in list(self.eng):
            self.sem[k] = es.enter_context(nc.semaphore("s_" + k))
            self.cnt[k] = 0
        self.dnames = {}
        self.drr = {}
        for q, n in NDSEM.items():
            self.dnames[q] = []
            self.drr[q] = 0
            for i in range(n):
                nm = "d%s%d" % (q, i)
                self.sem[nm] = es.enter_context(nc.semaphore("s_" + nm))
                self.cnt[nm] = 0
                self.dnames[q].append(nm)
        self.waited = {k: {} for k in self.eng}
        self.hist = {}

    def _deps(self, en, reads, writes, extra=None):
        deps = {}
        if extra is not None:
            deps[extra[0]] = extra[1]
        for b in reads:
            if b.lw is not None:
                s, v = b.lw
                if deps.get(s, 0) < v:
                    deps[s] = v
            if b.excl:
                for s, v in b.rd.items():
                    if s != en and deps.get(s, 0) < v:
                        deps[s] = v
        for b in writes:
            if b.lw is not None:
                s, v = b.lw
                if deps.get(s, 0) < v:
                    deps[s] = v
            for s, v in b.rd.items():
                if deps.get(s, 0) < v:
                    deps[s] = v
        e = self.eng[en]
        w = self.waited[en]
        need = []
        for s, v in deps.items():
            if v <= 0:
                continue
            if s == en and (en == "pe" or not SAME_ENG_SYNC):
                continue
            if w.get(s, 0) < v:
                need.append((s, v))
                w[s] = v
        for s, v in list(need):
            snap = self.hist.get((s, v))
            if snap:
                for s2, v2 in snap.items():
                    if w.get(s2, 0) < v2:
                        w[s2] = v2
        need = [(s, v) for (s, v) in need if w.get(s, 0) <= v]
        attach = need.pop() if (need and ATTACH_WAIT) else None
        for s, v in need:
            e.wait_ge(self.sem[s], v)
        return attach

    def barrier(self, skip=()):
        for en in self.eng:
            e = self.eng[en]
            w = self.waited[en]
            for s, v in self.cnt.items():
                if any(s.startswith(p) for p in skip):
                    continue
                if v > 0 and s != en and w.get(s, 0) < v:
                    e.wait_ge(self.sem[s], v)
                    w[s] = v

    def op(self, en, fn, reads=(), writes=(), serial=False):
        attach = self._deps(en, reads, writes)
        if serial and self.cnt[en] > self.waited[en].get(en, 0):
            self.eng[en].wait_ge(self.sem[en], self.cnt[en])
            self.waited[en][en] = self.cnt[en]
        ins = fn(self.eng[en])
        if attach is not None:
            ins._wait_ge(self.sem[attach[0]], attach[1])
        ins.then_inc(self.sem[en], 1)
        self.cnt[en] += 1
        c = self.cnt[en]
        self.hist[(en, c)] = dict(self.waited[en])
        for b in reads:
            b.rd[en] = c
        for b in writes:
            b.lw = (en, c)
            b.rd = {}
        return ins

    def dma(self, q, out, in_, reads=(), writes=()):
        i = self.drr[q] % len(self.dnames[q])
        self.drr[q] += 1
        ds = self.dnames[q][i]
        en = DQ_ENG[q]
        attach = self._deps(en, reads, writes, extra=(ds, self.cnt[ds]))
        ins = self.eng[en].dma_start(out=out, in_=in_)
        if attach is not None:
            ins._wait_ge(self.sem[attach[0]], attach[1])
        ins.then_inc(self.sem[ds], 16)
        self.cnt[ds] += 16
        c = self.cnt[ds]
        self.hist[(ds, c)] = dict(self.waited[en])
        for b in reads:
            b.rd[ds] = c
        for b in writes:
            b.lw = (ds, c)
            b.rd = {}
        return ins


class _Stop(Exception):
    pass


STOP = [None]


def _chk(tag):
    if STOP[0] == tag:
        raise _Stop()


def build_program():
    nc = bass.Bass("TRN2", target_bir_lowering=False)
    try:
        _build_body(nc)
    except AssertionError:
        if STOP[0] is None:
            raise
    return nc


DEBUG = [False]
DUMPS = {}


def _build_body(nc):
    DUMPS.clear()
    din = lambda n, s, d=F32: nc.dram_tensor(n, list(s), d, kind="ExternalInput").ap()
    dout = lambda n, s, d=F32: nc.dram_tensor(n, list(s), d, kind="ExternalOutput").ap()
    xall = din("xall", [NT, D])
    svec = din("svec", [128, 16])
    w_ada = din("w_ada", [D, 6 * D])
    bada_fm = din("bada_fm", [128, 48])
    bada_row = din("bada_row", [1, 6 * D])
    n12_fm = din("n12_fm", [128, 16])
    normf = din("normf", [1, D])
    win_a = din("win_a", [4, D, 640])
    win_b = din("win_b", [4, D, 512])
    win_g = din("win_g", [D, 16])
    hlb_fm = din("hlb_fm", [128, 16])
    convw_fm = din("convw_fm", [128, 36])
    alog_rep = din("alog_rep", [128, 288])
    dtb_rep = din("dtb_rep", [128, 288])
    onorm_fm = din("onorm_fm", [128, 8])
    st_a = din("st_a", [2, 4, 128, 128])
    st_b = din("st_b", [2, 4, 128, 128])
    w_out = din("w_out", [D, D])
    w_gate = din("w_gate", [D, DFF])
    w_up = din("w_up", [D, DFF])
    w_down = din("w_down", [DFF, D])
    cmask = din("cmask", [128, 7 * 128])
    csel = din("csel", [128, 256])
    cseg = din("cseg", [128, 512])
    y_own = dout("y_own", [NOWN, D])
    ns_a = dout("ns_a", [2, 2, 4, 128, 128])
    ns_b = dout("ns_b", [2, 2, 4, 128, 128])
    hT_scr = nc.dram_tensor("hT_scr", [128, 8, NT], BF16, kind="Internal").ap()
    oT_scr = nc.dram_tensor("oT_scr", [8, 128, NT], BF16, kind="Internal").ap()
    wo_bf = nc.dram_tensor("wo_bf", [D, D], BF16, kind="Internal").ap()
    wg_bf = nc.dram_tensor("wg_bf", [D, DFF], BF16, kind="Internal").ap()
    wu_bf = nc.dram_tensor("wu_bf", [D, DFF], BF16, kind="Internal").ap()
    wd_bf = nc.dram_tensor("wd_bf", [DFF, D], BF16, kind="Internal").ap()

    with ExitStack() as es:
        c = Ctx(nc, es)
        SB = lambda st, n, s, d=F32: st.enter_context(nc.sbuf_tensor(n, list(s), d))

        def dump(name, ap, rb):
            if not DEBUG[0] or name in DUMPS:
                return
            shp = list(ap.shape)
            t = nc.dram_tensor("dbg_" + name, shp, ap.dtype, kind="ExternalOutput").ap()
            DUMPS[name] = shp
            c.dma("sp", t, ap, reads=rb, writes=[Buf()])
        PB = [es.enter_context(nc.psum_tensor("pb%d" % i, [128, 512], F32)) for i in range(7)]
        PBb = [Buf(True) for _ in range(7)]
        PTh = [None, None]
        cm = SB(es, "cm", [128, 7, 128]); bcm = Buf()
        c.dma("sp", cm[:].rearrange("p a b -> p (a b)"), cmask, writes=[bcm])
        IDN, UT, SLO, LT, SUP, BLK, ONES = [cm[:, i, :] for i in range(7)]
        sel = SB(es, "sel", [128, 2, 128]); bsel = Buf()
        c.dma("sp", sel[:].rearrange("p a b -> p (a b)"), csel, writes=[bsel])
        seg = SB(es, "seg", [128, 512]); bseg = Buf()
        c.dma("sp", seg[:], cseg, writes=[bseg])
        idb = SB(es, "idb", [128, 128], BF16); bidb = Buf()
        c.op("dve", lambda e: e.tensor_copy(out=idb[:], in_=IDN), reads=[bcm], writes=[bidb])
        modp = SB(es, "modp", [128, 64]); bmod = Buf()
        lbt = SB(es, "lbt", [128, 16]); blb = Buf()
        cw = SB(es, "cw", [128, 36]); bcw = Buf()
        onw = SB(es, "onw", [128, 8]); bonw = Buf()
        c.dma("sp", cw[:], convw_fm, writes=[bcw])
        c.dma("sp", onw[:], onorm_fm, writes=[bonw])
        epsb = SB(es, "epsb", [128, 1]); beps = Buf()
        c.op("dve", lambda e: e.memset(epsb[:], EPS), writes=[beps])

        def rstd_from(ss_ap, out_ap, scale, rb, wb, n=1):
            c.op("act", lambda e: e.activation(out=out_ap, in_=ss_ap, func=AF.Ln, scale=scale, bias=epsb[:, 0:1]),
                 reads=rb + [beps], writes=wb)
            c.op("act", lambda e: e.activation(out=out_ap, in_=out_ap, func=AF.Exp, scale=-0.5), reads=wb, writes=wb)

        try:
            with ExitStack() as s0:
                sv = SB(s0, "sv", [128, 16]); bsv = Buf()
                c.dma("sp", sv[:], svec, writes=[bsv])
                ssil = SB(s0, "ssil", [128, 8, 2]); bss = Buf()
                c.op("act", lambda e: e.activation(out=ssil[:].rearrange("p k v -> p v k"), in_=sv[:].rearrange("p (v k) -> p v k", v=2), func=AF.Silu),
                     reads=[bsv], writes=[bss])
                bfm = SB(s0, "bfm", [128, 48]); bbfm = Buf()
                c.dma("sp", bfm[:], bada_fm, writes=[bbfm])
                n12 = SB(s0, "n12", [128, 16]); bn12 = Buf()
                c.dma("sp", n12[:], n12_fm, writes=[bn12])
                mfm = SB(s0, "mfm", [128, 48, 2]); bmfm = Buf()
                wad = [SB(s0, "wad%d" % i, [128, 8, 512]) for i in range(2)]
                bwad = bufs(2)
                wv = w_ada.rearrange("(k p) n -> p k n", p=128)
                ci = 0
                for cb in (0, 1, 2, 3, 6, 7, 8, 9):
                    t = wad[ci % 2]; bt = bwad[ci % 2]; ci += 1
                    c.dma("sp", t[:], wv[:, :, cb * 512:(cb + 1) * 512], writes=[bt])
                    pb = PB[ci % 2]; bpb = PBb[ci % 2]
                    for jj in range(4):
                        for k in range(8):
                            c.op("pe", lambda e: e.matmul(pb[:, jj * 2:jj * 2 + 2], lhsT=t[:, k, jj * 128:(jj + 1) * 128], rhs=ssil[:, k, :],
                                                          start=(k == 0), stop=(k == 7)), reads=[bt, bss], writes=[bpb])
                    j0 = cb * 4
                    c.op("dve", lambda e: e.tensor_tensor(out=mfm[:, j0:j0 + 4, :], in0=pb[:, 0:8].rearrange("p (j v) -> p j v", v=2),
                                                          in1=bfm[:, j0:j0 + 4].unsqueeze(2).broadcast_to([128, 4, 2]), op=ALU.add),
                         reads=[bpb, bbfm], writes=[bmfm])
                for which, (jsh, jsc, noff) in enumerate(((0, 8, 0), (24, 32, 8))):
                    for v in range(2):
                        o0 = (which * 2 + v) * 16
                        c.op("dve", lambda e: e.scalar_tensor_tensor(out=modp[:, o0:o0 + 8], in0=mfm[:, jsc:jsc + 8, v], scalar=1.0, in1=n12[:, noff:noff + 8],
                                                                     op0=ALU.add, op1=ALU.mult), reads=[bmfm, bn12], writes=[bmod])
                        c.op("dve", lambda e: e.tensor_copy(out=modp[:, o0 + 8:o0 + 16], in_=mfm[:, jsh:jsh + 8, v]), reads=[bmfm], writes=[bmod])
                hl = SB(s0, "hl", [128, 16]); bhl = Buf()
                c.dma("sp", hl[:], hlb_fm, writes=[bhl])
                c.op("dve", lambda e: e.tensor_tensor(out=hl[:, 0:8], in0=hl[:, 0:8], in1=hl[:, 8:16], op=ALU.subtract), reads=[bhl], writes=[bhl])
                c.op("act", lambda e: e.activation(out=lbt[:, 0:8], in_=hl[:, 0:8], func=AF.Sigmoid), reads=[bhl], writes=[blb])
                c.op("act", lambda e: e.activation(out=lbt[:, 8:16], in_=hl[:, 0:8], func=AF.Sigmoid, scale=-1.0), reads=[bhl], writes=[blb])

            c.barrier()
            dump('modp', modp[:], [bmod])
            dump('lbt', lbt[:], [blb])
            _chk('p0')
            A1 = lambda which, v, k: modp[:, (which * 2 + v) * 16 + k:(which * 2 + v) * 16 + k + 1]
            SH = lambda which, v, k: modp[:, (which * 2 + v) * 16 + 8 + k:(which * 2 + v) * 16 + 9 + k]

            bhT = bufs(NTI)
            boT = bufs(8)

            def norm_mod_transpose(st_pool, xt, bxt, which, v, hdst, bh, tmpn):
                ss, bs_, xn, bxn, junk, bj = tmpn
                PT, PTb = PTh
                c.op("pool", lambda e: e.memset(ss[:, 0:1], 0.0), writes=[bs_])
                c.op("act", lambda e: e.activation(out=junk[:], in_=xt, func=AF.Square, accum_out=ss[:, 0:1]), reads=[bxt], writes=[bj, bs_])
                rstd_from(ss[:, 0:1], ss[:, 1:2], 1.0 / D, [bs_], [bs_])
                c.op("act", lambda e: e.activation(out=xn[:], in_=xt, func=AF.Copy, scale=ss[:, 1:2]), reads=[bxt, bs_], writes=[bxn])
                if st_pool == "norm_only":
                    return
                mod_transpose(which, v, hdst, bh, tmpn)

            def mod_transpose(which, v, hdst, bh, tmpn):
                ss, bs_, xn, bxn, junk, bj = tmpn
                PT, PTb = PTh
                for k in range(8):
                    c.op("pe", lambda e: e.transpose(PT[:, k * 128:(k + 1) * 128], xn[:, k * 128:(k + 1) * 128], idb[:]), reads=[bxn, bidb], writes=[PTb])
                for k in range(8):
                    c.op("dve", lambda e: e.tensor_scalar(out=hdst[:, k, :], in0=PT[:, k * 128:(k + 1) * 128], scalar1=A1(which, v, k), scalar2=SH(which, v, k),
                                                          op0=ALU.mult, op1=ALU.add), reads=[PTb, bmod], writes=[bh])

            with ExitStack() as s2:
                SLOT = [SB(s2, "slot%d" % i, [128, NT]) for i in range(5)]
                SLB = [bufs(NTI) for _ in range(5)]
                ZT = SB(s2, "zt", [128, NT], BF16); bZT = bufs(NTI)
                gsm = {n: SB(s2, "g_" + n, [128, 2, NTI, 4]) for n in ("beta", "gg", "gc", "glt", "gl0", "gl1", "eg", "beg", "ekt", "dec0", "dec1")}
                bgs = Buf()
                with ExitStack() as s1:
                    PTh[0] = s1.enter_context(nc.psum_tensor("pbt1", [128, 1024], BF16)); PTh[1] = Buf(True)
                    GTM = SB(s1, "gtm", [128, NTI, 16]); bGTM = Buf()
                    W1 = 4
                    xts = [SB(s1, "xt%d" % i, [128, D]) for i in range(W1)]; bxts = bufs(W1)
                    hts = [SB(s1, "ht%d" % i, [128, 8, 128], BF16) for i in range(W1)]; bhts = bufs(W1)
                    ss1 = [SB(s1, "ss1_%d" % i, [128, 2]) for i in range(W1)]; bss1 = bufs(W1)
                    xn1 = [SB(s1, "xn1_%d" % i, [128, D], BF16) for i in range(W1)]; bxn1 = bufs(W1)
                    jk1 = [SB(s1, "jk1_%d" % i, [128, D], BF16) for i in range(W1)]; bjk1 = bufs(W1)
                    PT1 = [PTh[0]] * 2
                    bPT1 = [PTh[1]] * 2
                    wgf = SB(s1, "wgf", [128, 8, 16]); bwgf = Buf()
                    c.dma("sp", wgf[:], win_g.rearrange("(k p) n -> p k n", p=128), writes=[bwgf])
                    wgb = SB(s1, "wgb", [128, 8, 16], BF16); bwgb = Buf()
                    c.op("dve", lambda e: e.tensor_copy(out=wgb[:], in_=wgf[:]), reads=[bwgf], writes=[bwgb])
                    gT = SLOT[4]
                    bgT = SLB[4]

                    def p1_unit(i, sl):
                        xt = xts[sl]; bx = bxts[sl]; ht = hts[sl]; bh = bhts[sl]
                        ss = ss1[sl]; bs_ = bss1[sl]; xn = xn1[sl]; bxn = bxn1[sl]; junk = jk1[sl]; bj = bjk1[sl]
                        PT = PT1[sl % 2]; PTb = bPT1[sl % 2]
                        v = 1 if i < 32 else 0
                        c.dma("sp", xt[:], xall[i * 128:(i + 1) * 128, :], writes=[bx])
                        c.op("pool", lambda e: e.memset(ss[:, 0:1], 0.0), writes=[bs_])
                        yield
                        c.op("act", lambda e: e.activation(out=junk[:], in_=xt[:], func=AF.Square, accum_out=ss[:, 0:1]), reads=[bx], writes=[bj, bs_])
                        yield
                        c.op("act", lambda e: e.activation(out=ss[:, 1:2], in_=ss[:, 0:1], func=AF.Ln, scale=1.0 / D, bias=epsb[:, 0:1]), reads=[bs_, beps], writes=[bs_])
                        yield
                        c.op("act", lambda e: e.activation(out=ss[:, 1:2], in_=ss[:, 1:2], func=AF.Exp, scale=-0.5), reads=[bs_], writes=[bs_])
                        yield
                        c.op("act", lambda e: e.activation(out=xn[:], in_=xt[:], func=AF.Copy, scale=ss[:, 1:2]), reads=[bx, bs_], writes=[bxn])
                        yield
                        for k in range(8):
                            c.op("pe", lambda e: e.transpose(PT[:, k * 128:(k + 1) * 128], xn[:, k * 128:(k + 1) * 128], idb[:]), reads=[bxn, bidb], writes=[PTb])
                        for k in range(8):
                            eng_ = "dve" if k % 2 == 0 else "pool"
                            if eng_ == "pool":
                                eng_ = "dve"
                            c.op(eng_, lambda e: e.tensor_scalar(out=ht[:, k, :], in0=PT[:, k * 128:(k + 1) * 128], scalar1=A1(0, v, k), scalar2=SH(0, v, k),
                                                                  op0=ALU.mult, op1=ALU.add), reads=[PTb, bmod], writes=[bh])
                        yield
                        pg = PB[2 + sl]; bpg = PBb[2 + sl]
                        for k in range(8):
                            c.op("pe", lambda e: e.matmul(pg[0:16, 0:128], lhsT=wgb[:, k, :], rhs=ht[:, k, :], start=(k == 0), stop=(k == 7)),
                                 reads=[bwgb, bh], writes=[bpg])
                        c.dma("pool", hT_scr[:, :, i * 128:(i + 1) * 128], ht[:], reads=[bh], writes=[bhT[i]])
                        yield
                        c.op("act", lambda e: e.activation(out=gT[0:16, i * 128:(i + 1) * 128], in_=pg[0:16, 0:128], func=AF.Copy), reads=[bpg], writes=[bgT[i]])

                    pending = list(range(NTI))
                    active = []
                    free = list(range(W1))
                    while pending or active:
                        while pending and free:
                            i = pending.pop(0); sl = free.pop(0)
                            active.append((sl, p1_unit(i, sl)))
                        nxt_active = []
                        for sl, g in active:
                            try:
                                next(g)
                                nxt_active.append((sl, g))
                            except StopIteration:
                                free.append(sl)
                        active = nxt_active
                    gTc = SLOT[3]; bgTc = SLB[3]
                    c.op("act", lambda e: e.activation(out=gTc[0:16, 0:4096].rearrange("g (w r) -> g w r", r=64),
                                                       in_=gT[0:16, 0:4096].rearrange("g (r w) -> g w r", w=64), func=AF.Copy), reads=bgT[0:32], writes=bgTc[0:32])
                    c.op("act", lambda e: e.activation(out=gTc[0:16, 4096:NT], in_=gT[0:16, 4096:NT], func=AF.Copy), reads=bgT[32:], writes=bgTc[32:])
                    for j in range(NTI):
                        pg = PB[2 + j % 2]; bpg = PBb[2 + j % 2]
                        src = gTc[0:16, j * 128:(j + 1) * 128]
                        c.op("pe", lambda e: e.transpose(pg[:, 0:16], src, IDN[0:16, 0:16]), reads=[bgTc[j], bcm], writes=[bpg])
                        c.op("act", lambda e: e.activation(out=GTM[:, j, :], in_=pg[:, 0:16], func=AF.Copy), reads=[bpg], writes=[bGTM])
                    al = SB(s1, "al", [128, 2, NTI, 4]); dtb = SB(s1, "dtb", [128, 2, NTI, 4]); bal = Buf()
                    c.dma("sp", al[:].rearrange("p a b c -> p (a b c)"), alog_rep, writes=[bal])
                    c.dma("sp", dtb[:].rearrange("p a b c -> p (a b c)"), dtb_rep, writes=[bal])
                    gview = lambda lo: GTM[:, :, lo:lo + 8].rearrange("p t (d h) -> p d t h", d=2)
                    c.op("act", lambda e: e.activation(out=gsm["beta"][:], in_=gview(0), func=AF.Sigmoid), reads=[bGTM], writes=[bgs])
                    c.op("dve", lambda e: e.tensor_tensor(out=gsm["gg"][:], in0=gview(8), in1=dtb[:], op=ALU.add), reads=[bGTM, bal], writes=[bgs])
                    c.op("act", lambda e: e.activation(out=gsm["gg"][:], in_=gsm["gg"][:], func=AF.Exp), reads=[bgs], writes=[bgs])
                    c.op("act", lambda e: e.activation(out=gsm["gg"][:], in_=gsm["gg"][:], func=AF.Ln, bias=1.0), reads=[bgs], writes=[bgs])
                    c.op("act", lambda e: e.activation(out=al[:], in_=al[:], func=AF.Exp), reads=[bal], writes=[bal])
                    c.op("dve", lambda e: e.scalar_tensor_tensor(out=gsm["gg"][:], in0=gsm["gg"][:], scalar=-1.0, in1=al[:], op0=ALU.mult, op1=ALU.mult),
                         reads=[bgs, bal], writes=[bgs])
                    fl = lambda t: t[:].rearrange("p d t h -> p (d t h)")
                    pq = PB[4]; bpq = PBb[4]
                    for d in range(2):
                        rhs = gsm["gg"][:, d].rearrange("p t h -> p (t h)")
                        c.op("pe", lambda e: e.matmul(pq[:, d * 144:(d + 1) * 144], lhsT=(UT if d == 0 else LT), rhs=rhs, start=True, stop=True),
                             reads=[bgs, bcm], writes=[bpq])
                    c.op("dve", lambda e: e.tensor_copy(out=fl(gsm["gc"]), in_=pq[:, 0:288]), reads=[bpq], writes=[bgs])
                    for nm, lh, bl in (("glt", BLK, bcm), ("gl0", sel[:, 0, :], bsel), ("gl1", sel[:, 1, :], bsel)):
                        c.op("pe", lambda e: e.matmul(pq[:, 0:288], lhsT=lh, rhs=fl(gsm["gg"]), start=True, stop=True), reads=[bgs, bl], writes=[bpq])
                        c.op("dve", lambda e: e.tensor_copy(out=fl(gsm[nm]), in_=pq[:, 0:288]), reads=[bpq], writes=[bgs])
                    c.op("act", lambda e: e.activation(out=fl(gsm["eg"]), in_=fl(gsm["gc"]), func=AF.Exp), reads=[bgs], writes=[bgs])
                    c.op("dve", lambda e: e.tensor_tensor(out=fl(gsm["beg"]), in0=fl(gsm["beta"]), in1=fl(gsm["eg"]), op=ALU.mult), reads=[bgs], writes=[bgs])
                    c.op("dve", lambda e: e.tensor_tensor(out=fl(gsm["ekt"]), in0=fl(gsm["glt"]), in1=fl(gsm["gc"]), op=ALU.subtract), reads=[bgs], writes=[bgs])
                    c.op("act", lambda e: e.activation(out=fl(gsm["ekt"]), in_=fl(gsm["ekt"]), func=AF.Exp), reads=[bgs], writes=[bgs])
                    c.op("act", lambda e: e.activation(out=fl(gsm["dec0"]), in_=fl(gsm["gl0"]), func=AF.Exp), reads=[bgs], writes=[bgs])
                    c.op("act", lambda e: e.activation(out=fl(gsm["dec1"]), in_=fl(gsm["gl1"]), func=AF.Exp), reads=[bgs], writes=[bgs])

                c.barrier()
                for _n in gsm:
                    dump('g_' + _n, gsm[_n][:], [bgs])
                _chk('p1')
                bWo, bWg, bWu, bWd = bufs(8), bufs(8), bufs(8), bufs(NF)

                def issue_bg_precast():
                    for k in range(8):
                        c.dma("bg", wo_bf[k * 128:(k + 1) * 128, :], w_out[k * 128:(k + 1) * 128, :], writes=[bWo[k]])
                    for k in range(8):
                        c.dma("bg", wg_bf[k * 128:(k + 1) * 128, :], w_gate[k * 128:(k + 1) * 128, :], writes=[bWg[k]])
                        c.dma("bg", wu_bf[k * 128:(k + 1) * 128, :], w_up[k * 128:(k + 1) * 128, :], writes=[bWu[k]])
                    for f in range(NF):
                        c.dma("bg", wd_bf[f * 128:(f + 1) * 128, :], w_down[f * 128:(f + 1) * 128, :], writes=[bWd[f]])

                PBK = 256
                NPB = NT // PBK
                hTb = [SB(s2, "htb%d" % i, [128, 8, PBK], BF16) for i in range(2)]; bhTb = bufs(2)
                wbf = SB(s2, "wbf", [128, 8, 640], BF16); bwbf = Buf()
                PB.append(s2.enter_context(nc.psum_tensor("pb7", [128, 512], F32))); PBb.append(Buf(True))
                tmp512 = [SB(s2, "tmpw%d" % i, [128, 512]) for i in range(2)]; btmp512 = bufs(2)
                Sbuf = [SB(s2, "S%d" % i, [128, 128]) for i in range(18)]; bS = bufs(18)
                small = SB(s2, "small", [128, 16]); bsmall = bufs(4)
                s2a = ExitStack()
                TMPN = 0
                tmp = [SB(s2a, "tmp%d" % i, [128, 256]) for i in range(TMPN)]; btmp = bufs(TMPN)
                OF = SB(s2a, "of", [128, NT], BF16); bOF = bufs(NTI)
                seqs = [(0, 32), (32, 34), (34, 36)]
                hblk_state = [0]

                def load_hblk(blk):
                    i = hblk_state[0] % 2; hblk_state[0] += 1
                    c.dma("sp", hTb[i][:], hT_scr[:, :, blk * PBK:(blk + 1) * PBK], reads=bhT[blk * 2:blk * 2 + 2], writes=[bhTb[i]])
                    return hTb[i], bhTb[i]

                def proj_fm(ht, bh, col0, pbi):
                    pb = PB[pbi]; bpb = PBb[pbi]
                    for k in range(8):
                        c.op("pe", lambda e: e.matmul(pb[:, 0:PBK], lhsT=wbf[:, k, col0:col0 + 128], rhs=ht[:, k, :], start=(k == 0), stop=(k == 7)),
                             reads=[bwbf, bh], writes=[bpb])
                    return pb[:, 0:PBK], bpb

                rr = [0]

                def T(n=1):
                    i = rr[0] % TMPN; rr[0] += 1
                    return tmp[i], btmp[i]

                WUA = 6
                HNAMES = ("teg", "tqd", "tkd", "tkt", "tkt2", "tktm", "tat", "tos", "tsq")
                HW_ = {"tqd": 128, "tkd": 128, "tktm": 128, "tat": 128, "tkt": 128, "tkt2": 128, "tsq": 128}
                HBF = ("tqd", "tkd", "tkt2", "tktm", "tat", "tsq")
                hsets = []
                for w_ in range(WUA):
                    hsets.append({n: (SB(s2a, "hu%d_%s" % (w_, n), [128, HW_.get(n, 256)], BF16 if n in HBF else F32), Buf()) for n in HNAMES})
                Sbf = [SB(s2a, "Sbf%d" % i, [128, 128], BF16) for i in range(18)]; bSbf = bufs(18)
                onesb = SB(s2a, "onesb", [128, 128], BF16); bonesb = Buf()
                c.op("dve", lambda e: e.tensor_copy(out=onesb[:], in_=ONES), reads=[bcm], writes=[bonesb])
                VTMb = SLOT[1][:].bitcast(BF16)[:, 0:NT]
                hsm = [SB(s2a, "hsm%d" % w_, [128, 2]) for w_ in range(WUA)]; bhsm = bufs(WUA)
                GA1 = SB(s2a, "ga1", [128, NT]); bGA1 = bufs(NTI)

                def hgrn_unit(h, d, i, slot, ch, kidx, claimed, done):
                    FD, bFD = (FF, bFF) if d == 0 else (FB, bFB)
                    GA, bGA = GAs[d]
                    ts = hsets[slot]
                    pU, bU = PB[slot], PBb[slot]
                    pUb = pU[:, 0:64].bitcast(BF16)
                    MASK = UT if d == 0 else LT
                    glpos = 63 if d == 0 else 0
                    tsl = slice(i * 128, (i + 1) * 128)
                    G3 = GA[:, tsl].rearrange("p (c j) -> p c j", j=64)
                    teg, bteg = ts["teg"]; tqd, btqd = ts["tqd"]; tkd, btkd = ts["tkd"]; tkt, btkt = ts["tkt"]; tkt2, btkt2 = ts["tkt2"]
                    tktm, btktm = ts["tktm"]; tat, btat = ts["tat"]; tos, btos = ts["tos"]; tsq, btsq = ts["tsq"]
                    tdc = hsm[slot]; btdc = bhsm[slot]
                    c.op("act", lambda e: e.activation(out=teg[:, 0:128], in_=GA[:, tsl], func=AF.Exp), reads=[bGA[i]], writes=[bteg])
                    c.op("act", lambda e: e.activation(out=teg[:, 128:256], in_=GA[:, tsl], func=AF.Exp, scale=-1.0), reads=[bGA[i]], writes=[bteg])
                    c.op("dve", lambda e: e.tensor_tensor(out=tkt[:, 0:128].rearrange("p (c j) -> p c j", j=64), in0=G3[:, :, glpos:glpos + 1].broadcast_to([128, 2, 64]),
                                                          in1=G3, op=ALU.subtract), reads=[bGA[i]], writes=[btkt])
                    c.op("act", lambda e: e.activation(out=tdc[:, 0:2], in_=G3[:, :, glpos], func=AF.Exp), reads=[bGA[i]], writes=[btdc])
                    yield
                    c.op("pool", lambda e: e.tensor_tensor(out=tqd[:, 0:128], in0=QT[:, tsl], in1=teg[:, 0:128], op=ALU.mult), reads=[bQT[i], bteg], writes=[btqd])
                    c.op("pool", lambda e: e.tensor_tensor(out=tkd[:, 0:128], in0=FD[:, tsl], in1=teg[:, 128:256], op=ALU.mult), reads=[bFD[i], bteg], writes=[btkd])
                    c.op("act", lambda e: e.activation(out=tkt[:, 0:128], in_=tkt[:, 0:128], func=AF.Exp), reads=[btkt], writes=[btkt])
                    yield
                    c.op("dve", lambda e: e.tensor_tensor(out=tkt2[:, 0:128], in0=FD[:, tsl], in1=tkt[:, 0:128], op=ALU.mult), reads=[bFD[i], btkt], writes=[btkt2])
                    c.op("pe", lambda e: e.matmul(pU[:, 128:256], lhsT=tkd[:, 0:128], rhs=tqd[:, 0:128], start=True, stop=True), reads=[btkd, btqd], writes=[bU])
                    yield
                    c.op("pe", lambda e: e.transpose(pUb, tkt2[:, 0:128], idb[:]), reads=[btkt2, bidb], writes=[bU])
                    yield
                    c.op("dve", lambda e: e.tensor_tensor(out=tat[:, 0:128], in0=pU[:, 128:256], in1=MASK, op=ALU.mult), reads=[bU, bcm], writes=[btat])
                    c.op("dve", lambda e: e.tensor_copy(out=tktm[:, 0:128], in_=pUb), reads=[bU], writes=[btktm])
                    yield
                    corder = (0, 1) if d == 0 else (1, 0)
                    for ci_, cc in enumerate(corder):
                        pb_ = cc * 64
                        ucol = 384 if ci_ == 0 else 0
                        c.op("pe", lambda e: e.matmul(pU[:, ucol:ucol + 128], lhsT=tktm[pb_:pb_ + 64, 0:128], rhs=VTMb[pb_:pb_ + 64, tsl], start=True, stop=True),
                             reads=[btktm, bVTM[i]], writes=[bU], serial=(ci_ == 1))
                        if ci_ == 0:
                            yield
                    yield
                    while ch["turn"] != kidx:
                        yield
                    c.op("pe", lambda e: e.matmul(pU[:, 256:384], lhsT=VTMb[:, tsl], rhs=tat[:, 0:128], start=True, stop=False), reads=[bVTM[i], btat], writes=[bU])
                    for ci_, cc in enumerate(corder):
                        pb_ = cc * 64
                        S, bSc = ch["S"], ch["bS"]
                        Sb_, bSb_ = ch["Sb"], ch["bSb"]
                        ucol = 384 if ci_ == 0 else 0
                        c.op("pe", lambda e: e.matmul(pU[:, 256 + pb_:256 + pb_ + 64], lhsT=Sb_[:], rhs=tqd[:, pb_:pb_ + 64], start=False, stop=(ci_ == 1)),
                             reads=[bSb_, btqd], writes=[bU])
                        ch["si"] += 1
                        Sn, bSn = ch["bufs"][ch["si"] % 3]
                        Sbn, bSbn = ch["bbufs"][ch["si"] % 3]
                        c.op("dve", lambda e: e.scalar_tensor_tensor(out=Sn[:], in0=S[:], scalar=tdc[:, cc:cc + 1], in1=pU[:, ucol:ucol + 128],
                                                                     op0=ALU.mult, op1=ALU.add), reads=[bSc, btdc, bU], writes=[bSn])
                        c.op("pool", lambda e: e.tensor_copy(out=Sbn[:], in_=Sn[:]), reads=[bSn], writes=[bSbn])
                        ch["S"], ch["bS"] = Sn, bSn
                        ch["Sb"], ch["bSb"] = Sbn, bSbn
                        yield
                    if ch["last"] == i and ch["out"] is not None:
                        c.dma("sp", ch["out"], ch["S"][:], reads=[ch["bS"]], writes=[Buf()])
                    ch["turn"] += 1
                    if not claimed[i]:
                        claimed[i] = True
                        c.op("act", lambda e: e.activation(out=OF[:, tsl], in_=pU[:, 256:384], func=AF.Copy), reads=[bU], writes=[bOF[i]])
                        done[i] = True
                        return
                    while not done[i]:
                        yield
                    c.op("dve", lambda e: e.tensor_tensor(out=tos[:, 0:128], in0=pU[:, 256:384], in1=OF[:, tsl], op=ALU.add), reads=[bU, bOF[i]], writes=[btos])
                    yield
                    c.op("act", lambda e: e.activation(out=tsq[:, 0:128], in_=tos[:, 0:128], func=AF.Square), reads=[btos], writes=[btsq])
                    yield
                    c.op("pe", lambda e: e.matmul(pU[:, 128:256], lhsT=onesb[:], rhs=tsq[:, 0:128], start=True, stop=True), reads=[btsq, bonesb], writes=[bU])
                    yield
                    rstd_from(pU[:, 128:256], tos[:, 128:256], 1.0 / 128, [bU], [btos])
                    yield
                    c.op("dve", lambda e: e.tensor_tensor(out=tos[:, 0:128], in0=tos[:, 0:128], in1=tos[:, 128:256], op=ALU.mult), reads=[btos], writes=[btos])
                    yield
                    c.op("dve", lambda e: e.scalar_tensor_tensor(out=ZT[:, tsl], in0=tos[:, 0:128], scalar=onw[:, h:h + 1], in1=ZT[:, tsl],
                                                                 op0=ALU.mult, op1=ALU.mult), reads=[btos, bonw, bZT[i]], writes=[bZT[i]])

                def hgrn_scan(h):
                    claimed = [False] * NTI
                    done = [False] * NTI
                    per_dir = []
                    ci = 0
                    for d in range(2):
                        pend = []
                        for si_, (t0, t1) in enumerate(seqs):
                            bl = [(Sbuf[ci * 3 + i_], bS[ci * 3 + i_]) for i_ in range(3)]
                            ci += 1
                            bbl = [(Sbf[(ci - 1) * 3 + i_], bSbf[(ci - 1) * 3 + i_]) for i_ in range(3)]
                            ch = dict(S=bl[0][0], bS=bl[0][1], Sb=bbl[0][0], bSb=bbl[0][1], si=0, turn=0, bufs=bl, bbufs=bbl, last=(t1 - 1 if d == 0 else t0),
                                      out=(None if t0 == 0 else ns_a[(t0 - 32) // 2, d, h]))
                            if t0 == 0:
                                c.dma("sp", ch["S"][:], st_a[d, h], writes=[ch["bS"]])
                            else:
                                c.op("pool", lambda e: e.memset(ch["S"][:], 0.0), writes=[ch["bS"]])
                            c.op("act", lambda e: e.activation(out=ch["Sb"][:], in_=ch["S"][:], func=AF.Copy), reads=[ch["bS"]], writes=[ch["bSb"]])
                            tl = list(range(t0, t1)) if d == 0 else list(range(t1 - 1, t0 - 1, -1))
                            for kidx, i in enumerate(tl):
                                pend.append((d, i, ch, kidx))
                        per_dir.append(pend[:4] + pend[32:] + pend[4:32])
                    pending = []
                    for a_, b_ in zip(per_dir[0], per_dir[1]):
                        pending.append(a_); pending.append(b_)
                    active = []
                    free = list(range(WUA))
                    while pending or active:
                        while pending and free:
                            d, i, ch, kidx = pending.pop(0)
                            sl = free.pop(0)
                            active.append((sl, hgrn_unit(h, d, i, sl, ch, kidx, claimed, done)))
                        nxt_active = []
                        for sl, g in active:
                            try:
                                next(g)
                                nxt_active.append((sl, g))
                            except StopIteration:
                                free.append(sl)
                        active = nxt_active

                QT, VTM, FF, FB, GA = SLOT
                bQT, bVTM, bFF, bFB, bGA = SLB
                VTM3 = VTM[:].rearrange("p (t v) -> p t v", v=128)
                for h in range(4):
                    for k in range(8):
                        c.dma("pool", wbf[:, k, 0:640], win_a[h, k * 128:(k + 1) * 128, :], writes=[bwbf])
                    for blk in range(NPB):
                        ht, bh = load_hblk(blk)
                        bsl = slice(blk * PBK, (blk + 1) * PBK)
                        tb = list(range(blk * 2, blk * 2 + 2))
                        pbk = (blk % 2) * 4 if False else 0
                        pb, bpb = proj_fm(ht, bh, 0, 0)
                        c.op("act", lambda e: e.activation(out=QT[:, bsl], in_=pb, func=AF.Silu), reads=[bpb], writes=[bQT[t] for t in tb])
                        pb, bpb = proj_fm(ht, bh, 128, 1)
                        c.op("act", lambda e: e.activation(out=FF[:, bsl], in_=pb, func=AF.Sigmoid), reads=[bpb], writes=[bFF[t] for t in tb])
                        pb, bpb = proj_fm(ht, bh, 256, 2)
                        c.op("act", lambda e: e.activation(out=FB[:, bsl], in_=pb, func=AF.Sigmoid), reads=[bpb], writes=[bFB[t] for t in tb])
                        pb, bpb = proj_fm(ht, bh, 512, 3)
                        c.op("act", lambda e: e.activation(out=ZT[:, bsl], in_=pb, func=AF.Silu), reads=[bpb], writes=[bZT[t] for t in tb])
                        pb = PB[4 + blk % 2]; bpb = PBb[4 + blk % 2]
                        for tt in range(2):
                            for k in range(8):
                                c.op("pe", lambda e: e.matmul(pb[:, tt * 128:(tt + 1) * 128], lhsT=ht[:, k, tt * 128:(tt + 1) * 128], rhs=wbf[:, k, 384:512],
                                                              start=(k == 0), stop=(k == 7)), reads=[bwbf, bh], writes=[bpb])
                        c.op("dve", lambda e: e.tensor_copy(out=VTMb[:, bsl], in_=pb[:, 0:PBK]), reads=[bpb], writes=[bVTM[t] for t in tb])
                    _chk('a1')
                    if h == 0:
                        issue_bg_precast()
                    GAs = ((GA, bGA), (GA1, bGA1))
                    for d in range(2):
                        FD = FF if d == 0 else FB
                        bFD = bFF if d == 0 else bFB
                        GAd, bGAd = GAs[d]
                        lbc = lbt[:, d * 4 + h:d * 4 + h + 1]
                        omc = lbt[:, 8 + d * 4 + h:8 + d * 4 + h + 1]
                        for blk in range(NBLK):
                            bsl = slice(blk * 512, (blk + 1) * 512)
                            tb = list(range(blk * 4, blk * 4 + 4))
                            fbufs = [bFD[t] for t in tb]
                            gbufs = [bGAd[t] for t in tb]
                            c.op("dve", lambda e: e.tensor_scalar(out=FD[:, bsl], in0=FD[:, bsl], scalar1=omc, scalar2=lbc, op0=ALU.mult, op1=ALU.add),
                                 reads=fbufs + [blb], writes=fbufs)
                            tw = tmp512[blk % 2]; btw = btmp512[blk % 2]
                            c.op("act", lambda e: e.activation(out=tw[:], in_=FD[:, bsl], func=AF.Ln), reads=fbufs, writes=[btw])
                            if d == 0:
                                c.op("dve", lambda e: e.tensor_tensor_scan(out=GAd[:, bsl], data0=seg[:], data1=tw[:], initial=0.0, op0=ALU.mult, op1=ALU.add),
                                     reads=[btw, bseg], writes=gbufs)
                            else:
                                c.op("dve", lambda e: e.tensor_tensor_scan(out=GAd[:, bsl][:, ::-1], data0=seg[:], data1=tw[:][:, ::-1], initial=0.0,
                                                                           op0=ALU.mult, op1=ALU.add), reads=[btw, bseg], writes=gbufs)
                            c.op("pool", lambda e: e.tensor_scalar(out=FD[:, bsl], in0=FD[:, bsl], scalar1=-1.0, scalar2=1.0, op0=ALU.mult, op1=ALU.add),
                                 reads=fbufs, writes=fbufs)
                    hgrn_scan(h)
                    dump('OF', OF[:], bOF)
                    dump('ZTa', ZT[:], bZT)
                    c.dma("pool", oT_scr[h], ZT[:], reads=bZT, writes=[boT[h]])
                    _chk('a4')

                s2a.close()
                c.barrier()
                _chk('p2a')
                RAW, QN, KN, VC, VT2 = SLOT
                bRAW, bQN, bKN, bVC, bVT2 = SLB
                KTM = RAW; bKTM = bRAW
                OTM = VC; bOTM = bVC
                KTM3 = KTM[:].rearrange("p (t v) -> p t v", v=128)
                VT23 = VT2[:].rearrange("p (t v) -> p t v", v=128)
                OTM3 = OTM[:].rearrange("p (t v) -> p t v", v=128)

                def scanview(arr, j):
                    return arr[:, j * 128:(j + 1) * 128]

                def scantiles(j):
                    return [j]

                def zview(j):
                    if j < 32:
                        return ZT[:, 0:4096].rearrange("p (r w) -> p w r", w=64)[:, 2 * j:2 * j + 2, :]
                    return ZT[:, j * 128:(j + 1) * 128]

                def ztiles(j):
                    return list(range(32)) if j < 32 else [j]

                WU = 6
                TNAMES = ("tkb", "tr", "te", "tdm", "tdm2", "tat", "tul", "X", "tkk", "tvb", "twu", "tqb", "Xb", "twb", "tvnb")
                TW = {"tdm2": 128, "tat": 128, "tvb": 128, "twu": 128, "tqb": 128, "Xb": 128, "twb": 128, "tvnb": 128}
                TBF = ("tat", "tkk", "tvb", "tqb", "Xb", "twb", "tvnb")
                s2b = ExitStack()
                tsets = []
                for w_ in range(WU):
                    tsets.append({n: (SB(s2b, "u%d_%s" % (w_, n), [128, TW.get(n, 256)], BF16 if n in TBF else F32), Buf()) for n in TNAMES})
                Sbf2 = [SB(s2b, "Sbg%d" % i, [128, 128], BF16) for i in range(12)]; bSbf2 = bufs(12)
                smalls = [SB(s2b, "usm%d" % w_, [128, 4]) for w_ in range(WU)]; bsmalls = bufs(WU)

                def gdn_unit(h, d, j, slot, ch, done, claimed, kidx):
                    hg = 4 + h
                    ts = tsets[slot]
                    gs = lambda nm: gsm[nm][:, d, j, h:h + 1]
                    MA, MB = (UT, SLO) if d == 0 else (LT, SUP)
                    CMK, SMK, SMK2 = (UT, SUP, SLO) if d == 0 else (LT, SLO, SUP)
                    pU, bU = PB[slot], PBb[slot]
                    kT = scanview(KN, j); qT = scanview(QN, j)
                    rdK = [bKN[j]]; rdQ = [bQN[j]]
                    tkb, btkb = ts["tkb"]; tr, btr = ts["tr"]; te, bte = ts["te"]; tdm, btdm = ts["tdm"]; tdm2, btdm2 = ts["tdm2"]
                    tat, btat = ts["tat"]; tul, btul = ts["tul"]; X, bX = ts["X"]; tkk, btkk = ts["tkk"]; tvb, btvb = ts["tvb"]; twu, btwu = ts["twu"]
                    tqb, btqb = ts["tqb"]; Xb, bXb = ts["Xb"]; twb, btwb = ts["twb"]; tvnb, btvnb = ts["tvnb"]
                    c.op("pool", lambda e: e.tensor_copy(out=tqb[:], in_=qT), reads=rdQ, writes=[btqb])
                    c.op("dve", lambda e: e.tensor_scalar(out=tkb[:, 0:128], in0=KTM3[:, j, :], scalar1=gs("beta"), scalar2=None, op0=ALU.mult),
                         reads=[bKTM[j], bgs], writes=[btkb])
                    c.op("act", lambda e: e.activation(out=tr[:, 0:128], in_=MA, func=AF.Copy, scale=gs("gg")), reads=[bcm, bgs], writes=[btr])
                    yield
                    c.op("pe", lambda e: e.transpose(pU[:, 384:512], tkb[:, 0:128], IDN), reads=[btkb, bcm], writes=[bU])
                    c.op("pe", lambda e: e.matmul(pU[:, 0:128], lhsT=MB, rhs=tr[:, 0:128], start=True, stop=True), reads=[btr, bcm], writes=[bU])
                    yield
                    c.op("act", lambda e: e.activation(out=tkb[:, 128:256], in_=pU[:, 384:512], func=AF.Copy), reads=[bU], writes=[btkb])
                    kbT = tkb[:, 128:256]
                    c.op("act", lambda e: e.activation(out=te[:, 0:128], in_=pU[:, 0:128], func=AF.Exp), reads=[bU], writes=[bte])
                    yield
                    c.op("pe", lambda e: e.matmul(pU[:, 0:128], lhsT=kT, rhs=qT, start=True, stop=True), reads=rdK + rdQ, writes=[bU])
                    c.op("pe", lambda e: e.matmul(pU[:, 128:256], lhsT=kT, rhs=kbT, start=True, stop=True), reads=rdK + [btkb], writes=[bU])
                    c.op("pool", lambda e: e.tensor_tensor(out=tdm[:, 0:128], in0=te[:, 0:128], in1=CMK, op=ALU.mult), reads=[bte, bcm], writes=[btdm])
                    c.op("pool", lambda e: e.tensor_tensor(out=tdm[:, 128:256], in0=te[:, 0:128], in1=SMK, op=ALU.mult), reads=[bte, bcm], writes=[btdm])
                    yield
                    c.op("dve", lambda e: e.tensor_tensor(out=tat[:, 0:128], in0=pU[:, 0:128], in1=tdm[:, 0:128], op=ALU.mult), reads=[bU, btdm], writes=[btat])
                    c.op("dve", lambda e: e.tensor_tensor(out=tul[:, 0:128], in0=pU[:, 128:256], in1=tdm[:, 128:256], op=ALU.mult), reads=[bU, btdm], writes=[btul])
                    yield
                    c.op("pe", lambda e: e.transpose(pU[:, 256:384], tul[:, 0:128], IDN), reads=[btul, bcm], writes=[bU])
                    yield
                    c.op("act", lambda e: e.activation(out=tul[:, 128:256], in_=pU[:, 256:384], func=AF.Copy), reads=[bU], writes=[btul])
                    yield
                    c.op("dve", lambda e: e.tensor_tensor(out=X[:, 0:128], in0=IDN, in1=tul[:, 0:128], op=ALU.subtract), reads=[bcm, btul], writes=[bX])
                    cur, bcur = tul, btul
                    xs = 0
                    for lev in range(5):
                        nxt, bnxt = ts["te"] if lev % 2 == 0 else ts["tr"]
                        if lev < 4:
                            c.op("pe", lambda e: e.matmul(pU[:, 0:128], lhsT=cur[:, 128:256], rhs=cur[:, 0:128], start=True, stop=True), reads=[bcur], writes=[bU])
                        c.op("pe", lambda e: e.matmul(pU[:, 128:256], lhsT=cur[:, 0:128], rhs=cur[:, 128:256], start=True, stop=True), reads=[bcur], writes=[bU])
                        yield
                        if lev < 4:
                            c.op("act", lambda e: e.activation(out=nxt[:, :], in_=pU[:, 0:256], func=AF.Copy), reads=[bU], writes=[bnxt])
                        else:
                            c.op("act", lambda e: e.activation(out=nxt[:, 128:256], in_=pU[:, 128:256], func=AF.Copy), reads=[bU], writes=[bnxt])
                        yield
                        c.op("pe", lambda e: e.matmul(pU[:, 256:384], lhsT=nxt[:, 128:256], rhs=X[:, xs:xs + 128], start=True, stop=True), reads=[bnxt, bX], writes=[bU])
                        yield
                        c.op("dve", lambda e: e.tensor_tensor(out=X[:, 128 - xs:256 - xs], in0=X[:, xs:xs + 128], in1=pU[:, 256:384], op=ALU.add), reads=[bU, bX], writes=[bX])
                        xs = 128 - xs
                        cur, bcur = nxt, bnxt
                        yield
                    XT = X[:, xs:xs + 128]
                    c.op("act", lambda e: e.activation(out=Xb[:], in_=XT, func=AF.Copy), reads=[bX], writes=[bXb])
                    c.op("act", lambda e: e.activation(out=tkk[:, 0:128], in_=KTM3[:, j, :], func=AF.Copy, scale=gs("beg")), reads=[bKTM[j], bgs], writes=[btkk])
                    c.op("pool", lambda e: e.tensor_scalar(out=tkk[:, 128:256], in0=KTM3[:, j, :], scalar1=gs("ekt"), scalar2=None, op0=ALU.mult), reads=[bKTM[j], bgs], writes=[btkk])
                    c.op("act", lambda e: e.activation(out=tvb[:, 0:128], in_=VT23[:, j, :], func=AF.Copy, scale=gs("beta")), reads=[bVT2[j], bgs], writes=[btvb])
                    yield
                    c.op("pe", lambda e: e.matmul(pU[:, 0:128], lhsT=tkk[:, 0:128], rhs=Xb[:], start=True, stop=True), reads=[btkk, bXb], writes=[bU])
                    c.op("pe", lambda e: e.matmul(pU[:, 128:256], lhsT=Xb[:], rhs=tvb[:, 0:128], start=True, stop=True), reads=[btvb, bXb], writes=[bU])
                    yield
                    c.op("act", lambda e: e.activation(out=twb[:], in_=pU[:, 0:128], func=AF.Copy), reads=[bU], writes=[btwb])
                    c.op("act", lambda e: e.activation(out=twu[:], in_=pU[:, 128:256], func=AF.Copy), reads=[bU], writes=[btwu])
                    yield
                    while ch["turn"] != kidx:
                        yield
                    if not claimed[j]:
                        claimed[j] = True
                        first = True
                    else:
                        first = False
                        while not done[j]:
                            yield
                    tvns = (ts["tkb"], ts["tdm"])
                    corder = (0, 1) if d == 0 else (1, 0)
                    for ci_, cc in enumerate(corder):
                        pr = slice(cc * 64, cc * 64 + 64)
                        tvn, btvn = tvns[ci_]
                        pR, bR = pU, bU
                        S, bSc = ch["S"], ch["bS"]
                        Sb_, bSb_ = ch["Sb"], ch["bSb"]
                        c.op("pe", lambda e: e.matmul(pR[:, 0:128], lhsT=twb[:], rhs=Sb_[:], start=True, stop=True), reads=[btwb, bSb_], writes=[bR])
                        c.op("pe", lambda e: e.matmul(pR[:, 128:256], lhsT=tqb[:], rhs=Sb_[:], start=True, stop=True), reads=[btqb, bSb_], writes=[bR])
                        yield
                        c.op("dve", lambda e: e.tensor_tensor(out=tvnb[pr, :], in0=twu[pr, :], in1=pR[pr, 0:128], op=ALU.subtract), reads=[btwu, bR], writes=[btvnb])
                        yield
                        c.op("pe", lambda e: e.matmul(pR[:, 256:384], lhsT=tat[pr, 0:128], rhs=tvnb[pr, :], start=True, stop=True), reads=[btat, btvnb], writes=[bR])
                        c.op("pe", lambda e: e.matmul(pR[:, 384:512], lhsT=tkk[pr, 128:256], rhs=tvnb[pr, :], start=True, stop=True), reads=[btkk, btvnb], writes=[bR])
                        yield
                        ch["si"] += 1
                        Sn, bSn = ch["bufs"][ch["si"] % 3]
                        Sbn, bSbn = ch["bbufs"][ch["si"] % 2]
                        c.op("dve", lambda e: e.scalar_tensor_tensor(out=Sn[:], in0=S[:], scalar=gs("dec%d" % cc), in1=pR[:, 384:512], op0=ALU.mult, op1=ALU.add),
                             reads=[bSc, bgs, bR], writes=[bSn])
                        c.op("act", lambda e: e.activation(out=Sbn[:], in_=Sn[:], func=AF.Copy), reads=[bSn], writes=[bSbn])
                        ch["S"], ch["bS"] = Sn, bSn
                        ch["Sb"], ch["bSb"] = Sbn, bSbn
                        c.op("dve", lambda e: e.tensor_scalar(out=tvn[pr, 128:256], in0=pR[pr, 128:256], scalar1=gsm["eg"][pr, d, j, h:h + 1], scalar2=None, op0=ALU.mult),
                             reads=[bR, bgs], writes=[btvn])
                        c.op("dve", lambda e: e.tensor_tensor(out=tvn[pr, 128:256], in0=tvn[pr, 128:256], in1=pR[pr, 256:384], op=ALU.add), reads=[btvn, bR], writes=[btvn])
                        if first:
                            c.op("act", lambda e: e.activation(out=OTM3[pr, j, :], in_=tvn[pr, 128:256], func=AF.Copy), reads=[btvn], writes=[bOTM[j]])
                        else:
                            c.op("pool", lambda e: e.tensor_tensor(out=OTM3[pr, j, :], in0=OTM3[pr, j, :], in1=tvn[pr, 128:256], op=ALU.add), reads=[btvn, bOTM[j]], writes=[bOTM[j]])
                        yield
                    if ch["last"] == j and ch["out"] is not None:
                        c.dma("sp", ch["out"], ch["S"][:], reads=[ch["bS"]], writes=[Buf()])
                    ch["turn"] += 1
                    if first:
                        done[j] = True
                        return
                    tn, btn = ts["te"]
                    sm = smalls[slot]; bsm = bsmalls[slot]
                    c.op("pool", lambda e: e.memset(sm[:, 0:1], 0.0), writes=[bsm])
                    c.op("act", lambda e: e.activation(out=tn[:, 0:128], in_=OTM3[:, j, :], func=AF.Square, accum_out=sm[:, 0:1]), reads=[bOTM[j]], writes=[btn, bsm])
                    yield
                    rstd_from(sm[:, 0:1], sm[:, 1:2], 1.0 / 128, [bsm], [bsm])
                    yield
                    c.op("dve", lambda e: e.tensor_scalar(out=tn[:, 128:256], in0=OTM3[:, j, :], scalar1=sm[:, 1:2], scalar2=None, op0=ALU.mult), reads=[bOTM[j], bsm], writes=[btn])
                    yield
                    c.op("pe", lambda e: e.transpose(pU[:, 256:384], tn[:, 128:256], IDN), reads=[btn, bcm], writes=[bU])
                    yield
                    zv = zview(j)
                    zb = [bZT[t] for t in ztiles(j)]
                    pin = pU[:, 256:384].rearrange("p (c r) -> p c r", c=2) if j < 32 else pU[:, 256:384]
                    c.op("dve", lambda e: e.scalar_tensor_tensor(out=zv, in0=pin, scalar=onw[:, hg:hg + 1], in1=zv, op0=ALU.mult, op1=ALU.mult),
                         reads=[bU, bonw] + zb, writes=zb)

                def gdn_scan(h):
                    done = [False] * NTI
                    claimed = [False] * NTI
                    chains = {}
                    order = []
                    ci = 0
                    for si_, (t0, t1) in enumerate(seqs):
                        for d in range(2):
                            bl = [(Sbuf[ci * 3 + i], bS[ci * 3 + i]) for i in range(3)]
                            bbl = [(Sbf2[ci * 2 + i], bSbf2[ci * 2 + i]) for i in range(2)]
                            ch = dict(S=bl[0][0], bS=bl[0][1], Sb=bbl[0][0], bSb=bbl[0][1], si=0, turn=0, t0=t0, t1=t1, bufs=bl, bbufs=bbl,
                                      last=(t1 - 1 if d == 0 else t0), out=(None if t0 == 0 else ns_b[(t0 - 32) // 2, d, h]))
                            if t0 == 0:
                                c.dma("sp", ch["S"][:], st_b[d, h], writes=[ch["bS"]])
                            else:
                                c.op("pool", lambda e: e.memset(ch["S"][:], 0.0), writes=[ch["bS"]])
                            c.op("act", lambda e: e.activation(out=ch["Sb"][:], in_=ch["S"][:], func=AF.Copy), reads=[ch["bS"]], writes=[ch["bSb"]])
                            chains[(si_, d)] = ch
                            ci += 1
                    samp = []
                    for i in range(32):
                        samp.append((0, 0, i)); samp.append((0, 1, 31 - i))
                    pro = []
                    for si_ in (1, 2):
                        t0, t1 = seqs[si_]
                        for i in range(t1 - t0):
                            pro.append((si_, 0, t0 + i)); pro.append((si_, 1, t1 - 1 - i))
                    order = samp[:8] + pro + samp[8:]
                    pending = list(order)
                    active = []
                    free = list(range(WU))
                    while pending or active:
                        while pending and free:
                            si_, d, j = pending.pop(0)
                            sl = free.pop(0)
                            ch_ = chains[(si_, d)]
                            kidx = (j - ch_["t0"]) if d == 0 else (ch_["t1"] - 1 - j)
                            active.append((sl, gdn_unit(h, d, j, sl, ch_, done, claimed, kidx)))
                        nxt_active = []
                        for sl, g in active:
                            try:
                                next(g)
                                nxt_active.append((sl, g))
                            except StopIteration:
                                free.append(sl)
                        active = nxt_active

                for h in range(4):
                    hg = 4 + h
                    for k in range(8):
                        c.dma("pool", wbf[:, k, 0:512], win_b[h, k * 128:(k + 1) * 128, :], writes=[bwbf])
                    for blk in range(NPB):
                        ht, bh = load_hblk(blk)
                        bsl = slice(blk * PBK, (blk + 1) * PBK)
                        tb = list(range(blk * 2, blk * 2 + 2))
                        for qi, (DST, bDST) in enumerate(((QN, bQN), (KN, bKN), (VC, bVC))):
                            pb, bpb = proj_fm(ht, bh, qi * 128, qi)
                            if qi == 1:
                                c.op("dve", lambda e: e.tensor_copy(out=DST[:, bsl], in_=pb), reads=[bpb], writes=[bDST[t] for t in tb])
                            else:
                                c.op("act", lambda e: e.activation(out=DST[:, bsl], in_=pb, func=AF.Copy), reads=[bpb], writes=[bDST[t] for t in tb])
                        pb, bpb = proj_fm(ht, bh, 384, 3)
                        c.op("act", lambda e: e.activation(out=ZT[:, bsl], in_=pb, func=AF.Silu), reads=[bpb], writes=[bZT[t] for t in tb])
                    for qi, (DST, bDST) in enumerate(((QN, bQN), (KN, bKN), (VC, bVC))):
                        w0 = cw[:, h * 9 + qi * 3 + 0:h * 9 + qi * 3 + 1]
                        w1 = cw[:, h * 9 + qi * 3 + 1:h * 9 + qi * 3 + 2]
                        w2 = cw[:, h * 9 + qi * 3 + 2:h * 9 + qi * 3 + 3]
                        ceng = "dve"
                        RAWq, bRAWq = (RAW, bRAW) if qi != 1 else (VT2, bVT2)
                        c.op("act", lambda e: e.activation(out=RAWq[:, :], in_=DST[:, :], func=AF.Copy, scale=w1), reads=bDST + [bcw], writes=bRAWq)
                        for (lo, hi, sh) in ((0, 4096, 64), (4096, 4352, 1), (4352, 4608, 1)):
                            c.op(ceng, lambda e: e.scalar_tensor_tensor(out=RAWq[:, lo + sh:hi], in0=DST[:, lo:hi - sh], scalar=w0, in1=RAWq[:, lo + sh:hi],
                                                                         op0=ALU.mult, op1=ALU.add), reads=bDST + [bcw], writes=bRAWq)
                            c.op(ceng, lambda e: e.scalar_tensor_tensor(out=RAWq[:, lo:hi - sh], in0=DST[:, lo + sh:hi], scalar=w2, in1=RAWq[:, lo:hi - sh],
                                                                         op0=ALU.mult, op1=ALU.add), reads=bDST + [bcw], writes=bRAWq)
                        c.op("act", lambda e: e.activation(out=DST[:, 0:4096].rearrange("p (w r) -> p w r", r=64),
                                                           in_=RAWq[:, 0:4096].rearrange("p (r w) -> p w r", w=64), func=AF.Silu), reads=bRAWq[0:32], writes=bDST[0:32])
                        c.op("act", lambda e: e.activation(out=DST[:, 4096:NT], in_=RAWq[:, 4096:NT], func=AF.Silu), reads=bRAWq[32:], writes=bDST[32:])
                        if qi == 2:
                            for j in range(NTI):
                                p5 = PB[4 + j % 2]; b5 = PBb[4 + j % 2]
                                c.op("pe", lambda e: e.transpose(p5[:, 128:256], DST[:, j * 128:(j + 1) * 128], IDN), reads=[bDST[j], bcm], writes=[b5])
                                c.op("dve", lambda e: e.tensor_copy(out=VT23[:, j, :], in_=p5[:, 128:256]), reads=[b5], writes=[bVT2[j]])
                            continue
                        for blk in range(NBLK):
                            bsl = slice(blk * 512, (blk + 1) * 512)
                            db = [bDST[t] for t in range(blk * 4, blk * 4 + 4)]
                            tw = tmp512[blk % 2]; btw = btmp512[blk % 2]
                            c.op("act", lambda e: e.activation(out=tw[:], in_=DST[:, bsl], func=AF.Square), reads=db, writes=[btw])
                            pb = PB[4 + blk % 2]; bpb = PBb[4 + blk % 2]
                            c.op("pe", lambda e: e.matmul(pb[:, :], lhsT=ONES, rhs=tw[:], start=True, stop=True), reads=[btw, bcm], writes=[bpb])
                            rstd_from(pb[:, :], tw[:], 1.0, [bpb], [btw])
                            if qi == 0:
                                c.op("dve", lambda e: e.scalar_tensor_tensor(out=DST[:, bsl], in0=DST[:, bsl], scalar=float(128 ** -0.5), in1=tw[:],
                                                                             op0=ALU.mult, op1=ALU.mult), reads=db + [btw], writes=db)
                            else:
                                c.op("dve", lambda e: e.tensor_tensor(out=DST[:, bsl], in0=DST[:, bsl], in1=tw[:], op=ALU.mult), reads=db + [btw], writes=db)
                    for j in range(NTI):
                        p5 = PB[4 + j % 2]; b5 = PBb[4 + j % 2]
                        c.op("pe", lambda e: e.transpose(p5[:, 0:128], scanview(KN, j), IDN), reads=[bKN[j], bcm], writes=[b5])
                        c.op("act", lambda e: e.activation(out=KTM3[:, j, :], in_=p5[:, 0:128], func=AF.Copy), reads=[b5], writes=[bKTM[j]])
                    gdn_scan(h)
                    c.dma("pool", oT_scr[hg], ZT[:], reads=bZT, writes=[boT[hg]])

                s2b.close()
            PB.pop(); PBb.pop()
            c.barrier()
            _chk('p2b')
            with ExitStack() as s4:
                PTh[0] = s4.enter_context(nc.psum_tensor("pbt4", [128, 1024], BF16)); PTh[1] = Buf(True)
                wo = SB(s4, "wo", [128, 8, D], BF16); bwo = Buf()
                wg = SB(s4, "wg", [128, 8, DFF], BF16); bwg = Buf()
                wu = SB(s4, "wu", [128, 8, DFF], BF16); bwu = Buf()
                wd = SB(s4, "wd", [128, NF, D], BF16); bwd = Buf()
                for k in range(8):
                    c.dma("pool", wo[:, k, :], wo_bf[k * 128:(k + 1) * 128, :], reads=[bWo[k]], writes=[bwo])
                for k in range(8):
                    c.dma("pool", wg[:, k, :], wg_bf[k * 128:(k + 1) * 128, :], reads=[bWg[k]], writes=[bwg])
                    c.dma("pool", wu[:, k, :], wu_bf[k * 128:(k + 1) * 128, :], reads=[bWu[k]], writes=[bwu])
                for f in range(NF):
                    c.dma("pool", wd[:, f, :], wd_bf[f * 128:(f + 1) * 128, :], reads=[bWd[f]], writes=[bwd])
                gbc = SB(s4, "gbc", [128, 4, D]); bgbc = Buf()
                nfb = SB(s4, "nfb", [128, D]); bnfb = Buf()
                c.dma("sp", nfb[:], normf.partition_broadcast(128), writes=[bnfb])
                with ExitStack() as s40:
                    sv = SB(s40, "sv4", [128, 16]); bsv = Buf()
                    c.dma("sp", sv[:], svec, writes=[bsv])
                    c.op("act", lambda e: e.activation(out=sv[:], in_=sv[:], func=AF.Silu), reads=[bsv], writes=[bsv])
                    srep = SB(s40, "srep", [128, 16, 128]); bsrep = Buf()
                    c.op("dve", lambda e: e.tensor_copy(out=srep[:], in_=sv[:].unsqueeze(2).broadcast_to([128, 16, 128])), reads=[bsv], writes=[bsrep])
                    brow = SB(s40, "brow", [128, 512]); bbrow = Buf()
                    wad = [SB(s40, "wad40", [128, 8, 512])] * 2; bwad = [Buf()] * 2
                    wv = w_ada.rearrange("(k p) n -> p k n", p=128)
                    ci = 0
                    for gi, cb0 in enumerate((4, 10)):
                        for hh in range(2):
                            cb = cb0 + hh
                            t = wad[ci % 2]; bt = bwad[ci % 2]; ci += 1
                            c.dma("sp", t[:], wv[:, :, cb * 512:(cb + 1) * 512], writes=[bt])
                            c.dma("sp", brow[:], bada_row[:, cb * 512:(cb + 1) * 512].partition_broadcast(128), writes=[bbrow])
                            for v in range(2):
                                pb = PB[v]; bpb = PBb[v]
                                for k in range(8):
                                    c.op("pe", lambda e: e.matmul(pb[:, :], lhsT=srep[:, v * 8 + k, :], rhs=t[:, k, :], start=(k == 0), stop=(k == 7)), reads=[bsrep, bt], writes=[bpb])
                                c.op("dve", lambda e: e.tensor_tensor(out=gbc[:, gi * 2 + v, hh * 512:(hh + 1) * 512], in0=pb[:, :], in1=brow[:], op=ALU.add),
                                     reads=[bpb, bbrow], writes=[bgbc])
                c.barrier(skip=("dpool",))
                BT = 128
                oTb = [SB(s4, "otb0", [128, 8, BT], BF16)] * 2; boTb = [Buf()] * 2
                xts = [SB(s4, "x4%d" % i, [128, D]) for i in range(2)]; bxts = bufs(2)
                x1s = [SB(s4, "x1%d" % i, [128, D]) for i in range(2)]; bx1s = bufs(2)
                h2s = [SB(s4, "h20", [128, 8, BT], BF16)] * 2; bh2s = [Buf()] * 2
                aT = SB(s4, "aT", [128, NF, BT], BF16); baT = bufs(NF)
                ss = SB(s4, "ss4", [128, 4]); xn = SB(s4, "xn4", [128, D], BF16); junk = SB(s4, "junk4", [128, D], BF16)
                tmpn = (ss, Buf(), xn, Buf(), junk, Buf())
                tsg = [SB(s4, "tsg%d" % i, [128, BT]) for i in range(2)]; btsg = bufs(2)
                byout = Buf()
                nown_blk = NOWN // BT
                nsamp_blk = 2048 // BT
                vof = lambda blk: 1 if blk < nsamp_blk else 0

                def st_op(blk):
                    tok0 = blk * BT if blk < nsamp_blk else 4096 + (blk - nsamp_blk) * BT
                    v = vof(blk)
                    ob = oTb[blk % 2]; bob = boTb[blk % 2]
                    xt = xts[blk % 2]; bx = bxts[blk % 2]; x1 = x1s[blk % 2]; bx1 = bx1s[blk % 2]
                    c.dma("sp", ob[:], oT_scr[:, :, tok0:tok0 + BT].rearrange("h p t -> p h t"), reads=boT, writes=[bob])
                    c.dma("sp", xt[:], xall[tok0:tok0 + 128, :], writes=[bx])
                    for hh in range(2):
                        pb = PB[hh]; bpb = PBb[hh]
                        for hd in range(8):
                            c.op("pe", lambda e: e.matmul(pb[:, :], lhsT=ob[:, hd, :], rhs=wo[:, hd, hh * 512:(hh + 1) * 512],
                                                          start=(hd == 0), stop=(hd == 7)), reads=[bob, bwo], writes=[bpb])
                        hs = slice(hh * 512, (hh + 1) * 512)
                        c.op("dve", lambda e: e.tensor_tensor(out=x1[:, hs], in0=pb[:, :], in1=gbc[:, v, hs], op=ALU.mult), reads=[bpb, bgbc], writes=[bx1])
                        c.op("pool", lambda e: e.tensor_tensor(out=x1[:, hs], in0=x1[:, hs], in1=xt[:, hs], op=ALU.add), reads=[bx1, bx], writes=[bx1])
                    norm_mod_transpose("norm_only", x1[:], bx1, 1, v, None, None, tmpn)

                def st_tr(blk):
                    mod_transpose(1, vof(blk), h2s[blk % 2], bh2s[blk % 2], tmpn)

                def st_gu(blk):
                    h2 = h2s[blk % 2]; bh2 = bh2s[blk % 2]
                    for f in range(NF):
                        pg = PB[2 + f % 2]; bpg = PBb[2 + f % 2]
                        pu = PB[4 + f % 2]; bpu = PBb[4 + f % 2]
                        for k in range(8):
                            c.op("pe", lambda e: e.matmul(pg[:, 0:BT], lhsT=wg[:, k, f * 128:(f + 1) * 128], rhs=h2[:, k, :], start=(k == 0), stop=(k == 7)),
                                 reads=[bwg, bh2], writes=[bpg])
                        for k in range(8):
                            c.op("pe", lambda e: e.matmul(pu[:, 0:BT], lhsT=wu[:, k, f * 128:(f + 1) * 128], rhs=h2[:, k, :], start=(k == 0), stop=(k == 7)),
                                 reads=[bwu, bh2], writes=[bpu])
                        tg = tsg[f % 2]; btg = btsg[f % 2]
                        c.op("act", lambda e: e.activation(out=tg[:], in_=pg[:, 0:BT], func=AF.Silu), reads=[bpg], writes=[btg])
                        c.op("dve", lambda e: e.tensor_tensor(out=aT[:, f, :], in0=tg[:], in1=pu[:, 0:BT], op=ALU.mult), reads=[btg, bpu], writes=[baT[f]])

                def st_down(blk):
                    v = vof(blk)
                    x1 = x1s[blk % 2]; bx1 = bx1s[blk % 2]
                    yo = xts[blk % 2]; byo = bxts[blk % 2]
                    for hh in range(2):
                        pb = PB[hh]; bpb = PBb[hh]
                        for f in range(NF):
                            c.op("pe", lambda e: e.matmul(pb[:, :], lhsT=aT[:, f, :], rhs=wd[:, f, hh * 512:(hh + 1) * 512],
                                                          start=(f == 0), stop=(f == NF - 1)), reads=[baT[f], bwd], writes=[bpb])
                        hs = slice(hh * 512, (hh + 1) * 512)
                        c.op("dve", lambda e: e.tensor_tensor(out=yo[:, hs], in0=pb[:, :], in1=gbc[:, 2 + v, hs], op=ALU.mult), reads=[bpb, bgbc], writes=[byo])
                        c.op("pool", lambda e: e.tensor_tensor(out=yo[:, hs], in0=yo[:, hs], in1=x1[:, hs], op=ALU.add), reads=[byo, bx1], writes=[byo])
                    bs_ = tmpn[1]
                    c.op("pool", lambda e: e.memset(ss[:, 2:3], 0.0), writes=[bs_])
                    c.op("act", lambda e: e.activation(out=junk[:], in_=yo[:], func=AF.Square, accum_out=ss[:, 2:3]), reads=[byo], writes=[tmpn[5], bs_])
                    rstd_from(ss[:, 2:3], ss[:, 3:4], 1.0 / D, [bs_], [bs_])
                    c.op("dve", lambda e: e.scalar_tensor_tensor(out=yo[:], in0=yo[:], scalar=ss[:, 3:4], in1=nfb[:], op0=ALU.mult, op1=ALU.mult),
                         reads=[byo, bs_, bnfb], writes=[byo])
                    r0 = blk * BT
                    c.dma("sp", y_own[r0:r0 + 128, :], yo[:], reads=[byo], writes=[byout])

                st_op(0)
                st_tr(0)
                for blk in range(nown_blk):
                    if blk + 1 < nown_blk:
                        st_op(blk + 1)
                    st_gu(blk)
                    if blk + 1 < nown_blk:
                        st_tr(blk + 1)
                    st_down(blk)
        except _Stop:
            pass
        for q in NDSEM:
            for nm in c.dnames[q]:
                if c.cnt[nm]:
                    nc.sync.wait_ge(c.sem[nm], c.cnt[nm])


_CONST = {}


def _consts():
    if _CONST:
        return _CONST
    u = np.arange(128)[:, None]
    t = np.arange(128)[None, :]
    same = (u // 64) == (t // 64)
    ident = (u == t)
    UT = same & (u <= t)
    SLO = same & (u > t)
    LT = same & (u >= t)
    SUP = same & (u < t)
    ones = np.ones((128, 128), bool)
    cm = np.stack([ident, UT, SLO, LT, SUP, same, ones], axis=1).astype(np.float32).reshape(128, 7 * 128)
    sel = np.stack([np.broadcast_to(u < 64, (128, 128)), np.broadcast_to(u >= 64, (128, 128))], axis=1).astype(np.float32).reshape(128, 256)
    seg = np.ones((128, 512), np.float32)
    seg[:, ::64] = 0.0
    _CONST.update(cmask=np.ascontiguousarray(cm), csel=np.ascontiguousarray(sel), cseg=seg)
    return _CONST


def _fm(vec):
    return np.ascontiguousarray(np.asarray(vec, np.float32).reshape(-1, 128).T)


_PROG = {}


def kernel(x_prompt, x_sample, c, state_hgrn, state_gdn, c_ctx, w_ada, b_ada, norm1, norm2, w_in, conv_w, hgrn_lb,
           gdn_A_log, gdn_dt_bias, hgrn_out_norm, gdn_out_norm, w_out, w_gate, w_up, w_down, norm_f):
    f32 = lambda a: np.ascontiguousarray(np.asarray(a, dtype=np.float32))
    x_prompt, x_sample, c, state_hgrn, state_gdn, c_ctx = map(f32, (x_prompt, x_sample, c, state_hgrn, state_gdn, c_ctx))
    w_ada, b_ada, norm1, norm2, w_in, conv_w, hgrn_lb = map(f32, (w_ada, b_ada, norm1, norm2, w_in, conv_w, hgrn_lb))
    gdn_A_log, gdn_dt_bias, hgrn_out_norm, gdn_out_norm = map(f32, (gdn_A_log, gdn_dt_bias, hgrn_out_norm, gdn_out_norm))
    w_out, w_gate, w_up, w_down, norm_f = map(f32, (w_out, w_gate, w_up, w_down, norm_f))
    if "nc" not in _PROG:
        _PROG["nc"] = build_program()
    nc = _PROG["nc"]
    cst = _consts()
    W = w_in[0]
    offs = np.cumsum([0, 512, 512, 512, 512, 512, 512, 512, 512, 512, 8, 8])
    a_q, a_ff, a_fb, a_i, a_g, b_q, b_k, b_v, b_z, b_beta, b_a = [W[:, offs[i]:offs[i + 1]] for i in range(11)]
    a_f = [a_ff, a_fb]
    in_maps = []
    for core in range(8):
        p, e = core // 2, core % 2
        ds = [0, 1] if e == 0 else [1, 0]
        fl = (lambda a: a[::-1]) if e else (lambda a: a)
        pr = [4 * p + 2 * e, 4 * p + 2 * e + 1]
        xall = np.concatenate([fl(x_sample[p]), fl(x_prompt[pr[0]]), fl(x_prompt[pr[1]])], axis=0)
        hs = lambda a, h: a[:, h * 128:(h + 1) * 128]
        win_a = np.stack([np.concatenate([hs(a_q, h), hs(a_f[ds[0]], h), hs(a_f[ds[1]], h), hs(a_i, h), hs(a_g, h)], axis=1) for h in range(4)])
        win_b = np.stack([np.concatenate([hs(b_q, h), hs(b_k, h), hs(b_v, h), hs(b_z, h)], axis=1) for h in range(4)])
        win_g = np.concatenate([b_beta[:, ds[0] * 4:ds[0] * 4 + 4], b_beta[:, ds[1] * 4:ds[1] * 4 + 4],
                                b_a[:, ds[0] * 4:ds[0] * 4 + 4], b_a[:, ds[1] * 4:ds[1] * 4 + 4]], axis=1)
        hlb = np.stack([np.stack([_fm(hgrn_lb[l, ds[d]]) for d in range(2)], axis=1) for l in range(2)], axis=1)
        cwt = conv_w[0][::-1] if e else conv_w[0]
        convw = np.zeros((128, 4, 3, 3), np.float32)
        for h in range(4):
            for qi in range(3):
                convw[:, h, qi, :] = cwt[:, qi * 512 + h * 128:qi * 512 + (h + 1) * 128].T
        alog = np.broadcast_to(gdn_A_log[0][ds][:, None, :], (2, NTI, 4)).reshape(1, -1)
        dtb = np.broadcast_to(gdn_dt_bias[0][ds][:, None, :], (2, NTI, 4)).reshape(1, -1)
        m = dict(
            xall=np.ascontiguousarray(xall),
            svec=np.ascontiguousarray(np.concatenate([_fm(c_ctx), _fm(c[p])], axis=1)),
            w_ada=w_ada[0], bada_fm=_fm(b_ada[0]), bada_row=b_ada[0][None, :],
            n12_fm=np.ascontiguousarray(np.concatenate([_fm(norm1[0]), _fm(norm2[0])], axis=1)),
            normf=norm_f[None, :],
            win_a=np.ascontiguousarray(win_a), win_b=np.ascontiguousarray(win_b), win_g=np.ascontiguousarray(win_g),
            hlb_fm=np.ascontiguousarray(hlb.reshape(128, 16)),
            convw_fm=np.ascontiguousarray(convw.reshape(128, 36)),
            alog_rep=np.ascontiguousarray(np.broadcast_to(alog, (128, 288))),
            dtb_rep=np.ascontiguousarray(np.broadcast_to(dtb, (128, 288))),
            onorm_fm=np.ascontiguousarray(np.concatenate([_fm(hgrn_out_norm[0]), _fm(gdn_out_norm[0])], axis=1)),
            st_a=np.ascontiguousarray(state_hgrn[p, 0][ds]), st_b=np.ascontiguousarray(state_gdn[p, 0][ds]),
            w_out=w_out[0], w_gate=w_gate[0], w_up=w_up[0], w_down=w_down[0],
            cmask=cst["cmask"], csel=cst["csel"], cseg=cst["cseg"],
        )
        in_maps.append(m)
    if _PROG.get('debug_hook'):
        return _PROG['debug_hook'](nc, in_maps)
    res = run_bass_kernel_spmd(nc, in_maps, core_ids=list(range(8)))
    y_prompt = np.zeros((16, 256, D), np.float32)
    y_sample = np.zeros((4, 4096, D), np.float32)
    nsa = np.zeros((16, 1, 2, 4, 128, 128), np.float32)
    nsb = np.zeros((16, 1, 2, 4, 128, 128), np.float32)
    for core in range(8):
        p, e = core // 2, core % 2
        ds = [0, 1] if e == 0 else [1, 0]
        r = res.results[core]
        yo = np.asarray(r["y_own"], np.float32)
        if e == 0:
            y_sample[p, 0:2048] = yo[0:2048]
        else:
            y_sample[p, 2048:4096] = yo[0:2048][::-1]
        for j in range(2):
            seq = 4 * p + 2 * e + j
            blk = yo[2048 + 256 * j:2048 + 256 * (j + 1)]
            y_prompt[seq] = blk[::-1] if e else blk
            for d in range(2):
                nsa[seq, 0, ds[d]] = np.asarray(r["ns_a"], np.float32)[j, d]
                nsb[seq, 0, ds[d]] = np.asarray(r["ns_b"], np.float32)[j, d]
    return (y_prompt, y_sample, nsa, nsb)
```

```python
import numpy as np
import ml_dtypes
import concourse.bass as bass
import concourse.mybir as mybir
from concourse.bass_utils import run_bass_kernel_spmd
from contextlib import ExitStack

F32 = mybir.dt.float32
BF16 = mybir.dt.bfloat16
AF = mybir.ActivationFunctionType
ALU = mybir.AluOpType

D = 1024
NT = 4608
NTI = 36
NBLK = 9
NOWN = 2560
DFF = 2816
NF = 22
EPS = 1e-6
SAME_ENG_SYNC = True
ATTACH_WAIT = True


class Buf:
    __slots__ = ("lw", "rd", "excl")

    def __init__(self, excl=False):
        self.lw = None
        self.rd = {}
        self.excl = excl


def bufs(n):
    return [Buf() for _ in range(n)]


NDSEM = {"sp": 40, "pool": 16, "bg": 40}
DQ_ENG = {"sp": "sp", "pool": "pool", "bg": "pool"}


class Ctx:
    def __init__(self, nc, es):
        self.nc = nc
        self.eng = {"pe": nc.tensor, "dve": nc.vector, "act": nc.scalar, "pool": nc.gpsimd, "sp": nc.sync}
        self.sem = {}
        self.cnt = {}
        for k in list(self.eng):
            self.sem[k] = es.enter_context(nc.semaphore("s_" + k))
            self.cnt[k] = 0
        self.dnames = {}
        self.drr = {}
        for q, n in NDSEM.items():
            self.dnames[q] = []
            self.drr[q] = 0
            for i in range(n):
                nm = "d%s%d" % (q, i)
                self.sem[nm] = es.enter_context(nc.semaphore("s_" + nm))
                self.cnt[nm] = 0
                self.dnames[q].append(nm)
        self.waited = {k: {} for k in self.eng}
        self.hist = {}

    def _deps(self, en, reads, writes, extra=None):
        deps = {}
        if extra is not None:
            deps[extra[0]] = extra[1]
        for b in reads:
            if b.lw is not None:
                s, v = b.lw
                if deps.get(s, 0) < v:
                    deps[s] = v
            if b.excl:
                for s, v in b.rd.items():
                    if s != en and deps.get(s, 0) < v:
                        deps[s] = v
        for b in writes:
            if b.lw is not None:
                s, v = b.lw
                if deps.get(s, 0) < v:
                    deps[s] = v
            for s, v in b.rd.items():
                if deps.get(s, 0) < v:
                    deps[s] = v
        e = self.eng[en]
        w = self.waited[en]
        need = []
        for s, v in deps.items():
            if v <= 0:
                continue
            if s == en and (en == "pe" or not SAME_ENG_SYNC):
                continue
            if w.get(s, 0) < v:
                need.append((s, v))
                w[s] = v
        for s, v in list(need):
            snap = self.hist.get((s, v))
            if snap:
                for s2, v2 in snap.items():
                    if w.get(s2, 0) < v2:
                        w[s2] = v2
        need = [(s, v) for (s, v) in need if w.get(s, 0) <= v]
        attach = need.pop() if (need and ATTACH_WAIT) else None
        for s, v in need:
            e.wait_ge(self.sem[s], v)
        return attach

    def barrier(self, skip=()):
        for en in self.eng:
            e = self.eng[en]
            w = self.waited[en]
            for s, v in self.cnt.items():
                if any(s.startswith(p) for p in skip):
                    continue
                if v > 0 and s != en and w.get(s, 0) < v:
                    e.wait_ge(self.sem[s], v)
                    w[s] = v

    def op(self, en, fn, reads=(), writes=(), serial=False):
        attach = self._deps(en, reads, writes)
        if serial and self.cnt[en] > self.waited[en].get(en, 0):
            if attach is None and ATTACH_WAIT:
                attach = (en, self.cnt[en])
            else:
                self.eng[en].wait_ge(self.sem[en], self.cnt[en])
            self.waited[en][en] = self.cnt[en]
        ins = fn(self.eng[en])
        if attach is not None:
            ins._wait_ge(self.sem[attach[0]], attach[1])
        ins.then_inc(self.sem[en], 1)
        self.cnt[en] += 1
        c = self.cnt[en]
        self.hist[(en, c)] = dict(self.waited[en])
        for b in reads:
            b.rd[en] = c
        for b in writes:
            b.lw = (en, c)
            b.rd = {}
        return ins

    def dma(self, q, out, in_, reads=(), writes=()):
        i = self.drr[q] % len(self.dnames[q])
        self.drr[q] += 1
        ds = self.dnames[q][i]
        en = DQ_ENG[q]
        attach = self._deps(en, reads, writes, extra=(ds, self.cnt[ds]))
        ins = self.eng[en].dma_start(out=out, in_=in_)
        if attach is not None:
            ins._wait_ge(self.sem[attach[0]], attach[1])
        ins.then_inc(self.sem[ds], 16)
        self.cnt[ds] += 16
        c = self.cnt[ds]
        self.hist[(ds, c)] = dict(self.waited[en])
        for b in reads:
            b.rd[ds] = c
        for b in writes:
            b.lw = (ds, c)
            b.rd = {}
        return ins


class _Stop(Exception):
    pass


STOP = [None]


def _chk(tag):
    if STOP[0] == tag:
        raise _Stop()


def build_program():
    nc = bass.Bass("TRN2", target_bir_lowering=False)
    try:
        _build_body(nc)
    except AssertionError:
        if STOP[0] is None:
            raise
    return nc


DEBUG = [False]
DUMPS = {}


def _build_body(nc):
    DUMPS.clear()
    din = lambda n, s, d=F32: nc.dram_tensor(n, list(s), d, kind="ExternalInput").ap()
    dout = lambda n, s, d=F32: nc.dram_tensor(n, list(s), d, kind="ExternalOutput").ap()
    xall = din("xall", [NT, D])
    svec = din("svec", [128, 16])
    w_ada = din("w_ada", [D, 6 * D])
    bada_fm = din("bada_fm", [128, 48])
    bada_row = din("bada_row", [1, 6 * D])
    n12_fm = din("n12_fm", [128, 16])
    normf = din("normf", [1, D])
    win_a = din("win_a", [4, D, 640])
    win_b = din("win_b", [4, D, 512])
    win_g = din("win_g", [D, 16])
    hlb_fm = din("hlb_fm", [128, 16])
    convw_fm = din("convw_fm", [128, 36])
    alog_rep = din("alog_rep", [128, 288])
    dtb_rep = din("dtb_rep", [128, 288])
    onorm_fm = din("onorm_fm", [128, 8])
    st_a = din("st_a", [2, 4, 128, 128])
    st_b = din("st_b", [2, 4, 128, 128])
    w_out = din("w_out", [D, D])
    w_gate = din("w_gate", [D, DFF])
    w_up = din("w_up", [D, DFF])
    w_down = din("w_down", [DFF, D])
    cmask = din("cmask", [128, 7 * 128])
    csel = din("csel", [128, 256])
    cseg = din("cseg", [128, 512])
    y_own = dout("y_own", [NOWN, D])
    ns_a = dout("ns_a", [2, 2, 4, 128, 128])
    ns_b = dout("ns_b", [2, 2, 4, 128, 128])
    hT_scr = nc.dram_tensor("hT_scr", [128, 8, NT], BF16, kind="Internal").ap()
    oT_scr = nc.dram_tensor("oT_scr", [8, 128, NT], BF16, kind="Internal").ap()
    wo_bf = nc.dram_tensor("wo_bf", [D, D], BF16, kind="Internal").ap()
    wg_bf = nc.dram_tensor("wg_bf", [D, DFF], BF16, kind="Internal").ap()
    wu_bf = nc.dram_tensor("wu_bf", [D, DFF], BF16, kind="Internal").ap()
    wd_bf = nc.dram_tensor("wd_bf", [DFF, D], BF16, kind="Internal").ap()

    with ExitStack() as es:
        c = Ctx(nc, es)
        SB = lambda st, n, s, d=F32: st.enter_context(nc.sbuf_tensor(n, list(s), d))

        def dump(name, ap, rb):
            if not DEBUG[0] or name in DUMPS:
                return
            shp = list(ap.shape)
            t = nc.dram_tensor("dbg_" + name, shp, ap.dtype, kind="ExternalOutput").ap()
            DUMPS[name] = shp
            c.dma("sp", t, ap, reads=rb, writes=[Buf()])
        PB = [es.enter_context(nc.psum_tensor("pb%d" % i, [128, 512], F32)) for i in range(7)]
        PBb = [Buf(True) for _ in range(7)]
        PTh = [None, None]
        cm = SB(es, "cm", [128, 7, 128]); bcm = Buf()
        c.dma("sp", cm[:].rearrange("p a b -> p (a b)"), cmask, writes=[bcm])
        IDN, UT, SLO, LT, SUP, BLK, ONES = [cm[:, i, :] for i in range(7)]
        sel = SB(es, "sel", [128, 2, 128]); bsel = Buf()
        c.dma("sp", sel[:].rearrange("p a b -> p (a b)"), csel, writes=[bsel])
        seg = SB(es, "seg", [128, 512]); bseg = Buf()
        c.dma("sp", seg[:], cseg, writes=[bseg])
        idb = SB(es, "idb", [128, 128], BF16); bidb = Buf()
        c.op("dve", lambda e: e.tensor_copy(out=idb[:], in_=IDN), reads=[bcm], writes=[bidb])
        modp = SB(es, "modp", [128, 64]); bmod = Buf()
        lbt = SB(es, "lbt", [128, 16]); blb = Buf()
        cw = SB(es, "cw", [128, 36]); bcw = Buf()
        onw = SB(es, "onw", [128, 8]); bonw = Buf()
        c.dma("sp", cw[:], convw_fm, writes=[bcw])
        c.dma("sp", onw[:], onorm_fm, writes=[bonw])
        epsb = SB(es, "epsb", [128, 1]); beps = Buf()
        c.op("dve", lambda e: e.memset(epsb[:], EPS), writes=[beps])

        def rstd_from(ss_ap, out_ap, scale, rb, wb, n=1):
            c.op("act", lambda e: e.activation(out=out_ap, in_=ss_ap, func=AF.Ln, scale=scale, bias=epsb[:, 0:1]),
                 reads=rb + [beps], writes=wb)
            c.op("act", lambda e: e.activation(out=out_ap, in_=out_ap, func=AF.Exp, scale=-0.5), reads=wb, writes=wb)

        try:
            with ExitStack() as s0:
                sv = SB(s0, "sv", [128, 16]); bsv = Buf()
                c.dma("sp", sv[:], svec, writes=[bsv])
                ssil = SB(s0, "ssil", [128, 8, 2]); bss = Buf()
                c.op("act", lambda e: e.activation(out=ssil[:].rearrange("p k v -> p v k"), in_=sv[:].rearrange("p (v k) -> p v k", v=2), func=AF.Silu),
                     reads=[bsv], writes=[bss])
                bfm = SB(s0, "bfm", [128, 48]); bbfm = Buf()
                c.dma("sp", bfm[:], bada_fm, writes=[bbfm])
                n12 = SB(s0, "n12", [128, 16]); bn12 = Buf()
                c.dma("sp", n12[:], n12_fm, writes=[bn12])
                mfm = SB(s0, "mfm", [128, 48, 2]); bmfm = Buf()
                wad = [SB(s0, "wad%d" % i, [128, 8, 512]) for i in range(2)]
                bwad = bufs(2)
                wv = w_ada.rearrange("(k p) n -> p k n", p=128)
                ci = 0
                for cb in (0, 1, 2, 3, 6, 7, 8, 9):
                    t = wad[ci % 2]; bt = bwad[ci % 2]; ci += 1
                    c.dma("sp", t[:], wv[:, :, cb * 512:(cb + 1) * 512], writes=[bt])
                    pb = PB[ci % 2]; bpb = PBb[ci % 2]
                    for jj in range(4):
                        for k in range(8):
                            c.op("pe", lambda e: e.matmul(pb[:, jj * 2:jj * 2 + 2], lhsT=t[:, k, jj * 128:(jj + 1) * 128], rhs=ssil[:, k, :],
                                                          start=(k == 0), stop=(k == 7)), reads=[bt, bss], writes=[bpb])
                    j0 = cb * 4
                    c.op("dve", lambda e: e.tensor_tensor(out=mfm[:, j0:j0 + 4, :], in0=pb[:, 0:8].rearrange("p (j v) -> p j v", v=2),
                                                          in1=bfm[:, j0:j0 + 4].unsqueeze(2).broadcast_to([128, 4, 2]), op=ALU.add),
                         reads=[bpb, bbfm], writes=[bmfm])
                for which, (jsh, jsc, noff) in enumerate(((0, 8, 0), (24, 32, 8))):
                    for v in range(2):
                        o0 = (which * 2 + v) * 16
                        c.op("dve", lambda e: e.scalar_tensor_tensor(out=modp[:, o0:o0 + 8], in0=mfm[:, jsc:jsc + 8, v], scalar=1.0, in1=n12[:, noff:noff + 8],
                                                                     op0=ALU.add, op1=ALU.mult), reads=[bmfm, bn12], writes=[bmod])
                        c.op("dve", lambda e: e.tensor_copy(out=modp[:, o0 + 8:o0 + 16], in_=mfm[:, jsh:jsh + 8, v]), reads=[bmfm], writes=[bmod])
                hl = SB(s0, "hl", [128, 16]); bhl = Buf()
                c.dma("sp", hl[:], hlb_fm, writes=[bhl])
                c.op("dve", lambda e: e.tensor_tensor(out=hl[:, 0:8], in0=hl[:, 0:8], in1=hl[:, 8:16], op=ALU.subtract), reads=[bhl], writes=[bhl])
                c.op("act", lambda e: e.activation(out=lbt[:, 0:8], in_=hl[:, 0:8], func=AF.Sigmoid), reads=[bhl], writes=[blb])
                c.op("act", lambda e: e.activation(out=lbt[:, 8:16], in_=hl[:, 0:8], func=AF.Sigmoid, scale=-1.0), reads=[bhl], writes=[blb])

            c.barrier()
            dump('modp', modp[:], [bmod])
            dump('lbt', lbt[:], [blb])
            _chk('p0')
            A1 = lambda which, v, k: modp[:, (which * 2 + v) * 16 + k:(which * 2 + v) * 16 + k + 1]
            SH = lambda which, v, k: modp[:, (which * 2 + v) * 16 + 8 + k:(which * 2 + v) * 16 + 9 + k]

            bhT = bufs(NTI)
            boT = bufs(8)

            def norm_mod_transpose(st_pool, xt, bxt, which, v, hdst, bh, tmpn):
                ss, bs_, xn, bxn, junk, bj = tmpn
                PT, PTb = PTh
                c.op("pool", lambda e: e.memset(ss[:, 0:1], 0.0), writes=[bs_])
                c.op("act", lambda e: e.activation(out=junk[:], in_=xt, func=AF.Square, accum_out=ss[:, 0:1]), reads=[bxt], writes=[bj, bs_])
                rstd_from(ss[:, 0:1], ss[:, 1:2], 1.0 / D, [bs_], [bs_])
                c.op("act", lambda e: e.activation(out=xn[:], in_=xt, func=AF.Copy, scale=ss[:, 1:2]), reads=[bxt, bs_], writes=[bxn])
                if st_pool == "norm_only":
                    return
                mod_transpose(which, v, hdst, bh, tmpn)

            def mod_transpose(which, v, hdst, bh, tmpn):
                ss, bs_, xn, bxn, junk, bj = tmpn
                PT, PTb = PTh
                for k in range(8):
                    c.op("pe", lambda e: e.transpose(PT[:, k * 128:(k + 1) * 128], xn[:, k * 128:(k + 1) * 128], idb[:]), reads=[bxn, bidb], writes=[PTb])
                for k in range(8):
                    c.op("dve", lambda e: e.tensor_scalar(out=hdst[:, k, :], in0=PT[:, k * 128:(k + 1) * 128], scalar1=A1(which, v, k), scalar2=SH(which, v, k),
                                                          op0=ALU.mult, op1=ALU.add), reads=[PTb, bmod], writes=[bh])

            with ExitStack() as s2:
                SLOT = [SB(s2, "slot%d" % i, [128, NT]) for i in range(5)]
                SLB = [bufs(NTI) for _ in range(5)]
                ZT = SB(s2, "zt", [128, NT], BF16); bZT = bufs(NTI)
                gsm = {n: SB(s2, "g_" + n, [128, 2, NTI, 4]) for n in ("beta", "gg", "gc", "glt", "gl0", "gl1", "eg", "beg", "ekt", "dec0", "dec1")}
                bgs = Buf()
                with ExitStack() as s1:
                    PTh[0] = s1.enter_context(nc.psum_tensor("pbt1", [128, 1024], BF16)); PTh[1] = Buf(True)
                    GTM = SB(s1, "gtm", [128, NTI, 16]); bGTM = Buf()
                    W1 = 4
                    xts = [SB(s1, "xt%d" % i, [128, D]) for i in range(W1)]; bxts = bufs(W1)
                    hts = [SB(s1, "ht%d" % i, [128, 8, 128], BF16) for i in range(W1)]; bhts = bufs(W1)
                    ss1 = [SB(s1, "ss1_%d" % i, [128, 2]) for i in range(W1)]; bss1 = bufs(W1)
                    xn1 = [SB(s1, "xn1_%d" % i, [128, D], BF16) for i in range(W1)]; bxn1 = bufs(W1)
                    jk1 = [SB(s1, "jk1_%d" % i, [128, D], BF16) for i in range(W1)]; bjk1 = bufs(W1)
                    PT1 = [PTh[0]] * 2
                    bPT1 = [PTh[1]] * 2
                    wgf = SB(s1, "wgf", [128, 8, 16]); bwgf = Buf()
                    c.dma("sp", wgf[:], win_g.rearrange("(k p) n -> p k n", p=128), writes=[bwgf])
                    wgb = SB(s1, "wgb", [128, 8, 16], BF16); bwgb = Buf()
                    c.op("dve", lambda e: e.tensor_copy(out=wgb[:], in_=wgf[:]), reads=[bwgf], writes=[bwgb])
                    gT = SLOT[4]
                    bgT = SLB[4]

                    def p1_unit(i, sl):
                        xt = xts[sl]; bx = bxts[sl]; ht = hts[sl]; bh = bhts[sl]
                        ss = ss1[sl]; bs_ = bss1[sl]; xn = xn1[sl]; bxn = bxn1[sl]; junk = jk1[sl]; bj = bjk1[sl]
                        PT = PT1[sl % 2]; PTb = bPT1[sl % 2]
                        v = 1 if i < 32 else 0
                        c.dma("sp", xt[:], xall[i * 128:(i + 1) * 128, :], writes=[bx])
                        c.op("pool", lambda e: e.memset(ss[:, 0:1], 0.0), writes=[bs_])
                        yield
                        c.op("act", lambda e: e.activation(out=junk[:], in_=xt[:], func=AF.Square, accum_out=ss[:, 0:1]), reads=[bx], writes=[bj, bs_])
                        yield
                        c.op("act", lambda e: e.activation(out=ss[:, 1:2], in_=ss[:, 0:1], func=AF.Ln, scale=1.0 / D, bias=epsb[:, 0:1]), reads=[bs_, beps], writes=[bs_])
                        yield
                        c.op("act", lambda e: e.activation(out=ss[:, 1:2], in_=ss[:, 1:2], func=AF.Exp, scale=-0.5), reads=[bs_], writes=[bs_])
                        yield
                        c.op("act", lambda e: e.activation(out=xn[:], in_=xt[:], func=AF.Copy, scale=ss[:, 1:2]), reads=[bx, bs_], writes=[bxn])
                        yield
                        for k in range(8):
                            c.op("pe", lambda e: e.transpose(PT[:, k * 128:(k + 1) * 128], xn[:, k * 128:(k + 1) * 128], idb[:]), reads=[bxn, bidb], writes=[PTb])
                        for k in range(8):
                            eng_ = "dve" if k % 2 == 0 else "pool"
                            if eng_ == "pool":
                                eng_ = "dve"
                            c.op(eng_, lambda e: e.tensor_scalar(out=ht[:, k, :], in0=PT[:, k * 128:(k + 1) * 128], scalar1=A1(0, v, k), scalar2=SH(0, v, k),
                                                                  op0=ALU.mult, op1=ALU.add), reads=[PTb, bmod], writes=[bh])
                        yield
                        pg = PB[2 + sl]; bpg = PBb[2 + sl]
                        for k in range(8):
                            c.op("pe", lambda e: e.matmul(pg[0:16, 0:128], lhsT=wgb[:, k, :], rhs=ht[:, k, :], start=(k == 0), stop=(k == 7)),
                                 reads=[bwgb, bh], writes=[bpg])
                        c.dma("pool", hT_scr[:, :, i * 128:(i + 1) * 128], ht[:], reads=[bh], writes=[bhT[i]])
                        yield
                        c.op("act", lambda e: e.activation(out=gT[0:16, i * 128:(i + 1) * 128], in_=pg[0:16, 0:128], func=AF.Copy), reads=[bpg], writes=[bgT[i]])

                    pending = list(range(NTI))
                    active = []
                    free = list(range(W1))
                    while pending or active:
                        while pending and free:
                            i = pending.pop(0); sl = free.pop(0)
                            active.append((sl, p1_unit(i, sl)))
                        nxt_active = []
                        for sl, g in active:
                            try:
                                next(g)
                                nxt_active.append((sl, g))
                            except StopIteration:
                                free.append(sl)
                        active = nxt_active
                    gTc = SLOT[3]; bgTc = SLB[3]
                    c.op("act", lambda e: e.activation(out=gTc[0:16, 0:4096].rearrange("g (w r) -> g w r", r=64),
                                                       in_=gT[0:16, 0:4096].rearrange("g (r w) -> g w r", w=64), func=AF.Copy), reads=bgT[0:32], writes=bgTc[0:32])
                    c.op("act", lambda e: e.activation(out=gTc[0:16, 4096:NT], in_=gT[0:16, 4096:NT], func=AF.Copy), reads=bgT[32:], writes=bgTc[32:])
                    for j in range(NTI):
                        pg = PB[2 + j % 2]; bpg = PBb[2 + j % 2]
                        src = gTc[0:16, j * 128:(j + 1) * 128]
                        c.op("pe", lambda e: e.transpose(pg[:, 0:16], src, IDN[0:16, 0:16]), reads=[bgTc[j], bcm], writes=[bpg])
                        c.op("act", lambda e: e.activation(out=GTM[:, j, :], in_=pg[:, 0:16], func=AF.Copy), reads=[bpg], writes=[bGTM])
                    al = SB(s1, "al", [128, 2, NTI, 4]); dtb = SB(s1, "dtb", [128, 2, NTI, 4]); bal = Buf()
                    c.dma("sp", al[:].rearrange("p a b c -> p (a b c)"), alog_rep, writes=[bal])
                    c.dma("sp", dtb[:].rearrange("p a b c -> p (a b c)"), dtb_rep, writes=[bal])
                    gview = lambda lo: GTM[:, :, lo:lo + 8].rearrange("p t (d h) -> p d t h", d=2)
                    c.op("act", lambda e: e.activation(out=gsm["beta"][:], in_=gview(0), func=AF.Sigmoid), reads=[bGTM], writes=[bgs])
                    c.op("dve", lambda e: e.tensor_tensor(out=gsm["gg"][:], in0=gview(8), in1=dtb[:], op=ALU.add), reads=[bGTM, bal], writes=[bgs])
                    c.op("act", lambda e: e.activation(out=gsm["gg"][:], in_=gsm["gg"][:], func=AF.Exp), reads=[bgs], writes=[bgs])
                    c.op("act", lambda e: e.activation(out=gsm["gg"][:], in_=gsm["gg"][:], func=AF.Ln, bias=1.0), reads=[bgs], writes=[bgs])
                    c.op("act", lambda e: e.activation(out=al[:], in_=al[:], func=AF.Exp), reads=[bal], writes=[bal])
                    c.op("dve", lambda e: e.scalar_tensor_tensor(out=gsm["gg"][:], in0=gsm["gg"][:], scalar=-1.0, in1=al[:], op0=ALU.mult, op1=ALU.mult),
                         reads=[bgs, bal], writes=[bgs])
                    fl = lambda t: t[:].rearrange("p d t h -> p (d t h)")
                    pq = PB[4]; bpq = PBb[4]
                    for d in range(2):
                        rhs = gsm["gg"][:, d].rearrange("p t h -> p (t h)")
                        c.op("pe", lambda e: e.matmul(pq[:, d * 144:(d + 1) * 144], lhsT=(UT if d == 0 else LT), rhs=rhs, start=True, stop=True),
                             reads=[bgs, bcm], writes=[bpq])
                    c.op("dve", lambda e: e.tensor_copy(out=fl(gsm["gc"]), in_=pq[:, 0:288]), reads=[bpq], writes=[bgs])
                    for nm, lh, bl in (("glt", BLK, bcm), ("gl0", sel[:, 0, :], bsel), ("gl1", sel[:, 1, :], bsel)):
                        c.op("pe", lambda e: e.matmul(pq[:, 0:288], lhsT=lh, rhs=fl(gsm["gg"]), start=True, stop=True), reads=[bgs, bl], writes=[bpq])
                        c.op("dve", lambda e: e.tensor_copy(out=fl(gsm[nm]), in_=pq[:, 0:288]), reads=[bpq], writes=[bgs])
                    c.op("act", lambda e: e.activation(out=fl(gsm["eg"]), in_=fl(gsm["gc"]), func=AF.Exp), reads=[bgs], writes=[bgs])
                    c.op("dve", lambda e: e.tensor_tensor(out=fl(gsm["beg"]), in0=fl(gsm["beta"]), in1=fl(gsm["eg"]), op=ALU.mult), reads=[bgs], writes=[bgs])
                    c.op("dve", lambda e: e.tensor_tensor(out=fl(gsm["ekt"]), in0=fl(gsm["glt"]), in1=fl(gsm["gc"]), op=ALU.subtract), reads=[bgs], writes=[bgs])
                    c.op("act", lambda e: e.activation(out=fl(gsm["ekt"]), in_=fl(gsm["ekt"]), func=AF.Exp), reads=[bgs], writes=[bgs])
                    c.op("act", lambda e: e.activation(out=fl(gsm["dec0"]), in_=fl(gsm["gl0"]), func=AF.Exp), reads=[bgs], writes=[bgs])
                    c.op("act", lambda e: e.activation(out=fl(gsm["dec1"]), in_=fl(gsm["gl1"]), func=AF.Exp), reads=[bgs], writes=[bgs])

                c.barrier()
                for _n in gsm:
                    dump('g_' + _n, gsm[_n][:], [bgs])
                _chk('p1')
                bWo, bWg, bWu, bWd = bufs(8), bufs(8), bufs(8), bufs(NF)

                def issue_bg_precast():
                    for k in range(8):
                        c.dma("bg", wo_bf[k * 128:(k + 1) * 128, :], w_out[k * 128:(k + 1) * 128, :], writes=[bWo[k]])
                    for k in range(8):
                        c.dma("bg", wg_bf[k * 128:(k + 1) * 128, :], w_gate[k * 128:(k + 1) * 128, :], writes=[bWg[k]])
                        c.dma("bg", wu_bf[k * 128:(k + 1) * 128, :], w_up[k * 128:(k + 1) * 128, :], writes=[bWu[k]])
                    for f in range(NF):
                        c.dma("bg", wd_bf[f * 128:(f + 1) * 128, :], w_down[f * 128:(f + 1) * 128, :], writes=[bWd[f]])

                PBK = 256
                NPB = NT // PBK
                hTb = [SB(s2, "htb%d" % i, [128, 8, PBK], BF16) for i in range(2)]; bhTb = bufs(2)
                wbf = SB(s2, "wbf", [128, 8, 640], BF16); bwbf = Buf()
                PB.append(s2.enter_context(nc.psum_tensor("pb7", [128, 512], F32))); PBb.append(Buf(True))
                tmp512 = [SB(s2, "tmpw%d" % i, [128, 512]) for i in range(2)]; btmp512 = bufs(2)
                Sbuf = [SB(s2, "S%d" % i, [128, 128]) for i in range(18)]; bS = bufs(18)
                small = SB(s2, "small", [128, 16]); bsmall = bufs(4)
                s2a = ExitStack()
                TMPN = 0
                tmp = [SB(s2a, "tmp%d" % i, [128, 256]) for i in range(TMPN)]; btmp = bufs(TMPN)
                OF = SB(s2a, "of", [128, NT], BF16); bOF = bufs(NTI)
                seqs = [(0, 32), (32, 34), (34, 36)]
                hblk_state = [0]

                def load_hblk(blk):
                    i = hblk_state[0] % 2; hblk_state[0] += 1
                    c.dma("sp", hTb[i][:], hT_scr[:, :, blk * PBK:(blk + 1) * PBK], reads=bhT[blk * 2:blk * 2 + 2], writes=[bhTb[i]])
                    return hTb[i], bhTb[i]

                def proj_fm(ht, bh, col0, pbi):
                    pb = PB[pbi]; bpb = PBb[pbi]
                    for k in range(8):
                        c.op("pe", lambda e: e.matmul(pb[:, 0:PBK], lhsT=wbf[:, k, col0:col0 + 128], rhs=ht[:, k, :], start=(k == 0), stop=(k == 7)),
                             reads=[bwbf, bh], writes=[bpb])
                    return pb[:, 0:PBK], bpb

                rr = [0]

                def T(n=1):
                    i = rr[0] % TMPN; rr[0] += 1
                    return tmp[i], btmp[i]

                WUA = 6
                HNAMES = ("teg", "tqd", "tkd", "tkt", "tkt2", "tktm", "tat", "tos", "tsq")
                HW_ = {"tqd": 128, "tkd": 128, "tktm": 128, "tat": 128, "tkt": 128, "tkt2": 128, "tsq": 128}
                HBF = ("tqd", "tkd", "tkt2", "tktm", "tat", "tsq")
                hsets = []
                for w_ in range(WUA):
                    hsets.append({n: (SB(s2a, "hu%d_%s" % (w_, n), [128, HW_.get(n, 256)], BF16 if n in HBF else F32), Buf()) for n in HNAMES})
                Sbf = [SB(s2a, "Sbf%d" % i, [128, 128], BF16) for i in range(18)]; bSbf = bufs(18)
                onesb = SB(s2a, "onesb", [128, 128], BF16); bonesb = Buf()
                c.op("dve", lambda e: e.tensor_copy(out=onesb[:], in_=ONES), reads=[bcm], writes=[bonesb])
                VTMb = SLOT[1][:].bitcast(BF16)[:, 0:NT]
                hsm = [SB(s2a, "hsm%d" % w_, [128, 2]) for w_ in range(WUA)]; bhsm = bufs(WUA)
                GA1 = SB(s2a, "ga1", [128, NT]); bGA1 = bufs(NTI)

                def hgrn_unit(h, d, i, slot, ch, kidx, claimed, done):
                    FD, bFD = (FF, bFF) if d == 0 else (FB, bFB)
                    GA, bGA = GAs[d]
                    ts = hsets[slot]
                    pU, bU = PB[slot], PBb[slot]
                    pUb = pU[:, 0:64].bitcast(BF16)
                    MASK = UT if d == 0 else LT
                    glpos = 63 if d == 0 else 0
                    tsl = slice(i * 128, (i + 1) * 128)
                    G3 = GA[:, tsl].rearrange("p (c j) -> p c j", j=64)
                    teg, bteg = ts["teg"]; tqd, btqd = ts["tqd"]; tkd, btkd = ts["tkd"]; tkt, btkt = ts["tkt"]; tkt2, btkt2 = ts["tkt2"]
                    tktm, btktm = ts["tktm"]; tat, btat = ts["tat"]; tos, btos = ts["tos"]; tsq, btsq = ts["tsq"]
                    tdc = hsm[slot]; btdc = bhsm[slot]
                    c.op("act", lambda e: e.activation(out=teg[:, 0:128], in_=GA[:, tsl], func=AF.Exp), reads=[bGA[i]], writes=[bteg])
                    c.op("act", lambda e: e.activation(out=teg[:, 128:256], in_=GA[:, tsl], func=AF.Exp, scale=-1.0), reads=[bGA[i]], writes=[bteg])
                    c.op("dve", lambda e: e.tensor_tensor(out=tkt[:, 0:128].rearrange("p (c j) -> p c j", j=64), in0=G3[:, :, glpos:glpos + 1].broadcast_to([128, 2, 64]),
                                                          in1=G3, op=ALU.subtract), reads=[bGA[i]], writes=[btkt])
                    c.op("act", lambda e: e.activation(out=tdc[:, 0:2], in_=G3[:, :, glpos], func=AF.Exp), reads=[bGA[i]], writes=[btdc])
                    yield
                    c.op("pool", lambda e: e.tensor_tensor(out=tqd[:, 0:128], in0=QT[:, tsl], in1=teg[:, 0:128], op=ALU.mult), reads=[bQT[i], bteg], writes=[btqd])
                    c.op("pool", lambda e: e.tensor_tensor(out=tkd[:, 0:128], in0=FD[:, tsl], in1=teg[:, 128:256], op=ALU.mult), reads=[bFD[i], bteg], writes=[btkd])
                    c.op("act", lambda e: e.activation(out=tkt[:, 0:128], in_=tkt[:, 0:128], func=AF.Exp), reads=[btkt], writes=[btkt])
                    yield
                    c.op("dve", lambda e: e.tensor_tensor(out=tkt2[:, 0:128], in0=FD[:, tsl], in1=tkt[:, 0:128], op=ALU.mult), reads=[bFD[i], btkt], writes=[btkt2])
                    c.op("pe", lambda e: e.matmul(pU[:, 128:256], lhsT=tkd[:, 0:128], rhs=tqd[:, 0:128], start=True, stop=True), reads=[btkd, btqd], writes=[bU])
                    yield
                    c.op("pe", lambda e: e.transpose(pUb, tkt2[:, 0:128], idb[:]), reads=[btkt2, bidb], writes=[bU])
                    yield
                    c.op("dve", lambda e: e.tensor_tensor(out=tat[:, 0:128], in0=pU[:, 128:256], in1=MASK, op=ALU.mult), reads=[bU, bcm], writes=[btat])
                    c.op("dve", lambda e: e.tensor_copy(out=tktm[:, 0:128], in_=pUb), reads=[bU], writes=[btktm])
                    yield
                    corder = (0, 1) if d == 0 else (1, 0)
                    for ci_, cc in enumerate(corder):
                        pb_ = cc * 64
                        ucol = 384 if ci_ == 0 else 0
                        c.op("pe", lambda e: e.matmul(pU[:, ucol:ucol + 128], lhsT=tktm[pb_:pb_ + 64, 0:128], rhs=VTMb[pb_:pb_ + 64, tsl], start=True, stop=True),
                             reads=[btktm, bVTM[i]], writes=[bU], serial=(ci_ == 1))
                        if ci_ == 0:
                            yield
                    yield
                    while ch["turn"] != kidx:
                        yield
                    c.op("pe", lambda e: e.matmul(pU[:, 256:384], lhsT=VTMb[:, tsl], rhs=tat[:, 0:128], start=True, stop=False), reads=[bVTM[i], btat], writes=[bU])
                    for ci_, cc in enumerate(corder):
                        pb_ = cc * 64
                        S, bSc = ch["S"], ch["bS"]
                        Sb_, bSb_ = ch["Sb"], ch["bSb"]
                        ucol = 384 if ci_ == 0 else 0
                        c.op("pe", lambda e: e.matmul(pU[:, 256 + pb_:256 + pb_ + 64], lhsT=Sb_[:], rhs=tqd[:, pb_:pb_ + 64], start=False, stop=(ci_ == 1)),
                             reads=[bSb_, btqd], writes=[bU])
                        ch["si"] += 1
                        Sn, bSn = ch["bufs"][ch["si"] % 3]
                        Sbn, bSbn = ch["bbufs"][ch["si"] % 3]
                        c.op("dve", lambda e: e.scalar_tensor_tensor(out=Sn[:], in0=S[:], scalar=tdc[:, cc:cc + 1], in1=pU[:, ucol:ucol + 128],
                                                                     op0=ALU.mult, op1=ALU.add), reads=[bSc, btdc, bU], writes=[bSn])
                        c.op("pool", lambda e: e.tensor_copy(out=Sbn[:], in_=Sn[:]), reads=[bSn], writes=[bSbn])
                        ch["S"], ch["bS"] = Sn, bSn
                        ch["Sb"], ch["bSb"] = Sbn, bSbn
                        yield
                    if ch["last"] == i and ch["out"] is not None:
                        c.dma("sp", ch["out"], ch["S"][:], reads=[ch["bS"]], writes=[Buf()])
                    ch["turn"] += 1
                    if not claimed[i]:
                        claimed[i] = True
                        c.op("act", lambda e: e.activation(out=OF[:, tsl], in_=pU[:, 256:384], func=AF.Copy), reads=[bU], writes=[bOF[i]])
                        done[i] = True
                        return
                    while not done[i]:
                        yield
                    c.op("dve", lambda e: e.tensor_tensor(out=tos[:, 0:128], in0=pU[:, 256:384], in1=OF[:, tsl], op=ALU.add), reads=[bU, bOF[i]], writes=[btos])
                    yield
                    c.op("act", lambda e: e.activation(out=tsq[:, 0:128], in_=tos[:, 0:128], func=AF.Square), reads=[btos], writes=[btsq])
                    yield
                    c.op("pe", lambda e: e.matmul(pU[:, 128:256], lhsT=onesb[:], rhs=tsq[:, 0:128], start=True, stop=True), reads=[btsq, bonesb], writes=[bU])
                    yield
                    rstd_from(pU[:, 128:256], tos[:, 128:256], 1.0 / 128, [bU], [btos])
                    yield
                    c.op("dve", lambda e: e.tensor_tensor(out=tos[:, 0:128], in0=tos[:, 0:128], in1=tos[:, 128:256], op=ALU.mult), reads=[btos], writes=[btos])
                    yield
                    c.op("dve", lambda e: e.scalar_tensor_tensor(out=ZT[:, tsl], in0=tos[:, 0:128], scalar=onw[:, h:h + 1], in1=ZT[:, tsl],
                                                                 op0=ALU.mult, op1=ALU.mult), reads=[btos, bonw, bZT[i]], writes=[bZT[i]])

                def hgrn_scan(h):
                    claimed = [False] * NTI
                    done = [False] * NTI
                    per_dir = []
                    ci = 0
                    for d in range(2):
                        pend = []
                        for si_, (t0, t1) in enumerate(seqs):
                            bl = [(Sbuf[ci * 3 + i_], bS[ci * 3 + i_]) for i_ in range(3)]
                            ci += 1
                            bbl = [(Sbf[(ci - 1) * 3 + i_], bSbf[(ci - 1) * 3 + i_]) for i_ in range(3)]
                            ch = dict(S=bl[0][0], bS=bl[0][1], Sb=bbl[0][0], bSb=bbl[0][1], si=0, turn=0, bufs=bl, bbufs=bbl, last=(t1 - 1 if d == 0 else t0),
                                      out=(None if t0 == 0 else ns_a[(t0 - 32) // 2, d, h]))
                            if t0 == 0:
                                c.dma("sp", ch["S"][:], st_a[d, h], writes=[ch["bS"]])
                            else:
                                c.op("pool", lambda e: e.memset(ch["S"][:], 0.0), writes=[ch["bS"]])
                            c.op("act", lambda e: e.activation(out=ch["Sb"][:], in_=ch["S"][:], func=AF.Copy), reads=[ch["bS"]], writes=[ch["bSb"]])
                            tl = list(range(t0, t1)) if d == 0 else list(range(t1 - 1, t0 - 1, -1))
                            for kidx, i in enumerate(tl):
                                pend.append((d, i, ch, kidx))
                        per_dir.append(pend[:4] + pend[32:] + pend[4:32])
                    pending = []
                    for a_, b_ in zip(per_dir[0], per_dir[1]):
                        pending.append(a_); pending.append(b_)
                    active = []
                    free = list(range(WUA))
                    while pending or active:
                        while pending and free:
                            d, i, ch, kidx = pending.pop(0)
                            sl = free.pop(0)
                            active.append((sl, hgrn_unit(h, d, i, sl, ch, kidx, claimed, done)))
                        nxt_active = []
                        for sl, g in active:
                            try:
                                next(g)
                                nxt_active.append((sl, g))
                            except StopIteration:
                                free.append(sl)
                        active = nxt_active

                QT, VTM, FF, FB, GA = SLOT
                bQT, bVTM, bFF, bFB, bGA = SLB
                VTM3 = VTM[:].rearrange("p (t v) -> p t v", v=128)
                for h in range(4):
                    for k in range(8):
                        c.dma("pool", wbf[:, k, 0:640], win_a[h, k * 128:(k + 1) * 128, :], writes=[bwbf])
                    for blk in range(NPB):
                        ht, bh = load_hblk(blk)
                        bsl = slice(blk * PBK, (blk + 1) * PBK)
                        tb = list(range(blk * 2, blk * 2 + 2))
                        pbk = (blk % 2) * 4 if False else 0
                        pb, bpb = proj_fm(ht, bh, 0, 0)
                        c.op("act", lambda e: e.activation(out=QT[:, bsl], in_=pb, func=AF.Silu), reads=[bpb], writes=[bQT[t] for t in tb])
                        pb, bpb = proj_fm(ht, bh, 128, 1)
                        c.op("act", lambda e: e.activation(out=FF[:, bsl], in_=pb, func=AF.Sigmoid), reads=[bpb], writes=[bFF[t] for t in tb])
                        pb, bpb = proj_fm(ht, bh, 256, 2)
                        c.op("act", lambda e: e.activation(out=FB[:, bsl], in_=pb, func=AF.Sigmoid), reads=[bpb], writes=[bFB[t] for t in tb])
                        pb, bpb = proj_fm(ht, bh, 512, 3)
                        c.op("act", lambda e: e.activation(out=ZT[:, bsl], in_=pb, func=AF.Silu), reads=[bpb], writes=[bZT[t] for t in tb])
                        pb = PB[4 + blk % 2]; bpb = PBb[4 + blk % 2]
                        for tt in range(2):
                            for k in range(8):
                                c.op("pe", lambda e: e.matmul(pb[:, tt * 128:(tt + 1) * 128], lhsT=ht[:, k, tt * 128:(tt + 1) * 128], rhs=wbf[:, k, 384:512],
                                                              start=(k == 0), stop=(k == 7)), reads=[bwbf, bh], writes=[bpb])
                        c.op("dve", lambda e: e.tensor_copy(out=VTMb[:, bsl], in_=pb[:, 0:PBK]), reads=[bpb], writes=[bVTM[t] for t in tb])
                    _chk('a1')
                    if h == 0:
                        issue_bg_precast()
                    GAs = ((GA, bGA), (GA1, bGA1))
                    for d in range(2):
                        FD = FF if d == 0 else FB
                        bFD = bFF if d == 0 else bFB
                        GAd, bGAd = GAs[d]
                        lbc = lbt[:, d * 4 + h:d * 4 + h + 1]
                        omc = lbt[:, 8 + d * 4 + h:8 + d * 4 + h + 1]
                        for blk in range(NBLK):
                            bsl = slice(blk * 512, (blk + 1) * 512)
                            tb = list(range(blk * 4, blk * 4 + 4))
                            fbufs = [bFD[t] for t in tb]
                            gbufs = [bGAd[t] for t in tb]
                            c.op("dve", lambda e: e.tensor_scalar(out=FD[:, bsl], in0=FD[:, bsl], scalar1=omc, scalar2=lbc, op0=ALU.mult, op1=ALU.add),
                                 reads=fbufs + [blb], writes=fbufs)
                            tw = tmp512[blk % 2]; btw = btmp512[blk % 2]
                            c.op("act", lambda e: e.activation(out=tw[:], in_=FD[:, bsl], func=AF.Ln), reads=fbufs, writes=[btw])
                            if d == 0:
                                c.op("dve", lambda e: e.tensor_tensor_scan(out=GAd[:, bsl], data0=seg[:], data1=tw[:], initial=0.0, op0=ALU.mult, op1=ALU.add),
                                     reads=[btw, bseg], writes=gbufs)
                            else:
                                c.op("dve", lambda e: e.tensor_tensor_scan(out=GAd[:, bsl][:, ::-1], data0=seg[:], data1=tw[:][:, ::-1], initial=0.0,
                                                                           op0=ALU.mult, op1=ALU.add), reads=[btw, bseg], writes=gbufs)
                            c.op("pool", lambda e: e.tensor_scalar(out=FD[:, bsl], in0=FD[:, bsl], scalar1=-1.0, scalar2=1.0, op0=ALU.mult, op1=ALU.add),
                                 reads=fbufs, writes=fbufs)
                    hgrn_scan(h)
                    dump('OF', OF[:], bOF)
                    dump('ZTa', ZT[:], bZT)
                    c.dma("pool", oT_scr[h], ZT[:], reads=bZT, writes=[boT[h]])
                    _chk('a4')

                s2a.close()
                c.barrier()
                _chk('p2a')
                RAW, QN, KN, VC, VT2 = SLOT
                bRAW, bQN, bKN, bVC, bVT2 = SLB
                KTM = RAW; bKTM = bRAW
                OTM = VC; bOTM = bVC
                KTM3 = KTM[:].rearrange("p (t v) -> p t v", v=128)
                VT23 = VT2[:].rearrange("p (t v) -> p t v", v=128)
                OTM3 = OTM[:].rearrange("p (t v) -> p t v", v=128)

                def scanview(arr, j):
                    return arr[:, j * 128:(j + 1) * 128]

                def scantiles(j):
                    return [j]

                def zview(j):
                    if j < 32:
                        return ZT[:, 0:4096].rearrange("p (r w) -> p w r", w=64)[:, 2 * j:2 * j + 2, :]
                    return ZT[:, j * 128:(j + 1) * 128]

                def ztiles(j):
                    return list(range(32)) if j < 32 else [j]

                WU = 6
                TNAMES = ("tkb", "tr", "te", "tdm", "tdm2", "tat", "tul", "X", "tkk", "tvb", "twu", "tqb", "Xb", "twb", "tvnb")
                TW = {"tdm2": 128, "tat": 128, "tvb": 128, "twu": 128, "tqb": 128, "Xb": 128, "twb": 128, "tvnb": 128}
                TBF = ("tat", "tkk", "tvb", "tqb", "Xb", "twb", "tvnb")
                s2b = ExitStack()
                tsets = []
                for w_ in range(WU):
                    tsets.append({n: (SB(s2b, "u%d_%s" % (w_, n), [128, TW.get(n, 256)], BF16 if n in TBF else F32), Buf()) for n in TNAMES})
                Sbf2 = [SB(s2b, "Sbg%d" % i, [128, 128], BF16) for i in range(12)]; bSbf2 = bufs(12)
                smalls = [SB(s2b, "usm%d" % w_, [128, 4]) for w_ in range(WU)]; bsmalls = bufs(WU)

                def gdn_unit(h, d, j, slot, ch, done, claimed, kidx):
                    hg = 4 + h
                    ts = tsets[slot]
                    gs = lambda nm: gsm[nm][:, d, j, h:h + 1]
                    MA, MB = (UT, SLO) if d == 0 else (LT, SUP)
                    CMK, SMK, SMK2 = (UT, SUP, SLO) if d == 0 else (LT, SLO, SUP)
                    pU, bU = PB[slot], PBb[slot]
                    kT = scanview(KN, j); qT = scanview(QN, j)
                    rdK = [bKN[j]]; rdQ = [bQN[j]]
                    tkb, btkb = ts["tkb"]; tr, btr = ts["tr"]; te, bte = ts["te"]; tdm, btdm = ts["tdm"]; tdm2, btdm2 = ts["tdm2"]
                    tat, btat = ts["tat"]; tul, btul = ts["tul"]; X, bX = ts["X"]; tkk, btkk = ts["tkk"]; tvb, btvb = ts["tvb"]; twu, btwu = ts["twu"]
                    tqb, btqb = ts["tqb"]; Xb, bXb = ts["Xb"]; twb, btwb = ts["twb"]; tvnb, btvnb = ts["tvnb"]
                    c.op("pool", lambda e: e.tensor_copy(out=tqb[:], in_=qT), reads=rdQ, writes=[btqb])
                    c.op("dve", lambda e: e.tensor_scalar(out=tkb[:, 0:128], in0=KTM3[:, j, :], scalar1=gs("beta"), scalar2=None, op0=ALU.mult),
                         reads=[bKTM[j], bgs], writes=[btkb])
                    c.op("act", lambda e: e.activation(out=tr[:, 0:128], in_=MA, func=AF.Copy, scale=gs("gg")), reads=[bcm, bgs], writes=[btr])
                    yield
                    c.op("pe", lambda e: e.transpose(pU[:, 384:512], tkb[:, 0:128], IDN), reads=[btkb, bcm], writes=[bU])
                    c.op("pe", lambda e: e.matmul(pU[:, 0:128], lhsT=MB, rhs=tr[:, 0:128], start=True, stop=True), reads=[btr, bcm], writes=[bU])
                    yield
                    c.op("act", lambda e: e.activation(out=tkb[:, 128:256], in_=pU[:, 384:512], func=AF.Copy), reads=[bU], writes=[btkb])
                    kbT = tkb[:, 128:256]
                    c.op("act", lambda e: e.activation(out=te[:, 0:128], in_=pU[:, 0:128], func=AF.Exp), reads=[bU], writes=[bte])
                    yield
                    c.op("pe", lambda e: e.matmul(pU[:, 0:128], lhsT=kT, rhs=qT, start=True, stop=True), reads=rdK + rdQ, writes=[bU])
                    c.op("pe", lambda e: e.matmul(pU[:, 128:256], lhsT=kT, rhs=kbT, start=True, stop=True), reads=rdK + [btkb], writes=[bU])
                    c.op("pool", lambda e: e.tensor_tensor(out=tdm[:, 0:128], in0=te[:, 0:128], in1=CMK, op=ALU.mult), reads=[bte, bcm], writes=[btdm])
                    c.op("pool", lambda e: e.tensor_tensor(out=tdm[:, 128:256], in0=te[:, 0:128], in1=SMK, op=ALU.mult), reads=[bte, bcm], writes=[btdm])
                    yield
                    c.op("dve", lambda e: e.tensor_tensor(out=tat[:, 0:128], in0=pU[:, 0:128], in1=tdm[:, 0:128], op=ALU.mult), reads=[bU, btdm], writes=[btat])
                    c.op("dve", lambda e: e.tensor_tensor(out=tul[:, 0:128], in0=pU[:, 128:256], in1=tdm[:, 128:256], op=ALU.mult), reads=[bU, btdm], writes=[btul])
                    yield
                    c.op("pe", lambda e: e.transpose(pU[:, 256:384], tul[:, 0:128], IDN), reads=[btul, bcm], writes=[bU])
                    yield
                    c.op("act", lambda e: e.activation(out=tul[:, 128:256], in_=pU[:, 256:384], func=AF.Copy), reads=[bU], writes=[btul])
                    yield
                    c.op("dve", lambda e: e.tensor_tensor(out=X[:, 0:128], in0=IDN, in1=tul[:, 0:128], op=ALU.subtract), reads=[bcm, btul], writes=[bX])
                    cur, bcur = tul, btul
                    xs = 0
                    for lev in range(5):
                        nxt, bnxt = ts["te"] if lev % 2 == 0 else ts["tr"]
                        if lev < 4:
                            c.op("pe", lambda e: e.matmul(pU[:, 0:128], lhsT=cur[:, 128:256], rhs=cur[:, 0:128], start=True, stop=True), reads=[bcur], writes=[bU])
                        c.op("pe", lambda e: e.matmul(pU[:, 128:256], lhsT=cur[:, 0:128], rhs=cur[:, 128:256], start=True, stop=True), reads=[bcur], writes=[bU])
                        yield
                        if lev < 4:
                            c.op("act", lambda e: e.activation(out=nxt[:, :], in_=pU[:, 0:256], func=AF.Copy), reads=[bU], writes=[bnxt])
                        else:
                            c.op("act", lambda e: e.activation(out=nxt[:, 128:256], in_=pU[:, 128:256], func=AF.Copy), reads=[bU], writes=[bnxt])
                        yield
                        c.op("pe", lambda e: e.matmul(pU[:, 256:384], lhsT=nxt[:, 128:256], rhs=X[:, xs:xs + 128], start=True, stop=True), reads=[bnxt, bX], writes=[bU])
                        yield
                        c.op("dve", lambda e: e.tensor_tensor(out=X[:, 128 - xs:256 - xs], in0=X[:, xs:xs + 128], in1=pU[:, 256:384], op=ALU.add), reads=[bU, bX], writes=[bX])
                        xs = 128 - xs
                        cur, bcur = nxt, bnxt
                        yield
                    XT = X[:, xs:xs + 128]
                    c.op("act", lambda e: e.activation(out=Xb[:], in_=XT, func=AF.Copy), reads=[bX], writes=[bXb])
                    c.op("act", lambda e: e.activation(out=tkk[:, 0:128], in_=KTM3[:, j, :], func=AF.Copy, scale=gs("beg")), reads=[bKTM[j], bgs], writes=[btkk])
                    c.op("pool", lambda e: e.tensor_scalar(out=tkk[:, 128:256], in0=KTM3[:, j, :], scalar1=gs("ekt"), scalar2=None, op0=ALU.mult), reads=[bKTM[j], bgs], writes=[btkk])
                    c.op("act", lambda e: e.activation(out=tvb[:, 0:128], in_=VT23[:, j, :], func=AF.Copy, scale=gs("beta")), reads=[bVT2[j], bgs], writes=[btvb])
                    yield
                    c.op("pe", lambda e: e.matmul(pU[:, 0:128], lhsT=tkk[:, 0:128], rhs=Xb[:], start=True, stop=True), reads=[btkk, bXb], writes=[bU])
                    c.op("pe", lambda e: e.matmul(pU[:, 128:256], lhsT=Xb[:], rhs=tvb[:, 0:128], start=True, stop=True), reads=[btvb, bXb], writes=[bU])
                    yield
                    c.op("act", lambda e: e.activation(out=twb[:], in_=pU[:, 0:128], func=AF.Copy), reads=[bU], writes=[btwb])
                    c.op("act", lambda e: e.activation(out=twu[:], in_=pU[:, 128:256], func=AF.Copy), reads=[bU], writes=[btwu])
                    yield
                    while ch["turn"] != kidx:
                        yield
                    if not claimed[j]:
                        claimed[j] = True
                        first = True
                    else:
                        first = False
                        while not done[j]:
                            yield
                    tvns = (ts["tkb"], ts["tdm"])
                    corder = (0, 1) if d == 0 else (1, 0)
                    for ci_, cc in enumerate(corder):
                        pr = slice(cc * 64, cc * 64 + 64)
                        tvn, btvn = tvns[ci_]
                        pR, bR = pU, bU
                        S, bSc = ch["S"], ch["bS"]
                        Sb_, bSb_ = ch["Sb"], ch["bSb"]
                        c.op("pe", lambda e: e.matmul(pR[:, 0:128], lhsT=twb[:], rhs=Sb_[:], start=True, stop=True), reads=[btwb, bSb_], writes=[bR])
                        c.op("pe", lambda e: e.matmul(pR[:, 128:256], lhsT=tqb[:], rhs=Sb_[:], start=True, stop=True), reads=[btqb, bSb_], writes=[bR])
                        yield
                        c.op("dve", lambda e: e.tensor_tensor(out=tvnb[pr, :], in0=twu[pr, :], in1=pR[pr, 0:128], op=ALU.subtract), reads=[btwu, bR], writes=[btvnb])
                        yield
                        c.op("pe", lambda e: e.matmul(pR[:, 256:384], lhsT=tat[pr, 0:128], rhs=tvnb[pr, :], start=True, stop=True), reads=[btat, btvnb], writes=[bR])
                        c.op("pe", lambda e: e.matmul(pR[:, 384:512], lhsT=tkk[pr, 128:256], rhs=tvnb[pr, :], start=True, stop=True), reads=[btkk, btvnb], writes=[bR])
                        yield
                        ch["si"] += 1
                        Sn, bSn = ch["bufs"][ch["si"] % 3]
                        Sbn, bSbn = ch["bbufs"][ch["si"] % 2]
                        c.op("dve", lambda e: e.scalar_tensor_tensor(out=Sn[:], in0=S[:], scalar=gs("dec%d" % cc), in1=pR[:, 384:512], op0=ALU.mult, op1=ALU.add),
                             reads=[bSc, bgs, bR], writes=[bSn])
                        c.op("act", lambda e: e.activation(out=Sbn[:], in_=Sn[:], func=AF.Copy), reads=[bSn], writes=[bSbn])
                        ch["S"], ch["bS"] = Sn, bSn
                        ch["Sb"], ch["bSb"] = Sbn, bSbn
                        c.op("dve", lambda e: e.tensor_scalar(out=tvn[pr, 128:256], in0=pR[pr, 128:256], scalar1=gsm["eg"][pr, d, j, h:h + 1], scalar2=None, op0=ALU.mult),
                             reads=[bR, bgs], writes=[btvn])
                        c.op("dve", lambda e: e.tensor_tensor(out=tvn[pr, 128:256], in0=tvn[pr, 128:256], in1=pR[pr, 256:384], op=ALU.add), reads=[btvn, bR], writes=[btvn])
                        if first:
                            c.op("act", lambda e: e.activation(out=OTM3[pr, j, :], in_=tvn[pr, 128:256], func=AF.Copy), reads=[btvn], writes=[bOTM[j]])
                        else:
                            c.op("pool", lambda e: e.tensor_tensor(out=OTM3[pr, j, :], in0=OTM3[pr, j, :], in1=tvn[pr, 128:256], op=ALU.add), reads=[btvn, bOTM[j]], writes=[bOTM[j]])
                        yield
                    if ch["last"] == j and ch["out"] is not None:
                        c.dma("sp", ch["out"], ch["S"][:], reads=[ch["bS"]], writes=[Buf()])
                    ch["turn"] += 1
                    if first:
                        done[j] = True
                        return
                    tn, btn = ts["te"]
                    sm = smalls[slot]; bsm = bsmalls[slot]
                    c.op("pool", lambda e: e.memset(sm[:, 0:1], 0.0), writes=[bsm])
                    c.op("act", lambda e: e.activation(out=tn[:, 0:128], in_=OTM3[:, j, :], func=AF.Square, accum_out=sm[:, 0:1]), reads=[bOTM[j]], writes=[btn, bsm])
                    yield
                    rstd_from(sm[:, 0:1], sm[:, 1:2], 1.0 / 128, [bsm], [bsm])
                    yield
                    c.op("dve", lambda e: e.tensor_scalar(out=tn[:, 128:256], in0=OTM3[:, j, :], scalar1=sm[:, 1:2], scalar2=None, op0=ALU.mult), reads=[bOTM[j], bsm], writes=[btn])
                    yield
                    c.op("pe", lambda e: e.transpose(pU[:, 256:384], tn[:, 128:256], IDN), reads=[btn, bcm], writes=[bU])
                    yield
                    zv = zview(j)
                    zb = [bZT[t] for t in ztiles(j)]
                    pin = pU[:, 256:384].rearrange("p (c r) -> p c r", c=2) if j < 32 else pU[:, 256:384]
                    c.op("dve", lambda e: e.scalar_tensor_tensor(out=zv, in0=pin, scalar=onw[:, hg:hg + 1], in1=zv, op0=ALU.mult, op1=ALU.mult),
                         reads=[bU, bonw] + zb, writes=zb)

                def gdn_scan(h):
                    done = [False] * NTI
                    claimed = [False] * NTI
                    chains = {}
                    order = []
                    ci = 0
                    for si_, (t0, t1) in enumerate(seqs):
                        for d in range(2):
                            bl = [(Sbuf[ci * 3 + i], bS[ci * 3 + i]) for i in range(3)]
                            bbl = [(Sbf2[ci * 2 + i], bSbf2[ci * 2 + i]) for i in range(2)]
                            ch = dict(S=bl[0][0], bS=bl[0][1], Sb=bbl[0][0], bSb=bbl[0][1], si=0, turn=0, t0=t0, t1=t1, bufs=bl, bbufs=bbl,
                                      last=(t1 - 1 if d == 0 else t0), out=(None if t0 == 0 else ns_b[(t0 - 32) // 2, d, h]))
                            if t0 == 0:
                                c.dma("sp", ch["S"][:], st_b[d, h], writes=[ch["bS"]])
                            else:
                                c.op("pool", lambda e: e.memset(ch["S"][:], 0.0), writes=[ch["bS"]])
                            c.op("act", lambda e: e.activation(out=ch["Sb"][:], in_=ch["S"][:], func=AF.Copy), reads=[ch["bS"]], writes=[ch["bSb"]])
                            chains[(si_, d)] = ch
                            ci += 1
                    samp = []
                    for i in range(32):
                        samp.append((0, 0, i)); samp.append((0, 1, 31 - i))
                    pro = []
                    for si_ in (1, 2):
                        t0, t1 = seqs[si_]
                        for i in range(t1 - t0):
                            pro.append((si_, 0, t0 + i)); pro.append((si_, 1, t1 - 1 - i))
                    order = samp[:8] + pro + samp[8:]
                    pending = list(order)
                    active = []
                    free = list(range(WU))
                    while pending or active:
                        while pending and free:
                            si_, d, j = pending.pop(0)
                            sl = free.pop(0)
                            ch_ = chains[(si_, d)]
                            kidx = (j - ch_["t0"]) if d == 0 else (ch_["t1"] - 1 - j)
                            active.append((sl, gdn_unit(h, d, j, sl, ch_, done, claimed, kidx)))
                        nxt_active = []
                        for sl, g in active:
                            try:
                                next(g)
                                nxt_active.append((sl, g))
                            except StopIteration:
                                free.append(sl)
                        active = nxt_active

                for h in range(4):
                    hg = 4 + h
                    for k in range(8):
                        c.dma("pool", wbf[:, k, 0:512], win_b[h, k * 128:(k + 1) * 128, :], writes=[bwbf])
                    for blk in range(NPB):
                        ht, bh = load_hblk(blk)
                        bsl = slice(blk * PBK, (blk + 1) * PBK)
                        tb = list(range(blk * 2, blk * 2 + 2))
                        for qi, (DST, bDST) in enumerate(((QN, bQN), (KN, bKN), (VC, bVC))):
                            pb, bpb = proj_fm(ht, bh, qi * 128, qi)
                            if qi == 1:
                                c.op("dve", lambda e: e.tensor_copy(out=DST[:, bsl], in_=pb), reads=[bpb], writes=[bDST[t] for t in tb])
                            else:
                                c.op("act", lambda e: e.activation(out=DST[:, bsl], in_=pb, func=AF.Copy), reads=[bpb], writes=[bDST[t] for t in tb])
                        pb, bpb = proj_fm(ht, bh, 384, 3)
                        c.op("act", lambda e: e.activation(out=ZT[:, bsl], in_=pb, func=AF.Silu), reads=[bpb], writes=[bZT[t] for t in tb])
                    for qi, (DST, bDST) in enumerate(((QN, bQN), (KN, bKN), (VC, bVC))):
                        w0 = cw[:, h * 9 + qi * 3 + 0:h * 9 + qi * 3 + 1]
                        w1 = cw[:, h * 9 + qi * 3 + 1:h * 9 + qi * 3 + 2]
                        w2 = cw[:, h * 9 + qi * 3 + 2:h * 9 + qi * 3 + 3]
                        ceng = "dve"
                        RAWq, bRAWq = (RAW, bRAW) if qi != 1 else (VT2, bVT2)
                        c.op("act", lambda e: e.activation(out=RAWq[:, :], in_=DST[:, :], func=AF.Copy, scale=w1), reads=bDST + [bcw], writes=bRAWq)
                        for (lo, hi, sh) in ((0, 4096, 64), (4096, 4352, 1), (4352, 4608, 1)):
                            c.op(ceng, lambda e: e.scalar_tensor_tensor(out=RAWq[:, lo + sh:hi], in0=DST[:, lo:hi - sh], scalar=w0, in1=RAWq[:, lo + sh:hi],
                                                                         op0=ALU.mult, op1=ALU.add), reads=bDST + [bcw], writes=bRAWq)
                            c.op(ceng, lambda e: e.scalar_tensor_tensor(out=RAWq[:, lo:hi - sh], in0=DST[:, lo + sh:hi], scalar=w2, in1=RAWq[:, lo:hi - sh],
                                                                         op0=ALU.mult, op1=ALU.add), reads=bDST + [bcw], writes=bRAWq)
                        c.op("act", lambda e: e.activation(out=DST[:, 0:4096].rearrange("p (w r) -> p w r", r=64),
                                                           in_=RAWq[:, 0:4096].rearrange("p (r w) -> p w r", w=64), func=AF.Silu), reads=bRAWq[0:32], writes=bDST[0:32])
                        c.op("act", lambda e: e.activation(out=DST[:, 4096:NT], in_=RAWq[:, 4096:NT], func=AF.Silu), reads=bRAWq[32:], writes=bDST[32:])
                        if qi == 2:
                            for j in range(NTI):
                                p5 = PB[4 + j % 2]; b5 = PBb[4 + j % 2]
                                c.op("pe", lambda e: e.transpose(p5[:, 128:256], DST[:, j * 128:(j + 1) * 128], IDN), reads=[bDST[j], bcm], writes=[b5])
                                c.op("dve", lambda e: e.tensor_copy(out=VT23[:, j, :], in_=p5[:, 128:256]), reads=[b5], writes=[bVT2[j]])
                            continue
                        for blk in range(NBLK):
                            bsl = slice(blk * 512, (blk + 1) * 512)
                            db = [bDST[t] for t in range(blk * 4, blk * 4 + 4)]
                            tw = tmp512[blk % 2]; btw = btmp512[blk % 2]
                            c.op("act", lambda e: e.activation(out=tw[:], in_=DST[:, bsl], func=AF.Square), reads=db, writes=[btw])
                            pb = PB[4 + blk % 2]; bpb = PBb[4 + blk % 2]
                            c.op("pe", lambda e: e.matmul(pb[:, :], lhsT=ONES, rhs=tw[:], start=True, stop=True), reads=[btw, bcm], writes=[bpb])
                            rstd_from(pb[:, :], tw[:], 1.0, [bpb], [btw])
                            if qi == 0:
                                c.op("dve", lambda e: e.scalar_tensor_tensor(out=DST[:, bsl], in0=DST[:, bsl], scalar=float(128 ** -0.5), in1=tw[:],
                                                                             op0=ALU.mult, op1=ALU.mult), reads=db + [btw], writes=db)
                            else:
                                c.op("dve", lambda e: e.tensor_tensor(out=DST[:, bsl], in0=DST[:, bsl], in1=tw[:], op=ALU.mult), reads=db + [btw], writes=db)
                    for j in range(NTI):
                        p5 = PB[4 + j % 2]; b5 = PBb[4 + j % 2]
                        c.op("pe", lambda e: e.transpose(p5[:, 0:128], scanview(KN, j), IDN), reads=[bKN[j], bcm], writes=[b5])
                        c.op("act", lambda e: e.activation(out=KTM3[:, j, :], in_=p5[:, 0:128], func=AF.Copy), reads=[b5], writes=[bKTM[j]])
                    gdn_scan(h)
                    c.dma("pool", oT_scr[hg], ZT[:], reads=bZT, writes=[boT[hg]])

                s2b.close()
            PB.pop(); PBb.pop()
            c.barrier()
            _chk('p2b')
            with ExitStack() as s4:
                PTh[0] = s4.enter_context(nc.psum_tensor("pbt4", [128, 1024], BF16)); PTh[1] = Buf(True)
                wo = SB(s4, "wo", [128, 8, D], BF16); bwo = Buf()
                wg = SB(s4, "wg", [128, 8, DFF], BF16); bwg = Buf()
                wu = SB(s4, "wu", [128, 8, DFF], BF16); bwu = Buf()
                wd = SB(s4, "wd", [128, NF, D], BF16); bwd = Buf()
                for k in range(8):
                    c.dma("pool", wo[:, k, :], wo_bf[k * 128:(k + 1) * 128, :], reads=[bWo[k]], writes=[bwo])
                for k in range(8):
                    c.dma("pool", wg[:, k, :], wg_bf[k * 128:(k + 1) * 128, :], reads=[bWg[k]], writes=[bwg])
                    c.dma("pool", wu[:, k, :], wu_bf[k * 128:(k + 1) * 128, :], reads=[bWu[k]], writes=[bwu])
                for f in range(NF):
                    c.dma("pool", wd[:, f, :], wd_bf[f * 128:(f + 1) * 128, :], reads=[bWd[f]], writes=[bwd])
                gbc = SB(s4, "gbc", [128, 4, D]); bgbc = Buf()
                nfb = SB(s4, "nfb", [128, D]); bnfb = Buf()
                c.dma("sp", nfb[:], normf.partition_broadcast(128), writes=[bnfb])
                with ExitStack() as s40:
                    sv = SB(s40, "sv4", [128, 16]); bsv = Buf()
                    c.dma("sp", sv[:], svec, writes=[bsv])
                    c.op("act", lambda e: e.activation(out=sv[:], in_=sv[:], func=AF.Silu), reads=[bsv], writes=[bsv])
                    srep = SB(s40, "srep", [128, 16, 128]); bsrep = Buf()
                    c.op("dve", lambda e: e.tensor_copy(out=srep[:], in_=sv[:].unsqueeze(2).broadcast_to([128, 16, 128])), reads=[bsv], writes=[bsrep])
                    brow = SB(s40, "brow", [128, 512]); bbrow = Buf()
                    wad = [SB(s40, "wad40", [128, 8, 512])] * 2; bwad = [Buf()] * 2
                    wv = w_ada.rearrange("(k p) n -> p k n", p=128)
                    ci = 0
                    for gi, cb0 in enumerate((4, 10)):
                        for hh in range(2):
                            cb = cb0 + hh
                            t = wad[ci % 2]; bt = bwad[ci % 2]; ci += 1
                            c.dma("sp", t[:], wv[:, :, cb * 512:(cb + 1) * 512], writes=[bt])
                            c.dma("sp", brow[:], bada_row[:, cb * 512:(cb + 1) * 512].partition_broadcast(128), writes=[bbrow])
                            for v in range(2):
                                pb = PB[v]; bpb = PBb[v]
                                for k in range(8):
                                    c.op("pe", lambda e: e.matmul(pb[:, :], lhsT=srep[:, v * 8 + k, :], rhs=t[:, k, :], start=(k == 0), stop=(k == 7)), reads=[bsrep, bt], writes=[bpb])
                                c.op("dve", lambda e: e.tensor_tensor(out=gbc[:, gi * 2 + v, hh * 512:(hh + 1) * 512], in0=pb[:, :], in1=brow[:], op=ALU.add),
                                     reads=[bpb, bbrow], writes=[bgbc])
                c.barrier(skip=("dpool",))
                BT = 128
                oTb = [SB(s4, "otb0", [128, 8, BT], BF16)] * 2; boTb = [Buf()] * 2
                xts = [SB(s4, "x4%d" % i, [128, D]) for i in range(2)]; bxts = bufs(2)
                x1s = [SB(s4, "x1%d" % i, [128, D]) for i in range(2)]; bx1s = bufs(2)
                h2s = [SB(s4, "h20", [128, 8, BT], BF16)] * 2; bh2s = [Buf()] * 2
                aT = SB(s4, "aT", [128, NF, BT], BF16); baT = bufs(NF)
                ss = SB(s4, "ss4", [128, 4]); xn = SB(s4, "xn4", [128, D], BF16); junk = SB(s4, "junk4", [128, D], BF16)
                tmpn = (ss, Buf(), xn, Buf(), junk, Buf())
                tsg = [SB(s4, "tsg%d" % i, [128, BT]) for i in range(2)]; btsg = bufs(2)
                byout = Buf()
                nown_blk = NOWN // BT
                nsamp_blk = 2048 // BT
                vof = lambda blk: 1 if blk < nsamp_blk else 0

                def st_op(blk):
                    tok0 = blk * BT if blk < nsamp_blk else 4096 + (blk - nsamp_blk) * BT
                    v = vof(blk)
                    ob = oTb[blk % 2]; bob = boTb[blk % 2]
                    xt = xts[blk % 2]; bx = bxts[blk % 2]; x1 = x1s[blk % 2]; bx1 = bx1s[blk % 2]
                    c.dma("sp", ob[:], oT_scr[:, :, tok0:tok0 + BT].rearrange("h p t -> p h t"), reads=boT, writes=[bob])
                    c.dma("sp", xt[:], xall[tok0:tok0 + 128, :], writes=[bx])
                    for hh in range(2):
                        pb = PB[hh]; bpb = PBb[hh]
                        for hd in range(8):
                            c.op("pe", lambda e: e.matmul(pb[:, :], lhsT=ob[:, hd, :], rhs=wo[:, hd, hh * 512:(hh + 1) * 512],
                                                          start=(hd == 0), stop=(hd == 7)), reads=[bob, bwo], writes=[bpb])
                        hs = slice(hh * 512, (hh + 1) * 512)
                        c.op("dve", lambda e: e.tensor_tensor(out=x1[:, hs], in0=pb[:, :], in1=gbc[:, v, hs], op=ALU.mult), reads=[bpb, bgbc], writes=[bx1])
                        c.op("pool", lambda e: e.tensor_tensor(out=x1[:, hs], in0=x1[:, hs], in1=xt[:, hs], op=ALU.add), reads=[bx1, bx], writes=[bx1])
                    norm_mod_transpose("norm_only", x1[:], bx1, 1, v, None, None, tmpn)

                def st_tr(blk):
                    mod_transpose(1, vof(blk), h2s[blk % 2], bh2s[blk % 2], tmpn)

                def st_gu(blk):
                    h2 = h2s[blk % 2]; bh2 = bh2s[blk % 2]
                    for f in range(NF):
                        pg = PB[2 + f % 2]; bpg = PBb[2 + f % 2]
                        pu = PB[4 + f % 2]; bpu = PBb[4 + f % 2]
                        for k in range(8):
                            c.op("pe", lambda e: e.matmul(pg[:, 0:BT], lhsT=wg[:, k, f * 128:(f + 1) * 128], rhs=h2[:, k, :], start=(k == 0), stop=(k == 7)),
                                 reads=[bwg, bh2], writes=[bpg])
                        for k in range(8):
                            c.op("pe", lambda e: e.matmul(pu[:, 0:BT], lhsT=wu[:, k, f * 128:(f + 1) * 128], rhs=h2[:, k, :], start=(k == 0), stop=(k == 7)),
                                 reads=[bwu, bh2], writes=[bpu])
                        tg = tsg[f % 2]; btg = btsg[f % 2]
                        c.op("act", lambda e: e.activation(out=tg[:], in_=pg[:, 0:BT], func=AF.Silu), reads=[bpg], writes=[btg])
                        c.op("dve", lambda e: e.tensor_tensor(out=aT[:, f, :], in0=tg[:], in1=pu[:, 0:BT], op=ALU.mult), reads=[btg, bpu], writes=[baT[f]])

                def st_down(blk):
                    v = vof(blk)
                    x1 = x1s[blk % 2]; bx1 = bx1s[blk % 2]
                    yo = xts[blk % 2]; byo = bxts[blk % 2]
                    for hh in range(2):
                        pb = PB[hh]; bpb = PBb[hh]
                        for f in range(NF):
                            c.op("pe", lambda e: e.matmul(pb[:, :], lhsT=aT[:, f, :], rhs=wd[:, f, hh * 512:(hh + 1) * 512],
                                                          start=(f == 0), stop=(f == NF - 1)), reads=[baT[f], bwd], writes=[bpb])
                        hs = slice(hh * 512, (hh + 1) * 512)
                        c.op("dve", lambda e: e.tensor_tensor(out=yo[:, hs], in0=pb[:, :], in1=gbc[:, 2 + v, hs], op=ALU.mult), reads=[bpb, bgbc], writes=[byo])
                        c.op("pool", lambda e: e.tensor_tensor(out=yo[:, hs], in0=yo[:, hs], in1=x1[:, hs], op=ALU.add), reads=[byo, bx1], writes=[byo])
                    bs_ = tmpn[1]
                    c.op("pool", lambda e: e.memset(ss[:, 2:3], 0.0), writes=[bs_])
                    c.op("act", lambda e: e.activation(out=junk[:], in_=yo[:], func=AF.Square, accum_out=ss[:, 2:3]), reads=[byo], writes=[tmpn[5], bs_])
                    rstd_from(ss[:, 2:3], ss[:, 3:4], 1.0 / D, [bs_], [bs_])
                    c.op("dve", lambda e: e.scalar_tensor_tensor(out=yo[:], in0=yo[:], scalar=ss[:, 3:4], in1=nfb[:], op0=ALU.mult, op1=ALU.mult),
                         reads=[byo, bs_, bnfb], writes=[byo])
                    r0 = blk * BT
                    c.dma("sp", y_own[r0:r0 + 128, :], yo[:], reads=[byo], writes=[byout])

                st_op(0)
                st_tr(0)
                for blk in range(nown_blk):
                    if blk + 1 < nown_blk:
                        st_op(blk + 1)
                    st_gu(blk)
                    if blk + 1 < nown_blk:
                        st_tr(blk + 1)
                    st_down(blk)
        except _Stop:
            pass
        for q in NDSEM:
            for nm in c.dnames[q]:
                if c.cnt[nm]:
                    nc.sync.wait_ge(c.sem[nm], c.cnt[nm])


_CONST = {}


def _consts():
    if _CONST:
        return _CONST
    u = np.arange(128)[:, None]
    t = np.arange(128)[None, :]
    same = (u // 64) == (t // 64)
    ident = (u == t)
    UT = same & (u <= t)
    SLO = same & (u > t)
    LT = same & (u >= t)
    SUP = same & (u < t)
    ones = np.ones((128, 128), bool)
    cm = np.stack([ident, UT, SLO, LT, SUP, same, ones], axis=1).astype(np.float32).reshape(128, 7 * 128)
    sel = np.stack([np.broadcast_to(u < 64, (128, 128)), np.broadcast_to(u >= 64, (128, 128))], axis=1).astype(np.float32).reshape(128, 256)
    seg = np.ones((128, 512), np.float32)
    seg[:, ::64] = 0.0
    _CONST.update(cmask=np.ascontiguousarray(cm), csel=np.ascontiguousarray(sel), cseg=seg)
    return _CONST


def _fm(vec):
    return np.ascontiguousarray(np.asarray(vec, np.float32).reshape(-1, 128).T)


_PROG = {}


def kernel(x_prompt, x_sample, c, state_hgrn, state_gdn, c_ctx, w_ada, b_ada, norm1, norm2, w_in, conv_w, hgrn_lb,
           gdn_A_log, gdn_dt_bias, hgrn_out_norm, gdn_out_norm, w_out, w_gate, w_up, w_down, norm_f):
    f32 = lambda a: np.ascontiguousarray(np.asarray(a, dtype=np.float32))
    x_prompt, x_sample, c, state_hgrn, state_gdn, c_ctx = map(f32, (x_prompt, x_sample, c, state_hgrn, state_gdn, c_ctx))
    w_ada, b_ada, norm1, norm2, w_in, conv_w, hgrn_lb = map(f32, (w_ada, b_ada, norm1, norm2, w_in, conv_w, hgrn_lb))
    gdn_A_log, gdn_dt_bias, hgrn_out_norm, gdn_out_norm = map(f32, (gdn_A_log, gdn_dt_bias, hgrn_out_norm, gdn_out_norm))
    w_out, w_gate, w_up, w_down, norm_f = map(f32, (w_out, w_gate, w_up, w_down, norm_f))
    if "nc" not in _PROG:
        _PROG["nc"] = build_program()
    nc = _PROG["nc"]
    cst = _consts()
    W = w_in[0]
    offs = np.cumsum([0, 512, 512, 512, 512, 512, 512, 512, 512, 512, 8, 8])
    a_q, a_ff, a_fb, a_i, a_g, b_q, b_k, b_v, b_z, b_beta, b_a = [W[:, offs[i]:offs[i + 1]] for i in range(11)]
    a_f = [a_ff, a_fb]
    in_maps = []
    for core in range(8):
        p, e = core // 2, core % 2
        ds = [0, 1] if e == 0 else [1, 0]
        fl = (lambda a: a[::-1]) if e else (lambda a: a)
        pr = [4 * p + 2 * e, 4 * p + 2 * e + 1]
        xall = np.concatenate([fl(x_sample[p]), fl(x_prompt[pr[0]]), fl(x_prompt[pr[1]])], axis=0)
        hs = lambda a, h: a[:, h * 128:(h + 1) * 128]
        win_a = np.stack([np.concatenate([hs(a_q, h), hs(a_f[ds[0]], h), hs(a_f[ds[1]], h), hs(a_i, h), hs(a_g, h)], axis=1) for h in range(4)])
        win_b = np.stack([np.concatenate([hs(b_q, h), hs(b_k, h), hs(b_v, h), hs(b_z, h)], axis=1) for h in range(4)])
        win_g = np.concatenate([b_beta[:, ds[0] * 4:ds[0] * 4 + 4], b_beta[:, ds[1] * 4:ds[1] * 4 + 4],
                                b_a[:, ds[0] * 4:ds[0] * 4 + 4], b_a[:, ds[1] * 4:ds[1] * 4 + 4]], axis=1)
        hlb = np.stack([np.stack([_fm(hgrn_lb[l, ds[d]]) for d in range(2)], axis=1) for l in range(2)], axis=1)
        cwt = conv_w[0][::-1] if e else conv_w[0]
        convw = np.zeros((128, 4, 3, 3), np.float32)
        for h in range(4):
            for qi in range(3):
                convw[:, h, qi, :] = cwt[:, qi * 512 + h * 128:qi * 512 + (h + 1) * 128].T
        alog = np.broadcast_to(gdn_A_log[0][ds][:, None, :], (2, NTI, 4)).reshape(1, -1)
        dtb = np.broadcast_to(gdn_dt_bias[0][ds][:, None, :], (2, NTI, 4)).reshape(1, -1)
        m = dict(
            xall=np.ascontiguousarray(xall),
            svec=np.ascontiguousarray(np.concatenate([_fm(c_ctx), _fm(c[p])], axis=1)),
            w_ada=w_ada[0], bada_fm=_fm(b_ada[0]), bada_row=b_ada[0][None, :],
            n12_fm=np.ascontiguousarray(np.concatenate([_fm(norm1[0]), _fm(norm2[0])], axis=1)),
            normf=norm_f[None, :],
            win_a=np.ascontiguousarray(win_a), win_b=np.ascontiguousarray(win_b), win_g=np.ascontiguousarray(win_g),
            hlb_fm=np.ascontiguousarray(hlb.reshape(128, 16)),
            convw_fm=np.ascontiguousarray(convw.reshape(128, 36)),
            alog_rep=np.ascontiguousarray(np.broadcast_to(alog, (128, 288))),
            dtb_rep=np.ascontiguousarray(np.broadcast_to(dtb, (128, 288))),
            onorm_fm=np.ascontiguousarray(np.concatenate([_fm(hgrn_out_norm[0]), _fm(gdn_out_norm[0])], axis=1)),
            st_a=np.ascontiguousarray(state_hgrn[p, 0][ds]), st_b=np.ascontiguousarray(state_gdn[p, 0][ds]),
            w_out=w_out[0], w_gate=w_gate[0], w_up=w_up[0], w_down=w_down[0],
            cmask=cst["cmask"], csel=cst["csel"], cseg=cst["cseg"],
        )
        in_maps.append(m)
    if _PROG.get('debug_hook'):
        return _PROG['debug_hook'](nc, in_maps)
    res = run_bass_kernel_spmd(nc, in_maps, core_ids=list(range(8)))
    y_prompt = np.zeros((16, 256, D), np.float32)
    y_sample = np.zeros((4, 4096, D), np.float32)
    nsa = np.zeros((16, 1, 2, 4, 128, 128), np.float32)
    nsb = np.zeros((16, 1, 2, 4, 128, 128), np.float32)
    for core in range(8):
        p, e = core // 2, core % 2
        ds = [0, 1] if e == 0 else [1, 0]
        r = res.results[core]
        yo = np.asarray(r["y_own"], np.float32)
        if e == 0:
            y_sample[p, 0:2048] = yo[0:2048]
        else:
            y_sample[p, 2048:4096] = yo[0:2048][::-1]
        for j in range(2):
            seq = 4 * p + 2 * e + j
            blk = yo[2048 + 256 * j:2048 + 256 * (j + 1)]
            y_prompt[seq] = blk[::-1] if e else blk
            for d in range(2):
                nsa[seq, 0, ds[d]] = np.asarray(r["ns_a"], np.float32)[j, d]
                nsb[seq, 0, ds[d]] = np.asarray(r["ns_b"], np.float32)[j, d]
    return (y_prompt, y_sample, nsa, nsb)
```

```python
import numpy as np
import ml_dtypes
import concourse.bass as bass
import concourse.mybir as mybir
from concourse.bass_utils import run_bass_kernel_spmd
from contextlib import ExitStack

F32 = mybir.dt.float32
BF16 = mybir.dt.bfloat16
AF = mybir.ActivationFunctionType
ALU = mybir.AluOpType

D = 1024
NT = 4608
NTI = 36
NBLK = 9
NOWN = 2560
DFF = 2816
NF = 22
EPS = 1e-6
SAME_ENG_SYNC = True
ATTACH_WAIT = True


class Buf:
    __slots__ = ("lw", "rd", "excl")

    def __init__(self, excl=False):
        self.lw = None
        self.rd = {}
        self.excl = excl


def bufs(n):
    return [Buf() for _ in range(n)]


NDSEM = {"sp": 40, "pool": 16, "bg": 40}
DQ_ENG = {"sp": "sp", "pool": "pool", "bg": "pool"}


class Ctx:
    def __init__(self, nc, es):
        self.nc = nc
        self.eng = {"pe": nc.tensor, "dve": nc.vector, "act": nc.scalar, "pool": nc.gpsimd, "sp": nc.sync}
        self.sem = {}
        self.cnt = {}
        for k in list(self.eng):
            self.sem[k] = es.enter_context(nc.semaphore("s_" + k))
            self.cnt[k] = 0
        self.dnames = {}
        self.drr = {}
        for q, n in NDSEM.items():
            self.dnames[q] = []
            self.drr[q] = 0
            for i in range(n):
                nm = "d%s%d" % (q, i)
                self.sem[nm] = es.enter_context(nc.semaphore("s_" + nm))
                self.cnt[nm] = 0
                self.dnames[q].append(nm)
        self.waited = {k: {} for k in self.eng}
        self.hist = {}

    def _deps(self, en, reads, writes, extra=None):
        deps = {}
        if extra is not None:
            deps[extra[0]] = extra[1]
        for b in reads:
            if b.lw is not None:
                s, v = b.lw
                if deps.get(s, 0) < v:
                    deps[s] = v
            if b.excl:
                for s, v in b.rd.items():
                    if s != en and deps.get(s, 0) < v:
                        deps[s] = v
        for b in writes:
            if b.lw is not None:
                s, v = b.lw
                if deps.get(s, 0) < v:
                    deps[s] = v
            for s, v in b.rd.items():
                if deps.get(s, 0) < v:
                    deps[s] = v
        e = self.eng[en]
        w = self.waited[en]
        need = []
        for s, v in deps.items():
            if v <= 0:
                continue
            if s == en and (en == "pe" or not SAME_ENG_SYNC):
                continue
            if w.get(s, 0) < v:
                need.append((s, v))
                w[s] = v
        for s, v in list(need):
            snap = self.hist.get((s, v))
            if snap:
                for s2, v2 in snap.items():
                    if w.get(s2, 0) < v2:
                        w[s2] = v2
        need = [(s, v) for (s, v) in need if w.get(s, 0) <= v]
        attach = need.pop() if (need and ATTACH_WAIT) else None
        for s, v in need:
            e.wait_ge(self.sem[s], v)
        return attach

    def barrier(self, skip=()):
        for en in self.eng:
            e = self.eng[en]
            w = self.waited[en]
            for s, v in self.cnt.items():
                if any(s.startswith(p) for p in skip):
                    continue
                if v > 0 and s != en and w.get(s, 0) < v:
                    e.wait_ge(self.sem[s], v)
                    w[s] = v

    def op(self, en, fn, reads=(), writes=(), serial=False):
        attach = self._deps(en, reads, writes)
        if serial and self.cnt[en] > self.waited[en].get(en, 0):
            self.eng[en].wait_ge(self.sem[en], self.cnt[en])
            self.waited[en][en] = self.cnt[en]
        ins = fn(self.eng[en])
        if attach is not None:
            ins._wait_ge(self.sem[attach[0]], attach[1])
        ins.then_inc(self.sem[en], 1)
        self.cnt[en] += 1
        c = self.cnt[en]
        self.hist[(en, c)] = dict(self.waited[en])
        for b in reads:
            b.rd[en] = c
        for b in writes:
            b.lw = (en, c)
            b.rd = {}
        return ins

    def dma(self, q, out, in_, reads=(), writes=()):
        i = self.drr[q] % len(self.dnames[q])
        self.drr[q] += 1
        ds = self.dnames[q][i]
        en = DQ_ENG[q]
        attach = self._deps(en, reads, writes, extra=(ds, self.cnt[ds]))
        ins = self.eng[en].dma_start(out=out, in_=in_)
        if attach is not None:
            ins._wait_ge(self.sem[attach[0]], attach[1])
        ins.then_inc(self.sem[ds], 16)
        self.cnt[ds] += 16
        c = self.cnt[ds]
        self.hist[(ds, c)] = dict(self.waited[en])
        for b in reads:
            b.rd[ds] = c
        for b in writes:
            b.lw = (ds, c)
            b.rd = {}
        return ins


class _Stop(Exception):
    pass


STOP = [None]


def _chk(tag):
    if STOP[0] == tag:
        raise _Stop()


def build_program():
    nc = bass.Bass("TRN2", target_bir_lowering=False)
    try:
        _build_body(nc)
    except AssertionError:
        if STOP[0] is None:
            raise
    return nc


DEBUG = [False]
DUMPS = {}


def _build_body(nc):
    DUMPS.clear()
    din = lambda n, s, d=F32: nc.dram_tensor(n, list(s), d, kind="ExternalInput").ap()
    dout = lambda n, s, d=F32: nc.dram_tensor(n, list(s), d, kind="ExternalOutput").ap()
    xall = din("xall", [NT, D])
    svec = din("svec", [128, 16])
    w_ada = din("w_ada", [D, 6 * D])
    bada_fm = din("bada_fm", [128, 48])
    bada_row = din("bada_row", [1, 6 * D])
    n12_fm = din("n12_fm", [128, 16])
    normf = din("normf", [1, D])
    win_a = din("win_a", [4, D, 640])
    win_b = din("win_b", [4, D, 512])
    win_g = din("win_g", [D, 16])
    hlb_fm = din("hlb_fm", [128, 16])
    convw_fm = din("convw_fm", [128, 36])
    alog_rep = din("alog_rep", [128, 288])
    dtb_rep = din("dtb_rep", [128, 288])
    onorm_fm = din("onorm_fm", [128, 8])
    st_a = din("st_a", [2, 4, 128, 128])
    st_b = din("st_b", [2, 4, 128, 128])
    w_out = din("w_out", [D, D])
    w_gate = din("w_gate", [D, DFF])
    w_up = din("w_up", [D, DFF])
    w_down = din("w_down", [DFF, D])
    cmask = din("cmask", [128, 7 * 128])
    csel = din("csel", [128, 256])
    cseg = din("cseg", [128, 512])
    y_own = dout("y_own", [NOWN, D])
    ns_a = dout("ns_a", [2, 2, 4, 128, 128])
    ns_b = dout("ns_b", [2, 2, 4, 128, 128])
    hT_scr = nc.dram_tensor("hT_scr", [128, 8, NT], BF16, kind="Internal").ap()
    oT_scr = nc.dram_tensor("oT_scr", [8, 128, NT], BF16, kind="Internal").ap()
    wo_bf = nc.dram_tensor("wo_bf", [D, D], BF16, kind="Internal").ap()
    wg_bf = nc.dram_tensor("wg_bf", [D, DFF], BF16, kind="Internal").ap()
    wu_bf = nc.dram_tensor("wu_bf", [D, DFF], BF16, kind="Internal").ap()
    wd_bf = nc.dram_tensor("wd_bf", [DFF, D], BF16, kind="Internal").ap()

    with ExitStack() as es:
        c = Ctx(nc, es)
        SB = lambda st, n, s, d=F32: st.enter_context(nc.sbuf_tensor(n, list(s), d))

        def dump(name, ap, rb):
            if not DEBUG[0] or name in DUMPS:
                return
            shp = list(ap.shape)
            t = nc.dram_tensor("dbg_" + name, shp, ap.dtype, kind="ExternalOutput").ap()
            DUMPS[name] = shp
            c.dma("sp", t, ap, reads=rb, writes=[Buf()])
        PB = [es.enter_context(nc.psum_tensor("pb%d" % i, [128, 512], F32)) for i in range(7)]
        PBb = [Buf(True) for _ in range(7)]
        PTh = [None, None]
        cm = SB(es, "cm", [128, 7, 128]); bcm = Buf()
        c.dma("sp", cm[:].rearrange("p a b -> p (a b)"), cmask, writes=[bcm])
        IDN, UT, SLO, LT, SUP, BLK, ONES = [cm[:, i, :] for i in range(7)]
        sel = SB(es, "sel", [128, 2, 128]); bsel = Buf()
        c.dma("sp", sel[:].rearrange("p a b -> p (a b)"), csel, writes=[bsel])
        seg = SB(es, "seg", [128, 512]); bseg = Buf()
        c.dma("sp", seg[:], cseg, writes=[bseg])
        idb = SB(es, "idb", [128, 128], BF16); bidb = Buf()
        c.op("dve", lambda e: e.tensor_copy(out=idb[:], in_=IDN), reads=[bcm], writes=[bidb])
        modp = SB(es, "modp", [128, 64]); bmod = Buf()
        lbt = SB(es, "lbt", [128, 16]); blb = Buf()
        cw = SB(es, "cw", [128, 36]); bcw = Buf()
        onw = SB(es, "onw", [128, 8]); bonw = Buf()
        c.dma("sp", cw[:], convw_fm, writes=[bcw])
        c.dma("sp", onw[:], onorm_fm, writes=[bonw])
        epsb = SB(es, "epsb", [128, 1]); beps = Buf()
        c.op("dve", lambda e: e.memset(epsb[:], EPS), writes=[beps])

        def rstd_from(ss_ap, out_ap, scale, rb, wb, n=1):
            c.op("act", lambda e: e.activation(out=out_ap, in_=ss_ap, func=AF.Ln, scale=scale, bias=epsb[:, 0:1]),
                 reads=rb + [beps], writes=wb)
            c.op("act", lambda e: e.activation(out=out_ap, in_=out_ap, func=AF.Exp, scale=-0.5), reads=wb, writes=wb)

        try:
            with ExitStack() as s0:
                sv = SB(s0, "sv", [128, 16]); bsv = Buf()
                c.dma("sp", sv[:], svec, writes=[bsv])
                ssil = SB(s0, "ssil", [128, 8, 2]); bss = Buf()
                c.op("act", lambda e: e.activation(out=ssil[:].rearrange("p k v -> p v k"), in_=sv[:].rearrange("p (v k) -> p v k", v=2), func=AF.Silu),
                     reads=[bsv], writes=[bss])
                bfm = SB(s0, "bfm", [128, 48]); bbfm = Buf()
                c.dma("sp", bfm[:], bada_fm, writes=[bbfm])
                n12 = SB(s0, "n12", [128, 16]); bn12 = Buf()
                c.dma("sp", n12[:], n12_fm, writes=[bn12])
                mfm = SB(s0, "mfm", [128, 48, 2]); bmfm = Buf()
                wad = [SB(s0, "wad%d" % i, [128, 8, 512]) for i in range(2)]
                bwad = bufs(2)
                wv = w_ada.rearrange("(k p) n -> p k n", p=128)
                ci = 0
                for cb in (0, 1, 2, 3, 6, 7, 8, 9):
                    t = wad[ci % 2]; bt = bwad[ci % 2]; ci += 1
                    c.dma("sp", t[:], wv[:, :, cb * 512:(cb + 1) * 512], writes=[bt])
                    pb = PB[ci % 2]; bpb = PBb[ci % 2]
                    for jj in range(4):
                        for k in range(8):
                            c.op("pe", lambda e: e.matmul(pb[:, jj * 2:jj * 2 + 2], lhsT=t[:, k, jj * 128:(jj + 1) * 128], rhs=ssil[:, k, :],
                                                          start=(k == 0), stop=(k == 7)), reads=[bt, bss], writes=[bpb])
                    j0 = cb * 4
                    c.op("dve", lambda e: e.tensor_tensor(out=mfm[:, j0:j0 + 4, :], in0=pb[:, 0:8].rearrange("p (j v) -> p j v", v=2),
                                                          in1=bfm[:, j0:j0 + 4].unsqueeze(2).broadcast_to([128, 4, 2]), op=ALU.add),
                         reads=[bpb, bbfm], writes=[bmfm])
                for which, (jsh, jsc, noff) in enumerate(((0, 8, 0), (24, 32, 8))):
                    for v in range(2):
                        o0 = (which * 2 + v) * 16
                        c.op("dve", lambda e: e.scalar_tensor_tensor(out=modp[:, o0:o0 + 8], in0=mfm[:, jsc:jsc + 8, v], scalar=1.0, in1=n12[:, noff:noff + 8],
                                                                     op0=ALU.add, op1=ALU.mult), reads=[bmfm, bn12], writes=[bmod])
                        c.op("dve", lambda e: e.tensor_copy(out=modp[:, o0 + 8:o0 + 16], in_=mfm[:, jsh:jsh + 8, v]), reads=[bmfm], writes=[bmod])
                hl = SB(s0, "hl", [128, 16]); bhl = Buf()
                c.dma("sp", hl[:], hlb_fm, writes=[bhl])
                c.op("dve", lambda e: e.tensor_tensor(out=hl[:, 0:8], in0=hl[:, 0:8], in1=hl[:, 8:16], op=ALU.subtract), reads=[bhl], writes=[bhl])
                c.op("act", lambda e: e.activation(out=lbt[:, 0:8], in_=hl[:, 0:8], func=AF.Sigmoid), reads=[bhl], writes=[blb])
                c.op("act", lambda e: e.activation(out=lbt[:, 8:16], in_=hl[:, 0:8], func=AF.Sigmoid, scale=-1.0), reads=[bhl], writes=[blb])

            c.barrier()
            dump('modp', modp[:], [bmod])
            dump('lbt', lbt[:], [blb])
            _chk('p0')
            A1 = lambda which, v, k: modp[:, (which * 2 + v) * 16 + k:(which * 2 + v) * 16 + k + 1]
            SH = lambda which, v, k: modp[:, (which * 2 + v) * 16 + 8 + k:(which * 2 + v) * 16 + 9 + k]

            bhT = bufs(NTI)
            boT = bufs(8)

            def norm_mod_transpose(st_pool, xt, bxt, which, v, hdst, bh, tmpn):
                ss, bs_, xn, bxn, junk, bj = tmpn
                PT, PTb = PTh
                c.op("pool", lambda e: e.memset(ss[:, 0:1], 0.0), writes=[bs_])
                c.op("act", lambda e: e.activation(out=junk[:], in_=xt, func=AF.Square, accum_out=ss[:, 0:1]), reads=[bxt], writes=[bj, bs_])
                rstd_from(ss[:, 0:1], ss[:, 1:2], 1.0 / D, [bs_], [bs_])
                c.op("act", lambda e: e.activation(out=xn[:], in_=xt, func=AF.Copy, scale=ss[:, 1:2]), reads=[bxt, bs_], writes=[bxn])
                if st_pool == "norm_only":
                    return
                mod_transpose(which, v, hdst, bh, tmpn)

            def mod_transpose(which, v, hdst, bh, tmpn):
                ss, bs_, xn, bxn, junk, bj = tmpn
                PT, PTb = PTh
                for k in range(8):
                    c.op("pe", lambda e: e.transpose(PT[:, k * 128:(k + 1) * 128], xn[:, k * 128:(k + 1) * 128], idb[:]), reads=[bxn, bidb], writes=[PTb])
                for k in range(8):
                    c.op("dve", lambda e: e.tensor_scalar(out=hdst[:, k, :], in0=PT[:, k * 128:(k + 1) * 128], scalar1=A1(which, v, k), scalar2=SH(which, v, k),
                                                          op0=ALU.mult, op1=ALU.add), reads=[PTb, bmod], writes=[bh])

            with ExitStack() as s2:
                SLOT = [SB(s2, "slot%d" % i, [128, NT]) for i in range(5)]
                SLB = [bufs(NTI) for _ in range(5)]
                ZT = SB(s2, "zt", [128, NT], BF16); bZT = bufs(NTI)
                gsm = {n: SB(s2, "g_" + n, [128, 2, NTI, 4]) for n in ("beta", "gg", "gc", "glt", "gl0", "gl1", "eg", "beg", "ekt", "dec0", "dec1")}
                bgs = Buf()
                with ExitStack() as s1:
                    PTh[0] = s1.enter_context(nc.psum_tensor("pbt1", [128, 1024], BF16)); PTh[1] = Buf(True)
                    GTM = SB(s1, "gtm", [128, NTI, 16]); bGTM = Buf()
                    W1 = 4
                    xts = [SB(s1, "xt%d" % i, [128, D]) for i in range(W1)]; bxts = bufs(W1)
                    hts = [SB(s1, "ht%d" % i, [128, 8, 128], BF16) for i in range(W1)]; bhts = bufs(W1)
                    ss1 = [SB(s1, "ss1_%d" % i, [128, 2]) for i in range(W1)]; bss1 = bufs(W1)
                    xn1 = [SB(s1, "xn1_%d" % i, [128, D], BF16) for i in range(W1)]; bxn1 = bufs(W1)
                    jk1 = [SB(s1, "jk1_%d" % i, [128, D], BF16) for i in range(W1)]; bjk1 = bufs(W1)
                    PT1 = [PTh[0]] * 2
                    bPT1 = [PTh[1]] * 2
                    wgf = SB(s1, "wgf", [128, 8, 16]); bwgf = Buf()
                    c.dma("sp", wgf[:], win_g.rearrange("(k p) n -> p k n", p=128), writes=[bwgf])
                    wgb = SB(s1, "wgb", [128, 8, 16], BF16); bwgb = Buf()
                    c.op("dve", lambda e: e.tensor_copy(out=wgb[:], in_=wgf[:]), reads=[bwgf], writes=[bwgb])
                    gT = SLOT[4]
                    bgT = SLB[4]

                    def p1_unit(i, sl):
                        xt = xts[sl]; bx = bxts[sl]; ht = hts[sl]; bh = bhts[sl]
                        ss = ss1[sl]; bs_ = bss1[sl]; xn = xn1[sl]; bxn = bxn1[sl]; junk = jk1[sl]; bj = bjk1[sl]
                        PT = PT1[sl % 2]; PTb = bPT1[sl % 2]
                        v = 1 if i < 32 else 0
                        c.dma("sp", xt[:], xall[i * 128:(i + 1) * 128, :], writes=[bx])
                        c.op("pool", lambda e: e.memset(ss[:, 0:1], 0.0), writes=[bs_])
                        yield
                        c.op("act", lambda e: e.activation(out=junk[:], in_=xt[:], func=AF.Square, accum_out=ss[:, 0:1]), reads=[bx], writes=[bj, bs_])
                        yield
                        c.op("act", lambda e: e.activation(out=ss[:, 1:2], in_=ss[:, 0:1], func=AF.Ln, scale=1.0 / D, bias=epsb[:, 0:1]), reads=[bs_, beps], writes=[bs_])
                        yield
                        c.op("act", lambda e: e.activation(out=ss[:, 1:2], in_=ss[:, 1:2], func=AF.Exp, scale=-0.5), reads=[bs_], writes=[bs_])
                        yield
                        c.op("act", lambda e: e.activation(out=xn[:], in_=xt[:], func=AF.Copy, scale=ss[:, 1:2]), reads=[bx, bs_], writes=[bxn])
                        yield
                        for k in range(8):
                            c.op("pe", lambda e: e.transpose(PT[:, k * 128:(k + 1) * 128], xn[:, k * 128:(k + 1) * 128], idb[:]), reads=[bxn, bidb], writes=[PTb])
                        for k in range(8):
                            eng_ = "dve" if k % 2 == 0 else "pool"
                            if eng_ == "pool":
                                eng_ = "dve"
                            c.op(eng_, lambda e: e.tensor_scalar(out=ht[:, k, :], in0=PT[:, k * 128:(k + 1) * 128], scalar1=A1(0, v, k), scalar2=SH(0, v, k),
                                                                  op0=ALU.mult, op1=ALU.add), reads=[PTb, bmod], writes=[bh])
                        yield
                        pg = PB[2 + sl]; bpg = PBb[2 + sl]
                        for k in range(8):
                            c.op("pe", lambda e: e.matmul(pg[0:16, 0:128], lhsT=wgb[:, k, :], rhs=ht[:, k, :], start=(k == 0), stop=(k == 7)),
                                 reads=[bwgb, bh], writes=[bpg])
                        c.dma("pool", hT_scr[:, :, i * 128:(i + 1) * 128], ht[:], reads=[bh], writes=[bhT[i]])
                        yield
                        c.op("act", lambda e: e.activation(out=gT[0:16, i * 128:(i + 1) * 128], in_=pg[0:16, 0:128], func=AF.Copy), reads=[bpg], writes=[bgT[i]])

                    pending = list(range(NTI))
                    active = []
                    free = list(range(W1))
                    while pending or active:
                        while pending and free:
                            i = pending.pop(0); sl = free.pop(0)
                            active.append((sl, p1_unit(i, sl)))
                        nxt_active = []
                        for sl, g in active:
                            try:
                                next(g)
                                nxt_active.append((sl, g))
                            except StopIteration:
                                free.append(sl)
                        active = nxt_active
                    gTc = SLOT[3]; bgTc = SLB[3]
                    c.op("act", lambda e: e.activation(out=gTc[0:16, 0:4096].rearrange("g (w r) -> g w r", r=64),
                                                       in_=gT[0:16, 0:4096].rearrange("g (r w) -> g w r", w=64), func=AF.Copy), reads=bgT[0:32], writes=bgTc[0:32])
                    c.op("act", lambda e: e.activation(out=gTc[0:16, 4096:NT], in_=gT[0:16, 4096:NT], func=AF.Copy), reads=bgT[32:], writes=bgTc[32:])
                    for j in range(NTI):
                        pg = PB[2 + j % 2]; bpg = PBb[2 + j % 2]
                        src = gTc[0:16, j * 128:(j + 1) * 128]
                        c.op("pe", lambda e: e.transpose(pg[:, 0:16], src, IDN[0:16, 0:16]), reads=[bgTc[j], bcm], writes=[bpg])
                        c.op("act", lambda e: e.activation(out=GTM[:, j, :], in_=pg[:, 0:16], func=AF.Copy), reads=[bpg], writes=[bGTM])
                    al = SB(s1, "al", [128, 2, NTI, 4]); dtb = SB(s1, "dtb", [128, 2, NTI, 4]); bal = Buf()
                    c.dma("sp", al[:].rearrange("p a b c -> p (a b c)"), alog_rep, writes=[bal])
                    c.dma("sp", dtb[:].rearrange("p a b c -> p (a b c)"), dtb_rep, writes=[bal])
                    gview = lambda lo: GTM[:, :, lo:lo + 8].rearrange("p t (d h) -> p d t h", d=2)
                    c.op("act", lambda e: e.activation(out=gsm["beta"][:], in_=gview(0), func=AF.Sigmoid), reads=[bGTM], writes=[bgs])
                    c.op("dve", lambda e: e.tensor_tensor(out=gsm["gg"][:], in0=gview(8), in1=dtb[:], op=ALU.add), reads=[bGTM, bal], writes=[bgs])
                    c.op("act", lambda e: e.activation(out=gsm["gg"][:], in_=gsm["gg"][:], func=AF.Exp), reads=[bgs], writes=[bgs])
                    c.op("act", lambda e: e.activation(out=gsm["gg"][:], in_=gsm["gg"][:], func=AF.Ln, bias=1.0), reads=[bgs], writes=[bgs])
                    c.op("act", lambda e: e.activation(out=al[:], in_=al[:], func=AF.Exp), reads=[bal], writes=[bal])
                    c.op("dve", lambda e: e.scalar_tensor_tensor(out=gsm["gg"][:], in0=gsm["gg"][:], scalar=-1.0, in1=al[:], op0=ALU.mult, op1=ALU.mult),
                         reads=[bgs, bal], writes=[bgs])
                    fl = lambda t: t[:].rearrange("p d t h -> p (d t h)")
                    pq = PB[4]; bpq = PBb[4]
                    for d in range(2):
                        rhs = gsm["gg"][:, d].rearrange("p t h -> p (t h)")
                        c.op("pe", lambda e: e.matmul(pq[:, d * 144:(d + 1) * 144], lhsT=(UT if d == 0 else LT), rhs=rhs, start=True, stop=True),
                             reads=[bgs, bcm], writes=[bpq])
                    c.op("dve", lambda e: e.tensor_copy(out=fl(gsm["gc"]), in_=pq[:, 0:288]), reads=[bpq], writes=[bgs])
                    for nm, lh, bl in (("glt", BLK, bcm), ("gl0", sel[:, 0, :], bsel), ("gl1", sel[:, 1, :], bsel)):
                        c.op("pe", lambda e: e.matmul(pq[:, 0:288], lhsT=lh, rhs=fl(gsm["gg"]), start=True, stop=True), reads=[bgs, bl], writes=[bpq])
                        c.op("dve", lambda e: e.tensor_copy(out=fl(gsm[nm]), in_=pq[:, 0:288]), reads=[bpq], writes=[bgs])
                    c.op("act", lambda e: e.activation(out=fl(gsm["eg"]), in_=fl(gsm["gc"]), func=AF.Exp), reads=[bgs], writes=[bgs])
                    c.op("dve", lambda e: e.tensor_tensor(out=fl(gsm["beg"]), in0=fl(gsm["beta"]), in1=fl(gsm["eg"]), op=ALU.mult), reads=[bgs], writes=[bgs])
                    c.op("dve", lambda e: e.tensor_tensor(out=fl(gsm["ekt"]), in0=fl(gsm["glt"]), in1=fl(gsm["gc"]), op=ALU.subtract), reads=[bgs], writes=[bgs])
                    c.op("act", lambda e: e.activation(out=fl(gsm["ekt"]), in_=fl(gsm["ekt"]), func=AF.Exp), reads=[bgs], writes=[bgs])
                    c.op("act", lambda e: e.activation(out=fl(gsm["dec0"]), in_=fl(gsm["gl0"]), func=AF.Exp), reads=[bgs], writes=[bgs])
                    c.op("act", lambda e: e.activation(out=fl(gsm["dec1"]), in_=fl(gsm["gl1"]), func=AF.Exp), reads=[bgs], writes=[bgs])

                c.barrier()
                for _n in gsm:
                    dump('g_' + _n, gsm[_n][:], [bgs])
                _chk('p1')
                bWo, bWg, bWu, bWd = bufs(8), bufs(8), bufs(8), bufs(NF)

                def issue_bg_precast():
                    for k in range(8):
                        c.dma("bg", wo_bf[k * 128:(k + 1) * 128, :], w_out[k * 128:(k + 1) * 128, :], writes=[bWo[k]])
                    for k in range(8):
                        c.dma("bg", wg_bf[k * 128:(k + 1) * 128, :], w_gate[k * 128:(k + 1) * 128, :], writes=[bWg[k]])
                        c.dma("bg", wu_bf[k * 128:(k + 1) * 128, :], w_up[k * 128:(k + 1) * 128, :], writes=[bWu[k]])
                    for f in range(NF):
                        c.dma("bg", wd_bf[f * 128:(f + 1) * 128, :], w_down[f * 128:(f + 1) * 128, :], writes=[bWd[f]])

                PBK = 256
                NPB = NT // PBK
                hTb = [SB(s2, "htb%d" % i, [128, 8, PBK], BF16) for i in range(2)]; bhTb = bufs(2)
                wbf = SB(s2, "wbf", [128, 8, 640], BF16); bwbf = Buf()
                PB.append(s2.enter_context(nc.psum_tensor("pb7", [128, 512], F32))); PBb.append(Buf(True))
                tmp512 = [SB(s2, "tmpw%d" % i, [128, 512]) for i in range(2)]; btmp512 = bufs(2)
                Sbuf = [SB(s2, "S%d" % i, [128, 128]) for i in range(18)]; bS = bufs(18)
                small = SB(s2, "small", [128, 16]); bsmall = bufs(4)
                s2a = ExitStack()
                TMPN = 0
                tmp = [SB(s2a, "tmp%d" % i, [128, 256]) for i in range(TMPN)]; btmp = bufs(TMPN)
                OF = SB(s2a, "of", [128, NT], BF16); bOF = bufs(NTI)
                seqs = [(0, 32), (32, 34), (34, 36)]
                hblk_state = [0]

                def load_hblk(blk):
                    i = hblk_state[0] % 2; hblk_state[0] += 1
                    c.dma("sp", hTb[i][:], hT_scr[:, :, blk * PBK:(blk + 1) * PBK], reads=bhT[blk * 2:blk * 2 + 2], writes=[bhTb[i]])
                    return hTb[i], bhTb[i]

                def proj_fm(ht, bh, col0, pbi):
                    pb = PB[pbi]; bpb = PBb[pbi]
                    for k in range(8):
                        c.op("pe", lambda e: e.matmul(pb[:, 0:PBK], lhsT=wbf[:, k, col0:col0 + 128], rhs=ht[:, k, :], start=(k == 0), stop=(k == 7)),
                             reads=[bwbf, bh], writes=[bpb])
                    return pb[:, 0:PBK], bpb

                rr = [0]

                def T(n=1):
                    i = rr[0] % TMPN; rr[0] += 1
                    return tmp[i], btmp[i]

                WUA = 6
                HNAMES = ("teg", "tqd", "tkd", "tkt", "tkt2", "tktm", "tat", "tos", "tsq")
                HW_ = {"tqd": 128, "tkd": 128, "tktm": 128, "tat": 128, "tkt": 128, "tkt2": 128, "tsq": 128}
                HBF = ("tqd", "tkd", "tkt2", "tktm", "tat", "tsq")
                hsets = []
                for w_ in range(WUA):
                    hsets.append({n: (SB(s2a, "hu%d_%s" % (w_, n), [128, HW_.get(n, 256)], BF16 if n in HBF else F32), Buf()) for n in HNAMES})
                Sbf = [SB(s2a, "Sbf%d" % i, [128, 128], BF16) for i in range(18)]; bSbf = bufs(18)
                onesb = SB(s2a, "onesb", [128, 128], BF16); bonesb = Buf()
                c.op("dve", lambda e: e.tensor_copy(out=onesb[:], in_=ONES), reads=[bcm], writes=[bonesb])
                VTMb = SLOT[1][:].bitcast(BF16)[:, 0:NT]
                hsm = [SB(s2a, "hsm%d" % w_, [128, 2]) for w_ in range(WUA)]; bhsm = bufs(WUA)
                GA1 = SB(s2a, "ga1", [128, NT]); bGA1 = bufs(NTI)

                def hgrn_unit(h, d, i, slot, ch, kidx, claimed, done):
                    FD, bFD = (FF, bFF) if d == 0 else (FB, bFB)
                    GA, bGA = GAs[d]
                    ts = hsets[slot]
                    pU, bU = PB[slot], PBb[slot]
                    pUb = pU[:, 0:64].bitcast(BF16)
                    MASK = UT if d == 0 else LT
                    glpos = 63 if d == 0 else 0
                    tsl = slice(i * 128, (i + 1) * 128)
                    G3 = GA[:, tsl].rearrange("p (c j) -> p c j", j=64)
                    teg, bteg = ts["teg"]; tqd, btqd = ts["tqd"]; tkd, btkd = ts["tkd"]; tkt, btkt = ts["tkt"]; tkt2, btkt2 = ts["tkt2"]
                    tktm, btktm = ts["tktm"]; tat, btat = ts["tat"]; tos, btos = ts["tos"]; tsq, btsq = ts["tsq"]
                    tdc = hsm[slot]; btdc = bhsm[slot]
                    c.op("act", lambda e: e.activation(out=teg[:, 0:128], in_=GA[:, tsl], func=AF.Exp), reads=[bGA[i]], writes=[bteg])
                    c.op("act", lambda e: e.activation(out=teg[:, 128:256], in_=GA[:, tsl], func=AF.Exp, scale=-1.0), reads=[bGA[i]], writes=[bteg])
                    c.op("dve", lambda e: e.tensor_tensor(out=tkt[:, 0:128].rearrange("p (c j) -> p c j", j=64), in0=G3[:, :, glpos:glpos + 1].broadcast_to([128, 2, 64]),
                                                          in1=G3, op=ALU.subtract), reads=[bGA[i]], writes=[btkt])
                    c.op("act", lambda e: e.activation(out=tdc[:, 0:2], in_=G3[:, :, glpos], func=AF.Exp), reads=[bGA[i]], writes=[btdc])
                    yield
                    c.op("pool", lambda e: e.tensor_tensor(out=tqd[:, 0:128], in0=QT[:, tsl], in1=teg[:, 0:128], op=ALU.mult), reads=[bQT[i], bteg], writes=[btqd])
                    c.op("pool", lambda e: e.tensor_tensor(out=tkd[:, 0:128], in0=FD[:, tsl], in1=teg[:, 128:256], op=ALU.mult), reads=[bFD[i], bteg], writes=[btkd])
                    c.op("act", lambda e: e.activation(out=tkt[:, 0:128], in_=tkt[:, 0:128], func=AF.Exp), reads=[btkt], writes=[btkt])
                    yield
                    c.op("dve", lambda e: e.tensor_tensor(out=tkt2[:, 0:128], in0=FD[:, tsl], in1=tkt[:, 0:128], op=ALU.mult), reads=[bFD[i], btkt], writes=[btkt2])
                    c.op("pe", lambda e: e.matmul(pU[:, 128:256], lhsT=tkd[:, 0:128], rhs=tqd[:, 0:128], start=True, stop=True), reads=[btkd, btqd], writes=[bU])
                    yield
                    c.op("pe", lambda e: e.transpose(pUb, tkt2[:, 0:128], idb[:]), reads=[btkt2, bidb], writes=[bU])
                    yield
                    c.op("dve", lambda e: e.tensor_tensor(out=tat[:, 0:128], in0=pU[:, 128:256], in1=MASK, op=ALU.mult), reads=[bU, bcm], writes=[btat])
                    c.op("dve", lambda e: e.tensor_copy(out=tktm[:, 0:128], in_=pUb), reads=[bU], writes=[btktm])
                    yield
                    corder = (0, 1) if d == 0 else (1, 0)
                    for ci_, cc in enumerate(corder):
                        pb_ = cc * 64
                        ucol = 384 if ci_ == 0 else 0
                        c.op("pe", lambda e: e.matmul(pU[:, ucol:ucol + 128], lhsT=tktm[pb_:pb_ + 64, 0:128], rhs=VTMb[pb_:pb_ + 64, tsl], start=True, stop=True),
                             reads=[btktm, bVTM[i]], writes=[bU], serial=(ci_ == 1))
                        if ci_ == 0:
                            yield
                    yield
                    while ch["turn"] != kidx:
                        yield
                    c.op("pe", lambda e: e.matmul(pU[:, 256:384], lhsT=VTMb[:, tsl], rhs=tat[:, 0:128], start=True, stop=False), reads=[bVTM[i], btat], writes=[bU])
                    for ci_, cc in enumerate(corder):
                        pb_ = cc * 64
                        S, bSc = ch["S"], ch["bS"]
                        Sb_, bSb_ = ch["Sb"], ch["bSb"]
                        ucol = 384 if ci_ == 0 else 0
                        c.op("pe", lambda e: e.matmul(pU[:, 256 + pb_:256 + pb_ + 64], lhsT=Sb_[:], rhs=tqd[:, pb_:pb_ + 64], start=False, stop=(ci_ == 1)),
                             reads=[bSb_, btqd], writes=[bU])
                        ch["si"] += 1
                        Sn, bSn = ch["bufs"][ch["si"] % 3]
                        Sbn, bSbn = ch["bbufs"][ch["si"] % 3]
                        c.op("dve", lambda e: e.scalar_tensor_tensor(out=Sn[:], in0=S[:], scalar=tdc[:, cc:cc + 1], in1=pU[:, ucol:ucol + 128],
                                                                     op0=ALU.mult, op1=ALU.add), reads=[bSc, btdc, bU], writes=[bSn])
                        c.op("act", lambda e: e.activation(out=Sbn[:], in_=Sn[:], func=AF.Copy), reads=[bSn], writes=[bSbn])
                        ch["S"], ch["bS"] = Sn, bSn
                        ch["Sb"], ch["bSb"] = Sbn, bSbn
                        yield
                    if ch["last"] == i and ch["out"] is not None:
                        c.dma("sp", ch["out"], ch["S"][:], reads=[ch["bS"]], writes=[Buf()])
                    ch["turn"] += 1
                    if not claimed[i]:
                        claimed[i] = True
                        c.op("act", lambda e: e.activation(out=OF[:, tsl], in_=pU[:, 256:384], func=AF.Copy), reads=[bU], writes=[bOF[i]])
                        done[i] = True
                        return
                    while not done[i]:
                        yield
                    c.op("dve", lambda e: e.tensor_tensor(out=tos[:, 0:128], in0=pU[:, 256:384], in1=OF[:, tsl], op=ALU.add), reads=[bU, bOF[i]], writes=[btos])
                    yield
                    c.op("act", lambda e: e.activation(out=tsq[:, 0:128], in_=tos[:, 0:128], func=AF.Square), reads=[btos], writes=[btsq])
                    yield
                    c.op("pe", lambda e: e.matmul(pU[:, 128:256], lhsT=onesb[:], rhs=tsq[:, 0:128], start=True, stop=True), reads=[btsq, bonesb], writes=[bU])
                    yield
                    rstd_from(pU[:, 128:256], tos[:, 128:256], 1.0 / 128, [bU], [btos])
                    yield
                    c.op("dve", lambda e: e.tensor_tensor(out=tos[:, 0:128], in0=tos[:, 0:128], in1=tos[:, 128:256], op=ALU.mult), reads=[btos], writes=[btos])
                    yield
                    c.op("dve", lambda e: e.scalar_tensor_tensor(out=ZT[:, tsl], in0=tos[:, 0:128], scalar=onw[:, h:h + 1], in1=ZT[:, tsl],
                                                                 op0=ALU.mult, op1=ALU.mult), reads=[btos, bonw, bZT[i]], writes=[bZT[i]])

                def hgrn_scan(h):
                    claimed = [False] * NTI
                    done = [False] * NTI
                    per_dir = []
                    ci = 0
                    for d in range(2):
                        pend = []
                        for si_, (t0, t1) in enumerate(seqs):
                            bl = [(Sbuf[ci * 3 + i_], bS[ci * 3 + i_]) for i_ in range(3)]
                            ci += 1
                            bbl = [(Sbf[(ci - 1) * 3 + i_], bSbf[(ci - 1) * 3 + i_]) for i_ in range(3)]
                            ch = dict(S=bl[0][0], bS=bl[0][1], Sb=bbl[0][0], bSb=bbl[0][1], si=0, turn=0, bufs=bl, bbufs=bbl, last=(t1 - 1 if d == 0 else t0),
                                      out=(None if t0 == 0 else ns_a[(t0 - 32) // 2, d, h]))
                            if t0 == 0:
                                c.dma("sp", ch["S"][:], st_a[d, h], writes=[ch["bS"]])
                            else:
                                c.op("pool", lambda e: e.memset(ch["S"][:], 0.0), writes=[ch["bS"]])
                            c.op("act", lambda e: e.activation(out=ch["Sb"][:], in_=ch["S"][:], func=AF.Copy), reads=[ch["bS"]], writes=[ch["bSb"]])
                            tl = list(range(t0, t1)) if d == 0 else list(range(t1 - 1, t0 - 1, -1))
                            for kidx, i in enumerate(tl):
                                pend.append((d, i, ch, kidx))
                        per_dir.append(pend[:4] + pend[32:] + pend[4:32])
                    pending = []
                    for a_, b_ in zip(per_dir[0], per_dir[1]):
                        pending.append(a_); pending.append(b_)
                    active = []
                    free = list(range(WUA))
                    while pending or active:
                        while pending and free:
                            d, i, ch, kidx = pending.pop(0)
                            sl = free.pop(0)
                            active.append((sl, hgrn_unit(h, d, i, sl, ch, kidx, claimed, done)))
                        nxt_active = []
                        for sl, g in active:
                            try:
                                next(g)
                                nxt_active.append((sl, g))
                            except StopIteration:
                                free.append(sl)
                        active = nxt_active

                QT, VTM, FF, FB, GA = SLOT
                bQT, bVTM, bFF, bFB, bGA = SLB
                VTM3 = VTM[:].rearrange("p (t v) -> p t v", v=128)
                for h in range(4):
                    for k in range(8):
                        c.dma("pool", wbf[:, k, 0:640], win_a[h, k * 128:(k + 1) * 128, :], writes=[bwbf])
                    for blk in range(NPB):
                        ht, bh = load_hblk(blk)
                        bsl = slice(blk * PBK, (blk + 1) * PBK)
                        tb = list(range(blk * 2, blk * 2 + 2))
                        pbk = (blk % 2) * 4 if False else 0
                        pb, bpb = proj_fm(ht, bh, 0, 0)
                        c.op("act", lambda e: e.activation(out=QT[:, bsl], in_=pb, func=AF.Silu), reads=[bpb], writes=[bQT[t] for t in tb])
                        pb, bpb = proj_fm(ht, bh, 128, 1)
                        c.op("act", lambda e: e.activation(out=FF[:, bsl], in_=pb, func=AF.Sigmoid), reads=[bpb], writes=[bFF[t] for t in tb])
                        pb, bpb = proj_fm(ht, bh, 256, 2)
                        c.op("act", lambda e: e.activation(out=FB[:, bsl], in_=pb, func=AF.Sigmoid), reads=[bpb], writes=[bFB[t] for t in tb])
                        pb, bpb = proj_fm(ht, bh, 512, 3)
                        c.op("act", lambda e: e.activation(out=ZT[:, bsl], in_=pb, func=AF.Silu), reads=[bpb], writes=[bZT[t] for t in tb])
                        pb = PB[4 + blk % 2]; bpb = PBb[4 + blk % 2]
                        for tt in range(2):
                            for k in range(8):
                                c.op("pe", lambda e: e.matmul(pb[:, tt * 128:(tt + 1) * 128], lhsT=ht[:, k, tt * 128:(tt + 1) * 128], rhs=wbf[:, k, 384:512],
                                                              start=(k == 0), stop=(k == 7)), reads=[bwbf, bh], writes=[bpb])
                        c.op("dve", lambda e: e.tensor_copy(out=VTMb[:, bsl], in_=pb[:, 0:PBK]), reads=[bpb], writes=[bVTM[t] for t in tb])
                    _chk('a1')
                    if h == 0:
                        issue_bg_precast()
                    GAs = ((GA, bGA), (GA1, bGA1))
                    for d in range(2):
                        FD = FF if d == 0 else FB
                        bFD = bFF if d == 0 else bFB
                        GAd, bGAd = GAs[d]
                        lbc = lbt[:, d * 4 + h:d * 4 + h + 1]
                        omc = lbt[:, 8 + d * 4 + h:8 + d * 4 + h + 1]
                        for blk in range(NBLK):
                            bsl = slice(blk * 512, (blk + 1) * 512)
                            tb = list(range(blk * 4, blk * 4 + 4))
                            fbufs = [bFD[t] for t in tb]
                            gbufs = [bGAd[t] for t in tb]
                            c.op("dve", lambda e: e.tensor_scalar(out=FD[:, bsl], in0=FD[:, bsl], scalar1=omc, scalar2=lbc, op0=ALU.mult, op1=ALU.add),
                                 reads=fbufs + [blb], writes=fbufs)
                            tw = tmp512[blk % 2]; btw = btmp512[blk % 2]
                            c.op("act", lambda e: e.activation(out=tw[:], in_=FD[:, bsl], func=AF.Ln), reads=fbufs, writes=[btw])
                            if d == 0:
                                c.op("dve", lambda e: e.tensor_tensor_scan(out=GAd[:, bsl], data0=seg[:], data1=tw[:], initial=0.0, op0=ALU.mult, op1=ALU.add),
                                     reads=[btw, bseg], writes=gbufs)
                            else:
                                c.op("dve", lambda e: e.tensor_tensor_scan(out=GAd[:, bsl][:, ::-1], data0=seg[:], data1=tw[:][:, ::-1], initial=0.0,
                                                                           op0=ALU.mult, op1=ALU.add), reads=[btw, bseg], writes=gbufs)
                            c.op("pool", lambda e: e.tensor_scalar(out=FD[:, bsl], in0=FD[:, bsl], scalar1=-1.0, scalar2=1.0, op0=ALU.mult, op1=ALU.add),
                                 reads=fbufs, writes=fbufs)
                    hgrn_scan(h)
                    dump('OF', OF[:], bOF)
                    dump('ZTa', ZT[:], bZT)
                    c.dma("pool", oT_scr[h], ZT[:], reads=bZT, writes=[boT[h]])
                    _chk('a4')

                s2a.close()
                c.barrier()
                _chk('p2a')
                RAW, QN, KN, VC, VT2 = SLOT
                bRAW, bQN, bKN, bVC, bVT2 = SLB
                KTM = RAW; bKTM = bRAW
                OTM = VC; bOTM = bVC
                KTM3 = KTM[:].rearrange("p (t v) -> p t v", v=128)
                VT23 = VT2[:].rearrange("p (t v) -> p t v", v=128)
                OTM3 = OTM[:].rearrange("p (t v) -> p t v", v=128)

                def scanview(arr, j):
                    return arr[:, j * 128:(j + 1) * 128]

                def scantiles(j):
                    return [j]

                def zview(j):
                    if j < 32:
                        return ZT[:, 0:4096].rearrange("p (r w) -> p w r", w=64)[:, 2 * j:2 * j + 2, :]
                    return ZT[:, j * 128:(j + 1) * 128]

                def ztiles(j):
                    return list(range(32)) if j < 32 else [j]

                WU = 6
                TNAMES = ("tkb", "tr", "te", "tdm", "tdm2", "tat", "tul", "X", "tkk", "tvb", "twu", "tqb", "Xb", "twb", "tvnb")
                TW = {"tdm2": 128, "tat": 128, "tvb": 128, "twu": 128, "tqb": 128, "Xb": 128, "twb": 128, "tvnb": 128}
                TBF = ("tat", "tkk", "tvb", "tqb", "Xb", "twb", "tvnb")
                s2b = ExitStack()
                tsets = []
                for w_ in range(WU):
                    tsets.append({n: (SB(s2b, "u%d_%s" % (w_, n), [128, TW.get(n, 256)], BF16 if n in TBF else F32), Buf()) for n in TNAMES})
                Sbf2 = [SB(s2b, "Sbg%d" % i, [128, 128], BF16) for i in range(12)]; bSbf2 = bufs(12)
                smalls = [SB(s2b, "usm%d" % w_, [128, 4]) for w_ in range(WU)]; bsmalls = bufs(WU)

                def gdn_unit(h, d, j, slot, ch, done, claimed, kidx):
                    hg = 4 + h
                    ts = tsets[slot]
                    gs = lambda nm: gsm[nm][:, d, j, h:h + 1]
                    MA, MB = (UT, SLO) if d == 0 else (LT, SUP)
                    CMK, SMK, SMK2 = (UT, SUP, SLO) if d == 0 else (LT, SLO, SUP)
                    pU, bU = PB[slot], PBb[slot]
                    kT = scanview(KN, j); qT = scanview(QN, j)
                    rdK = [bKN[j]]; rdQ = [bQN[j]]
                    tkb, btkb = ts["tkb"]; tr, btr = ts["tr"]; te, bte = ts["te"]; tdm, btdm = ts["tdm"]; tdm2, btdm2 = ts["tdm2"]
                    tat, btat = ts["tat"]; tul, btul = ts["tul"]; X, bX = ts["X"]; tkk, btkk = ts["tkk"]; tvb, btvb = ts["tvb"]; twu, btwu = ts["twu"]
                    tqb, btqb = ts["tqb"]; Xb, bXb = ts["Xb"]; twb, btwb = ts["twb"]; tvnb, btvnb = ts["tvnb"]
                    c.op("pool", lambda e: e.tensor_copy(out=tqb[:], in_=qT), reads=rdQ, writes=[btqb])
                    c.op("dve", lambda e: e.tensor_scalar(out=tkb[:, 0:128], in0=KTM3[:, j, :], scalar1=gs("beta"), scalar2=None, op0=ALU.mult),
                         reads=[bKTM[j], bgs], writes=[btkb])
                    c.op("act", lambda e: e.activation(out=tr[:, 0:128], in_=MA, func=AF.Copy, scale=gs("gg")), reads=[bcm, bgs], writes=[btr])
                    yield
                    c.op("pe", lambda e: e.transpose(pU[:, 384:512], tkb[:, 0:128], IDN), reads=[btkb, bcm], writes=[bU])
                    c.op("pe", lambda e: e.matmul(pU[:, 0:128], lhsT=MB, rhs=tr[:, 0:128], start=True, stop=True), reads=[btr, bcm], writes=[bU])
                    yield
                    c.op("act", lambda e: e.activation(out=tkb[:, 128:256], in_=pU[:, 384:512], func=AF.Copy), reads=[bU], writes=[btkb])
                    kbT = tkb[:, 128:256]
                    c.op("act", lambda e: e.activation(out=te[:, 0:128], in_=pU[:, 0:128], func=AF.Exp), reads=[bU], writes=[bte])
                    yield
                    c.op("pe", lambda e: e.matmul(pU[:, 0:128], lhsT=kT, rhs=qT, start=True, stop=True), reads=rdK + rdQ, writes=[bU])
                    c.op("pe", lambda e: e.matmul(pU[:, 128:256], lhsT=kT, rhs=kbT, start=True, stop=True), reads=rdK + [btkb], writes=[bU])
                    c.op("pool", lambda e: e.tensor_tensor(out=tdm[:, 0:128], in0=te[:, 0:128], in1=CMK, op=ALU.mult), reads=[bte, bcm], writes=[btdm])
                    c.op("pool", lambda e: e.tensor_tensor(out=tdm[:, 128:256], in0=te[:, 0:128], in1=SMK, op=ALU.mult), reads=[bte, bcm], writes=[btdm])
                    yield
                    c.op("dve", lambda e: e.tensor_tensor(out=tat[:, 0:128], in0=pU[:, 0:128], in1=tdm[:, 0:128], op=ALU.mult), reads=[bU, btdm], writes=[btat])
                    c.op("dve", lambda e: e.tensor_tensor(out=tul[:, 0:128], in0=pU[:, 128:256], in1=tdm[:, 128:256], op=ALU.mult), reads=[bU, btdm], writes=[btul])
                    yield
                    c.op("pe", lambda e: e.transpose(pU[:, 256:384], tul[:, 0:128], IDN), reads=[btul, bcm], writes=[bU])
                    yield
                    c.op("act", lambda e: e.activation(out=tul[:, 128:256], in_=pU[:, 256:384], func=AF.Copy), reads=[bU], writes=[btul])
                    yield
                    c.op("dve", lambda e: e.tensor_tensor(out=X[:, 0:128], in0=IDN, in1=tul[:, 0:128], op=ALU.subtract), reads=[bcm, btul], writes=[bX])
                    cur, bcur = tul, btul
                    xs = 0
                    for lev in range(5):
                        nxt, bnxt = ts["te"] if lev % 2 == 0 else ts["tr"]
                        if lev < 4:
                            c.op("pe", lambda e: e.matmul(pU[:, 0:128], lhsT=cur[:, 128:256], rhs=cur[:, 0:128], start=True, stop=True), reads=[bcur], writes=[bU])
                        c.op("pe", lambda e: e.matmul(pU[:, 128:256], lhsT=cur[:, 0:128], rhs=cur[:, 128:256], start=True, stop=True), reads=[bcur], writes=[bU])
                        yield
                        if lev < 4:
                            c.op("act", lambda e: e.activation(out=nxt[:, :], in_=pU[:, 0:256], func=AF.Copy), reads=[bU], writes=[bnxt])
                        else:
                            c.op("act", lambda e: e.activation(out=nxt[:, 128:256], in_=pU[:, 128:256], func=AF.Copy), reads=[bU], writes=[bnxt])
                        yield
                        c.op("pe", lambda e: e.matmul(pU[:, 256:384], lhsT=nxt[:, 128:256], rhs=X[:, xs:xs + 128], start=True, stop=True), reads=[bnxt, bX], writes=[bU])
                        yield
                        c.op("dve", lambda e: e.tensor_tensor(out=X[:, 128 - xs:256 - xs], in0=X[:, xs:xs + 128], in1=pU[:, 256:384], op=ALU.add), reads=[bU, bX], writes=[bX])
                        xs = 128 - xs
                        cur, bcur = nxt, bnxt
                        yield
                    XT = X[:, xs:xs + 128]
                    c.op("act", lambda e: e.activation(out=Xb[:], in_=XT, func=AF.Copy), reads=[bX], writes=[bXb])
                    c.op("act", lambda e: e.activation(out=tkk[:, 0:128], in_=KTM3[:, j, :], func=AF.Copy, scale=gs("beg")), reads=[bKTM[j], bgs], writes=[btkk])
                    c.op("pool", lambda e: e.tensor_scalar(out=tkk[:, 128:256], in0=KTM3[:, j, :], scalar1=gs("ekt"), scalar2=None, op0=ALU.mult), reads=[bKTM[j], bgs], writes=[btkk])
                    c.op("act", lambda e: e.activation(out=tvb[:, 0:128], in_=VT23[:, j, :], func=AF.Copy, scale=gs("beta")), reads=[bVT2[j], bgs], writes=[btvb])
                    yield
                    c.op("pe", lambda e: e.matmul(pU[:, 0:128], lhsT=tkk[:, 0:128], rhs=Xb[:], start=True, stop=True), reads=[btkk, bXb], writes=[bU])
                    c.op("pe", lambda e: e.matmul(pU[:, 128:256], lhsT=Xb[:], rhs=tvb[:, 0:128], start=True, stop=True), reads=[btvb, bXb], writes=[bU])
                    yield
                    c.op("act", lambda e: e.activation(out=twb[:], in_=pU[:, 0:128], func=AF.Copy), reads=[bU], writes=[btwb])
                    c.op("act", lambda e: e.activation(out=twu[:], in_=pU[:, 128:256], func=AF.Copy), reads=[bU], writes=[btwu])
                    yield
                    while ch["turn"] != kidx:
                        yield
                    if not claimed[j]:
                        claimed[j] = True
                        first = True
                    else:
                        first = False
                        while not done[j]:
                            yield
                    tvns = (ts["tkb"], ts["tdm"])
                    corder = (0, 1) if d == 0 else (1, 0)
                    for ci_, cc in enumerate(corder):
                        pr = slice(cc * 64, cc * 64 + 64)
                        tvn, btvn = tvns[ci_]
                        pR, bR = pU, bU
                        S, bSc = ch["S"], ch["bS"]
                        Sb_, bSb_ = ch["Sb"], ch["bSb"]
                        c.op("pe", lambda e: e.matmul(pR[:, 0:128], lhsT=twb[:], rhs=Sb_[:], start=True, stop=True), reads=[btwb, bSb_], writes=[bR])
                        c.op("pe", lambda e: e.matmul(pR[:, 128:256], lhsT=tqb[:], rhs=Sb_[:], start=True, stop=True), reads=[btqb, bSb_], writes=[bR])
                        yield
                        c.op("dve", lambda e: e.tensor_tensor(out=tvnb[pr, :], in0=twu[pr, :], in1=pR[pr, 0:128], op=ALU.subtract), reads=[btwu, bR], writes=[btvnb])
                        yield
                        c.op("pe", lambda e: e.matmul(pR[:, 256:384], lhsT=tat[pr, 0:128], rhs=tvnb[pr, :], start=True, stop=True), reads=[btat, btvnb], writes=[bR])
                        c.op("pe", lambda e: e.matmul(pR[:, 384:512], lhsT=tkk[pr, 128:256], rhs=tvnb[pr, :], start=True, stop=True), reads=[btkk, btvnb], writes=[bR])
                        yield
                        ch["si"] += 1
                        Sn, bSn = ch["bufs"][ch["si"] % 3]
                        Sbn, bSbn = ch["bbufs"][ch["si"] % 2]
                        c.op("dve", lambda e: e.scalar_tensor_tensor(out=Sn[:], in0=S[:], scalar=gs("dec%d" % cc), in1=pR[:, 384:512], op0=ALU.mult, op1=ALU.add),
                             reads=[bSc, bgs, bR], writes=[bSn])
                        c.op("act", lambda e: e.activation(out=Sbn[:], in_=Sn[:], func=AF.Copy), reads=[bSn], writes=[bSbn])
                        ch["S"], ch["bS"] = Sn, bSn
                        ch["Sb"], ch["bSb"] = Sbn, bSbn
                        c.op("dve", lambda e: e.tensor_scalar(out=tvn[pr, 128:256], in0=pR[pr, 128:256], scalar1=gsm["eg"][pr, d, j, h:h + 1], scalar2=None, op0=ALU.mult),
                             reads=[bR, bgs], writes=[btvn])
                        c.op("dve", lambda e: e.tensor_tensor(out=tvn[pr, 128:256], in0=tvn[pr, 128:256], in1=pR[pr, 256:384], op=ALU.add), reads=[btvn, bR], writes=[btvn])
                        if first:
                            c.op("act", lambda e: e.activation(out=OTM3[pr, j, :], in_=tvn[pr, 128:256], func=AF.Copy), reads=[btvn], writes=[bOTM[j]])
                        else:
                            c.op("pool", lambda e: e.tensor_tensor(out=OTM3[pr, j, :], in0=OTM3[pr, j, :], in1=tvn[pr, 128:256], op=ALU.add), reads=[btvn, bOTM[j]], writes=[bOTM[j]])
                        yield
                    if ch["last"] == j and ch["out"] is not None:
                        c.dma("sp", ch["out"], ch["S"][:], reads=[ch["bS"]], writes=[Buf()])
                    ch["turn"] += 1
                    if first:
                        done[j] = True
                        return
                    tn, btn = ts["te"]
                    sm = smalls[slot]; bsm = bsmalls[slot]
                    c.op("pool", lambda e: e.memset(sm[:, 0:1], 0.0), writes=[bsm])
                    c.op("act", lambda e: e.activation(out=tn[:, 0:128], in_=OTM3[:, j, :], func=AF.Square, accum_out=sm[:, 0:1]), reads=[bOTM[j]], writes=[btn, bsm])
                    yield
                    rstd_from(sm[:, 0:1], sm[:, 1:2], 1.0 / 128, [bsm], [bsm])
                    yield
                    c.op("dve", lambda e: e.tensor_scalar(out=tn[:, 128:256], in0=OTM3[:, j, :], scalar1=sm[:, 1:2], scalar2=None, op0=ALU.mult), reads=[bOTM[j], bsm], writes=[btn])
                    yield
                    c.op("pe", lambda e: e.transpose(pU[:, 256:384], tn[:, 128:256], IDN), reads=[btn, bcm], writes=[bU])
                    yield
                    zv = zview(j)
                    zb = [bZT[t] for t in ztiles(j)]
                    pin = pU[:, 256:384].rearrange("p (c r) -> p c r", c=2) if j < 32 else pU[:, 256:384]
                    c.op("dve", lambda e: e.scalar_tensor_tensor(out=zv, in0=pin, scalar=onw[:, hg:hg + 1], in1=zv, op0=ALU.mult, op1=ALU.mult),
                         reads=[bU, bonw] + zb, writes=zb)

                def gdn_scan(h):
                    done = [False] * NTI
                    claimed = [False] * NTI
                    chains = {}
                    order = []
                    ci = 0
                    for si_, (t0, t1) in enumerate(seqs):
                        for d in range(2):
                            bl = [(Sbuf[ci * 3 + i], bS[ci * 3 + i]) for i in range(3)]
                            bbl = [(Sbf2[ci * 2 + i], bSbf2[ci * 2 + i]) for i in range(2)]
                            ch = dict(S=bl[0][0], bS=bl[0][1], Sb=bbl[0][0], bSb=bbl[0][1], si=0, turn=0, t0=t0, t1=t1, bufs=bl, bbufs=bbl,
                                      last=(t1 - 1 if d == 0 else t0), out=(None if t0 == 0 else ns_b[(t0 - 32) // 2, d, h]))
                            if t0 == 0:
                                c.dma("sp", ch["S"][:], st_b[d, h], writes=[ch["bS"]])
                            else:
                                c.op("pool", lambda e: e.memset(ch["S"][:], 0.0), writes=[ch["bS"]])
                            c.op("act", lambda e: e.activation(out=ch["Sb"][:], in_=ch["S"][:], func=AF.Copy), reads=[ch["bS"]], writes=[ch["bSb"]])
                            chains[(si_, d)] = ch
                            ci += 1
                    samp = []
                    for i in range(32):
                        samp.append((0, 0, i)); samp.append((0, 1, 31 - i))
                    pro = []
                    for si_ in (1, 2):
                        t0, t1 = seqs[si_]
                        for i in range(t1 - t0):
                            pro.append((si_, 0, t0 + i)); pro.append((si_, 1, t1 - 1 - i))
                    order = samp[:8] + pro + samp[8:]
                    pending = list(order)
                    active = []
                    free = list(range(WU))
                    while pending or active:
                        while pending and free:
                            si_, d, j = pending.pop(0)
                            sl = free.pop(0)
                            ch_ = chains[(si_, d)]
                            kidx = (j - ch_["t0"]) if d == 0 else (ch_["t1"] - 1 - j)
                            active.append((sl, gdn_unit(h, d, j, sl, ch_, done, claimed, kidx)))
                        nxt_active = []
                        for sl, g in active:
                            try:
                                next(g)
                                nxt_active.append((sl, g))
                            except StopIteration:
                                free.append(sl)
                        active = nxt_active

                for h in range(4):
                    hg = 4 + h
                    for k in range(8):
                        c.dma("pool", wbf[:, k, 0:512], win_b[h, k * 128:(k + 1) * 128, :], writes=[bwbf])
                    for blk in range(NPB):
                        ht, bh = load_hblk(blk)
                        bsl = slice(blk * PBK, (blk + 1) * PBK)
                        tb = list(range(blk * 2, blk * 2 + 2))
                        for qi, (DST, bDST) in enumerate(((QN, bQN), (KN, bKN), (VC, bVC))):
                            pb, bpb = proj_fm(ht, bh, qi * 128, qi)
                            if qi == 1:
                                c.op("dve", lambda e: e.tensor_copy(out=DST[:, bsl], in_=pb), reads=[bpb], writes=[bDST[t] for t in tb])
                            else:
                                c.op("act", lambda e: e.activation(out=DST[:, bsl], in_=pb, func=AF.Copy), reads=[bpb], writes=[bDST[t] for t in tb])
                        pb, bpb = proj_fm(ht, bh, 384, 3)
                        c.op("act", lambda e: e.activation(out=ZT[:, bsl], in_=pb, func=AF.Silu), reads=[bpb], writes=[bZT[t] for t in tb])
                    for qi, (DST, bDST) in enumerate(((QN, bQN), (KN, bKN), (VC, bVC))):
                        w0 = cw[:, h * 9 + qi * 3 + 0:h * 9 + qi * 3 + 1]
                        w1 = cw[:, h * 9 + qi * 3 + 1:h * 9 + qi * 3 + 2]
                        w2 = cw[:, h * 9 + qi * 3 + 2:h * 9 + qi * 3 + 3]
                        ceng = "dve"
                        RAWq, bRAWq = (RAW, bRAW) if qi != 1 else (VT2, bVT2)
                        c.op("act", lambda e: e.activation(out=RAWq[:, :], in_=DST[:, :], func=AF.Copy, scale=w1), reads=bDST + [bcw], writes=bRAWq)
                        for (lo, hi, sh) in ((0, 4096, 64), (4096, 4352, 1), (4352, 4608, 1)):
                            c.op(ceng, lambda e: e.scalar_tensor_tensor(out=RAWq[:, lo + sh:hi], in0=DST[:, lo:hi - sh], scalar=w0, in1=RAWq[:, lo + sh:hi],
                                                                         op0=ALU.mult, op1=ALU.add), reads=bDST + [bcw], writes=bRAWq)
                            c.op(ceng, lambda e: e.scalar_tensor_tensor(out=RAWq[:, lo:hi - sh], in0=DST[:, lo + sh:hi], scalar=w2, in1=RAWq[:, lo:hi - sh],
                                                                         op0=ALU.mult, op1=ALU.add), reads=bDST + [bcw], writes=bRAWq)
                        c.op("act", lambda e: e.activation(out=DST[:, 0:4096].rearrange("p (w r) -> p w r", r=64),
                                                           in_=RAWq[:, 0:4096].rearrange("p (r w) -> p w r", w=64), func=AF.Silu), reads=bRAWq[0:32], writes=bDST[0:32])
                        c.op("act", lambda e: e.activation(out=DST[:, 4096:NT], in_=RAWq[:, 4096:NT], func=AF.Silu), reads=bRAWq[32:], writes=bDST[32:])
                        if qi == 2:
                            for j in range(NTI):
                                p5 = PB[4 + j % 2]; b5 = PBb[4 + j % 2]
                                c.op("pe", lambda e: e.transpose(p5[:, 128:256], DST[:, j * 128:(j + 1) * 128], IDN), reads=[bDST[j], bcm], writes=[b5])
                                c.op("dve", lambda e: e.tensor_copy(out=VT23[:, j, :], in_=p5[:, 128:256]), reads=[b5], writes=[bVT2[j]])
                            continue
                        for blk in range(NBLK):
                            bsl = slice(blk * 512, (blk + 1) * 512)
                            db = [bDST[t] for t in range(blk * 4, blk * 4 + 4)]
                            tw = tmp512[blk % 2]; btw = btmp512[blk % 2]
                            c.op("act", lambda e: e.activation(out=tw[:], in_=DST[:, bsl], func=AF.Square), reads=db, writes=[btw])
                            pb = PB[4 + blk % 2]; bpb = PBb[4 + blk % 2]
                            c.op("pe", lambda e: e.matmul(pb[:, :], lhsT=ONES, rhs=tw[:], start=True, stop=True), reads=[btw, bcm], writes=[bpb])
                            rstd_from(pb[:, :], tw[:], 1.0, [bpb], [btw])
                            if qi == 0:
                                c.op("dve", lambda e: e.scalar_tensor_tensor(out=DST[:, bsl], in0=DST[:, bsl], scalar=float(128 ** -0.5), in1=tw[:],
                                                                             op0=ALU.mult, op1=ALU.mult), reads=db + [btw], writes=db)
                            else:
                                c.op("dve", lambda e: e.tensor_tensor(out=DST[:, bsl], in0=DST[:, bsl], in1=tw[:], op=ALU.mult), reads=db + [btw], writes=db)
                    for j in range(NTI):
                        p5 = PB[4 + j % 2]; b5 = PBb[4 + j % 2]
                        c.op("pe", lambda e: e.transpose(p5[:, 0:128], scanview(KN, j), IDN), reads=[bKN[j], bcm], writes=[b5])
                        c.op("act", lambda e: e.activation(out=KTM3[:, j, :], in_=p5[:, 0:128], func=AF.Copy), reads=[b5], writes=[bKTM[j]])
                    gdn_scan(h)
                    c.dma("pool", oT_scr[hg], ZT[:], reads=bZT, writes=[boT[hg]])

                s2b.close()
            PB.pop(); PBb.pop()
            c.barrier()
            _chk('p2b')
            with ExitStack() as s4:
                PTh[0] = s4.enter_context(nc.psum_tensor("pbt4", [128, 1024], BF16)); PTh[1] = Buf(True)
                wo = SB(s4, "wo", [128, 8, D], BF16); bwo = Buf()
                wg = SB(s4, "wg", [128, 8, DFF], BF16); bwg = Buf()
                wu = SB(s4, "wu", [128, 8, DFF], BF16); bwu = Buf()
                wd = SB(s4, "wd", [128, NF, D], BF16); bwd = Buf()
                for k in range(8):
                    c.dma("pool", wo[:, k, :], wo_bf[k * 128:(k + 1) * 128, :], reads=[bWo[k]], writes=[bwo])
                for k in range(8):
                    c.dma("pool", wg[:, k, :], wg_bf[k * 128:(k + 1) * 128, :], reads=[bWg[k]], writes=[bwg])
                    c.dma("pool", wu[:, k, :], wu_bf[k * 128:(k + 1) * 128, :], reads=[bWu[k]], writes=[bwu])
                for f in range(NF):
                    c.dma("pool", wd[:, f, :], wd_bf[f * 128:(f + 1) * 128, :], reads=[bWd[f]], writes=[bwd])
                gbc = SB(s4, "gbc", [128, 4, D]); bgbc = Buf()
                nfb = SB(s4, "nfb", [128, D]); bnfb = Buf()
                c.dma("sp", nfb[:], normf.partition_broadcast(128), writes=[bnfb])
                with ExitStack() as s40:
                    sv = SB(s40, "sv4", [128, 16]); bsv = Buf()
                    c.dma("sp", sv[:], svec, writes=[bsv])
                    c.op("act", lambda e: e.activation(out=sv[:], in_=sv[:], func=AF.Silu), reads=[bsv], writes=[bsv])
                    srep = SB(s40, "srep", [128, 16, 128]); bsrep = Buf()
                    c.op("dve", lambda e: e.tensor_copy(out=srep[:], in_=sv[:].unsqueeze(2).broadcast_to([128, 16, 128])), reads=[bsv], writes=[bsrep])
                    brow = SB(s40, "brow", [128, 512]); bbrow = Buf()
                    wad = [SB(s40, "wad40", [128, 8, 512])] * 2; bwad = [Buf()] * 2
                    wv = w_ada.rearrange("(k p) n -> p k n", p=128)
                    ci = 0
                    for gi, cb0 in enumerate((4, 10)):
                        for hh in range(2):
                            cb = cb0 + hh
                            t = wad[ci % 2]; bt = bwad[ci % 2]; ci += 1
                            c.dma("sp", t[:], wv[:, :, cb * 512:(cb + 1) * 512], writes=[bt])
                            c.dma("sp", brow[:], bada_row[:, cb * 512:(cb + 1) * 512].partition_broadcast(128), writes=[bbrow])
                            for v in range(2):
                                pb = PB[v]; bpb = PBb[v]
                                for k in range(8):
                                    c.op("pe", lambda e: e.matmul(pb[:, :], lhsT=srep[:, v * 8 + k, :], rhs=t[:, k, :], start=(k == 0), stop=(k == 7)), reads=[bsrep, bt], writes=[bpb])
                                c.op("dve", lambda e: e.tensor_tensor(out=gbc[:, gi * 2 + v, hh * 512:(hh + 1) * 512], in0=pb[:, :], in1=brow[:], op=ALU.add),
                                     reads=[bpb, bbrow], writes=[bgbc])
                c.barrier(skip=("dpool",))
                BT = 128
                oTb = [SB(s4, "otb0", [128, 8, BT], BF16)] * 2; boTb = [Buf()] * 2
                xts = [SB(s4, "x4%d" % i, [128, D]) for i in range(2)]; bxts = bufs(2)
                x1s = [SB(s4, "x1%d" % i, [128, D]) for i in range(2)]; bx1s = bufs(2)
                h2s = [SB(s4, "h20", [128, 8, BT], BF16)] * 2; bh2s = [Buf()] * 2
                aT = SB(s4, "aT", [128, NF, BT], BF16); baT = bufs(NF)
                ss = SB(s4, "ss4", [128, 4]); xn = SB(s4, "xn4", [128, D], BF16); junk = SB(s4, "junk4", [128, D], BF16)
                tmpn = (ss, Buf(), xn, Buf(), junk, Buf())
                tsg = [SB(s4, "tsg%d" % i, [128, BT]) for i in range(2)]; btsg = bufs(2)
                byout = Buf()
                nown_blk = NOWN // BT
                nsamp_blk = 2048 // BT
                vof = lambda blk: 1 if blk < nsamp_blk else 0

                def st_op(blk):
                    tok0 = blk * BT if blk < nsamp_blk else 4096 + (blk - nsamp_blk) * BT
                    v = vof(blk)
                    ob = oTb[blk % 2]; bob = boTb[blk % 2]
                    xt = xts[blk % 2]; bx = bxts[blk % 2]; x1 = x1s[blk % 2]; bx1 = bx1s[blk % 2]
                    c.dma("sp", ob[:], oT_scr[:, :, tok0:tok0 + BT].rearrange("h p t -> p h t"), reads=boT, writes=[bob])
                    c.dma("sp", xt[:], xall[tok0:tok0 + 128, :], writes=[bx])
                    for hh in range(2):
                        pb = PB[hh]; bpb = PBb[hh]
                        for hd in range(8):
                            c.op("pe", lambda e: e.matmul(pb[:, :], lhsT=ob[:, hd, :], rhs=wo[:, hd, hh * 512:(hh + 1) * 512],
                                                          start=(hd == 0), stop=(hd == 7)), reads=[bob, bwo], writes=[bpb])
                        hs = slice(hh * 512, (hh + 1) * 512)
                        c.op("dve", lambda e: e.tensor_tensor(out=x1[:, hs], in0=pb[:, :], in1=gbc[:, v, hs], op=ALU.mult), reads=[bpb, bgbc], writes=[bx1])
                        c.op("pool", lambda e: e.tensor_tensor(out=x1[:, hs], in0=x1[:, hs], in1=xt[:, hs], op=ALU.add), reads=[bx1, bx], writes=[bx1])
                    norm_mod_transpose("norm_only", x1[:], bx1, 1, v, None, None, tmpn)

                def st_tr(blk):
                    mod_transpose(1, vof(blk), h2s[blk % 2], bh2s[blk % 2], tmpn)

                def st_gu(blk):
                    h2 = h2s[blk % 2]; bh2 = bh2s[blk % 2]
                    for f in range(NF):
                        pg = PB[2 + f % 2]; bpg = PBb[2 + f % 2]
                        pu = PB[4 + f % 2]; bpu = PBb[4 + f % 2]
                        for k in range(8):
                            c.op("pe", lambda e: e.matmul(pg[:, 0:BT], lhsT=wg[:, k, f * 128:(f + 1) * 128], rhs=h2[:, k, :], start=(k == 0), stop=(k == 7)),
                                 reads=[bwg, bh2], writes=[bpg])
                        for k in range(8):
                            c.op("pe", lambda e: e.matmul(pu[:, 0:BT], lhsT=wu[:, k, f * 128:(f + 1) * 128], rhs=h2[:, k, :], start=(k == 0), stop=(k == 7)),
                                 reads=[bwu, bh2], writes=[bpu])
                        tg = tsg[f % 2]; btg = btsg[f % 2]
                        c.op("act", lambda e: e.activation(out=tg[:], in_=pg[:, 0:BT], func=AF.Silu), reads=[bpg], writes=[btg])
                        c.op("dve", lambda e: e.tensor_tensor(out=aT[:, f, :], in0=tg[:], in1=pu[:, 0:BT], op=ALU.mult), reads=[btg, bpu], writes=[baT[f]])

                def st_down(blk):
                    v = vof(blk)
                    x1 = x1s[blk % 2]; bx1 = bx1s[blk % 2]
                    yo = xts[blk % 2]; byo = bxts[blk % 2]
                    for hh in range(2):
                        pb = PB[hh]; bpb = PBb[hh]
                        for f in range(NF):
                            c.op("pe", lambda e: e.matmul(pb[:, :], lhsT=aT[:, f, :], rhs=wd[:, f, hh * 512:(hh + 1) * 512],
                                                          start=(f == 0), stop=(f == NF - 1)), reads=[baT[f], bwd], writes=[bpb])
                        hs = slice(hh * 512, (hh + 1) * 512)
                        c.op("dve", lambda e: e.tensor_tensor(out=yo[:, hs], in0=pb[:, :], in1=gbc[:, 2 + v, hs], op=ALU.mult), reads=[bpb, bgbc], writes=[byo])
                        c.op("pool", lambda e: e.tensor_tensor(out=yo[:, hs], in0=yo[:, hs], in1=x1[:, hs], op=ALU.add), reads=[byo, bx1], writes=[byo])
                    bs_ = tmpn[1]
                    c.op("pool", lambda e: e.memset(ss[:, 2:3], 0.0), writes=[bs_])
                    c.op("act", lambda e: e.activation(out=junk[:], in_=yo[:], func=AF.Square, accum_out=ss[:, 2:3]), reads=[byo], writes=[tmpn[5], bs_])
                    rstd_from(ss[:, 2:3], ss[:, 3:4], 1.0 / D, [bs_], [bs_])
                    c.op("dve", lambda e: e.scalar_tensor_tensor(out=yo[:], in0=yo[:], scalar=ss[:, 3:4], in1=nfb[:], op0=ALU.mult, op1=ALU.mult),
                         reads=[byo, bs_, bnfb], writes=[byo])
                    r0 = blk * BT
                    c.dma("sp", y_own[r0:r0 + 128, :], yo[:], reads=[byo], writes=[byout])

                st_op(0)
                st_tr(0)
                for blk in range(nown_blk):
                    if blk + 1 < nown_blk:
                        st_op(blk + 1)
                    st_gu(blk)
                    if blk + 1 < nown_blk:
                        st_tr(blk + 1)
                    st_down(blk)
        except _Stop:
            pass
        for q in NDSEM:
            for nm in c.dnames[q]:
                if c.cnt[nm]:
                    nc.sync.wait_ge(c.sem[nm], c.cnt[nm])


_CONST = {}


def _consts():
    if _CONST:
        return _CONST
    u = np.arange(128)[:, None]
    t = np.arange(128)[None, :]
    same = (u // 64) == (t // 64)
    ident = (u == t)
    UT = same & (u <= t)
    SLO = same & (u > t)
    LT = same & (u >= t)
    SUP = same & (u < t)
    ones = np.ones((128, 128), bool)
    cm = np.stack([ident, UT, SLO, LT, SUP, same, ones], axis=1).astype(np.float32).reshape(128, 7 * 128)
    sel = np.stack([np.broadcast_to(u < 64, (128, 128)), np.broadcast_to(u >= 64, (128, 128))], axis=1).astype(np.float32).reshape(128, 256)
    seg = np.ones((128, 512), np.float32)
    seg[:, ::64] = 0.0
    _CONST.update(cmask=np.ascontiguousarray(cm), csel=np.ascontiguousarray(sel), cseg=seg)
    return _CONST


def _fm(vec):
    return np.ascontiguousarray(np.asarray(vec, np.float32).reshape(-1, 128).T)


_PROG = {}


def kernel(x_prompt, x_sample, c, state_hgrn, state_gdn, c_ctx, w_ada, b_ada, norm1, norm2, w_in, conv_w, hgrn_lb,
           gdn_A_log, gdn_dt_bias, hgrn_out_norm, gdn_out_norm, w_out, w_gate, w_up, w_down, norm_f):
    f32 = lambda a: np.ascontiguousarray(np.asarray(a, dtype=np.float32))
    x_prompt, x_sample, c, state_hgrn, state_gdn, c_ctx = map(f32, (x_prompt, x_sample, c, state_hgrn, state_gdn, c_ctx))
    w_ada, b_ada, norm1, norm2, w_in, conv_w, hgrn_lb = map(f32, (w_ada, b_ada, norm1, norm2, w_in, conv_w, hgrn_lb))
    gdn_A_log, gdn_dt_bias, hgrn_out_norm, gdn_out_norm = map(f32, (gdn_A_log, gdn_dt_bias, hgrn_out_norm, gdn_out_norm))
    w_out, w_gate, w_up, w_down, norm_f = map(f32, (w_out, w_gate, w_up, w_down, norm_f))
    if "nc" not in _PROG:
        _PROG["nc"] = build_program()
    nc = _PROG["nc"]
    cst = _consts()
    W = w_in[0]
    offs = np.cumsum([0, 512, 512, 512, 512, 512, 512, 512, 512, 512, 8, 8])
    a_q, a_ff, a_fb, a_i, a_g, b_q, b_k, b_v, b_z, b_beta, b_a = [W[:, offs[i]:offs[i + 1]] for i in range(11)]
    a_f = [a_ff, a_fb]
    in_maps = []
    for core in range(8):
        p, e = core // 2, core % 2
        ds = [0, 1] if e == 0 else [1, 0]
        fl = (lambda a: a[::-1]) if e else (lambda a: a)
        pr = [4 * p + 2 * e, 4 * p + 2 * e + 1]
        xall = np.concatenate([fl(x_sample[p]), fl(x_prompt[pr[0]]), fl(x_prompt[pr[1]])], axis=0)
        hs = lambda a, h: a[:, h * 128:(h + 1) * 128]
        win_a = np.stack([np.concatenate([hs(a_q, h), hs(a_f[ds[0]], h), hs(a_f[ds[1]], h), hs(a_i, h), hs(a_g, h)], axis=1) for h in range(4)])
        win_b = np.stack([np.concatenate([hs(b_q, h), hs(b_k, h), hs(b_v, h), hs(b_z, h)], axis=1) for h in range(4)])
        win_g = np.concatenate([b_beta[:, ds[0] * 4:ds[0] * 4 + 4], b_beta[:, ds[1] * 4:ds[1] * 4 + 4],
                                b_a[:, ds[0] * 4:ds[0] * 4 + 4], b_a[:, ds[1] * 4:ds[1] * 4 + 4]], axis=1)
        hlb = np.stack([np.stack([_fm(hgrn_lb[l, ds[d]]) for d in range(2)], axis=1) for l in range(2)], axis=1)
        cwt = conv_w[0][::-1] if e else conv_w[0]
        convw = np.zeros((128, 4, 3, 3), np.float32)
        for h in range(4):
            for qi in range(3):
                convw[:, h, qi, :] = cwt[:, qi * 512 + h * 128:qi * 512 + (h + 1) * 128].T
        alog = np.broadcast_to(gdn_A_log[0][ds][:, None, :], (2, NTI, 4)).reshape(1, -1)
        dtb = np.broadcast_to(gdn_dt_bias[0][ds][:, None, :], (2, NTI, 4)).reshape(1, -1)
        m = dict(
            xall=np.ascontiguousarray(xall),
            svec=np.ascontiguousarray(np.concatenate([_fm(c_ctx), _fm(c[p])], axis=1)),
            w_ada=w_ada[0], bada_fm=_fm(b_ada[0]), bada_row=b_ada[0][None, :],
            n12_fm=np.ascontiguousarray(np.concatenate([_fm(norm1[0]), _fm(norm2[0])], axis=1)),
            normf=norm_f[None, :],
            win_a=np.ascontiguousarray(win_a), win_b=np.ascontiguousarray(win_b), win_g=np.ascontiguousarray(win_g),
            hlb_fm=np.ascontiguousarray(hlb.reshape(128, 16)),
            convw_fm=np.ascontiguousarray(convw.reshape(128, 36)),
            alog_rep=np.ascontiguousarray(np.broadcast_to(alog, (128, 288))),
            dtb_rep=np.ascontiguousarray(np.broadcast_to(dtb, (128, 288))),
            onorm_fm=np.ascontiguousarray(np.concatenate([_fm(hgrn_out_norm[0]), _fm(gdn_out_norm[0])], axis=1)),
            st_a=np.ascontiguousarray(state_hgrn[p, 0][ds]), st_b=np.ascontiguousarray(state_gdn[p, 0][ds]),
            w_out=w_out[0], w_gate=w_gate[0], w_up=w_up[0], w_down=w_down[0],
            cmask=cst["cmask"], csel=cst["csel"], cseg=cst["cseg"],
        )
        in_maps.append(m)
    if _PROG.get('debug_hook'):
        return _PROG['debug_hook'](nc, in_maps)
    res = run_bass_kernel_spmd(nc, in_maps, core_ids=list(range(8)))
    y_prompt = np.zeros((16, 256, D), np.float32)
    y_sample = np.zeros((4, 4096, D), np.float32)
    nsa = np.zeros((16, 1, 2, 4, 128, 128), np.float32)
    nsb = np.zeros((16, 1, 2, 4, 128, 128), np.float32)
    for core in range(8):
        p, e = core // 2, core % 2
        ds = [0, 1] if e == 0 else [1, 0]
        r = res.results[core]
        yo = np.asarray(r["y_own"], np.float32)
        if e == 0:
            y_sample[p, 0:2048] = yo[0:2048]
        else:
            y_sample[p, 2048:4096] = yo[0:2048][::-1]
        for j in range(2):
            seq = 4 * p + 2 * e + j
            blk = yo[2048 + 256 * j:2048 + 256 * (j + 1)]
            y_prompt[seq] = blk[::-1] if e else blk
            for d in range(2):
                nsa[seq, 0, ds[d]] = np.asarray(r["ns_a"], np.float32)[j, d]
                nsb[seq, 0, ds[d]] = np.asarray(r["ns_b"], np.float32)[j, d]
    return (y_prompt, y_sample, nsa, nsb)
```
